# Optimizing a Trainium2 kernel written in Bass

```python
import math
import jax
import jax.numpy as jnp
from jax import lax
import numpy as np

D_MODEL = 1024
BATCH = 1
SEQ = 16384
DEPTH = 2
DEC_BATCH = 8
DEC_SEQ = 4096
PAST_LEN = 128

N_MEM = 256
N_MIXERS = 2
N_GLA_LAYERS = (DEPTH + N_MIXERS - 1) // N_MIXERS
N_GDN_LAYERS = DEPTH // N_MIXERS
CHUNK = 64

GLA_HEADS = 4
GLA_DK = D_MODEL // 2 // GLA_HEADS
GLA_DV = D_MODEL // GLA_HEADS
GLA_RANK = 16
GLA_GATE_NORMALIZER = 16.0

GDN_QK_HEADS = D_MODEL // 128
GDN_V_HEADS = 2 * GDN_QK_HEADS
GDN_DK = 128
GDN_DV = 128
GDN_CONV = 4
GDN_CONV_PAD = (2, 1)

XA_HEADS = 4
XA_DH = D_MODEL // XA_HEADS

D_FF = 2816

ALPHA = (2.0 * DEPTH) ** 0.25
BETA_INIT = (8.0 * DEPTH) ** -0.25
LN_EPS = 1e-5
NORM_EPS = 1e-6

kernel_name = 'hybrid_gla_gdn_macaron_encoder'


def layer_norm(x, g, b):
    xf = x.astype(jnp.float32)
    mu = jnp.mean(xf, axis=-1, keepdims=True)
    var = jnp.mean(jnp.square(xf - mu), axis=-1, keepdims=True)
    return ((xf - mu) * lax.rsqrt(var + LN_EPS) * g.astype(jnp.float32) + b.astype(jnp.float32)).astype(x.dtype)


def rms_norm(x, w):
    xf = x.astype(jnp.float32)
    return xf * lax.rsqrt(jnp.mean(jnp.square(xf), axis=-1, keepdims=True) + NORM_EPS) * w.astype(jnp.float32)


def l2_normalize(x):
    return x * lax.rsqrt(jnp.sum(jnp.square(x), axis=-1, keepdims=True) + NORM_EPS)


def flip_seq(t, reverse):
    return jnp.flip(t, axis=1) if reverse else t


def to_chunks(t):
    b, s = t.shape[:2]
    t = t.reshape((b, s // CHUNK, CHUNK) + t.shape[2:])
    return jnp.moveaxis(t, 3, 1)


def from_chunks(t):
    t = jnp.moveaxis(t, 1, 3)
    return t.reshape((t.shape[0], t.shape[1] * t.shape[2]) + t.shape[3:])


def scan_chunks(step, state0, xs):
    xs = tuple(jnp.moveaxis(a, 2, 0) for a in xs)
    _, out = lax.scan(step, state0, xs)
    return jnp.moveaxis(out, 0, 2)


def swiglu_ffn(x, w_in, w_out):
    gate, up = jnp.split(jnp.matmul(x, w_in), 2, axis=-1)
    return jnp.matmul(jax.nn.silu(gate) * up, w_out)


def gla_chunked(q, k, v, g):
    c = q.shape[3]
    b = jnp.cumsum(g, axis=3)
    b_last = b[:, :, :, -1:, :]
    b_mid = b[:, :, :, c // 2 - 1:c // 2, :]
    causal = jnp.tril(jnp.ones((c, c), dtype=bool))
    scores = jnp.einsum('bhncd,bhnsd->bhncs', q * jnp.exp(b - b_mid), k * jnp.exp(b_mid - b))
    o_intra = jnp.einsum('bhncs,bhnsv->bhncv', jnp.where(causal, scores, 0.0), v)
    q_start = q * jnp.exp(b)
    k_end = k * jnp.exp(b_last - b)
    chunk_decay = jnp.exp(b_last[:, :, :, 0, :])

    def step(state, xs):
        q_c, k_c, v_c, d_c = xs
        o = jnp.einsum('bhcd,bhdv->bhcv', q_c, state)
        state = state * d_c[..., None] + jnp.einsum('bhcd,bhcv->bhdv', k_c, v_c)
        return state, o

    bsz, h, _, _, dk = q.shape
    state0 = jnp.zeros((bsz, h, dk, v.shape[-1]), jnp.float32)
    o_inter = scan_chunks(step, state0, (q_start, k_end, v, chunk_decay))
    return o_intra + o_inter


def gla_mixer(x, w_in, w_gate_down, w_gate_up, b_gate, norm_w, w_out):
    bsz, s, _ = x.shape
    h, dk, dv = GLA_HEADS, GLA_DK, GLA_DV
    proj = jnp.matmul(x, w_in).astype(jnp.float32)
    q, k, v, r = jnp.split(proj, [h * dk, 2 * h * dk, 2 * h * dk + h * dv], axis=-1)
    q = q.reshape(bsz, s, h, dk) * (dk ** -0.5)
    k = k.reshape(bsz, s, h, dk)
    v = v.reshape(bsz, s, h, dv)

    def direction(d):
        rev = d == 1
        logit = jnp.matmul(jnp.matmul(x, w_gate_down[d]), w_gate_up[d]).astype(jnp.float32) + b_gate[d].astype(jnp.float32)
        g = (jax.nn.log_sigmoid(logit) / GLA_GATE_NORMALIZER).reshape(bsz, s, h, dk)
        od = gla_chunked(to_chunks(flip_seq(q, rev)), to_chunks(flip_seq(k, rev)),
                         to_chunks(flip_seq(v, rev)), to_chunks(flip_seq(g, rev)))
        return flip_seq(from_chunks(od), rev)

    o = direction(0) + direction(1)
    o = rms_norm(o, norm_w) * jax.nn.silu(r.reshape(bsz, s, h, dv))
    return jnp.matmul(o.reshape(bsz, s, h * dv).astype(x.dtype), w_out)


def gated_delta_chunked(q, k, v, beta, g):
    c = q.shape[3]
    gc = jnp.cumsum(g, axis=3)
    gc_last = gc[..., -1]
    incl = jnp.tril(jnp.ones((c, c), dtype=bool))
    strict = jnp.tril(jnp.ones((c, c), dtype=bool), -1)
    decay = jnp.exp(jnp.where(incl, gc[..., :, None] - gc[..., None, :], -jnp.inf))
    k_beta = k * beta[..., None]
    a = jnp.where(strict, jnp.einsum('bhncd,bhnsd->bhncs', k_beta, k) * decay, 0.0)
    eye = jnp.eye(c, dtype=a.dtype)
    t_mat = lax.linalg.triangular_solve(a + eye, jnp.broadcast_to(eye, a.shape),
                                        left_side=True, lower=True, unit_diagonal=True)
    u = jnp.einsum('bhncs,bhnsv->bhncv', t_mat, v * beta[..., None])
    w = jnp.einsum('bhncs,bhnsd->bhncd', t_mat, k_beta * jnp.exp(gc)[..., None])
    attn = jnp.einsum('bhncd,bhnsd->bhncs', q, k) * decay
    q_start = q * jnp.exp(gc)[..., None]
    k_end = k * jnp.exp(gc_last[..., None] - gc)[..., None]
    chunk_decay = jnp.exp(gc_last)

    def step(state, xs):
        u_c, w_c, attn_c, q_c, k_c, d_c = xs
        v_new = u_c - jnp.einsum('bhcd,bhdv->bhcv', w_c, state)
        o = jnp.einsum('bhcd,bhdv->bhcv', q_c, state) + jnp.einsum('bhcs,bhsv->bhcv', attn_c, v_new)
        state = state * d_c[..., None, None] + jnp.einsum('bhcd,bhcv->bhdv', k_c, v_new)
        return state, o

    bsz, h, _, _, dk = q.shape
    state0 = jnp.zeros((bsz, h, dk, v.shape[-1]), jnp.float32)
    return scan_chunks(step, state0, (u, w, attn, q_start, k_end, chunk_decay))


def centred_depthwise_conv(x, w):
    ch = x.shape[-1]
    return lax.conv_general_dilated(x, w.reshape(GDN_CONV, 1, ch).astype(x.dtype), window_strides=(1,),
                                    padding=[GDN_CONV_PAD], dimension_numbers=('NWC', 'WIO', 'NWC'),
                                    feature_group_count=ch)


def gdn_mixer(x, w_in, conv_w, w_ab, a_log, dt_bias, norm_w, w_out):
    bsz, s, _ = x.shape
    hk, hv, dk, dv = GDN_QK_HEADS, GDN_V_HEADS, GDN_DK, GDN_DV
    n_qkv = 2 * hk * dk + hv * dv
    proj = jnp.matmul(x, w_in).astype(jnp.float32)
    qkv = jax.nn.silu(centred_depthwise_conv(proj[..., :n_qkv], conv_w))
    z = proj[..., n_qkv:]
    q, k, v = jnp.split(qkv, [hk * dk, 2 * hk * dk], axis=-1)
    rep = hv // hk
    q = jnp.repeat(l2_normalize(q.reshape(bsz, s, hk, dk)), rep, axis=2) * (dk ** -0.5)
    k = jnp.repeat(l2_normalize(k.reshape(bsz, s, hk, dk)), rep, axis=2)
    v = v.reshape(bsz, s, hv, dv)

    def direction(d):
        rev = d == 1
        a, bt = jnp.split(jnp.matmul(x, w_ab[d]).astype(jnp.float32), 2, axis=-1)
        g = -jnp.exp(a_log[d].astype(jnp.float32)) * jax.nn.softplus(a + dt_bias[d].astype(jnp.float32))
        beta = jax.nn.sigmoid(bt)
        od = gated_delta_chunked(to_chunks(flip_seq(q, rev)), to_chunks(flip_seq(k, rev)),
                                 to_chunks(flip_seq(v, rev)), to_chunks(flip_seq(beta, rev)),
                                 to_chunks(flip_seq(g, rev)))
        return flip_seq(from_chunks(od), rev)

    o = direction(0) + direction(1)
    o = rms_norm(o, norm_w) * jax.nn.silu(z.reshape(bsz, s, hv, dv))
    return jnp.matmul(o.reshape(bsz, s, hv * dv).astype(x.dtype), w_out)


def cross_attention(x, mem, w_q, w_kv, w_o):
    bsz, s, _ = x.shape
    q = jnp.matmul(x, w_q).reshape(bsz, s, XA_HEADS, XA_DH)
    k, v = jnp.split(jnp.matmul(mem, w_kv), 2, axis=-1)
    k = k.reshape(bsz, N_MEM, XA_HEADS, XA_DH)
    v = v.reshape(bsz, N_MEM, XA_HEADS, XA_DH)
    scores = jnp.einsum('bthd,bmhd->bhtm', q, k).astype(jnp.float32) * (XA_DH ** -0.5)
    p = jax.nn.softmax(scores, axis=-1).astype(v.dtype)
    o = jnp.einsum('bhtm,bmhd->bthd', p, v).reshape(bsz, s, D_MODEL)
    return jnp.matmul(o, w_o)


def encoder_trunk(x, mem, ffn_w_in, ffn_w_out, ln_g, ln_b,
                  gla_w_in, gla_w_gate_down, gla_w_gate_up, gla_b_gate, gla_norm_w, gla_w_out,
                  gdn_w_in, gdn_conv_w, gdn_w_ab, gdn_a_log, gdn_dt_bias, gdn_norm_w, gdn_w_out,
                  xa_w_q, xa_w_kv, xa_w_o):
    for i in range(DEPTH):
        x = layer_norm(ALPHA * x + 0.5 * swiglu_ffn(x, ffn_w_in[i, 0], ffn_w_out[i, 0]), ln_g[i, 0], ln_b[i, 0])
        j = i // N_MIXERS
        if i % N_MIXERS == 0:
            h = gla_mixer(x, gla_w_in[j], gla_w_gate_down[j], gla_w_gate_up[j], gla_b_gate[j],
                          gla_norm_w[j], gla_w_out[j])
        else:
            h = gdn_mixer(x, gdn_w_in[j], gdn_conv_w[j], gdn_w_ab[j], gdn_a_log[j], gdn_dt_bias[j],
                          gdn_norm_w[j], gdn_w_out[j])
        x = layer_norm(ALPHA * x + h, ln_g[i, 1], ln_b[i, 1])
        x = layer_norm(ALPHA * x + cross_attention(x, mem, xa_w_q[i], xa_w_kv[i], xa_w_o[i]), ln_g[i, 2], ln_b[i, 2])
        x = layer_norm(ALPHA * x + 0.5 * swiglu_ffn(x, ffn_w_in[i, 1], ffn_w_out[i, 1]), ln_g[i, 3], ln_b[i, 3])
    return x


def setup_inputs(seed: int = 0) -> dict:
    key = jax.random.key(seed)
    ks = jax.random.split(key, 24)

    def nrm(k, shape, scale):
        return jax.random.normal(k, shape, jnp.float32) * scale

    d = D_MODEL
    gla_cols = 2 * GLA_HEADS * GLA_DK + 2 * GLA_HEADS * GLA_DV
    gdn_qkv = 2 * GDN_QK_HEADS * GDN_DK + GDN_V_HEADS * GDN_DV
    gdn_cols = gdn_qkv + GDN_V_HEADS * GDN_DV
    dt = jnp.exp(jax.random.uniform(ks[18], (N_GDN_LAYERS, 2, GDN_V_HEADS), jnp.float32,
                                    math.log(1e-3), math.log(1e-1)))
    return {
        'x_prompt': nrm(ks[0], (BATCH, SEQ, d), 1.0),
        'x_sample': nrm(ks[1], (DEC_BATCH, DEC_SEQ, d), 1.0),
        'mem_prompt': nrm(ks[2], (BATCH, N_MEM, d), 1.0),
        'mem_sample': nrm(ks[3], (DEC_BATCH, N_MEM, d), 1.0),
        'ffn_w_in': nrm(ks[4], (DEPTH, 2, d, 2 * D_FF), d ** -0.5),
        'ffn_w_out': nrm(ks[5], (DEPTH, 2, D_FF, d), BETA_INIT * D_FF ** -0.5),
        'ln_g': 1.0 + nrm(ks[6], (DEPTH, 4, d), 0.02),
        'ln_b': nrm(ks[7], (DEPTH, 4, d), 0.02),
        'gla_w_in': nrm(ks[8], (N_GLA_LAYERS, d, gla_cols), d ** -0.5),
        'gla_w_gate_down': nrm(ks[9], (N_GLA_LAYERS, 2, d, GLA_RANK), d ** -0.5),
        'gla_w_gate_up': nrm(ks[10], (N_GLA_LAYERS, 2, GLA_RANK, GLA_HEADS * GLA_DK), GLA_RANK ** -0.5),
        'gla_b_gate': nrm(ks[11], (N_GLA_LAYERS, 2, GLA_HEADS * GLA_DK), 0.1),
        'gla_norm_w': 1.0 + nrm(ks[12], (N_GLA_LAYERS, GLA_DV), 0.02),
        'gla_w_out': nrm(ks[13], (N_GLA_LAYERS, GLA_HEADS * GLA_DV, d), BETA_INIT * (GLA_HEADS * GLA_DV) ** -0.5),
        'gdn_w_in': nrm(ks[14], (N_GDN_LAYERS, d, gdn_cols), d ** -0.5),
        'gdn_conv_w': nrm(ks[15], (N_GDN_LAYERS, GDN_CONV, gdn_qkv), GDN_CONV ** -0.5),
        'gdn_w_ab': nrm(ks[16], (N_GDN_LAYERS, 2, d, 2 * GDN_V_HEADS), d ** -0.5),
        'gdn_a_log': jnp.log(jax.random.uniform(ks[17], (N_GDN_LAYERS, 2, GDN_V_HEADS), jnp.float32, 1.0, 16.0)),
        'gdn_dt_bias': dt + jnp.log(-jnp.expm1(-dt)),
        'gdn_norm_w': 1.0 + nrm(ks[19], (N_GDN_LAYERS, GDN_DV), 0.02),
        'gdn_w_out': nrm(ks[20], (N_GDN_LAYERS, GDN_V_HEADS * GDN_DV, d), BETA_INIT * (GDN_V_HEADS * GDN_DV) ** -0.5),
        'xa_w_q': nrm(ks[21], (DEPTH, d, d), d ** -0.5),
        'xa_w_kv': nrm(ks[22], (DEPTH, d, 2 * d), d ** -0.5),
        'xa_w_o': nrm(ks[23], (DEPTH, d, d), BETA_INIT * d ** -0.5),
    }


def reference(x_prompt, x_sample, mem_prompt, mem_sample, ffn_w_in, ffn_w_out, ln_g, ln_b,
              gla_w_in, gla_w_gate_down, gla_w_gate_up, gla_b_gate, gla_norm_w, gla_w_out,
              gdn_w_in, gdn_conv_w, gdn_w_ab, gdn_a_log, gdn_dt_bias, gdn_norm_w, gdn_w_out,
              xa_w_q, xa_w_kv, xa_w_o):
    weights = (ffn_w_in, ffn_w_out, ln_g, ln_b,
               gla_w_in, gla_w_gate_down, gla_w_gate_up, gla_b_gate, gla_norm_w, gla_w_out,
               gdn_w_in, gdn_conv_w, gdn_w_ab, gdn_a_log, gdn_dt_bias, gdn_norm_w, gdn_w_out,
               xa_w_q, xa_w_kv, xa_w_o)
    y_prompt = encoder_trunk(x_prompt, mem_prompt, *weights)
    y_sample = encoder_trunk(x_sample, mem_sample, *weights)
    return (y_prompt, y_sample)
```

```python
import numpy as np
from contextlib import ExitStack
import concourse.bass as bass
import concourse.mybir as mybir
from concourse.bass_utils import run_bass_kernel_spmd

F32 = mybir.dt.float32
BF16 = mybir.dt.bfloat16
AF = mybir.ActivationFunctionType
ALU = mybir.AluOpType

D = 1024
DFF = 2816
NMEM = 256
DEPTH = 2
ALPHA = (2.0 * DEPTH) ** 0.25
LN_EPS = 1e-5
NORM_EPS = 1e-6
GLA_H, GLA_DK, GLA_DV = 4, 128, 256
GDN_HK, GDN_HV = 8, 16


class Sem:
    def __init__(self, h, name):
        self.h = h
        self.name = name
        self.count = 0


class Buf:
    def __init__(self, name, ap=None):
        self.name = name
        self.ap = ap
        self.wtoks = {}
        self.rtoks = {}
        self.dsem = None
        self.excl = False


class Q:
    def __init__(self, K, name):
        self.K = K
        self.name = name
        self.sem = K.new_sem("q_" + name)
        self.seen = {}
        self.ops = []

    def wait_tok(self, sem, val):
        if sem is self.sem and self.name == "pe":
            return
        if self.seen.get(sem, 0) >= val:
            return
        self.seen[sem] = val
        self.ops.append(("w", sem, val))

    def deps(self, reads, writes):
        for b in reads:
            for s, v in b.wtoks.items():
                self.wait_tok(s, v)
            if b.excl:
                for s, v in b.rtoks.items():
                    if s is not self.sem:
                        self.wait_tok(s, v)
        for b in writes:
            for s, v in b.wtoks.items():
                self.wait_tok(s, v)
            for s, v in b.rtoks.items():
                self.wait_tok(s, v)

    def mark(self, tok, reads, writes):
        s, v = tok
        for b in reads:
            if b.rtoks.get(s, 0) < v:
                b.rtoks[s] = v
        for b in writes:
            b.wtoks[s] = v
            b.rtoks = {}

    def op(self, fn, reads=(), writes=(), signal=True):
        self.deps(reads, writes)
        if signal:
            self.sem.count += 1
            tok = (self.sem, self.sem.count)
            self.ops.append(("i", fn, self.sem, 1))
        else:
            tok = (self.sem, self.sem.count + 1)
            self.ops.append(("i", fn, None, 0))
        self.mark(tok, reads, writes)
        return tok

    def tt(self, out, in0, in1, op, reads=(), writes=()):
        return self.op(lambda e: e.tensor_tensor(out, in0, in1, op), reads, writes)

    def ts(self, out, in0, s1, s2, op0, op1=None, reads=(), writes=(), accum=None):
        if op1 is None:
            return self.op(lambda e: e.tensor_scalar(out, in0, s1, None, op0), reads, writes)
        if accum is not None:
            return self.op(lambda e: e.tensor_scalar(out, in0, s1, s2, op0, op1, accum), reads, writes)
        return self.op(lambda e: e.tensor_scalar(out, in0, s1, s2, op0, op1), reads, writes)

    def stt(self, out, in0, scalar, in1, op0, op1, reads=(), writes=()):
        return self.op(lambda e: e.scalar_tensor_tensor(out, in0, scalar, in1, op0, op1), reads, writes)

    def actv(self, out, in_, func, bias=None, scale=None, accum=None, reads=(), writes=()):
        kw = {}
        if bias is not None:
            kw["bias"] = bias
        if scale is not None:
            kw["scale"] = scale
        if accum is not None:
            kw["accum_out"] = accum
        return self.op(lambda e: e.activation(out, in_, func, **kw), reads, writes)

    def cp(self, out, in_, scale=None, reads=(), writes=()):
        return self.op(copy_op(self, out, in_, scale), reads, writes)

    def memset(self, ap, val, writes=()):
        return self.op(lambda e: e.memset(ap, float(val)), (), writes)

    def recip(self, out, in_, reads=(), writes=()):
        return self.op(lambda e: e.reciprocal(out, in_), reads, writes)

    def rmax(self, out, in_, reads=(), writes=()):
        return self.op(lambda e: e.reduce_max(out, in_, mybir.AxisListType.X), reads, writes)

    def bnstats(self, out, in_, reads=(), writes=()):
        return self.op(lambda e: e.bn_stats(out, in_), reads, writes)

    def bnaggr(self, out, in_, reads=(), writes=()):
        return self.op(lambda e: e.bn_aggr(out, in_), reads, writes)

    def dma(self, sb, pairs, reads=(), writes=()):
        K = self.K
        if sb.dsem is None:
            sb.dsem = K.get_dsem()
        ds = sb.dsem
        if ds.count > 0:
            self.wait_tok(ds, ds.count)
        self.deps(reads, writes)
        for (o, i) in pairs:
            ds.count += 16
            self.ops.append(("d", o, i, ds))
        tok = (ds, ds.count)
        self.mark(tok, reads, writes)
        return tok

    def replay(self, eng):
        for o in self.ops:
            if o[0] == "w":
                eng.wait_ge(o[1].h, o[2])
            elif o[0] == "i":
                ins = o[1](eng)
                if o[2] is not None:
                    ins.then_inc(o[2].h, o[3])
            elif o[0] == "d":
                eng.dma_start(out=o[1], in_=o[2]).then_inc(o[3].h, 16)


class Kern:
    def __init__(self):
        self.nc = bass.Bass("TRN2", target_bir_lowering=False)
        self.sems = []
        self.free_dsems = []
        self.phase_bufs = []

    def new_sem(self, name):
        h = self.gstack.enter_context(self.nc.semaphore(name))
        s = Sem(h, name)
        self.sems.append(s)
        return s

    def get_dsem(self):
        if self.free_dsems:
            return self.free_dsems.pop()
        return self.new_sem("d%d" % len(self.sems))

    def setup(self, gstack):
        self.gstack = gstack
        self.pe = Q(self, "pe")
        self.act = Q(self, "act")
        self.dve = Q(self, "dve")
        self.pool = Q(self, "pool")
        self.sp = Q(self, "sp")
        self.queues = [self.pe, self.act, self.dve, self.pool, self.sp]
        self.banks = []
        for i in range(8):
            t = gstack.enter_context(self.nc.psum_tensor("psb%d" % i, [128, 512], F32))
            self.banks.append(Buf("psb%d" % i, t))
            self.banks[-1].excl = True
        self.bank_i = 0
        self.rot = list(range(8))
        self.rr = 0

    def set_rot(self, idx):
        self.rot = list(idx)
        self.bank_i = 0

    def ps(self):
        b = self.banks[self.rot[self.bank_i % len(self.rot)]]
        self.bank_i += 1
        return b

    def cst(self, name):
        o, w = CONST_LAYOUT[name]
        return self.consts.ap[:, o:o + w]

    def sb(self, name, shape, dt):
        nbytes = int(np.prod(shape[1:])) * (2 if dt == BF16 else 4)
        base = self.nc.sbuf_base
        t = self.pstack.enter_context(self.nc.sbuf_tensor("p%d_%s" % (self.phase_id, name), shape, dt))
        b = Buf(name, t)
        self.phase_bufs.append(b)
        return b

    def begin_phase(self, pstack):
        self.phase_id = getattr(self, "phase_id", 0) + 1
        self.pstack = pstack
        self.phase_bufs = []

    def barrier(self):
        for q in self.queues:
            for s in self.sems:
                if s.count > 0:
                    q.wait_tok(s, s.count)

    def end_phase(self):
        self.barrier()
        for b in self.phase_bufs:
            if b.dsem is not None:
                self.free_dsems.append(b.dsem)
                b.dsem = None
        self.phase_bufs = []

    def ev(self):
        self.rr += 1
        return self.act if (self.rr & 1) else self.dve

    def finish(self):
        self.barrier()
        nc = self.nc
        with nc.Block() as block:
            @block.tensor
            def _(e):
                self.pe.replay(e)

            @block.scalar
            def _(e):
                self.act.replay(e)

            @block.vector
            def _(e):
                self.dve.replay(e)

            @block.gpsimd
            def _(e):
                self.pool.replay(e)

            @block.sync
            def _(e):
                self.sp.replay(e)


def copy_op(q, out_ap, in_ap, scale=None):
    if q.name == "act":
        if scale is None:
            return lambda e: e.activation(out_ap, in_ap, AF.Copy)
        return lambda e: e.activation(out_ap, in_ap, AF.Copy, scale=float(scale))
    if scale is None:
        return lambda e: e.tensor_copy(out_ap, in_ap)
    return lambda e: e.tensor_scalar(out_ap, in_ap, float(scale), None, ALU.mult)


def mm(K, bank, out_ap, pairs, reads, start=True, stop=True):
    n = len(pairs)
    tok = None
    for i, (l, r) in enumerate(pairs):
        st = start and i == 0
        sp = stop and i == n - 1
        last = i == n - 1
        tok = K.pe.op(lambda e, l=l, r=r, st=st, sp=sp: e.matmul(out_ap, l, r, start=st, stop=sp),
                      reads=reads if (last or i == 0) else (), writes=[bank], signal=last)
    return tok


def tr(K, bank, out_ap, in_ap, ident_ap, reads, signal=True):
    return K.pe.op(lambda e: e.transpose(out_ap, in_ap, ident_ap), reads=reads, writes=[bank], signal=signal)


def load_weight_bf16(K, wdram, rows, cols, name, stage, col_group=2048):
    nch = rows // 128
    t = K.sb(name, [128, nch, cols], BF16)
    bufs = []
    si = 0
    for c in range(nch):
        b = Buf("%s_%d" % (name, c), t.ap)
        eng = K.act if (c & 1) else K.dve
        for c0 in range(0, cols, col_group):
            c1 = min(cols, c0 + col_group)
            st = stage[si % len(stage)]
            si += 1
            K.sp.dma(st, [(st.ap[:, 0:c1 - c0], wdram[c * 128:(c + 1) * 128, c0:c1])], writes=[st])
            o = t.ap[:, c, c0:c1]
            i = st.ap[:, 0:c1 - c0]
            eng.cp(o, i, reads=[st], writes=[b])
        bufs.append(b)
    return t.ap, bufs


def load_bcast(K, dram_row, n, name):
    b = K.sb(name, [128, n], F32)
    K.sp.dma(b, [(b.ap[:], dram_row.partition_broadcast(128))], writes=[b])
    return b


def transpose_x(K, xt_ap, x_buf, xT_ap, xT_buf, ident_ap, ident_buf, tcols=128, col0=0, nrows=128):
    for g in range(2):
        bank = K.ps()
        for j in range(4):
            kc = g * 4 + j
            tr(K, bank, bank.ap[:, j * 128:j * 128 + nrows], xt_ap[0:nrows, kc * 128:(kc + 1) * 128],
               ident_ap[0:nrows, 0:nrows], reads=[x_buf, ident_buf], signal=(j == 3))
        q = K.ev()
        src = bank.ap[:].rearrange("p (j t) -> p j t", j=4)[:, :, 0:nrows]
        dst = xT_ap[:, g * 4:(g + 1) * 4, col0:col0 + nrows]
        q.cp(dst, src, reads=[bank], writes=[xT_buf])


def residual_ln(K, bank_list, x_buf, x_ap, fscale, g_b, b_b, z_buf, eps_adj, st_bufs):
    z = z_buf.ap
    stats, mv, rstd = st_bufs
    for h, bank in enumerate(bank_list):
        sl = slice(h * 512, (h + 1) * 512)
        K.dve.stt(z[:, sl], bank.ap[:, 0:512], float(fscale), x_ap[:, sl], ALU.mult, ALU.add, reads=[bank, x_buf], writes=[z_buf])
        K.dve.bnstats(stats.ap[:, h, :], z[:, sl], reads=[z_buf], writes=[stats])
    K.dve.bnaggr(mv.ap[:], stats.ap[:].rearrange("p a b -> p (a b)"), reads=[stats], writes=[mv])
    eps = K.eps_tiles[eps_adj]
    K.act.actv(rstd.ap[:], mv.ap[:, 1:2], AF.Sqrt, bias=eps.ap[:], scale=1.0, reads=[mv, eps], writes=[rstd])
    K.dve.recip(rstd.ap[:], rstd.ap[:], reads=[rstd], writes=[rstd])
    K.dve.ts(z[:], z[:], mv.ap[:, 0:1], rstd.ap[:], ALU.subtract, ALU.mult, reads=[z_buf, mv, rstd], writes=[z_buf])
    K.pool.tt(z[:], z[:], g_b.ap[:], ALU.mult, reads=[z_buf, g_b], writes=[z_buf])
    K.pool.tt(z[:], z[:], b_b.ap[:], ALU.add, reads=[z_buf, b_b], writes=[z_buf])


def make_eps_tiles(K, gstack):
    K.eps_tiles = {}
    for key, val in (("ln1", LN_EPS), ("lna", LN_EPS / (ALPHA * ALPHA)), ("l2", NORM_EPS), ("one", 1.0)):
        t = gstack.enter_context(K.nc.sbuf_tensor("eps_" + key, [128, 1], F32))
        b = Buf("eps_" + key, t)
        K.dve.memset(t[:], val, writes=[b])
        K.eps_tiles[key] = b


def ffn_phase(K, cfg, Xin, Xout, w_in, w_out, lng, lnb, TW=256):
    L = cfg["L"]
    NS = TW // 128
    with ExitStack() as ps_:
        K.begin_phase(ps_)
        stage = [K.sb("wst%d" % i, [128, 1024], F32) for i in range(2)]
        W1, W1b = load_weight_bf16(K, w_in, D, 2 * DFF, "W1", stage, col_group=704)
        W2, W2b = load_weight_bf16(K, w_out, DFF, D, "W2", stage, col_group=1024)
        g_b = load_bcast(K, lng, D, "lng")
        b_b = load_bcast(K, lnb, D, "lnb")
        xts = [K.sb("xt%d" % i, [128, NS, D], F32) for i in range(2)]
        xTs = [K.sb("xT%d" % i, [128, 8, TW], BF16) for i in range(2)]
        hTs = [K.sb("hT0", [128, 22, TW], BF16)] * 2
        sgs = [K.sb("sg%d" % i, [128, TW], F32) for i in range(2)]
        zs = stage
        stats = K.sb("stats", [128, 2, 6], F32)
        mv = K.sb("mv", [128, 2], F32)
        rstd = K.sb("rstd", [128, 1], F32)
        ident = K.ident
        zi = 0
        import os
        dbg = os.environ.get("DBG", "")
        if "a" in dbg: xts = [xts[1], xts[1]]
        if "b" in dbg: xTs = [xTs[1], xTs[1]]
        if "c" in dbg: hTs = [hTs[1], hTs[1]]
        if "d" in dbg: xts = [xts[0], xts[0]]
        if "e" in dbg: xTs = [xTs[0], xTs[0]]
        if "f" in dbg: hTs = [hTs[0], hTs[0]]
        for ti in range(L // TW):
            t0 = ti * TW
            xt = xts[ti % 2]
            xT = xTs[ti % 2]
            hT = hTs[ti % 2]
            K.sp.dma(xt, [(xt.ap[:, s, :], Xin[t0 + s * 128:t0 + (s + 1) * 128, :]) for s in range(NS)], writes=[xt])
            for s in range(NS):
                transpose_x(K, xt.ap[:, s, :], xt, xT.ap, xT, ident.ap, ident, col0=s * 128)
            for hc in range(22):
                bank = K.ps()
                mm(K, bank, bank.ap[:, 0:TW],
                   [(W1[:, kc, hc * 128:(hc + 1) * 128], xT.ap[:, kc, :]) for kc in range(8)], reads=[xT] + W1b)
                mm(K, bank, bank.ap[:, TW:2 * TW],
                   [(W1[:, kc, DFF + hc * 128:DFF + (hc + 1) * 128], xT.ap[:, kc, :]) for kc in range(8)], reads=[xT] + W1b)
                sg = sgs[hc % 2]
                K.act.actv(sg.ap[:], bank.ap[:, 0:TW], AF.Silu, reads=[bank], writes=[sg])
                K.dve.tt(hT.ap[:, hc, :], sg.ap[:], bank.ap[:, TW:2 * TW], ALU.mult, reads=[bank, sg], writes=[hT])
            for s in range(NS):
                banks = []
                for fh in range(2):
                    bank = K.ps()
                    mm(K, bank, bank.ap[:, 0:512],
                       [(hT.ap[:, hc, s * 128:(s + 1) * 128], W2[:, hc, fh * 512:(fh + 1) * 512]) for hc in range(22)],
                       reads=[hT] + W2b)
                    banks.append(bank)
                z = zs[zi % 2]
                zi += 1
                residual_ln(K, banks, xt, xt.ap[:, s, :], 0.5 / ALPHA, g_b, b_b, z, "lna", (stats, mv, rstd))
                K.pool.dma(z, [(Xout[t0 + s * 128:t0 + (s + 1) * 128, :], z.ap[:])], reads=[z])
            if "B" in dbg:
                K.barrier()
        K.end_phase()


WEIGHT_SPECS = [
    ("ffn_w_in", [2, 2, D, 2 * DFF]), ("ffn_w_out", [2, 2, DFF, D]), ("ln_g", [2, 4, D]), ("ln_b", [2, 4, D]),
    ("gla_w_in", [1, D, 3072]), ("gla_w_gate_down", [1, 2, D, 16]), ("gla_w_gate_up", [1, 2, 16, 512]),
    ("gla_b_gate", [1, 2, 512]), ("gla_norm_w", [1, 256]), ("gla_w_out", [1, D, D]),
    ("gdn_w_in", [1, D, 6144]), ("gdn_conv_w", [1, 4, 4096]), ("gdn_w_ab", [1, 2, D, 32]),
    ("gdn_a_log", [1, 2, 16]), ("gdn_dt_bias", [1, 2, 16]), ("gdn_norm_w", [1, 128]), ("gdn_w_out", [1, 2048, D]),
    ("xa_w_q", [2, D, D]), ("xa_w_kv", [2, D, 2 * D]), ("xa_w_o", [2, D, D]),
]


def build_program(cfg):
    L = cfg["L"]
    NSEG = cfg["NSEG"]
    phases = cfg.get("phases", None)
    K = Kern()
    nc = K.nc
    xs = nc.dram_tensor("xs", [L, D], F32, kind="ExternalInput").ap()
    mem = nc.dram_tensor("mem", [NSEG, NMEM, D], F32, kind="ExternalInput").ap()
    carry = nc.dram_tensor("carry", [128, 1], F32, kind="ExternalInput").ap()
    consts = nc.dram_tensor("consts", [128, cfg["NCONST"]], F32, kind="ExternalInput").ap()
    W = {}
    for name, shape in WEIGHT_SPECS:
        W[name] = nc.dram_tensor(name, shape, F32, kind="ExternalInput").ap()
    y = nc.dram_tensor("y", [L, D], F32, kind="ExternalOutput").ap()
    XA = nc.dram_tensor("scrA", [L, D], F32).ap()
    XB = nc.dram_tensor("scrB", [L, D], F32).ap()
    OF = nc.dram_tensor("scrO", [L, 2048], F32).ap()
    with ExitStack() as gs:
        K.setup(gs)
        make_eps_tiles(K, gs)
        ct = gs.enter_context(nc.sbuf_tensor("consts_sb", [128, cfg["NCONST"]], F32))
        K.consts = Buf("consts", ct)
        K.sp.dma(K.consts, [(ct[:], consts)], writes=[K.consts])
        K.ident = Buf("ident", ct[:, 0:128])
        K.ident.wtoks = K.consts.wtoks
        ibt = gs.enter_context(nc.sbuf_tensor("identb", [128, 128], BF16))
        K.identb = Buf("identb", ibt)
        K.dve.cp(ibt[:], ct[:, 0:128], reads=[K.consts], writes=[K.identb])
        cyt = gs.enter_context(nc.sbuf_tensor("carry_sb", [128, 1], F32))
        K.carry = Buf("carry", cyt)
        K.sp.dma(K.carry, [(cyt[:], carry)], writes=[K.carry])
        K.cfg = cfg
        K.ofbuf = [Buf("of%d" % i) for i in range(L // 128)]
        plan = []
        for i in range(DEPTH):
            plan.append(("ffn", i, 0))
            plan.append(("mix", i))
            plan.append(("xa", i))
            plan.append(("ffn", i, 1))
        if phases is not None:
            plan = plan[:phases]
        if cfg.get("only"):
            plan = {"ffn": [("ffn", 0, 0)], "xa": [("xa", 0)], "gla": [("mix", 0)], "gdn": [("mix", 1)]}[cfg["only"]]
        cur = xs
        for pi, ph in enumerate(plan):
            last = pi == len(plan) - 1
            dst = y if last else (XA if (pi % 2 == 0) else XB)
            if ph[0] == "ffn":
                i, j = ph[1], ph[2]
                li = 0 if j == 0 else 3
                ffn_phase(K, cfg, cur, dst, W["ffn_w_in"][i, j], W["ffn_w_out"][i, j], W["ln_g"][i, li], W["ln_b"][i, li])
            elif ph[0] == "mix":
                i = ph[1]
                if i % 2 == 0:
                    gla_phase(K, cfg, cur, dst, OF, W, i // 2, W["ln_g"][i, 1], W["ln_b"][i, 1])
                else:
                    gdn_phase(K, cfg, cur, dst, OF, W, i // 2, W["ln_g"][i, 1], W["ln_b"][i, 1])
            elif ph[0] == "xa":
                i = ph[1]
                xa_phase(K, cfg, cur, dst, mem, W["xa_w_q"][i], W["xa_w_kv"][i], W["xa_w_o"][i], W["ln_g"][i, 2], W["ln_b"][i, 2])
            cur = dst
        K.finish()
    return nc


def _build_consts():
    s_ = np.arange(128)[:, None]
    c_ = np.arange(128)[None, :]
    same = (s_ // 64) == (c_ // 64)
    pos_s = s_ % 64
    out = {}
    out["IDENT"] = np.eye(128, dtype=np.float32)
    for dn in ("f", "b"):
        if dn == "f":
            tri = same & (s_ <= c_)
            mid = same & (pos_s <= 31) & (c_ >= 0)
        else:
            tri = same & (s_ >= c_)
            mid = same & (pos_s >= 32) & (c_ >= 0)
        tri = tri.astype(np.float32)
        mid = mid.astype(np.float32)
        sm = same.astype(np.float32)
        out["M1" + dn] = (-1.0 / 16.0) * (tri - mid)
        out["M3" + dn] = (-1.0 / 16.0) * tri
        out["M4" + dn] = (-1.0 / 16.0) * (sm - tri)
        out["MASK4" + dn] = np.tile(tri, (1, 4))
        out["TRI" + dn] = tri
        out["AFTER" + dn] = sm - tri
        out["NEG" + dn] = (1.0 - tri) * (-30000.0)
    ind = np.zeros((128, 2), np.float32)
    ind[0:64, 0] = -1.0 / 16.0
    ind[64:128, 1] = -1.0 / 16.0
    out["IND"] = ind
    out["ONES"] = np.ones((128, 128), np.float32)
    out["NEGONES"] = -np.ones((128, 128), np.float32)
    out["OFFD4"] = np.tile(1.0 - np.eye(128, dtype=np.float32), (1, 4))
    out["IDENT4"] = np.tile(np.eye(128, dtype=np.float32), (1, 4))
    ch0 = np.zeros((128, 128), np.float32)
    ch0[0:64, :] = 1.0
    out["CH0"] = ch0
    out["CH1"] = 1.0 - ch0
    return out


_CONSTS = _build_consts()
CONST_LAYOUT = {}
_off = 0
for _k, _v in _CONSTS.items():
    CONST_LAYOUT[_k] = (_off, _v.shape[1])
    _off += _v.shape[1]
NCONST = _off


def make_consts():
    return np.ascontiguousarray(np.concatenate([_CONSTS[k] for k in CONST_LAYOUT], axis=1).astype(np.float32))


def bank_bf(bank):
    return bank.ap[:].bitcast(BF16)


def xa_phase(K, cfg, Xin, Xout, mem, w_q, w_kv, w_o, lng, lnb):
    L, NSEG, SEG = cfg["L"], cfg["NSEG"], cfg["SEG"]
    TW = 256
    NS = TW // 128
    with ExitStack() as ps_:
        K.begin_phase(ps_)
        stage = [K.sb("wst%d" % i, [128, 1024], F32) for i in range(2)]
        Wq, Wqb = load_weight_bf16(K, w_q, D, D, "Wq", stage, col_group=1024)
        Wkv, Wkvb = load_weight_bf16(K, w_kv, D, 2 * D, "Wkv", stage, col_group=1024)
        Wo, Wob = load_weight_bf16(K, w_o, D, D, "Wo", stage, col_group=1024)
        g_b = load_bcast(K, lng, D, "lng")
        b_b = load_bcast(K, lnb, D, "lnb")
        ident, identb = K.ident, K.identb
        KT = K.sb("KT", [128, NSEG, 8, NMEM], BF16)
        V = K.sb("V", [128, NSEG, 2, D], BF16)
        memT = K.sb("memT", [128, 8, NMEM], BF16)
        xts = [K.sb("xt%d" % i, [128, NS, D], F32) for i in range(2)]
        xTs = [K.sb("xT%d" % i, [128, 8, TW], BF16) for i in range(2)]
        qT = K.sb("qT", [128, 8, TW], BF16)
        oT = K.sb("oT", [128, 8, TW], BF16)
        pT = K.sb("pT", [128, 4, 2, TW], BF16)
        pf = K.sb("pf", [128, 4, NMEM], F32)
        pb = K.sb("pb", [128, 4, NMEM], BF16)
        mx = K.sb("mx", [128, 4], F32)
        sm = K.sb("sm", [128, 4], F32)
        zs = [K.sb("z%d" % i, [128, D], F32) for i in range(2)]
        stats = K.sb("stats", [128, 2, 6], F32)
        mv = K.sb("mv", [128, 2], F32)
        rstd = K.sb("rstd", [128, 1], F32)
        for sg in range(NSEG):
            for mt in range(2):
                xt = xts[mt]
                K.sp.dma(xt, [(xt.ap[:, 0, :], mem[sg, mt * 128:(mt + 1) * 128, :])], writes=[xt])
                transpose_x(K, xt.ap[:, 0, :], xt, memT.ap, memT, ident.ap, ident, col0=mt * 128)
            for dhc in range(8):
                bank = K.ps()
                mm(K, bank, bank.ap[:, 0:NMEM], [(Wkv[:, kc, dhc * 128:(dhc + 1) * 128], memT.ap[:, kc, :]) for kc in range(8)],
                   reads=[memT] + Wkvb)
                q = K.ev()
                q.cp(KT.ap[:, sg, dhc, :], bank.ap[:, 0:NMEM], reads=[bank], writes=[KT])
            for mc in range(2):
                for half in range(2):
                    bank = K.ps()
                    mm(K, bank, bank.ap[:, 0:512],
                       [(memT.ap[:, kc, mc * 128:(mc + 1) * 128], Wkv[:, kc, D + half * 512:D + (half + 1) * 512]) for kc in range(8)],
                       reads=[memT] + Wkvb)
                    q = K.ev()
                    q.cp(V.ap[:, sg, mc, half * 512:(half + 1) * 512], bank.ap[:, 0:512], reads=[bank], writes=[V])
        zi = 0
        for ti in range(L // TW):
            t0 = ti * TW
            sg = t0 // SEG
            xt = xts[ti % 2]
            xT = xTs[ti % 2]
            K.sp.dma(xt, [(xt.ap[:, s, :], Xin[t0 + s * 128:t0 + (s + 1) * 128, :]) for s in range(NS)], writes=[xt])
            for s in range(NS):
                transpose_x(K, xt.ap[:, s, :], xt, xT.ap, xT, ident.ap, ident, col0=s * 128)
            for dhc in range(8):
                bank = K.ps()
                mm(K, bank, bank.ap[:, 0:TW], [(Wq[:, kc, dhc * 128:(dhc + 1) * 128], xT.ap[:, kc, :]) for kc in range(8)],
                   reads=[xT] + Wqb)
                q = K.ev()
                q.cp(qT.ap[:, dhc, :], bank.ap[:, 0:TW], scale=1.0 / 16.0, reads=[bank], writes=[qT])
            for s in range(NS):
                sb_ = []
                for j in range(2):
                    bank = K.ps()
                    for hh in range(2):
                        h = 2 * j + hh
                        mm(K, bank, bank.ap[:, hh * 256:(hh + 1) * 256],
                           [(qT.ap[:, 2 * h + c, s * 128:(s + 1) * 128], KT.ap[:, sg, 2 * h + c, :]) for c in range(2)], reads=[qT, KT])
                    K.dve.rmax(mx.ap[:, 2 * j:2 * j + 2], bank.ap[:].rearrange("p (h m) -> p h m", h=2), reads=[bank], writes=[mx])
                    sb_.append(bank)
                K.dve.ts(mx.ap[:], mx.ap[:], -1.0, None, ALU.mult, reads=[mx], writes=[mx])
                for h in range(4):
                    bank = sb_[h // 2]
                    K.act.actv(pf.ap[:, h, :], bank.ap[:, (h % 2) * 256:(h % 2 + 1) * 256], AF.Exp, bias=mx.ap[:, h:h + 1], scale=1.0,
                               accum=sm.ap[:, h:h + 1], reads=[bank, mx], writes=[pf, sm])
                K.dve.recip(sm.ap[:], sm.ap[:], reads=[sm], writes=[sm])
                for h in range(4):
                    K.dve.ts(pb.ap[:, h, :], pf.ap[:, h, :], sm.ap[:, h:h + 1], None, ALU.mult, reads=[pf, sm], writes=[pb])
                bank = K.ps()
                bb = bank_bf(bank)
                for h in range(4):
                    for mc in range(2):
                        idx = h * 2 + mc
                        tr(K, bank, bb[:, idx * 128:(idx + 1) * 128], pb.ap[:, h, mc * 128:(mc + 1) * 128], identb.ap[:],
                           reads=[pb, identb], signal=(idx == 7))
                q = K.ev()
                q.cp(pT.ap[:, :, :, s * 128:(s + 1) * 128], bb.rearrange("p (h c t) -> p h c t", h=4, c=2), reads=[bank], writes=[pT])
            for dhc in range(8):
                h = dhc // 2
                bank = K.ps()
                mm(K, bank, bank.ap[:, 0:TW], [(V.ap[:, sg, mc, dhc * 128:(dhc + 1) * 128], pT.ap[:, h, mc, :]) for mc in range(2)],
                   reads=[V, pT])
                q = K.ev()
                q.cp(oT.ap[:, dhc, :], bank.ap[:, 0:TW], reads=[bank], writes=[oT])
            for s in range(NS):
                banks = []
                for fh in range(2):
                    bank = K.ps()
                    mm(K, bank, bank.ap[:, 0:512],
                       [(oT.ap[:, dhc, s * 128:(s + 1) * 128], Wo[:, dhc, fh * 512:(fh + 1) * 512]) for dhc in range(8)],
                       reads=[oT] + Wob)
                    banks.append(bank)
                z = zs[zi % 2]
                zi += 1
                residual_ln(K, banks, xt, xt.ap[:, s, :], 1.0 / ALPHA, g_b, b_b, z, "lna", (stats, mv, rstd))
                K.pool.dma(z, [(Xout[t0 + s * 128:t0 + (s + 1) * 128, :], z.ap[:])], reads=[z])
        K.end_phase()


import os
GLA_STOP = os.environ.get('GLA_STOP', '')


def gla_phase(K, cfg, Xin, Xout, OF, W, j, lng, lnb):
    L, NSEG, SEG = cfg["L"], cfg["NSEG"], cfg["SEG"]
    NT = L // 128
    TPS = SEG // 128
    cst = K.cst
    with ExitStack() as ps_:
        K.begin_phase(ps_)
        K.set_rot([4, 5, 6, 7])
        OB = [K.banks[i] for i in range(4)]
        stage = [K.sb("wst%d" % i, [128, 1024], F32) for i in range(2)]
        Win, Winb = load_weight_bf16(K, W["gla_w_in"][j], D, 3072, "Win", stage, col_group=1024)
        Wout, Woutb = load_weight_bf16(K, W["gla_w_out"][j], D, D, "Wout", stage, col_group=1024)
        Wd = []
        for d in range(2):
            Wd.append(load_weight_bf16(K, W["gla_w_gate_down"][j, d], D, 16, "Wd%d" % d, stage, col_group=16))
        Wu = K.sb("Wu", [32, 2, 512], F32)
        K.dve.memset(Wu.ap[:], 0.0, writes=[Wu])
        K.sp.dma(Wu, [(Wu.ap[0:16, d, :], W["gla_w_gate_up"][j, d]) for d in range(2)] +
                 [(Wu.ap[16:17, d, :], W["gla_b_gate"][j, d:d + 1, :]) for d in range(2)], writes=[Wu])
        uext = K.sb("uext", [32, 128], F32)
        K.dve.memset(uext.ap[:], 1.0, writes=[uext])
        g_b = load_bcast(K, lng, D, "lng")
        b_b = load_bcast(K, lnb, D, "lnb")
        nw_b = load_bcast(K, W["gla_norm_w"][j], 256, "nw")
        ident, identb = K.ident, K.identb
        xts = [K.sb("xt%d" % i, [128, D], F32) for i in range(2)]
        xTs = [K.sb("xT%d" % i, [128, 8, 128], BF16) for i in range(2)]
        q_sb = K.sb("q_sb", [128, 512], F32)
        k_sb = K.sb("k_sb", [128, 512], F32)
        v_bf = K.sb("v_bf", [128, 1024], BF16)
        sr_bf = K.sb("sr_bf", [128, 1024], BF16)
        sp = K.sb("sp", [128, 512], F32)
        ets = [K.sb("et%d" % i, [128, 512], F32) for i in range(2)]
        qt_bf = K.sb("qt_bf", [128, 512], BF16)
        kt_bf = K.sb("kt_bf", [128, 512], BF16)
        qh_bf = K.sb("qh_bf", [128, 512], BF16)
        kh_bf = K.sb("kh_bf", [128, 512], BF16)
        qtT = K.sb("qtT", [128, 4, 128], BF16)
        ktT = K.sb("ktT", [128, 4, 128], BF16)
        qhA = K.sb("qhA", [128, 4, 128], BF16)
        qhB = K.sb("qhB", [128, 4, 128], BF16)
        K.dve.memset(qhA.ap[:], 0.0, writes=[qhA])
        K.dve.memset(qhB.ap[:], 0.0, writes=[qhB])
        scm = K.sb("scm", [128, 4, 128], BF16)
        S = K.sb("S", [128, 4, 256], F32)
        S_bf = K.sb("S_bf", [128, 4, 256], BF16)
        dec = K.sb("dec", [128, 8], F32)
        o_sb = K.sb("o_sb", [128, 1024], F32)
        of_sb = K.sb("of_sb", [128, 1024], F32)
        junk = K.sb("junk", [128, 256], F32)
        ss = K.sb("ss", [128, 4], F32)
        og_bf = K.sb("og_bf", [128, 1024], BF16)
        ogT = K.sb("ogT", [128, 8, 128], BF16)
        zs = [K.sb("z%d" % i, [128, D], F32) for i in range(2)]
        stats = K.sb("stats", [128, 2, 6], F32)
        mv = K.sb("mv", [128, 2], F32)
        rstd = K.sb("rstd", [128, 1], F32)
        eps_l2 = K.eps_tiles["l2"]
        one_t = K.eps_tiles["one"]
        zi = 0
        it = 0
        for dr in range(2):
            K.dve.memset(S.ap[:], 0.0, writes=[S])
            K.dve.memset(S_bf.ap[:], 0.0, writes=[S_bf])
            order = list(range(NT)) if dr == 0 else list(range(NT - 1, -1, -1))
            dn = "f" if dr == 0 else "b"
            for oi, ti in enumerate(order):
                t0 = ti * 128
                xt = xts[it % 2]
                xT = xTs[it % 2]
                it += 1
                if oi > 0 and (oi % TPS) == 0:
                    K.dve.ts(S.ap[:], S.ap[:], K.carry.ap[:, 0:1], None, ALU.mult, reads=[S, K.carry], writes=[S])
                    K.act.cp(S_bf.ap[:], S.ap[:], reads=[S], writes=[S_bf])
                K.sp.dma(xt, [(xt.ap[:], Xin[t0:t0 + 128, :])], writes=[xt])
                if dr == 1:
                    K.sp.dma(of_sb, [(of_sb.ap[:], OF[t0:t0 + 128, 0:1024])], reads=[K.ofbuf[ti]], writes=[of_sb])
                transpose_x(K, xt.ap[:], xt, xT.ap, xT, ident.ap, ident, col0=0)
                ngrp = 6 if dr == 1 else 4
                for gi in range(ngrp):
                    bank = K.ps()
                    mm(K, bank, bank.ap[:, 0:512], [(xT.ap[:, kc, :], Win[:, kc, gi * 512:(gi + 1) * 512]) for kc in range(8)],
                       reads=[xT] + Winb)
                    if gi == 0:
                        K.act.cp(q_sb.ap[:], bank.ap[:, 0:512], scale=128.0 ** -0.5, reads=[bank], writes=[q_sb])
                    elif gi == 1:
                        K.dve.cp(k_sb.ap[:], bank.ap[:, 0:512], reads=[bank], writes=[k_sb])
                    elif gi < 4:
                        q = K.ev()
                        q.cp(v_bf.ap[:, (gi - 2) * 512:(gi - 1) * 512], bank.ap[:, 0:512], reads=[bank], writes=[v_bf])
                    else:
                        K.act.actv(sr_bf.ap[:, (gi - 4) * 512:(gi - 3) * 512], bank.ap[:, 0:512], AF.Silu, reads=[bank], writes=[sr_bf])
                if GLA_STOP == "A":
                    continue
                bank = K.ps()
                mm(K, bank, bank.ap[0:16, 0:128], [(Wd[dr][0][:, kc, :], xT.ap[:, kc, :]) for kc in range(8)], reads=[xT] + Wd[dr][1])
                K.dve.cp(uext.ap[0:16, :], bank.ap[0:16, 0:128], reads=[bank], writes=[uext])
                bank = K.ps()
                mm(K, bank, bank.ap[:, 0:512], [(uext.ap[:], Wu.ap[:, dr, :])], reads=[uext, Wu])
                et = ets[0]
                K.act.actv(et.ap[:], bank.ap[:, 0:512], AF.Exp, scale=-1.0, reads=[bank], writes=[et])
                K.act.actv(sp.ap[:], et.ap[:], AF.Ln, bias=one_t.ap[:], scale=1.0, reads=[et, one_t], writes=[sp])
                if GLA_STOP == "B":
                    continue
                specs = [("M1" + dn, q_sb, qt_bf, 1.0), ("M1" + dn, k_sb, kt_bf, -1.0), ("M3" + dn, q_sb, qh_bf, 1.0), ("M4" + dn, k_sb, kh_bf, 1.0)]
                bank1 = None
                for si, (mname, src, dst, sc) in enumerate(specs):
                    if si != 1:
                        bank = K.ps()
                        mm(K, bank, bank.ap[:, 0:512], [(cst(mname), sp.ap[:])], reads=[K.consts, sp])
                        if si == 0:
                            bank1 = bank
                    else:
                        bank = bank1
                    et = ets[si % 2]
                    K.act.actv(et.ap[:], bank.ap[:, 0:512], AF.Exp, scale=sc, reads=[bank], writes=[et])
                    eng = K.dve if si % 2 == 0 else K.pool
                    eng.tt(dst.ap[:], src.ap[:], et.ap[:], ALU.mult, reads=[src, et], writes=[dst])
                if GLA_STOP == "C":
                    continue
                bank = K.ps()
                for h in range(4):
                    mm(K, bank, bank.ap[:, 2 * h:2 * h + 2], [(sp.ap[:, h * 128:(h + 1) * 128], cst("IND"))], reads=[sp, K.consts])
                K.act.actv(dec.ap[:], bank.ap[:, 0:8], AF.Exp, reads=[bank], writes=[dec])
                if GLA_STOP == "D":
                    continue
                bank = K.ps()
                bb = bank_bf(bank)
                for h in range(4):
                    tr(K, bank, bb[:, h * 128:(h + 1) * 128], qt_bf.ap[:, h * 128:(h + 1) * 128], identb.ap[:], reads=[qt_bf, identb], signal=False)
                for h in range(4):
                    tr(K, bank, bb[:, (4 + h) * 128:(5 + h) * 128], kt_bf.ap[:, h * 128:(h + 1) * 128], identb.ap[:], reads=[kt_bf, identb],
                       signal=(h == 3))
                K.act.cp(qtT.ap[:], bb[:, 0:512].rearrange("p (h t) -> p h t", h=4), reads=[bank], writes=[qtT])
                K.act.cp(ktT.ap[:], bb[:, 512:1024].rearrange("p (h t) -> p h t", h=4), reads=[bank], writes=[ktT])
                if GLA_STOP == "D1":
                    continue
                bank = K.ps()
                bb = bank_bf(bank)
                for h in range(4):
                    tr(K, bank, bb[:, h * 128:(h + 1) * 128], qh_bf.ap[:, h * 128:(h + 1) * 128], identb.ap[:], reads=[qh_bf, identb],
                       signal=(h == 3))
                bv = bb[:, 0:512].rearrange("p (h t) -> p h t", h=4)
                K.dve.cp(qhA.ap[:, :, 0:64], bv[:, :, 0:64], reads=[bank], writes=[qhA])
                K.dve.cp(qhB.ap[:, :, 64:128], bv[:, :, 64:128], reads=[bank], writes=[qhB])
                if GLA_STOP == "E":
                    continue
                bank = K.ps()
                for h in range(4):
                    mm(K, bank, bank.ap[:, h * 128:(h + 1) * 128], [(ktT.ap[:, h, :], qtT.ap[:, h, :])], reads=[ktT, qtT])
                K.dve.tt(scm.ap[:].rearrange("p h t -> p (h t)"), bank.ap[:, 0:512], cst("MASK4" + dn), ALU.mult,
                         reads=[bank, K.consts], writes=[scm])
                if GLA_STOP == "F":
                    continue
                for h in range(4):
                    ob = OB[h]
                    mm(K, ob, ob.ap[:, 0:256], [(scm.ap[:, h, :], v_bf.ap[:, h * 256:(h + 1) * 256])],
                       reads=[scm, v_bf], start=True, stop=False)
                chunks = [0, 1] if dr == 0 else [1, 0]
                for ci, ck in enumerate(chunks):
                    qh = qhA if ck == 0 else qhB
                    for h in range(4):
                        ob = OB[h]
                        mm(K, ob, ob.ap[:, 0:256], [(qh.ap[:, h, :], S_bf.ap[:, h, :])],
                           reads=[qh, S_bf], start=False, stop=(ci == 1))
                    r0 = ck * 64
                    kvb = []
                    for hp in range(2):
                        bank = K.ps()
                        for hh in range(2):
                            h = hp * 2 + hh
                            mm(K, bank, bank.ap[:, hh * 256:(hh + 1) * 256],
                               [(kh_bf.ap[r0:r0 + 64, h * 128:(h + 1) * 128], v_bf.ap[r0:r0 + 64, h * 256:(h + 1) * 256])], reads=[kh_bf, v_bf])
                        kvb.append(bank)
                    for h in range(4):
                        bank = kvb[h // 2]
                        K.dve.stt(S.ap[:, h, :], S.ap[:, h, :], dec.ap[:, 2 * h + ck:2 * h + ck + 1], bank.ap[:, (h % 2) * 256:(h % 2 + 1) * 256],
                                  ALU.mult, ALU.add, reads=[S, dec, bank], writes=[S])
                    K.act.cp(S_bf.ap[:], S.ap[:], reads=[S], writes=[S_bf])
                if dr == 0:
                    for h in range(4):
                        q = K.ev()
                        q.cp(o_sb.ap[:, h * 256:(h + 1) * 256], OB[h].ap[:, 0:256], reads=[OB[h]], writes=[o_sb])
                    K.pool.dma(o_sb, [(OF[t0:t0 + 128, 0:1024], o_sb.ap[:])], reads=[o_sb], writes=[K.ofbuf[ti]])
                else:
                    for h in range(4):
                        K.dve.tt(o_sb.ap[:, h * 256:(h + 1) * 256], OB[h].ap[:, 0:256], of_sb.ap[:, h * 256:(h + 1) * 256], ALU.add,
                                 reads=[OB[h], of_sb], writes=[o_sb])
                    if GLA_STOP == "O":
                        K.pool.dma(o_sb, [(Xout[t0:t0 + 128, :], o_sb.ap[:])], reads=[o_sb])
                        continue
                    for h in range(4):
                        K.act.actv(junk.ap[:], o_sb.ap[:, h * 256:(h + 1) * 256], AF.Square, accum=ss.ap[:, h:h + 1], reads=[o_sb], writes=[junk, ss])
                    K.act.actv(ss.ap[:], ss.ap[:], AF.Sqrt, bias=eps_l2.ap[:], scale=1.0 / 256.0, reads=[ss, eps_l2], writes=[ss])
                    K.dve.recip(ss.ap[:], ss.ap[:], reads=[ss], writes=[ss])
                    for h in range(4):
                        K.dve.stt(o_sb.ap[:, h * 256:(h + 1) * 256], o_sb.ap[:, h * 256:(h + 1) * 256], ss.ap[:, h:h + 1], nw_b.ap[:],
                                  ALU.mult, ALU.mult, reads=[o_sb, ss, nw_b], writes=[o_sb])
                    K.pool.tt(og_bf.ap[:], o_sb.ap[:], sr_bf.ap[:], ALU.mult, reads=[o_sb, sr_bf], writes=[og_bf])
                    bank = K.ps()
                    bb = bank_bf(bank)
                    for c in range(8):
                        tr(K, bank, bb[:, c * 128:(c + 1) * 128], og_bf.ap[:, c * 128:(c + 1) * 128], identb.ap[:], reads=[og_bf, identb],
                           signal=(c == 7))
                    K.act.cp(ogT.ap[:], bb.rearrange("p (c t) -> p c t", c=8), reads=[bank], writes=[ogT])
                    banks = []
                    for fh in range(2):
                        bank = K.ps()
                        mm(K, bank, bank.ap[:, 0:512], [(ogT.ap[:, c, :], Wout[:, c, fh * 512:(fh + 1) * 512]) for c in range(8)],
                           reads=[ogT] + Woutb)
                        banks.append(bank)
                    z = zs[zi % 2]
                    zi += 1
                    residual_ln(K, banks, xt, xt.ap[:], 1.0 / ALPHA, g_b, b_b, z, "lna", (stats, mv, rstd))
                    K.pool.dma(z, [(Xout[t0:t0 + 128, :], z.ap[:])], reads=[z])
        K.set_rot(list(range(8)))
        K.end_phase()


def gdn_scan_pass(K, cfg, Xin, OF, W, j, dr):
    L, NSEG, SEG = cfg["L"], cfg["NSEG"], cfg["SEG"]
    NT = L // 128
    TPS = SEG // 128
    cst = K.cst
    dn = "f" if dr == 0 else "b"
    with ExitStack() as ps_:
        K.begin_phase(ps_)
        K.set_rot(list(range(8)))
        stage = [K.sb("wst%d" % i, [128, 1024], F32) for i in range(2)]
        Win, Winb = load_weight_bf16(K, W["gdn_w_in"][j][:, 0:4096], D, 4096, "Win", stage, col_group=1024)
        Wab, Wabb = load_weight_bf16(K, W["gdn_w_ab"][j, dr], D, 32, "Wab", stage, col_group=32)
        ident, identb = K.ident, K.identb
        cwT = K.sb("cwT", [128, 128], F32)
        cw = K.sb("cw", [128, 128], F32)
        K.sp.dma(cwT, [(cwT.ap[:], W["gdn_conv_w"][j].rearrange("t (c p) -> (t c) p", p=128))], writes=[cwT])
        bank = K.ps()
        tr(K, bank, bank.ap[:, 0:128], cwT.ap[:], ident.ap[:], reads=[cwT, ident])
        K.dve.cp(cw.ap[:], bank.ap[:, 0:128], reads=[bank], writes=[cw])
        alog = load_bcast(K, W["gdn_a_log"][j].rearrange("a b -> (a b)"), 32, "alog")
        dtb = load_bcast(K, W["gdn_dt_bias"][j].rearrange("a b -> (a b)"), 32, "dtb")
        negA = K.sb("negA", [128, 32], F32)
        K.act.actv(negA.ap[:], alog.ap[:], AF.Exp, reads=[alog], writes=[negA])
        K.dve.ts(negA.ap[:], negA.ap[:], -1.0, None, ALU.mult, reads=[negA], writes=[negA])
        ones_bf = K.sb("ones_bf", [128, 128], BF16)
        K.dve.cp(ones_bf.ap[:], cst("ONES"), reads=[K.consts], writes=[ones_bf])
        xts = [K.sb("xt%d" % i, [128, D], F32) for i in range(2)]
        xTs = [K.sb("xTe%d" % i, [128, 8, 132], BF16) for i in range(2)]
        xhp = K.sb("xhp", [2, D], F32)
        xhn = K.sb("xhn", [1, D], F32)
        ctmp = [K.sb("ctmp%d" % i, [128, 132], F32) for i in range(2)]
        accs = [K.sb("acc%d" % i, [128, 128], F32) for i in range(2)] * 2
        s4q = K.sb("s4q", [128, 8, 128], F32)
        s4k = K.sb("s4k", [128, 8, 128], F32)
        sq4 = K.sb("sq4", [128, 4, 128], BF16)
        rn4 = K.sb("rn4", [128, 512], F32)
        qTn = K.sb("qTn", [128, 8, 128], BF16)
        kTn = K.sb("kTn", [128, 8, 128], BF16)
        vT = K.sb("vT", [128, 16, 128], BF16)
        v_tok = K.sb("v_tok", [128, 16, 128], BF16)
        kend = K.sb("kend", [128, 16, 128], BF16)
        t16 = K.sb("t16", [128, 16], F32)
        e16 = K.sb("e16", [128, 16], F32)
        g16 = K.sb("g16", [128, 16], F32)
        beta16 = K.sb("beta16", [128, 16], F32)
        egk = K.sb("egk", [128, 64], F32)
        negeg = K.sb("negeg", [128, 16], F32)
        Pm = [K.sb("Pm%d" % i, [128, 128], F32) for i in range(2)]
        decT4 = K.sb("decT4", [128, 4, 128], F32)
        decTs4 = K.sb("decTs4", [128, 4, 128], F32)
        Ms = [K.sb("M%d" % i, [128, 4, 128], BF16) for i in range(2)]
        MTs = [K.sb("MT%d" % i, [128, 4, 128], BF16) for i in range(2)]
        Qf = K.sb("Qf", [128, 4, 128], F32)
        Qb = K.sb("Qb", [128, 4, 128], BF16)
        T2T = K.sb("T2T", [128, 16, 128], BF16)
        attnT = K.sb("attnT", [128, 16, 128], BF16)
        resid = [K.sb("resid%d" % i, [128, 16, 128], BF16) for i in range(2)]
        vn = [K.sb("vn%d" % i, [128, 16, 128], BF16) for i in range(2)]
        for b in resid + vn:
            K.dve.memset(b.ap[:], 0.0, writes=[b])
        tmpq = K.sb("tmpq", [128, 4, 128], F32)
        S = K.sb("S", [128, 16, 128], F32)
        S_bf = K.sb("S_bf", [128, 16, 128], BF16)
        K.dve.memset(S.ap[:], 0.0, writes=[S])
        K.dve.memset(S_bf.ap[:], 0.0, writes=[S_bf])
        o_sb = K.sb("o_sb", [128, 16, 128], F32)
        of_sb = K.sb("of_sb", [128, 2048], F32)
        eps_l2 = K.eps_tiles["l2"]
        one_t = K.eps_tiles["one"]
        order = list(range(NT)) if dr == 0 else list(range(NT - 1, -1, -1))
        for oi, ti in enumerate(order):
            t0 = ti * 128
            xt = xts[oi % 2]
            xT = xTs[oi % 2]
            if oi > 0 and (oi % TPS) == 0:
                K.dve.ts(S.ap[:], S.ap[:], K.carry.ap[:, 0:1], None, ALU.mult, reads=[S, K.carry], writes=[S])
                K.act.cp(S_bf.ap[:], S.ap[:], reads=[S], writes=[S_bf])
            K.sp.dma(xt, [(xt.ap[:], Xin[t0:t0 + 128, :])], writes=[xt])
            if t0 == 0:
                K.dve.memset(xhp.ap[:], 0.0, writes=[xhp])
            else:
                K.sp.dma(xhp, [(xhp.ap[:], Xin[t0 - 2:t0, :])], writes=[xhp])
                if t0 % SEG == 0:
                    K.dve.ts(xhp.ap[:], xhp.ap[:], K.carry.ap[0:2, 0:1], None, ALU.mult, reads=[xhp, K.carry], writes=[xhp])
            if t0 + 128 == L:
                K.dve.memset(xhn.ap[:], 0.0, writes=[xhn])
            else:
                K.sp.dma(xhn, [(xhn.ap[:], Xin[t0 + 128:t0 + 129, :])], writes=[xhn])
                if (t0 + 128) % SEG == 0:
                    K.dve.ts(xhn.ap[:], xhn.ap[:], K.carry.ap[0:1, 0:1], None, ALU.mult, reads=[xhn, K.carry], writes=[xhn])
            if dr == 1:
                K.sp.dma(of_sb, [(of_sb.ap[:], OF[t0:t0 + 128, :])], reads=[K.ofbuf[ti]], writes=[of_sb])
            transpose_x(K, xt.ap[:], xt, xT.ap, xT, ident.ap, ident, col0=2)
            bank = K.ps()
            for kc in range(8):
                tr(K, bank, bank.ap[:, kc * 4:kc * 4 + 2], xhp.ap[0:2, kc * 128:(kc + 1) * 128], ident.ap[0:2, 0:2], reads=[xhp, ident], signal=False)
                tr(K, bank, bank.ap[:, kc * 4 + 2:kc * 4 + 3], xhn.ap[0:1, kc * 128:(kc + 1) * 128], ident.ap[0:1, 0:1], reads=[xhn, ident],
                   signal=(kc == 7))
            hv_ = bank.ap[:, 0:32].rearrange("p (k c) -> p k c", c=4)
            K.act.cp(xT.ap[:, :, 0:2], hv_[:, :, 0:2], reads=[bank], writes=[xT])
            K.act.cp(xT.ap[:, :, 130:131], hv_[:, :, 2:3], reads=[bank], writes=[xT])
            for c3 in range(0, 32, 3):
                bank = K.ps()
                cs = list(range(c3, min(32, c3 + 3)))
                for i, c in enumerate(cs):
                    mm(K, bank, bank.ap[:, i * 132:i * 132 + 131], [(Win[:, kc, c * 128:(c + 1) * 128], xT.ap[:, kc, 0:131]) for kc in range(8)],
                       reads=[xT] + Winb)
                for i, c in enumerate(cs):
                    acc = accs[c % 4]
                    eng = K.dve
                    src = bank.ap[:, i * 132:i * 132 + 131]
                    sbuf_ = bank
                    eng.ts(acc.ap[:], src[:, 0:128], cw.ap[:, c:c + 1], None, ALU.mult, reads=[sbuf_, cw], writes=[acc])
                    for tp in range(1, 4):
                        eng.stt(acc.ap[:], src[:, tp:tp + 128], cw.ap[:, tp * 32 + c:tp * 32 + c + 1], acc.ap[:], ALU.mult, ALU.add,
                                reads=[sbuf_, cw, acc], writes=[acc])
                    if c < 8:
                        K.act.actv(s4q.ap[:, c, :], acc.ap[:], AF.Silu, reads=[acc], writes=[s4q])
                    elif c < 16:
                        K.act.actv(s4k.ap[:, c - 8, :], acc.ap[:], AF.Silu, reads=[acc], writes=[s4k])
                    else:
                        K.act.actv(vT.ap[:, c - 16, :], acc.ap[:], AF.Silu, reads=[acc], writes=[vT])
            for (src4, dstn, scl) in ((s4q, qTn, 128.0 ** -0.5), (s4k, kTn, None)):
                for g in range(2):
                    sv = src4.ap[:, g * 4:(g + 1) * 4, :]
                    K.pool.tt(sq4.ap[:], sv, sv, ALU.mult, reads=[src4], writes=[sq4])
                    bank = K.ps()
                    for i in range(4):
                        mm(K, bank, bank.ap[:, i * 128:(i + 1) * 128], [(ones_bf.ap[:], sq4.ap[:, i, :])], reads=[ones_bf, sq4])
                    K.act.actv(rn4.ap[:], bank.ap[:, 0:512], AF.Sqrt, bias=eps_l2.ap[:], scale=1.0, reads=[bank, eps_l2], writes=[rn4])
                    K.dve.recip(rn4.ap[:], rn4.ap[:], reads=[rn4], writes=[rn4])
                    dv_ = dstn.ap[:, g * 4:(g + 1) * 4, :].rearrange("p a b -> p (a b)")
                    sv2 = sv.rearrange("p a b -> p (a b)")
                    if scl is None:
                        K.dve.tt(dv_, sv2, rn4.ap[:], ALU.mult, reads=[src4, rn4], writes=[dstn])
                    else:
                        K.dve.stt(dv_, sv2, float(scl), rn4.ap[:], ALU.mult, ALU.mult, reads=[src4, rn4], writes=[dstn])
            bank = K.ps()
            mm(K, bank, bank.ap[:, 0:32], [(xT.ap[:, kc, 2:130], Wab[:, kc, :]) for kc in range(8)], reads=[xT] + Wabb)
            K.dve.tt(t16.ap[:], bank.ap[:, 0:16], dtb.ap[:, dr * 16:(dr + 1) * 16], ALU.add, reads=[bank, dtb], writes=[t16])
            K.act.actv(e16.ap[:], t16.ap[:], AF.Exp, reads=[t16], writes=[e16])
            K.act.actv(e16.ap[:], e16.ap[:], AF.Ln, bias=one_t.ap[:], scale=1.0, reads=[e16, one_t], writes=[e16])
            K.dve.tt(g16.ap[:], e16.ap[:], negA.ap[:, dr * 16:(dr + 1) * 16], ALU.mult, reads=[e16, negA], writes=[g16])
            K.act.actv(beta16.ap[:], bank.ap[:, 16:32], AF.Exp, scale=-1.0, reads=[bank], writes=[beta16])
            K.dve.ts(beta16.ap[:], beta16.ap[:], 1.0, None, ALU.add, reads=[beta16], writes=[beta16])
            K.dve.recip(beta16.ap[:], beta16.ap[:], reads=[beta16], writes=[beta16])
            bank = K.ps()
            for i, mname in enumerate(("TRI" + dn, "AFTER" + dn, "CH0", "CH1")):
                mm(K, bank, bank.ap[:, i * 16:(i + 1) * 16], [(cst(mname), g16.ap[:])], reads=[K.consts, g16])
            K.act.actv(egk.ap[:], bank.ap[:, 0:64], AF.Exp, reads=[bank], writes=[egk])
            K.dve.ts(negeg.ap[:], egk.ap[:, 0:16], -1.0, None, ALU.mult, reads=[egk], writes=[negeg])
            bank = K.ps()
            bb = bank_bf(bank)
            for hk in range(8):
                tr(K, bank, bb[:, hk * 128:(hk + 1) * 128], kTn.ap[:, hk, :], identb.ap[:], reads=[kTn, identb], signal=(hk == 7))
            for hv in range(16):
                q = K.act if (hv % 2) else K.dve
                if q is K.act:
                    q.actv(kend.ap[:, hv, :], bb[:, (hv // 2) * 128:(hv // 2 + 1) * 128], AF.Copy, scale=egk.ap[:, 16 + hv:17 + hv],
                           reads=[bank, egk], writes=[kend])
                else:
                    q.ts(kend.ap[:, hv, :], bb[:, (hv // 2) * 128:(hv // 2 + 1) * 128], egk.ap[:, 16 + hv:17 + hv], None, ALU.mult,
                         reads=[bank, egk], writes=[kend])
            for g in range(2):
                bank = K.ps()
                bb = bank_bf(bank)
                for i in range(8):
                    tr(K, bank, bb[:, i * 128:(i + 1) * 128], vT.ap[:, g * 8 + i, :], identb.ap[:], reads=[vT, identb], signal=(i == 7))
                q = K.ev()
                q.cp(v_tok.ap[:, g * 8:(g + 1) * 8, :], bb.rearrange("p (a b) -> p a b", a=8), reads=[bank], writes=[v_tok])
            for gi in range(4):
                hv0 = gi * 4
                gbank = K.ps()
                for i in range(2):
                    hk = 2 * gi + i
                    mm(K, gbank, gbank.ap[:, i * 128:(i + 1) * 128], [(kTn.ap[:, hk, :], kTn.ap[:, hk, :])], reads=[kTn])
                    mm(K, gbank, gbank.ap[:, (2 + i) * 128:(3 + i) * 128], [(kTn.ap[:, hk, :], qTn.ap[:, hk, :])], reads=[kTn, qTn])
                dbank = K.ps()
                for i in range(4):
                    hv = hv0 + i
                    P = Pm[i % 2]
                    K.dve.ts(P.ap[:], cst("TRI" + dn), g16.ap[:, hv:hv + 1], None, ALU.mult, reads=[K.consts, g16], writes=[P])
                    mm(K, dbank, dbank.ap[:, i * 128:(i + 1) * 128],
                       [(cst("ONES"), P.ap[:]), (P.ap[:], cst("NEGONES")), (cst("IDENT"), cst("NEG" + dn))], reads=[P, K.consts])
                K.act.actv(decT4.ap[:].rearrange("p a b -> p (a b)"), dbank.ap[:, 0:512], AF.Exp, reads=[dbank], writes=[decT4])
                K.pool.tt(decTs4.ap[:].rearrange("p a b -> p (a b)"), decT4.ap[:].rearrange("p a b -> p (a b)"), cst("OFFD4"), ALU.mult,
                          reads=[decT4, K.consts], writes=[decTs4])
                M, MT = Ms[0], MTs[0]
                for i in range(4):
                    hv = hv0 + i
                    K.dve.stt(M.ap[:, i, :], gbank.ap[:, (i // 2) * 128:(i // 2 + 1) * 128], beta16.ap[:, hv:hv + 1], decTs4.ap[:, i, :],
                              ALU.mult, ALU.mult, reads=[gbank, beta16, decTs4], writes=[M])
                    K.dve.tt(attnT.ap[:, hv, :], gbank.ap[:, (2 + i // 2) * 128:(3 + i // 2) * 128], decT4.ap[:, i, :], ALU.mult,
                             reads=[gbank, decT4], writes=[attnT])
                bank = K.ps()
                bb = bank_bf(bank)
                for i in range(4):
                    tr(K, bank, bb[:, i * 128:(i + 1) * 128], M.ap[:, i, :], identb.ap[:], reads=[M, identb], signal=(i == 3))
                K.act.cp(MT.ap[:].rearrange("p a b -> p (a b)"), bb[:, 0:512], reads=[bank], writes=[MT])
                K.dve.tt(Qf.ap[:].rearrange("p a b -> p (a b)"), cst("IDENT4"), M.ap[:].rearrange("p a b -> p (a b)"), ALU.subtract,
                         reads=[K.consts, M], writes=[Qf])
                K.act.cp(Qb.ap[:], Qf.ap[:], reads=[Qf], writes=[Qb])
                for lev in range(1, 6):
                    Mn, MTn = Ms[lev % 2], MTs[lev % 2]
                    if lev < 5:
                        bank = K.ps()
                        for i in range(4):
                            mm(K, bank, bank.ap[:, i * 128:(i + 1) * 128], [(MT.ap[:, i, :], M.ap[:, i, :])], reads=[MT, M])
                        K.act.cp(Mn.ap[:].rearrange("p a b -> p (a b)"), bank.ap[:, 0:512], reads=[bank], writes=[Mn])
                    bank = K.ps()
                    for i in range(4):
                        mm(K, bank, bank.ap[:, i * 128:(i + 1) * 128], [(M.ap[:, i, :], MT.ap[:, i, :])], reads=[MT, M])
                    K.dve.cp(MTn.ap[:].rearrange("p a b -> p (a b)"), bank.ap[:, 0:512], reads=[bank], writes=[MTn])
                    bank = K.ps()
                    for i in range(4):
                        mm(K, bank, bank.ap[:, i * 128:(i + 1) * 128], [(MTn.ap[:, i, :], Qb.ap[:, i, :])], reads=[MTn, Qb])
                    K.dve.tt(Qf.ap[:].rearrange("p a b -> p (a b)"), Qf.ap[:].rearrange("p a b -> p (a b)"), bank.ap[:, 0:512], ALU.add,
                             reads=[Qf, bank], writes=[Qf])
                    if lev < 5:
                        K.act.cp(Qb.ap[:], Qf.ap[:], reads=[Qf], writes=[Qb])
                    else:
                        K.act.cp(T2T.ap[:, hv0:hv0 + 4, :], Qf.ap[:], reads=[Qf], writes=[T2T])
                    M, MT = Mn, MTn
            chunks = [0, 1] if dr == 0 else [1, 0]
            for ck in chunks:
                r0, r1 = ck * 64, ck * 64 + 64
                rs, vnb = resid[ck], vn[ck]
                for gi in range(4):
                    hv0 = gi * 4
                    kb = K.ps()
                    for i in range(2):
                        hk = 2 * gi + i
                        mm(K, kb, kb.ap[:, i * 256:(i + 1) * 256], [(kTn.ap[:, hk, :], S_bf.ap[:, 2 * hk:2 * hk + 2, :])], reads=[kTn, S_bf])
                    for i in range(4):
                        hv = hv0 + i
                        K.dve.stt(rs.ap[r0:r1, hv, :], kb.ap[r0:r1, i * 128:(i + 1) * 128], negeg.ap[r0:r1, hv:hv + 1], v_tok.ap[r0:r1, hv, :],
                                  ALU.mult, ALU.add, reads=[kb, negeg, v_tok], writes=[rs])
                    nb = K.ps()
                    for i in range(4):
                        hv = hv0 + i
                        mm(K, nb, nb.ap[:, i * 128:(i + 1) * 128], [(T2T.ap[:, hv, :], rs.ap[:, hv, :])], reads=[T2T, rs])
                    for i in range(4):
                        hv = hv0 + i
                        K.act.actv(vnb.ap[r0:r1, hv, :], nb.ap[r0:r1, i * 128:(i + 1) * 128], AF.Copy, scale=beta16.ap[r0:r1, hv:hv + 1],
                                   reads=[nb, beta16], writes=[vnb])
                    qb_ = K.ps()
                    for i in range(2):
                        hk = 2 * gi + i
                        mm(K, qb_, qb_.ap[:, i * 256:(i + 1) * 256], [(qTn.ap[:, hk, :], S_bf.ap[:, 2 * hk:2 * hk + 2, :])], reads=[qTn, S_bf])
                    for i in range(4):
                        hv = hv0 + i
                        K.act.actv(tmpq.ap[r0:r1, i, :], qb_.ap[r0:r1, i * 128:(i + 1) * 128], AF.Copy, scale=egk.ap[r0:r1, hv:hv + 1],
                                   reads=[qb_, egk], writes=[tmpq])
                    ab_ = K.ps()
                    for i in range(4):
                        hv = hv0 + i
                        mm(K, ab_, ab_.ap[:, i * 128:(i + 1) * 128], [(attnT.ap[:, hv, :], vnb.ap[:, hv, :])], reads=[attnT, vnb])
                    for i in range(4):
                        hv = hv0 + i
                        K.dve.tt(o_sb.ap[r0:r1, hv, :], ab_.ap[r0:r1, i * 128:(i + 1) * 128], tmpq.ap[r0:r1, i, :], ALU.add,
                                 reads=[ab_, tmpq], writes=[o_sb])
                    vb_ = K.ps()
                    for i in range(4):
                        hv = hv0 + i
                        mm(K, vb_, vb_.ap[:, i * 128:(i + 1) * 128], [(kend.ap[:, hv, :], vnb.ap[:, hv, :])], reads=[kend, vnb])
                    for i in range(4):
                        hv = hv0 + i
                        K.dve.stt(S.ap[:, hv, :], S.ap[:, hv, :], egk.ap[:, 32 + ck * 16 + hv:33 + ck * 16 + hv], vb_.ap[:, i * 128:(i + 1) * 128],
                                  ALU.mult, ALU.add, reads=[S, egk, vb_], writes=[S])
                    K.pool.cp(S_bf.ap[:, hv0:hv0 + 4, :], S.ap[:, hv0:hv0 + 4, :], reads=[S], writes=[S_bf])
            ov = o_sb.ap[:].rearrange("p a b -> p (a b)")
            if dr == 1:
                K.pool.tt(ov, ov, of_sb.ap[:], ALU.add, reads=[o_sb, of_sb], writes=[o_sb])
            K.pool.dma(o_sb, [(OF[t0:t0 + 128, :], ov)], reads=[o_sb], writes=[K.ofbuf[ti]])
        K.end_phase()


def gdn_finish_pass(K, cfg, Xin, Xout, OF, W, j, lng, lnb):
    L = cfg["L"]
    NT = L // 128
    with ExitStack() as ps_:
        K.begin_phase(ps_)
        K.set_rot(list(range(8)))
        stage = [K.sb("wst%d" % i, [128, 1024], F32) for i in range(2)]
        Wz, Wzb = load_weight_bf16(K, W["gdn_w_in"][j][:, 4096:6144], D, 2048, "Wz", stage, col_group=1024)
        Wout, Woutb = load_weight_bf16(K, W["gdn_w_out"][j], 2048, D, "Wout", stage, col_group=1024)
        g_b = load_bcast(K, lng, D, "lng")
        b_b = load_bcast(K, lnb, D, "lnb")
        nw_b = load_bcast(K, W["gdn_norm_w"][j], 128, "nw")
        ident, identb = K.ident, K.identb
        xts = [K.sb("xt%d" % i, [128, D], F32) for i in range(2)]
        xTs = [K.sb("xT%d" % i, [128, 8, 128], BF16) for i in range(2)]
        os_ = [K.sb("o%d" % i, [128, 16, 128], F32) for i in range(2)]
        sqo = K.sb("sqo", [128, 16, 128], F32)
        ss = K.sb("ss", [128, 16], F32)
        sz_bf = K.sb("sz_bf", [128, 2048], BF16)
        og_bf = K.sb("og_bf", [128, 2048], BF16)
        ogT = K.sb("ogT", [128, 16, 128], BF16)
        zs = [K.sb("z%d" % i, [128, D], F32) for i in range(2)]
        stats = K.sb("stats", [128, 2, 6], F32)
        mv = K.sb("mv", [128, 2], F32)
        rstd = K.sb("rstd", [128, 1], F32)
        eps_l2 = K.eps_tiles["l2"]
        for ti in range(NT):
            t0 = ti * 128
            xt, xT, o_sb = xts[ti % 2], xTs[ti % 2], os_[ti % 2]
            ov = o_sb.ap[:].rearrange("p a b -> p (a b)")
            K.sp.dma(xt, [(xt.ap[:], Xin[t0:t0 + 128, :])], writes=[xt])
            K.sp.dma(o_sb, [(ov, OF[t0:t0 + 128, :])], reads=[K.ofbuf[ti]], writes=[o_sb])
            transpose_x(K, xt.ap[:], xt, xT.ap, xT, ident.ap, ident, col0=0)
            for gi in range(4):
                bank = K.ps()
                mm(K, bank, bank.ap[:, 0:512], [(xT.ap[:, kc, :], Wz[:, kc, gi * 512:(gi + 1) * 512]) for kc in range(8)], reads=[xT] + Wzb)
                K.act.actv(sz_bf.ap[:, gi * 512:(gi + 1) * 512], bank.ap[:, 0:512], AF.Silu, reads=[bank], writes=[sz_bf])
            K.pool.tt(sqo.ap[:], o_sb.ap[:], o_sb.ap[:], ALU.mult, reads=[o_sb], writes=[sqo])
            K.dve.op(lambda e, a=ss.ap[:], b=sqo.ap[:]: e.tensor_reduce(a, b, mybir.AxisListType.X, ALU.add), reads=[sqo], writes=[ss])
            K.act.actv(ss.ap[:], ss.ap[:], AF.Sqrt, bias=eps_l2.ap[:], scale=1.0 / 128.0, reads=[ss, eps_l2], writes=[ss])
            K.dve.recip(ss.ap[:], ss.ap[:], reads=[ss], writes=[ss])
            for hv in range(16):
                K.dve.stt(o_sb.ap[:, hv, :], o_sb.ap[:, hv, :], ss.ap[:, hv:hv + 1], nw_b.ap[:], ALU.mult, ALU.mult, reads=[o_sb, ss, nw_b], writes=[o_sb])
            K.pool.tt(og_bf.ap[:], ov, sz_bf.ap[:], ALU.mult, reads=[o_sb, sz_bf], writes=[og_bf])
            for g in range(2):
                bank = K.ps()
                bb = bank_bf(bank)
                for i in range(8):
                    c = g * 8 + i
                    tr(K, bank, bb[:, i * 128:(i + 1) * 128], og_bf.ap[:, c * 128:(c + 1) * 128], identb.ap[:], reads=[og_bf, identb], signal=(i == 7))
                q = K.ev()
                q.cp(ogT.ap[:, g * 8:(g + 1) * 8, :], bb.rearrange("p (a b) -> p a b", a=8), reads=[bank], writes=[ogT])
            banks = []
            for fh in range(2):
                bank = K.ps()
                mm(K, bank, bank.ap[:, 0:512], [(ogT.ap[:, c, :], Wout[:, c, fh * 512:(fh + 1) * 512]) for c in range(16)], reads=[ogT] + Woutb)
                banks.append(bank)
            z = zs[ti % 2]
            residual_ln(K, banks, xt, xt.ap[:], 1.0 / ALPHA, g_b, b_b, z, "lna", (stats, mv, rstd))
            K.pool.dma(z, [(Xout[t0:t0 + 128, :], z.ap[:])], reads=[z])
        K.end_phase()


def gdn_phase(K, cfg, Xin, Xout, OF, W, j, lng, lnb):
    gdn_scan_pass(K, cfg, Xin, OF, W, j, 0)
    gdn_scan_pass(K, cfg, Xin, OF, W, j, 1)
    gdn_finish_pass(K, cfg, Xin, Xout, OF, W, j, lng, lnb)


_PROGRAM_CACHE = {}


def kernel(**inputs):
    xp = np.asarray(inputs["x_prompt"], dtype=np.float32)
    xsm = np.asarray(inputs["x_sample"], dtype=np.float32)
    mp = np.asarray(inputs["mem_prompt"], dtype=np.float32)
    ms = np.asarray(inputs["mem_sample"], dtype=np.float32)
    L, NSEG, SEG = 16384, 4, 4096
    consts = make_consts()
    cfg = {"L": L, "NSEG": NSEG, "SEG": SEG, "NCONST": consts.shape[1]}
    if "nc" not in _PROGRAM_CACHE:
        _PROGRAM_CACHE["nc"] = build_program(cfg)
    nc = _PROGRAM_CACHE["nc"]
    wmap = {name: np.ascontiguousarray(np.asarray(inputs[name], dtype=np.float32)) for name, _ in WEIGHT_SPECS}
    one = np.ones((128, 1), np.float32)
    zero = np.zeros((128, 1), np.float32)
    streams = [
        (np.ascontiguousarray(xp.reshape(L, D)), np.ascontiguousarray(np.broadcast_to(mp, (NSEG, NMEM, D))), one),
        (np.ascontiguousarray(xsm[0:4].reshape(L, D)), np.ascontiguousarray(ms[0:4]), zero),
        (np.ascontiguousarray(xsm[4:8].reshape(L, D)), np.ascontiguousarray(ms[4:8]), zero),
    ]
    in_maps = []
    for c in range(8):
        xs_, mem_, carry_ = streams[min(c, 2)]
        m = {"xs": xs_, "mem": mem_, "carry": carry_, "consts": consts}
        m.update(wmap)
        in_maps.append(m)
    res = run_bass_kernel_spmd(nc, in_maps, core_ids=list(range(8)))
    y_prompt = np.asarray(res.results[0]["y"], dtype=np.float32).reshape(1, L, D)
    y_sample = np.concatenate([np.asarray(res.results[1]["y"], dtype=np.float32).reshape(4, SEG, D),
                               np.asarray(res.results[2]["y"], dtype=np.float32).reshape(4, SEG, D)], axis=0)
    return (y_prompt, y_sample)
```

```python
import numpy as np
from contextlib import ExitStack
import concourse.bass as bass
import concourse.mybir as mybir
from concourse.bass_utils import run_bass_kernel_spmd

F32 = mybir.dt.float32
BF16 = mybir.dt.bfloat16
AF = mybir.ActivationFunctionType
ALU = mybir.AluOpType

D = 1024
DFF = 2816
NMEM = 256
DEPTH = 2
ALPHA = (2.0 * DEPTH) ** 0.25
LN_EPS = 1e-5
NORM_EPS = 1e-6
GLA_H, GLA_DK, GLA_DV = 4, 128, 256
GDN_HK, GDN_HV = 8, 16


class Sem:
    def __init__(self, h, name):
        self.h = h
        self.name = name
        self.count = 0


class Buf:
    def __init__(self, name, ap=None):
        self.name = name
        self.ap = ap
        self.wtoks = {}
        self.rtoks = {}
        self.dsem = None
        self.excl = False


class Q:
    def __init__(self, K, name):
        self.K = K
        self.name = name
        self.sem = K.new_sem("q_" + name)
        self.seen = {}
        self.ops = []

    def wait_tok(self, sem, val):
        if sem is self.sem and self.name == "pe":
            return
        if self.seen.get(sem, 0) >= val:
            return
        self.seen[sem] = val
        self.ops.append(("w", sem, val))

    def deps(self, reads, writes):
        for b in reads:
            for s, v in b.wtoks.items():
                self.wait_tok(s, v)
            if b.excl:
                for s, v in b.rtoks.items():
                    if s is not self.sem:
                        self.wait_tok(s, v)
        for b in writes:
            for s, v in b.wtoks.items():
                self.wait_tok(s, v)
            for s, v in b.rtoks.items():
                self.wait_tok(s, v)

    def mark(self, tok, reads, writes):
        s, v = tok
        for b in reads:
            if b.rtoks.get(s, 0) < v:
                b.rtoks[s] = v
        for b in writes:
            b.wtoks[s] = v
            b.rtoks = {}

    def op(self, fn, reads=(), writes=(), signal=True):
        self.deps(reads, writes)
        if signal:
            self.sem.count += 1
            tok = (self.sem, self.sem.count)
            self.ops.append(("i", fn, self.sem, 1))
        else:
            tok = (self.sem, self.sem.count + 1)
            self.ops.append(("i", fn, None, 0))
        self.mark(tok, reads, writes)
        return tok

    def tt(self, out, in0, in1, op, reads=(), writes=()):
        return self.op(lambda e: e.tensor_tensor(out, in0, in1, op), reads, writes)

    def ts(self, out, in0, s1, s2, op0, op1=None, reads=(), writes=(), accum=None):
        if op1 is None:
            return self.op(lambda e: e.tensor_scalar(out, in0, s1, None, op0), reads, writes)
        if accum is not None:
            return self.op(lambda e: e.tensor_scalar(out, in0, s1, s2, op0, op1, accum), reads, writes)
        return self.op(lambda e: e.tensor_scalar(out, in0, s1, s2, op0, op1), reads, writes)

    def stt(self, out, in0, scalar, in1, op0, op1, reads=(), writes=()):
        return self.op(lambda e: e.scalar_tensor_tensor(out, in0, scalar, in1, op0, op1), reads, writes)

    def actv(self, out, in_, func, bias=None, scale=None, accum=None, reads=(), writes=()):
        kw = {}
        if bias is not None:
            kw["bias"] = bias
        if scale is not None:
            kw["scale"] = scale
        if accum is not None:
            kw["accum_out"] = accum
        return self.op(lambda e: e.activation(out, in_, func, **kw), reads, writes)

    def cp(self, out, in_, scale=None, reads=(), writes=()):
        return self.op(copy_op(self, out, in_, scale), reads, writes)

    def memset(self, ap, val, writes=()):
        return self.op(lambda e: e.memset(ap, float(val)), (), writes)

    def recip(self, out, in_, reads=(), writes=()):
        return self.op(lambda e: e.reciprocal(out, in_), reads, writes)

    def rmax(self, out, in_, reads=(), writes=()):
        return self.op(lambda e: e.reduce_max(out, in_, mybir.AxisListType.X), reads, writes)

    def bnstats(self, out, in_, reads=(), writes=()):
        return self.op(lambda e: e.bn_stats(out, in_), reads, writes)

    def bnaggr(self, out, in_, reads=(), writes=()):
        return self.op(lambda e: e.bn_aggr(out, in_), reads, writes)

    def dma(self, sb, pairs, reads=(), writes=()):
        K = self.K
        if sb.dsem is None:
            sb.dsem = K.get_dsem()
        ds = sb.dsem
        if ds.count > 0:
            self.wait_tok(ds, ds.count)
        self.deps(reads, writes)
        for (o, i) in pairs:
            ds.count += 16
            self.ops.append(("d", o, i, ds))
        tok = (ds, ds.count)
        self.mark(tok, reads, writes)
        return tok

    def replay(self, eng):
        for o in self.ops:
            if o[0] == "w":
                eng.wait_ge(o[1].h, o[2])
            elif o[0] == "i":
                ins = o[1](eng)
                if o[2] is not None:
                    ins.then_inc(o[2].h, o[3])
            elif o[0] == "d":
                eng.dma_start(out=o[1], in_=o[2]).then_inc(o[3].h, 16)


class Kern:
    def __init__(self):
        self.nc = bass.Bass("TRN2", target_bir_lowering=False)
        self.sems = []
        self.free_dsems = []
        self.phase_bufs = []

    def new_sem(self, name):
        h = self.gstack.enter_context(self.nc.semaphore(name))
        s = Sem(h, name)
        self.sems.append(s)
        return s

    def get_dsem(self):
        if self.free_dsems:
            return self.free_dsems.pop()
        return self.new_sem("d%d" % len(self.sems))

    def setup(self, gstack):
        self.gstack = gstack
        self.pe = Q(self, "pe")
        self.act = Q(self, "act")
        self.dve = Q(self, "dve")
        self.pool = Q(self, "pool")
        self.sp = Q(self, "sp")
        self.queues = [self.pe, self.act, self.dve, self.pool, self.sp]
        self.banks = []
        for i in range(8):
            t = gstack.enter_context(self.nc.psum_tensor("psb%d" % i, [128, 512], F32))
            self.banks.append(Buf("psb%d" % i, t))
            self.banks[-1].excl = True
        self.bank_i = 0
        self.rot = list(range(8))
        self.rr = 0

    def set_rot(self, idx):
        self.rot = list(idx)
        self.bank_i = 0

    def ps(self):
        b = self.banks[self.rot[self.bank_i % len(self.rot)]]
        self.bank_i += 1
        return b

    def cst(self, name):
        o, w = CONST_LAYOUT[name]
        return self.consts.ap[:, o:o + w]

    def sb(self, name, shape, dt):
        nbytes = int(np.prod(shape[1:])) * (2 if dt == BF16 else 4)
        base = self.nc.sbuf_base
        t = self.pstack.enter_context(self.nc.sbuf_tensor("p%d_%s" % (self.phase_id, name), shape, dt))
        b = Buf(name, t)
        self.phase_bufs.append(b)
        return b

    def begin_phase(self, pstack):
        self.phase_id = getattr(self, "phase_id", 0) + 1
        self.pstack = pstack
        self.phase_bufs = []

    def barrier(self):
        for q in self.queues:
            for s in self.sems:
                if s.count > 0:
                    q.wait_tok(s, s.count)

    def end_phase(self):
        self.barrier()
        for b in self.phase_bufs:
            if b.dsem is not None:
                self.free_dsems.append(b.dsem)
                b.dsem = None
        self.phase_bufs = []

    def ev(self):
        self.rr += 1
        return self.act if (self.rr & 1) else self.dve

    def finish(self):
        self.barrier()
        nc = self.nc
        with nc.Block() as block:
            @block.tensor
            def _(e):
                self.pe.replay(e)

            @block.scalar
            def _(e):
                self.act.replay(e)

            @block.vector
            def _(e):
                self.dve.replay(e)

            @block.gpsimd
            def _(e):
                self.pool.replay(e)

            @block.sync
            def _(e):
                self.sp.replay(e)


def copy_op(q, out_ap, in_ap, scale=None):
    if q.name == "act":
        if scale is None:
            return lambda e: e.activation(out_ap, in_ap, AF.Copy)
        return lambda e: e.activation(out_ap, in_ap, AF.Copy, scale=float(scale))
    if scale is None:
        return lambda e: e.tensor_copy(out_ap, in_ap)
    return lambda e: e.tensor_scalar(out_ap, in_ap, float(scale), None, ALU.mult)


def mm(K, bank, out_ap, pairs, reads, start=True, stop=True):
    n = len(pairs)
    tok = None
    for i, (l, r) in enumerate(pairs):
        st = start and i == 0
        sp = stop and i == n - 1
        last = i == n - 1
        tok = K.pe.op(lambda e, l=l, r=r, st=st, sp=sp: e.matmul(out_ap, l, r, start=st, stop=sp),
                      reads=reads if (last or i == 0) else (), writes=[bank], signal=last)
    return tok


def tr(K, bank, out_ap, in_ap, ident_ap, reads, signal=True):
    return K.pe.op(lambda e: e.transpose(out_ap, in_ap, ident_ap), reads=reads, writes=[bank], signal=signal)


def load_weight_bf16(K, wdram, rows, cols, name, stage, col_group=2048):
    nch = rows // 128
    t = K.sb(name, [128, nch, cols], BF16)
    bufs = []
    si = 0
    for c in range(nch):
        b = Buf("%s_%d" % (name, c), t.ap)
        eng = K.act if (c & 1) else K.dve
        for c0 in range(0, cols, col_group):
            c1 = min(cols, c0 + col_group)
            st = stage[si % len(stage)]
            si += 1
            K.sp.dma(st, [(st.ap[:, 0:c1 - c0], wdram[c * 128:(c + 1) * 128, c0:c1])], writes=[st])
            o = t.ap[:, c, c0:c1]
            i = st.ap[:, 0:c1 - c0]
            eng.cp(o, i, reads=[st], writes=[b])
        bufs.append(b)
    return t.ap, bufs


def load_bcast(K, dram_row, n, name):
    b = K.sb(name, [128, n], F32)
    K.sp.dma(b, [(b.ap[:], dram_row.partition_broadcast(128))], writes=[b])
    return b


def transpose_x(K, xt_ap, x_buf, xT_ap, xT_buf, ident_ap, ident_buf, tcols=128, col0=0, nrows=128):
    for g in range(2):
        bank = K.ps()
        for j in range(4):
            kc = g * 4 + j
            tr(K, bank, bank.ap[:, j * 128:j * 128 + nrows], xt_ap[0:nrows, kc * 128:(kc + 1) * 128],
               ident_ap[0:nrows, 0:nrows], reads=[x_buf, ident_buf], signal=(j == 3))
        q = K.ev()
        src = bank.ap[:].rearrange("p (j t) -> p j t", j=4)[:, :, 0:nrows]
        dst = xT_ap[:, g * 4:(g + 1) * 4, col0:col0 + nrows]
        q.cp(dst, src, reads=[bank], writes=[xT_buf])


def residual_ln(K, bank_list, x_buf, x_ap, fscale, g_b, b_b, z_buf, eps_adj, st_bufs):
    z = z_buf.ap
    stats, mv, rstd = st_bufs
    for h, bank in enumerate(bank_list):
        sl = slice(h * 512, (h + 1) * 512)
        K.dve.stt(z[:, sl], bank.ap[:, 0:512], float(fscale), x_ap[:, sl], ALU.mult, ALU.add, reads=[bank, x_buf], writes=[z_buf])
        K.dve.bnstats(stats.ap[:, h, :], z[:, sl], reads=[z_buf], writes=[stats])
    K.dve.bnaggr(mv.ap[:], stats.ap[:].rearrange("p a b -> p (a b)"), reads=[stats], writes=[mv])
    eps = K.eps_tiles[eps_adj]
    K.act.actv(rstd.ap[:], mv.ap[:, 1:2], AF.Sqrt, bias=eps.ap[:], scale=1.0, reads=[mv, eps], writes=[rstd])
    K.dve.recip(rstd.ap[:], rstd.ap[:], reads=[rstd], writes=[rstd])
    K.dve.ts(z[:], z[:], mv.ap[:, 0:1], rstd.ap[:], ALU.subtract, ALU.mult, reads=[z_buf, mv, rstd], writes=[z_buf])
    K.pool.tt(z[:], z[:], g_b.ap[:], ALU.mult, reads=[z_buf, g_b], writes=[z_buf])
    K.pool.tt(z[:], z[:], b_b.ap[:], ALU.add, reads=[z_buf, b_b], writes=[z_buf])


def make_eps_tiles(K, gstack):
    K.eps_tiles = {}
    for key, val in (("ln1", LN_EPS), ("lna", LN_EPS / (ALPHA * ALPHA)), ("l2", NORM_EPS), ("one", 1.0)):
        t = gstack.enter_context(K.nc.sbuf_tensor("eps_" + key, [128, 1], F32))
        b = Buf("eps_" + key, t)
        K.dve.memset(t[:], val, writes=[b])
        K.eps_tiles[key] = b


def ffn_phase(K, cfg, Xin, Xout, w_in, w_out, lng, lnb, TW=256):
    L = cfg["L"]
    NS = TW // 128
    with ExitStack() as ps_:
        K.begin_phase(ps_)
        stage = [K.sb("wst%d" % i, [128, 1024], F32) for i in range(2)]
        W1, W1b = load_weight_bf16(K, w_in, D, 2 * DFF, "W1", stage, col_group=704)
        W2, W2b = load_weight_bf16(K, w_out, DFF, D, "W2", stage, col_group=1024)
        g_b = load_bcast(K, lng, D, "lng")
        b_b = load_bcast(K, lnb, D, "lnb")
        xts = [K.sb("xt%d" % i, [128, NS, D], F32) for i in range(2)]
        xTs = [K.sb("xT%d" % i, [128, 8, TW], BF16) for i in range(2)]
        hTs = [K.sb("hT0", [128, 22, TW], BF16)] * 2
        sgs = [K.sb("sg%d" % i, [128, TW], F32) for i in range(2)]
        zs = stage
        stats = K.sb("stats", [128, 2, 6], F32)
        mv = K.sb("mv", [128, 2], F32)
        rstd = K.sb("rstd", [128, 1], F32)
        ident = K.ident
        zi = 0
        import os
        dbg = os.environ.get("DBG", "")
        if "a" in dbg: xts = [xts[1], xts[1]]
        if "b" in dbg: xTs = [xTs[1], xTs[1]]
        if "c" in dbg: hTs = [hTs[1], hTs[1]]
        if "d" in dbg: xts = [xts[0], xts[0]]
        if "e" in dbg: xTs = [xTs[0], xTs[0]]
        if "f" in dbg: hTs = [hTs[0], hTs[0]]
        for ti in range(L // TW):
            t0 = ti * TW
            xt = xts[ti % 2]
            xT = xTs[ti % 2]
            hT = hTs[ti % 2]
            K.sp.dma(xt, [(xt.ap[:, s, :], Xin[t0 + s * 128:t0 + (s + 1) * 128, :]) for s in range(NS)], writes=[xt])
            for s in range(NS):
                transpose_x(K, xt.ap[:, s, :], xt, xT.ap, xT, ident.ap, ident, col0=s * 128)
            for hc in range(22):
                bank = K.ps()
                mm(K, bank, bank.ap[:, 0:TW],
                   [(W1[:, kc, hc * 128:(hc + 1) * 128], xT.ap[:, kc, :]) for kc in range(8)], reads=[xT] + W1b)
                mm(K, bank, bank.ap[:, TW:2 * TW],
                   [(W1[:, kc, DFF + hc * 128:DFF + (hc + 1) * 128], xT.ap[:, kc, :]) for kc in range(8)], reads=[xT] + W1b)
                sg = sgs[hc % 2]
                K.act.actv(sg.ap[:], bank.ap[:, 0:TW], AF.Silu, reads=[bank], writes=[sg])
                K.dve.tt(hT.ap[:, hc, :], sg.ap[:], bank.ap[:, TW:2 * TW], ALU.mult, reads=[bank, sg], writes=[hT])
            for s in range(NS):
                banks = []
                for fh in range(2):
                    bank = K.ps()
                    mm(K, bank, bank.ap[:, 0:512],
                       [(hT.ap[:, hc, s * 128:(s + 1) * 128], W2[:, hc, fh * 512:(fh + 1) * 512]) for hc in range(22)],
                       reads=[hT] + W2b)
                    banks.append(bank)
                z = zs[zi % 2]
                zi += 1
                residual_ln(K, banks, xt, xt.ap[:, s, :], 0.5 / ALPHA, g_b, b_b, z, "lna", (stats, mv, rstd))
                K.pool.dma(z, [(Xout[t0 + s * 128:t0 + (s + 1) * 128, :], z.ap[:])], reads=[z])
            if "B" in dbg:
                K.barrier()
        K.end_phase()


WEIGHT_SPECS = [
    ("ffn_w_in", [2, 2, D, 2 * DFF]), ("ffn_w_out", [2, 2, DFF, D]), ("ln_g", [2, 4, D]), ("ln_b", [2, 4, D]),
    ("gla_w_in", [1, D, 3072]), ("gla_w_gate_down", [1, 2, D, 16]), ("gla_w_gate_up", [1, 2, 16, 512]),
    ("gla_b_gate", [1, 2, 512]), ("gla_norm_w", [1, 256]), ("gla_w_out", [1, D, D]),
    ("gdn_w_in", [1, D, 6144]), ("gdn_conv_w", [1, 4, 4096]), ("gdn_w_ab", [1, 2, D, 32]),
    ("gdn_a_log", [1, 2, 16]), ("gdn_dt_bias", [1, 2, 16]), ("gdn_norm_w", [1, 128]), ("gdn_w_out", [1, 2048, D]),
    ("xa_w_q", [2, D, D]), ("xa_w_kv", [2, D, 2 * D]), ("xa_w_o", [2, D, D]),
]


def build_program(cfg):
    L = cfg["L"]
    NSEG = cfg["NSEG"]
    phases = cfg.get("phases", None)
    K = Kern()
    nc = K.nc
    xs = nc.dram_tensor("xs", [L, D], F32, kind="ExternalInput").ap()
    mem = nc.dram_tensor("mem", [NSEG, NMEM, D], F32, kind="ExternalInput").ap()
    carry = nc.dram_tensor("carry", [128, 1], F32, kind="ExternalInput").ap()
    consts = nc.dram_tensor("consts", [128, cfg["NCONST"]], F32, kind="ExternalInput").ap()
    W = {}
    for name, shape in WEIGHT_SPECS:
        W[name] = nc.dram_tensor(name, shape, F32, kind="ExternalInput").ap()
    y = nc.dram_tensor("y", [L, D], F32, kind="ExternalOutput").ap()
    XA = nc.dram_tensor("scrA", [L, D], F32).ap()
    XB = nc.dram_tensor("scrB", [L, D], F32).ap()
    OF = nc.dram_tensor("scrO", [L, 2048], F32).ap()
    with ExitStack() as gs:
        K.setup(gs)
        make_eps_tiles(K, gs)
        ct = gs.enter_context(nc.sbuf_tensor("consts_sb", [128, cfg["NCONST"]], F32))
        K.consts = Buf("consts", ct)
        K.sp.dma(K.consts, [(ct[:], consts)], writes=[K.consts])
        K.ident = Buf("ident", ct[:, 0:128])
        K.ident.wtoks = K.consts.wtoks
        ibt = gs.enter_context(nc.sbuf_tensor("identb", [128, 128], BF16))
        K.identb = Buf("identb", ibt)
        K.dve.cp(ibt[:], ct[:, 0:128], reads=[K.consts], writes=[K.identb])
        cyt = gs.enter_context(nc.sbuf_tensor("carry_sb", [128, 1], F32))
        K.carry = Buf("carry", cyt)
        K.sp.dma(K.carry, [(cyt[:], carry)], writes=[K.carry])
        K.cfg = cfg
        K.ofbuf = [Buf("of%d" % i) for i in range(L // 128)]
        plan = []
        for i in range(DEPTH):
            plan.append(("ffn", i, 0))
            plan.append(("mix", i))
            plan.append(("xa", i))
            plan.append(("ffn", i, 1))
        if phases is not None:
            plan = plan[:phases]
        if cfg.get("only"):
            plan = {"ffn": [("ffn", 0, 0)], "xa": [("xa", 0)], "gla": [("mix", 0)], "gdn": [("mix", 1)]}[cfg["only"]]
        cur = xs
        for pi, ph in enumerate(plan):
            last = pi == len(plan) - 1
            dst = y if last else (XA if (pi % 2 == 0) else XB)
            if ph[0] == "ffn":
                i, j = ph[1], ph[2]
                li = 0 if j == 0 else 3
                ffn_phase(K, cfg, cur, dst, W["ffn_w_in"][i, j], W["ffn_w_out"][i, j], W["ln_g"][i, li], W["ln_b"][i, li])
            elif ph[0] == "mix":
                i = ph[1]
                if i % 2 == 0:
                    gla_phase(K, cfg, cur, dst, OF, W, i // 2, W["ln_g"][i, 1], W["ln_b"][i, 1])
                else:
                    gdn_phase(K, cfg, cur, dst, OF, W, i // 2, W["ln_g"][i, 1], W["ln_b"][i, 1])
            elif ph[0] == "xa":
                i = ph[1]
                xa_phase(K, cfg, cur, dst, mem, W["xa_w_q"][i], W["xa_w_kv"][i], W["xa_w_o"][i], W["ln_g"][i, 2], W["ln_b"][i, 2])
            cur = dst
        K.finish()
    return nc


def _build_consts():
    s_ = np.arange(128)[:, None]
    c_ = np.arange(128)[None, :]
    same = (s_ // 64) == (c_ // 64)
    pos_s = s_ % 64
    out = {}
    out["IDENT"] = np.eye(128, dtype=np.float32)
    for dn in ("f", "b"):
        if dn == "f":
            tri = same & (s_ <= c_)
            mid = same & (pos_s <= 31) & (c_ >= 0)
        else:
            tri = same & (s_ >= c_)
            mid = same & (pos_s >= 32) & (c_ >= 0)
        tri = tri.astype(np.float32)
        mid = mid.astype(np.float32)
        sm = same.astype(np.float32)
        out["M1" + dn] = (-1.0 / 16.0) * (tri - mid)
        out["M3" + dn] = (-1.0 / 16.0) * tri
        out["M4" + dn] = (-1.0 / 16.0) * (sm - tri)
        out["MASK4" + dn] = np.tile(tri, (1, 4))
        out["TRI" + dn] = tri
        out["AFTER" + dn] = sm - tri
        out["NEG" + dn] = (1.0 - tri) * (-30000.0)
    ind = np.zeros((128, 2), np.float32)
    ind[0:64, 0] = -1.0 / 16.0
    ind[64:128, 1] = -1.0 / 16.0
    out["IND"] = ind
    out["ONES"] = np.ones((128, 128), np.float32)
    out["NEGONES"] = -np.ones((128, 128), np.float32)
    out["OFFD4"] = np.tile(1.0 - np.eye(128, dtype=np.float32), (1, 4))
    out["IDENT4"] = np.tile(np.eye(128, dtype=np.float32), (1, 4))
    ch0 = np.zeros((128, 128), np.float32)
    ch0[0:64, :] = 1.0
    out["CH0"] = ch0
    out["CH1"] = 1.0 - ch0
    return out


_CONSTS = _build_consts()
CONST_LAYOUT = {}
_off = 0
for _k, _v in _CONSTS.items():
    CONST_LAYOUT[_k] = (_off, _v.shape[1])
    _off += _v.shape[1]
NCONST = _off


def make_consts():
    return np.ascontiguousarray(np.concatenate([_CONSTS[k] for k in CONST_LAYOUT], axis=1).astype(np.float32))


def bank_bf(bank):
    return bank.ap[:].bitcast(BF16)


def xa_phase(K, cfg, Xin, Xout, mem, w_q, w_kv, w_o, lng, lnb):
    L, NSEG, SEG = cfg["L"], cfg["NSEG"], cfg["SEG"]
    TW = 256
    NS = TW // 128
    with ExitStack() as ps_:
        K.begin_phase(ps_)
        stage = [K.sb("wst%d" % i, [128, 1024], F32) for i in range(2)]
        Wq, Wqb = load_weight_bf16(K, w_q, D, D, "Wq", stage, col_group=1024)
        Wkv, Wkvb = load_weight_bf16(K, w_kv, D, 2 * D, "Wkv", stage, col_group=1024)
        Wo, Wob = load_weight_bf16(K, w_o, D, D, "Wo", stage, col_group=1024)
        g_b = load_bcast(K, lng, D, "lng")
        b_b = load_bcast(K, lnb, D, "lnb")
        ident, identb = K.ident, K.identb
        KT = K.sb("KT", [128, NSEG, 8, NMEM], BF16)
        V = K.sb("V", [128, NSEG, 2, D], BF16)
        memT = K.sb("memT", [128, 8, NMEM], BF16)
        xts = [K.sb("xt%d" % i, [128, NS, D], F32) for i in range(2)]
        xTs = [K.sb("xT%d" % i, [128, 8, TW], BF16) for i in range(2)]
        qT = K.sb("qT", [128, 8, TW], BF16)
        oT = K.sb("oT", [128, 8, TW], BF16)
        pT = K.sb("pT", [128, 4, 2, TW], BF16)
        pf = K.sb("pf", [128, 4, NMEM], F32)
        pb = K.sb("pb", [128, 4, NMEM], BF16)
        mx = K.sb("mx", [128, 4], F32)
        sm = K.sb("sm", [128, 4], F32)
        zs = [K.sb("z%d" % i, [128, D], F32) for i in range(2)]
        stats = K.sb("stats", [128, 2, 6], F32)
        mv = K.sb("mv", [128, 2], F32)
        rstd = K.sb("rstd", [128, 1], F32)
        for sg in range(NSEG):
            for mt in range(2):
                xt = xts[mt]
                K.sp.dma(xt, [(xt.ap[:, 0, :], mem[sg, mt * 128:(mt + 1) * 128, :])], writes=[xt])
                transpose_x(K, xt.ap[:, 0, :], xt, memT.ap, memT, ident.ap, ident, col0=mt * 128)
            for dhc in range(8):
                bank = K.ps()
                mm(K, bank, bank.ap[:, 0:NMEM], [(Wkv[:, kc, dhc * 128:(dhc + 1) * 128], memT.ap[:, kc, :]) for kc in range(8)],
                   reads=[memT] + Wkvb)
                q = K.ev()
                q.cp(KT.ap[:, sg, dhc, :], bank.ap[:, 0:NMEM], reads=[bank], writes=[KT])
            for mc in range(2):
                for half in range(2):
                    bank = K.ps()
                    mm(K, bank, bank.ap[:, 0:512],
                       [(memT.ap[:, kc, mc * 128:(mc + 1) * 128], Wkv[:, kc, D + half * 512:D + (half + 1) * 512]) for kc in range(8)],
                       reads=[memT] + Wkvb)
                    q = K.ev()
                    q.cp(V.ap[:, sg, mc, half * 512:(half + 1) * 512], bank.ap[:, 0:512], reads=[bank], writes=[V])
        zi = 0
        for ti in range(L // TW):
            t0 = ti * TW
            sg = t0 // SEG
            xt = xts[ti % 2]
            xT = xTs[ti % 2]
            K.sp.dma(xt, [(xt.ap[:, s, :], Xin[t0 + s * 128:t0 + (s + 1) * 128, :]) for s in range(NS)], writes=[xt])
            for s in range(NS):
                transpose_x(K, xt.ap[:, s, :], xt, xT.ap, xT, ident.ap, ident, col0=s * 128)
            for dhc in range(8):
                bank = K.ps()
                mm(K, bank, bank.ap[:, 0:TW], [(Wq[:, kc, dhc * 128:(dhc + 1) * 128], xT.ap[:, kc, :]) for kc in range(8)],
                   reads=[xT] + Wqb)
                q = K.ev()
                q.cp(qT.ap[:, dhc, :], bank.ap[:, 0:TW], scale=1.0 / 16.0, reads=[bank], writes=[qT])
            for s in range(NS):
                sb_ = []
                for j in range(2):
                    bank = K.ps()
                    for hh in range(2):
                        h = 2 * j + hh
                        mm(K, bank, bank.ap[:, hh * 256:(hh + 1) * 256],
                           [(qT.ap[:, 2 * h + c, s * 128:(s + 1) * 128], KT.ap[:, sg, 2 * h + c, :]) for c in range(2)], reads=[qT, KT])
                    K.dve.rmax(mx.ap[:, 2 * j:2 * j + 2], bank.ap[:].rearrange("p (h m) -> p h m", h=2), reads=[bank], writes=[mx])
                    sb_.append(bank)
                K.dve.ts(mx.ap[:], mx.ap[:], -1.0, None, ALU.mult, reads=[mx], writes=[mx])
                for h in range(4):
                    bank = sb_[h // 2]
                    K.act.actv(pf.ap[:, h, :], bank.ap[:, (h % 2) * 256:(h % 2 + 1) * 256], AF.Exp, bias=mx.ap[:, h:h + 1], scale=1.0,
                               accum=sm.ap[:, h:h + 1], reads=[bank, mx], writes=[pf, sm])
                K.dve.recip(sm.ap[:], sm.ap[:], reads=[sm], writes=[sm])
                for h in range(4):
                    K.dve.ts(pb.ap[:, h, :], pf.ap[:, h, :], sm.ap[:, h:h + 1], None, ALU.mult, reads=[pf, sm], writes=[pb])
                bank = K.ps()
                bb = bank_bf(bank)
                for h in range(4):
                    for mc in range(2):
                        idx = h * 2 + mc
                        tr(K, bank, bb[:, idx * 128:(idx + 1) * 128], pb.ap[:, h, mc * 128:(mc + 1) * 128], identb.ap[:],
                           reads=[pb, identb], signal=(idx == 7))
                q = K.ev()
                q.cp(pT.ap[:, :, :, s * 128:(s + 1) * 128], bb.rearrange("p (h c t) -> p h c t", h=4, c=2), reads=[bank], writes=[pT])
            for dhc in range(8):
                h = dhc // 2
                bank = K.ps()
                mm(K, bank, bank.ap[:, 0:TW], [(V.ap[:, sg, mc, dhc * 128:(dhc + 1) * 128], pT.ap[:, h, mc, :]) for mc in range(2)],
                   reads=[V, pT])
                q = K.ev()
                q.cp(oT.ap[:, dhc, :], bank.ap[:, 0:TW], reads=[bank], writes=[oT])
            for s in range(NS):
                banks = []
                for fh in range(2):
                    bank = K.ps()
                    mm(K, bank, bank.ap[:, 0:512],
                       [(oT.ap[:, dhc, s * 128:(s + 1) * 128], Wo[:, dhc, fh * 512:(fh + 1) * 512]) for dhc in range(8)],
                       reads=[oT] + Wob)
                    banks.append(bank)
                z = zs[zi % 2]
                zi += 1
                residual_ln(K, banks, xt, xt.ap[:, s, :], 1.0 / ALPHA, g_b, b_b, z, "lna", (stats, mv, rstd))
                K.pool.dma(z, [(Xout[t0 + s * 128:t0 + (s + 1) * 128, :], z.ap[:])], reads=[z])
        K.end_phase()


import os
GLA_STOP = os.environ.get('GLA_STOP', '')


def gla_phase(K, cfg, Xin, Xout, OF, W, j, lng, lnb):
    L, NSEG, SEG = cfg["L"], cfg["NSEG"], cfg["SEG"]
    NT = L // 128
    TPS = SEG // 128
    cst = K.cst
    with ExitStack() as ps_:
        K.begin_phase(ps_)
        K.set_rot([4, 5, 6, 7])
        OB = [K.banks[i] for i in range(4)]
        stage = [K.sb("wst%d" % i, [128, 1024], F32) for i in range(2)]
        Win, Winb = load_weight_bf16(K, W["gla_w_in"][j], D, 3072, "Win", stage, col_group=1024)
        Wout, Woutb = load_weight_bf16(K, W["gla_w_out"][j], D, D, "Wout", stage, col_group=1024)
        Wd = []
        for d in range(2):
            Wd.append(load_weight_bf16(K, W["gla_w_gate_down"][j, d], D, 16, "Wd%d" % d, stage, col_group=16))
        Wu = K.sb("Wu", [32, 2, 512], F32)
        K.dve.memset(Wu.ap[:], 0.0, writes=[Wu])
        K.sp.dma(Wu, [(Wu.ap[0:16, d, :], W["gla_w_gate_up"][j, d]) for d in range(2)] +
                 [(Wu.ap[16:17, d, :], W["gla_b_gate"][j, d:d + 1, :]) for d in range(2)], writes=[Wu])
        uext = K.sb("uext", [32, 128], F32)
        K.dve.memset(uext.ap[:], 1.0, writes=[uext])
        g_b = load_bcast(K, lng, D, "lng")
        b_b = load_bcast(K, lnb, D, "lnb")
        nw_b = load_bcast(K, W["gla_norm_w"][j], 256, "nw")
        ident, identb = K.ident, K.identb
        xts = [K.sb("xt%d" % i, [128, D], F32) for i in range(2)]
        xTs = [K.sb("xT%d" % i, [128, 8, 128], BF16) for i in range(2)]
        q_sb = K.sb("q_sb", [128, 512], F32)
        k_sb = K.sb("k_sb", [128, 512], F32)
        v_bf = K.sb("v_bf", [128, 1024], BF16)
        sr_bf = K.sb("sr_bf", [128, 1024], BF16)
        sp = K.sb("sp", [128, 512], F32)
        ets = [K.sb("et%d" % i, [128, 512], F32) for i in range(2)]
        qt_bf = K.sb("qt_bf", [128, 512], BF16)
        kt_bf = K.sb("kt_bf", [128, 512], BF16)
        qh_bf = K.sb("qh_bf", [128, 512], BF16)
        kh_bf = K.sb("kh_bf", [128, 512], BF16)
        qtT = K.sb("qtT", [128, 4, 128], BF16)
        ktT = K.sb("ktT", [128, 4, 128], BF16)
        qhA = K.sb("qhA", [128, 4, 128], BF16)
        qhB = K.sb("qhB", [128, 4, 128], BF16)
        K.dve.memset(qhA.ap[:], 0.0, writes=[qhA])
        K.dve.memset(qhB.ap[:], 0.0, writes=[qhB])
        scm = K.sb("scm", [128, 4, 128], BF16)
        S = K.sb("S", [128, 4, 256], F32)
        S_bf = K.sb("S_bf", [128, 4, 256], BF16)
        dec = K.sb("dec", [128, 8], F32)
        o_sb = K.sb("o_sb", [128, 1024], F32)
        of_sb = K.sb("of_sb", [128, 1024], F32)
        junk = K.sb("junk", [128, 256], F32)
        ss = K.sb("ss", [128, 4], F32)
        og_bf = K.sb("og_bf", [128, 1024], BF16)
        ogT = K.sb("ogT", [128, 8, 128], BF16)
        zs = [K.sb("z%d" % i, [128, D], F32) for i in range(2)]
        stats = K.sb("stats", [128, 2, 6], F32)
        mv = K.sb("mv", [128, 2], F32)
        rstd = K.sb("rstd", [128, 1], F32)
        eps_l2 = K.eps_tiles["l2"]
        one_t = K.eps_tiles["one"]
        zi = 0
        it = 0
        for dr in range(2):
            K.dve.memset(S.ap[:], 0.0, writes=[S])
            K.dve.memset(S_bf.ap[:], 0.0, writes=[S_bf])
            order = list(range(NT)) if dr == 0 else list(range(NT - 1, -1, -1))
            dn = "f" if dr == 0 else "b"
            for oi, ti in enumerate(order):
                t0 = ti * 128
                xt = xts[it % 2]
                xT = xTs[it % 2]
                it += 1
                if oi > 0 and (oi % TPS) == 0:
                    K.dve.ts(S.ap[:], S.ap[:], K.carry.ap[:, 0:1], None, ALU.mult, reads=[S, K.carry], writes=[S])
                    K.act.cp(S_bf.ap[:], S.ap[:], reads=[S], writes=[S_bf])
                K.sp.dma(xt, [(xt.ap[:], Xin[t0:t0 + 128, :])], writes=[xt])
                if dr == 1:
                    K.sp.dma(of_sb, [(of_sb.ap[:], OF[t0:t0 + 128, 0:1024])], reads=[K.ofbuf[ti]], writes=[of_sb])
                transpose_x(K, xt.ap[:], xt, xT.ap, xT, ident.ap, ident, col0=0)
                ngrp = 6 if dr == 1 else 4
                for gi in range(ngrp):
                    bank = K.ps()
                    mm(K, bank, bank.ap[:, 0:512], [(xT.ap[:, kc, :], Win[:, kc, gi * 512:(gi + 1) * 512]) for kc in range(8)],
                       reads=[xT] + Winb)
                    if gi == 0:
                        K.act.cp(q_sb.ap[:], bank.ap[:, 0:512], scale=128.0 ** -0.5, reads=[bank], writes=[q_sb])
                    elif gi == 1:
                        K.dve.cp(k_sb.ap[:], bank.ap[:, 0:512], reads=[bank], writes=[k_sb])
                    elif gi < 4:
                        q = K.ev()
                        q.cp(v_bf.ap[:, (gi - 2) * 512:(gi - 1) * 512], bank.ap[:, 0:512], reads=[bank], writes=[v_bf])
                    else:
                        K.act.actv(sr_bf.ap[:, (gi - 4) * 512:(gi - 3) * 512], bank.ap[:, 0:512], AF.Silu, reads=[bank], writes=[sr_bf])
                if GLA_STOP == "A":
                    continue
                bank = K.ps()
                mm(K, bank, bank.ap[0:16, 0:128], [(Wd[dr][0][:, kc, :], xT.ap[:, kc, :]) for kc in range(8)], reads=[xT] + Wd[dr][1])
                K.dve.cp(uext.ap[0:16, :], bank.ap[0:16, 0:128], reads=[bank], writes=[uext])
                bank = K.ps()
                mm(K, bank, bank.ap[:, 0:512], [(uext.ap[:], Wu.ap[:, dr, :])], reads=[uext, Wu])
                et = ets[0]
                K.act.actv(et.ap[:], bank.ap[:, 0:512], AF.Exp, scale=-1.0, reads=[bank], writes=[et])
                K.act.actv(sp.ap[:], et.ap[:], AF.Ln, bias=one_t.ap[:], scale=1.0, reads=[et, one_t], writes=[sp])
                if GLA_STOP == "B":
                    continue
                specs = [("M1" + dn, q_sb, qt_bf, 1.0), ("M1" + dn, k_sb, kt_bf, -1.0), ("M3" + dn, q_sb, qh_bf, 1.0), ("M4" + dn, k_sb, kh_bf, 1.0)]
                bank1 = None
                for si, (mname, src, dst, sc) in enumerate(specs):
                    if si != 1:
                        bank = K.ps()
                        mm(K, bank, bank.ap[:, 0:512], [(cst(mname), sp.ap[:])], reads=[K.consts, sp])
                        if si == 0:
                            bank1 = bank
                    else:
                        bank = bank1
                    et = ets[si % 2]
                    K.act.actv(et.ap[:], bank.ap[:, 0:512], AF.Exp, scale=sc, reads=[bank], writes=[et])
                    eng = K.dve if si % 2 == 0 else K.pool
                    eng.tt(dst.ap[:], src.ap[:], et.ap[:], ALU.mult, reads=[src, et], writes=[dst])
                if GLA_STOP == "C":
                    continue
                bank = K.ps()
                for h in range(4):
                    mm(K, bank, bank.ap[:, 2 * h:2 * h + 2], [(sp.ap[:, h * 128:(h + 1) * 128], cst("IND"))], reads=[sp, K.consts])
                K.act.actv(dec.ap[:], bank.ap[:, 0:8], AF.Exp, reads=[bank], writes=[dec])
                if GLA_STOP == "D":
                    continue
                bank = K.ps()
                bb = bank_bf(bank)
                for h in range(4):
                    tr(K, bank, bb[:, h * 128:(h + 1) * 128], qt_bf.ap[:, h * 128:(h + 1) * 128], identb.ap[:], reads=[qt_bf, identb], signal=False)
                for h in range(4):
                    tr(K, bank, bb[:, (4 + h) * 128:(5 + h) * 128], kt_bf.ap[:, h * 128:(h + 1) * 128], identb.ap[:], reads=[kt_bf, identb],
                       signal=(h == 3))
                K.act.cp(qtT.ap[:], bb[:, 0:512].rearrange("p (h t) -> p h t", h=4), reads=[bank], writes=[qtT])
                K.act.cp(ktT.ap[:], bb[:, 512:1024].rearrange("p (h t) -> p h t", h=4), reads=[bank], writes=[ktT])
                if GLA_STOP == "D1":
                    continue
                bank = K.ps()
                bb = bank_bf(bank)
                for h in range(4):
                    tr(K, bank, bb[:, h * 128:(h + 1) * 128], qh_bf.ap[:, h * 128:(h + 1) * 128], identb.ap[:], reads=[qh_bf, identb],
                       signal=(h == 3))
                bv = bb[:, 0:512].rearrange("p (h t) -> p h t", h=4)
                K.dve.cp(qhA.ap[:, :, 0:64], bv[:, :, 0:64], reads=[bank], writes=[qhA])
                K.dve.cp(qhB.ap[:, :, 64:128], bv[:, :, 64:128], reads=[bank], writes=[qhB])
                if GLA_STOP == "E":
                    continue
                bank = K.ps()
                for h in range(4):
                    mm(K, bank, bank.ap[:, h * 128:(h + 1) * 128], [(ktT.ap[:, h, :], qtT.ap[:, h, :])], reads=[ktT, qtT])
                K.dve.tt(scm.ap[:].rearrange("p h t -> p (h t)"), bank.ap[:, 0:512], cst("MASK4" + dn), ALU.mult,
                         reads=[bank, K.consts], writes=[scm])
                if GLA_STOP == "F":
                    continue
                for h in range(4):
                    ob = OB[h]
                    mm(K, ob, ob.ap[:, 0:256], [(scm.ap[:, h, :], v_bf.ap[:, h * 256:(h + 1) * 256])],
                       reads=[scm, v_bf], start=True, stop=False)
                chunks = [0, 1] if dr == 0 else [1, 0]
                for ci, ck in enumerate(chunks):
                    qh = qhA if ck == 0 else qhB
                    for h in range(4):
                        ob = OB[h]
                        mm(K, ob, ob.ap[:, 0:256], [(qh.ap[:, h, :], S_bf.ap[:, h, :])],
                           reads=[qh, S_bf], start=False, stop=(ci == 1))
                    r0 = ck * 64
                    kvb = []
                    for hp in range(2):
                        bank = K.ps()
                        for hh in range(2):
                            h = hp * 2 + hh
                            mm(K, bank, bank.ap[:, hh * 256:(hh + 1) * 256],
                               [(kh_bf.ap[r0:r0 + 64, h * 128:(h + 1) * 128], v_bf.ap[r0:r0 + 64, h * 256:(h + 1) * 256])], reads=[kh_bf, v_bf])
                        kvb.append(bank)
                    for h in range(4):
                        bank = kvb[h // 2]
                        K.dve.stt(S.ap[:, h, :], S.ap[:, h, :], dec.ap[:, 2 * h + ck:2 * h + ck + 1], bank.ap[:, (h % 2) * 256:(h % 2 + 1) * 256],
                                  ALU.mult, ALU.add, reads=[S, dec, bank], writes=[S])
                    K.act.cp(S_bf.ap[:], S.ap[:], reads=[S], writes=[S_bf])
                if dr == 0:
                    for h in range(4):
                        q = K.ev()
                        q.cp(o_sb.ap[:, h * 256:(h + 1) * 256], OB[h].ap[:, 0:256], reads=[OB[h]], writes=[o_sb])
                    K.pool.dma(o_sb, [(OF[t0:t0 + 128, 0:1024], o_sb.ap[:])], reads=[o_sb], writes=[K.ofbuf[ti]])
                else:
                    for h in range(4):
                        K.dve.tt(o_sb.ap[:, h * 256:(h + 1) * 256], OB[h].ap[:, 0:256], of_sb.ap[:, h * 256:(h + 1) * 256], ALU.add,
                                 reads=[OB[h], of_sb], writes=[o_sb])
                    if GLA_STOP == "O":
                        K.pool.dma(o_sb, [(Xout[t0:t0 + 128, :], o_sb.ap[:])], reads=[o_sb])
                        continue
                    for h in range(4):
                        K.act.actv(junk.ap[:], o_sb.ap[:, h * 256:(h + 1) * 256], AF.Square, accum=ss.ap[:, h:h + 1], reads=[o_sb], writes=[junk, ss])
                    K.act.actv(ss.ap[:], ss.ap[:], AF.Sqrt, bias=eps_l2.ap[:], scale=1.0 / 256.0, reads=[ss, eps_l2], writes=[ss])
                    K.dve.recip(ss.ap[:], ss.ap[:], reads=[ss], writes=[ss])
                    for h in range(4):
                        K.dve.stt(o_sb.ap[:, h * 256:(h + 1) * 256], o_sb.ap[:, h * 256:(h + 1) * 256], ss.ap[:, h:h + 1], nw_b.ap[:],
                                  ALU.mult, ALU.mult, reads=[o_sb, ss, nw_b], writes=[o_sb])
                    K.pool.tt(og_bf.ap[:], o_sb.ap[:], sr_bf.ap[:], ALU.mult, reads=[o_sb, sr_bf], writes=[og_bf])
                    bank = K.ps()
                    bb = bank_bf(bank)
                    for c in range(8):
                        tr(K, bank, bb[:, c * 128:(c + 1) * 128], og_bf.ap[:, c * 128:(c + 1) * 128], identb.ap[:], reads=[og_bf, identb],
                           signal=(c == 7))
                    K.act.cp(ogT.ap[:], bb.rearrange("p (c t) -> p c t", c=8), reads=[bank], writes=[ogT])
                    banks = []
                    for fh in range(2):
                        bank = K.ps()
                        mm(K, bank, bank.ap[:, 0:512], [(ogT.ap[:, c, :], Wout[:, c, fh * 512:(fh + 1) * 512]) for c in range(8)],
                           reads=[ogT] + Woutb)
                        banks.append(bank)
                    z = zs[zi % 2]
                    zi += 1
                    residual_ln(K, banks, xt, xt.ap[:], 1.0 / ALPHA, g_b, b_b, z, "lna", (stats, mv, rstd))
                    K.pool.dma(z, [(Xout[t0:t0 + 128, :], z.ap[:])], reads=[z])
        K.set_rot(list(range(8)))
        K.end_phase()


def gdn_scan_pass(K, cfg, Xin, OF, W, j, dr):
    L, NSEG, SEG = cfg["L"], cfg["NSEG"], cfg["SEG"]
    NT = L // 128
    TPS = SEG // 128
    cst = K.cst
    dn = "f" if dr == 0 else "b"
    with ExitStack() as ps_:
        K.begin_phase(ps_)
        K.set_rot(list(range(8)))
        stage_big = K.sb("wst", [128, 2048], F32)
        stage = [Buf("wst0", stage_big.ap[:, 0:1024]), Buf("wst1", stage_big.ap[:, 1024:2048])]
        K.phase_bufs.extend(stage)
        Win, Winb = load_weight_bf16(K, W["gdn_w_in"][j][:, 0:4096], D, 4096, "Win", stage, col_group=1024)
        Wab, Wabb = load_weight_bf16(K, W["gdn_w_ab"][j, dr], D, 32, "Wab", stage, col_group=32)
        ident, identb = K.ident, K.identb
        cwT = K.sb("cwT", [128, 128], F32)
        cw = K.sb("cw", [128, 128], F32)
        K.sp.dma(cwT, [(cwT.ap[:], W["gdn_conv_w"][j].rearrange("t (c p) -> (t c) p", p=128))], writes=[cwT])
        bank = K.ps()
        tr(K, bank, bank.ap[:, 0:128], cwT.ap[:], ident.ap[:], reads=[cwT, ident])
        K.dve.cp(cw.ap[:], bank.ap[:, 0:128], reads=[bank], writes=[cw])
        alog = load_bcast(K, W["gdn_a_log"][j].rearrange("a b -> (a b)"), 32, "alog")
        dtb = load_bcast(K, W["gdn_dt_bias"][j].rearrange("a b -> (a b)"), 32, "dtb")
        negA = K.sb("negA", [128, 32], F32)
        K.act.actv(negA.ap[:], alog.ap[:], AF.Exp, reads=[alog], writes=[negA])
        K.dve.ts(negA.ap[:], negA.ap[:], -1.0, None, ALU.mult, reads=[negA], writes=[negA])
        ones_bf = K.sb("ones_bf", [128, 128], BF16)
        K.dve.cp(ones_bf.ap[:], cst("ONES"), reads=[K.consts], writes=[ones_bf])
        xts = [K.sb("xt0", [128, D], F32)] * 2
        xTs = [K.sb("xTe0", [128, 8, 132], BF16)] * 2
        xhp = K.sb("xhp", [2, D], F32)
        xhn = K.sb("xhn", [1, D], F32)
        accs = [K.sb("acc%d" % i, [128, 128], F32) for i in range(2)] * 2
        s4q = K.sb("s4q", [128, 8, 128], F32)
        s4k = K.sb("s4k", [128, 8, 128], F32)
        sq4 = K.sb("sq4", [128, 4, 128], BF16)
        rn4 = K.sb("rn4", [128, 512], F32)
        qTn = K.sb("qTn", [128, 8, 128], BF16)
        kTn = K.sb("kTn", [128, 8, 128], BF16)
        vT = K.sb("vT", [128, 16, 128], BF16)
        v_tok = K.sb("v_tok", [128, 16, 128], BF16)
        kend = K.sb("kend", [128, 16, 128], BF16)
        t16 = K.sb("t16", [128, 16], F32)
        e16 = K.sb("e16", [128, 16], F32)
        g16 = K.sb("g16", [128, 16], F32)
        beta16 = K.sb("beta16", [128, 16], F32)
        egk = K.sb("egk", [128, 64], F32)
        negeg = K.sb("negeg", [128, 16], F32)
        Pm = [K.sb("Pm%d" % i, [128, 128], F32) for i in range(2)]
        decT4s = [K.sb("decT4_%d" % i, [128, 4, 128], F32) for i in range(2)]
        decTs4s = [K.sb("decTs4_%d" % i, [128, 4, 128], F32) for i in range(2)]
        Ms = [K.sb("M%d" % i, [128, 16, 128], BF16) for i in range(2)]
        MTs = [K.sb("MT%d" % i, [128, 16, 128], BF16) for i in range(2)]
        Qf = Buf("Qf", stage_big.ap[:].rearrange("p (a b) -> p a b", a=16))
        K.phase_bufs.append(Qf)
        Qb = K.sb("Qb", [128, 16, 128], BF16)
        T2T = K.sb("T2T", [128, 16, 128], BF16)
        attnT = K.sb("attnT", [128, 16, 128], BF16)
        resid1 = K.sb("resid", [128, 16, 128], BF16)
        vn1 = K.sb("vn", [128, 16, 128], BF16)
        for b in (resid1, vn1):
            K.dve.memset(b.ap[:], 0.0, writes=[b])
        S = K.sb("S", [128, 16, 128], F32)
        S_bf = K.sb("S_bf", [128, 16, 128], BF16)
        K.dve.memset(S.ap[:], 0.0, writes=[S])
        K.dve.memset(S_bf.ap[:], 0.0, writes=[S_bf])
        o_sb = K.sb("o_sb", [128, 16, 128], F32)
        of_sb = K.sb("of_sb", [128, 2048], F32)
        eps_l2 = K.eps_tiles["l2"]
        one_t = K.eps_tiles["one"]
        order = list(range(NT)) if dr == 0 else list(range(NT - 1, -1, -1))
        for oi, ti in enumerate(order):
            t0 = ti * 128
            xt = xts[oi % 2]
            xT = xTs[oi % 2]
            if oi > 0 and (oi % TPS) == 0:
                K.dve.ts(S.ap[:], S.ap[:], K.carry.ap[:, 0:1], None, ALU.mult, reads=[S, K.carry], writes=[S])
                K.act.cp(S_bf.ap[:], S.ap[:], reads=[S], writes=[S_bf])
            K.sp.dma(xt, [(xt.ap[:], Xin[t0:t0 + 128, :])], writes=[xt])
            if t0 == 0:
                K.dve.memset(xhp.ap[:], 0.0, writes=[xhp])
            else:
                K.sp.dma(xhp, [(xhp.ap[:], Xin[t0 - 2:t0, :])], writes=[xhp])
                if t0 % SEG == 0:
                    K.dve.ts(xhp.ap[:], xhp.ap[:], K.carry.ap[0:2, 0:1], None, ALU.mult, reads=[xhp, K.carry], writes=[xhp])
            if t0 + 128 == L:
                K.dve.memset(xhn.ap[:], 0.0, writes=[xhn])
            else:
                K.sp.dma(xhn, [(xhn.ap[:], Xin[t0 + 128:t0 + 129, :])], writes=[xhn])
                if (t0 + 128) % SEG == 0:
                    K.dve.ts(xhn.ap[:], xhn.ap[:], K.carry.ap[0:1, 0:1], None, ALU.mult, reads=[xhn, K.carry], writes=[xhn])
            if dr == 1:
                K.sp.dma(of_sb, [(of_sb.ap[:], OF[t0:t0 + 128, :])], reads=[K.ofbuf[ti]], writes=[of_sb])
            transpose_x(K, xt.ap[:], xt, xT.ap, xT, ident.ap, ident, col0=2)
            bank = K.ps()
            for kc in range(8):
                tr(K, bank, bank.ap[:, kc * 4:kc * 4 + 2], xhp.ap[0:2, kc * 128:(kc + 1) * 128], ident.ap[0:2, 0:2], reads=[xhp, ident], signal=False)
                tr(K, bank, bank.ap[:, kc * 4 + 2:kc * 4 + 3], xhn.ap[0:1, kc * 128:(kc + 1) * 128], ident.ap[0:1, 0:1], reads=[xhn, ident],
                   signal=(kc == 7))
            hv_ = bank.ap[:, 0:32].rearrange("p (k c) -> p k c", c=4)
            K.act.cp(xT.ap[:, :, 0:2], hv_[:, :, 0:2], reads=[bank], writes=[xT])
            K.act.cp(xT.ap[:, :, 130:131], hv_[:, :, 2:3], reads=[bank], writes=[xT])
            for c3 in range(0, 32, 3):
                bank = K.ps()
                cs = list(range(c3, min(32, c3 + 3)))
                for i, c in enumerate(cs):
                    mm(K, bank, bank.ap[:, i * 132:i * 132 + 131], [(Win[:, kc, c * 128:(c + 1) * 128], xT.ap[:, kc, 0:131]) for kc in range(8)],
                       reads=[xT] + Winb)
                for i, c in enumerate(cs):
                    acc = accs[c % 4]
                    eng = K.dve
                    src = bank.ap[:, i * 132:i * 132 + 131]
                    sbuf_ = bank
                    eng.ts(acc.ap[:], src[:, 0:128], cw.ap[:, c:c + 1], None, ALU.mult, reads=[sbuf_, cw], writes=[acc])
                    for tp in range(1, 4):
                        eng.stt(acc.ap[:], src[:, tp:tp + 128], cw.ap[:, tp * 32 + c:tp * 32 + c + 1], acc.ap[:], ALU.mult, ALU.add,
                                reads=[sbuf_, cw, acc], writes=[acc])
                    if c < 8:
                        K.act.actv(s4q.ap[:, c, :], acc.ap[:], AF.Silu, reads=[acc], writes=[s4q])
                    elif c < 16:
                        K.act.actv(s4k.ap[:, c - 8, :], acc.ap[:], AF.Silu, reads=[acc], writes=[s4k])
                    else:
                        K.act.actv(vT.ap[:, c - 16, :], acc.ap[:], AF.Silu, reads=[acc], writes=[vT])
            for (src4, dstn, scl) in ((s4q, qTn, 128.0 ** -0.5), (s4k, kTn, None)):
                for g in range(2):
                    sv = src4.ap[:, g * 4:(g + 1) * 4, :]
                    K.pool.tt(sq4.ap[:], sv, sv, ALU.mult, reads=[src4], writes=[sq4])
                    bank = K.ps()
                    for i in range(4):
                        mm(K, bank, bank.ap[:, i * 128:(i + 1) * 128], [(ones_bf.ap[:], sq4.ap[:, i, :])], reads=[ones_bf, sq4])
                    K.act.actv(rn4.ap[:], bank.ap[:, 0:512], AF.Sqrt, bias=eps_l2.ap[:], scale=1.0, reads=[bank, eps_l2], writes=[rn4])
                    K.dve.recip(rn4.ap[:], rn4.ap[:], reads=[rn4], writes=[rn4])
                    dv_ = dstn.ap[:, g * 4:(g + 1) * 4, :].rearrange("p a b -> p (a b)")
                    sv2 = sv.rearrange("p a b -> p (a b)")
                    if scl is None:
                        K.dve.tt(dv_, sv2, rn4.ap[:], ALU.mult, reads=[src4, rn4], writes=[dstn])
                    else:
                        K.dve.stt(dv_, sv2, float(scl), rn4.ap[:], ALU.mult, ALU.mult, reads=[src4, rn4], writes=[dstn])
            bank = K.ps()
            mm(K, bank, bank.ap[:, 0:32], [(xT.ap[:, kc, 2:130], Wab[:, kc, :]) for kc in range(8)], reads=[xT] + Wabb)
            K.dve.tt(t16.ap[:], bank.ap[:, 0:16], dtb.ap[:, dr * 16:(dr + 1) * 16], ALU.add, reads=[bank, dtb], writes=[t16])
            K.act.actv(e16.ap[:], t16.ap[:], AF.Exp, reads=[t16], writes=[e16])
            K.act.actv(e16.ap[:], e16.ap[:], AF.Ln, bias=one_t.ap[:], scale=1.0, reads=[e16, one_t], writes=[e16])
            K.dve.tt(g16.ap[:], e16.ap[:], negA.ap[:, dr * 16:(dr + 1) * 16], ALU.mult, reads=[e16, negA], writes=[g16])
            K.act.actv(beta16.ap[:], bank.ap[:, 16:32], AF.Exp, scale=-1.0, reads=[bank], writes=[beta16])
            K.dve.ts(beta16.ap[:], beta16.ap[:], 1.0, None, ALU.add, reads=[beta16], writes=[beta16])
            K.dve.recip(beta16.ap[:], beta16.ap[:], reads=[beta16], writes=[beta16])
            bank = K.ps()
            for i, mname in enumerate(("TRI" + dn, "AFTER" + dn, "CH0", "CH1")):
                mm(K, bank, bank.ap[:, i * 16:(i + 1) * 16], [(cst(mname), g16.ap[:])], reads=[K.consts, g16])
            K.act.actv(egk.ap[:], bank.ap[:, 0:64], AF.Exp, reads=[bank], writes=[egk])
            K.dve.ts(negeg.ap[:], egk.ap[:, 0:16], -1.0, None, ALU.mult, reads=[egk], writes=[negeg])
            bank = K.ps()
            bb = bank_bf(bank)
            for hk in range(8):
                tr(K, bank, bb[:, hk * 128:(hk + 1) * 128], kTn.ap[:, hk, :], identb.ap[:], reads=[kTn, identb], signal=(hk == 7))
            for hv in range(16):
                q = K.act if (hv % 2) else K.dve
                if q is K.act:
                    q.actv(kend.ap[:, hv, :], bb[:, (hv // 2) * 128:(hv // 2 + 1) * 128], AF.Copy, scale=egk.ap[:, 16 + hv:17 + hv],
                           reads=[bank, egk], writes=[kend])
                else:
                    q.ts(kend.ap[:, hv, :], bb[:, (hv // 2) * 128:(hv // 2 + 1) * 128], egk.ap[:, 16 + hv:17 + hv], None, ALU.mult,
                         reads=[bank, egk], writes=[kend])
            for g in range(2):
                bank = K.ps()
                bb = bank_bf(bank)
                for i in range(8):
                    tr(K, bank, bb[:, i * 128:(i + 1) * 128], vT.ap[:, g * 8 + i, :], identb.ap[:], reads=[vT, identb], signal=(i == 7))
                q = K.ev()
                q.cp(v_tok.ap[:, g * 8:(g + 1) * 8, :], bb.rearrange("p (a b) -> p a b", a=8), reads=[bank], writes=[v_tok])
            M, MT = Ms[0], MTs[0]
            for gi in range(4):
                hv0 = gi * 4
                decT4, decTs4 = decT4s[gi % 2], decTs4s[gi % 2]
                gbank = K.ps()
                for i in range(2):
                    hk = 2 * gi + i
                    mm(K, gbank, gbank.ap[:, i * 128:(i + 1) * 128], [(kTn.ap[:, hk, :], kTn.ap[:, hk, :])], reads=[kTn])
                    mm(K, gbank, gbank.ap[:, (2 + i) * 128:(3 + i) * 128], [(kTn.ap[:, hk, :], qTn.ap[:, hk, :])], reads=[kTn, qTn])
                dbank = K.ps()
                for i in range(4):
                    hv = hv0 + i
                    P = Pm[i % 2]
                    K.dve.ts(P.ap[:], cst("TRI" + dn), g16.ap[:, hv:hv + 1], None, ALU.mult, reads=[K.consts, g16], writes=[P])
                    mm(K, dbank, dbank.ap[:, i * 128:(i + 1) * 128],
                       [(cst("ONES"), P.ap[:]), (P.ap[:], cst("NEGONES")), (cst("IDENT"), cst("NEG" + dn))], reads=[P, K.consts])
                K.act.actv(decT4.ap[:].rearrange("p a b -> p (a b)"), dbank.ap[:, 0:512], AF.Exp, reads=[dbank], writes=[decT4])
                K.pool.tt(decTs4.ap[:].rearrange("p a b -> p (a b)"), decT4.ap[:].rearrange("p a b -> p (a b)"), cst("OFFD4"), ALU.mult,
                          reads=[decT4, K.consts], writes=[decTs4])
                for i in range(4):
                    hv = hv0 + i
                    K.dve.stt(M.ap[:, hv, :], gbank.ap[:, (i // 2) * 128:(i // 2 + 1) * 128], beta16.ap[:, hv:hv + 1], decTs4.ap[:, i, :],
                              ALU.mult, ALU.mult, reads=[gbank, beta16, decTs4], writes=[M])
                    K.dve.tt(attnT.ap[:, hv, :], gbank.ap[:, (2 + i // 2) * 128:(3 + i // 2) * 128], decT4.ap[:, i, :], ALU.mult,
                             reads=[gbank, decT4], writes=[attnT])
                bank = K.ps()
                bb = bank_bf(bank)
                for i in range(4):
                    tr(K, bank, bb[:, i * 128:(i + 1) * 128], M.ap[:, hv0 + i, :], identb.ap[:], reads=[M, identb], signal=(i == 3))
                K.act.cp(MT.ap[:, hv0:hv0 + 4, :].rearrange("p a b -> p (a b)"), bb[:, 0:512], reads=[bank], writes=[MT])
                K.pool.tt(Qf.ap[:, hv0:hv0 + 4, :].rearrange("p a b -> p (a b)"), cst("IDENT4"),
                          M.ap[:, hv0:hv0 + 4, :].rearrange("p a b -> p (a b)"), ALU.subtract, reads=[K.consts, M], writes=[Qf])
                K.act.cp(Qb.ap[:, hv0:hv0 + 4, :], Qf.ap[:, hv0:hv0 + 4, :], reads=[Qf], writes=[Qb])
            for lev in range(1, 6):
                Mn, MTn = Ms[lev % 2], MTs[lev % 2]
                if lev < 5:
                    for gi in range(4):
                        hv0 = gi * 4
                        bank = K.ps()
                        for i in range(4):
                            mm(K, bank, bank.ap[:, i * 128:(i + 1) * 128], [(MT.ap[:, hv0 + i, :], M.ap[:, hv0 + i, :])], reads=[MT, M])
                        K.act.cp(Mn.ap[:, hv0:hv0 + 4, :].rearrange("p a b -> p (a b)"), bank.ap[:, 0:512], reads=[bank], writes=[Mn])
                for gi in range(4):
                    hv0 = gi * 4
                    bank = K.ps()
                    for i in range(4):
                        mm(K, bank, bank.ap[:, i * 128:(i + 1) * 128], [(M.ap[:, hv0 + i, :], MT.ap[:, hv0 + i, :])], reads=[MT, M])
                    q = K.act if (gi % 2) else K.dve
                    q.cp(MTn.ap[:, hv0:hv0 + 4, :].rearrange("p a b -> p (a b)"), bank.ap[:, 0:512], reads=[bank], writes=[MTn])
                for gi in range(4):
                    hv0 = gi * 4
                    bank = K.ps()
                    for i in range(4):
                        mm(K, bank, bank.ap[:, i * 128:(i + 1) * 128], [(MTn.ap[:, hv0 + i, :], Qb.ap[:, hv0 + i, :])], reads=[MTn, Qb])
                    qv = Qf.ap[:, hv0:hv0 + 4, :].rearrange("p a b -> p (a b)")
                    K.dve.tt(qv, qv, bank.ap[:, 0:512], ALU.add, reads=[Qf, bank], writes=[Qf])
                    if lev < 5:
                        K.act.cp(Qb.ap[:, hv0:hv0 + 4, :], Qf.ap[:, hv0:hv0 + 4, :], reads=[Qf], writes=[Qb])
                    else:
                        K.act.cp(T2T.ap[:, hv0:hv0 + 4, :], Qf.ap[:, hv0:hv0 + 4, :], reads=[Qf], writes=[T2T])
                M, MT = Mn, MTn
            chunks = [0, 1] if dr == 0 else [1, 0]
            for ck in chunks:
                r0, r1 = ck * 64, ck * 64 + 64
                z0, z1 = (64, 128) if ck == 0 else (0, 64)
                rs, vnb = resid1, vn1
                K.pool.memset(rs.ap[z0:z1, :, :], 0.0, writes=[rs])
                K.pool.memset(vnb.ap[z0:z1, :, :], 0.0, writes=[vnb])
                kbs, qbs, nbs, abs_, vbs = [], [], [], [], []
                for gi in range(4):
                    kb = K.ps()
                    for i in range(2):
                        hk = 2 * gi + i
                        mm(K, kb, kb.ap[:, i * 256:(i + 1) * 256], [(kTn.ap[:, hk, :], S_bf.ap[:, 2 * hk:2 * hk + 2, :])], reads=[kTn, S_bf])
                    kbs.append(kb)
                for gi in range(4):
                    qb_ = K.ps()
                    for i in range(2):
                        hk = 2 * gi + i
                        mm(K, qb_, qb_.ap[:, i * 256:(i + 1) * 256], [(qTn.ap[:, hk, :], S_bf.ap[:, 2 * hk:2 * hk + 2, :])], reads=[qTn, S_bf])
                    qbs.append(qb_)
                for gi in range(4):
                    for i in range(4):
                        hv = gi * 4 + i
                        K.dve.stt(rs.ap[r0:r1, hv, :], kbs[gi].ap[r0:r1, i * 128:(i + 1) * 128], negeg.ap[r0:r1, hv:hv + 1], v_tok.ap[r0:r1, hv, :],
                                  ALU.mult, ALU.add, reads=[kbs[gi], negeg, v_tok], writes=[rs])
                for gi in range(4):
                    for i in range(4):
                        hv = gi * 4 + i
                        K.act.actv(o_sb.ap[r0:r1, hv, :], qbs[gi].ap[r0:r1, i * 128:(i + 1) * 128], AF.Copy, scale=egk.ap[r0:r1, hv:hv + 1],
                                   reads=[qbs[gi], egk], writes=[o_sb])
                for gi in range(4):
                    nb = K.ps()
                    for i in range(4):
                        hv = gi * 4 + i
                        mm(K, nb, nb.ap[:, i * 128:(i + 1) * 128], [(T2T.ap[:, hv, :], rs.ap[:, hv, :])], reads=[T2T, rs])
                    nbs.append(nb)
                for gi in range(4):
                    for i in range(4):
                        hv = gi * 4 + i
                        K.act.actv(vnb.ap[r0:r1, hv, :], nbs[gi].ap[r0:r1, i * 128:(i + 1) * 128], AF.Copy, scale=beta16.ap[r0:r1, hv:hv + 1],
                                   reads=[nbs[gi], beta16], writes=[vnb])
                for gi in range(4):
                    ab_ = K.ps()
                    for i in range(4):
                        hv = gi * 4 + i
                        mm(K, ab_, ab_.ap[:, i * 128:(i + 1) * 128], [(attnT.ap[:, hv, :], vnb.ap[:, hv, :])], reads=[attnT, vnb])
                    abs_.append(ab_)
                for gi in range(4):
                    vb_ = K.ps()
                    for i in range(4):
                        hv = gi * 4 + i
                        mm(K, vb_, vb_.ap[:, i * 128:(i + 1) * 128], [(kend.ap[:, hv, :], vnb.ap[:, hv, :])], reads=[kend, vnb])
                    vbs.append(vb_)
                for gi in range(4):
                    for i in range(4):
                        hv = gi * 4 + i
                        K.dve.stt(S.ap[:, hv, :], S.ap[:, hv, :], egk.ap[:, 32 + ck * 16 + hv:33 + ck * 16 + hv], vbs[gi].ap[:, i * 128:(i + 1) * 128],
                                  ALU.mult, ALU.add, reads=[S, egk, vbs[gi]], writes=[S])
                    K.pool.cp(S_bf.ap[:, gi * 4:gi * 4 + 4, :], S.ap[:, gi * 4:gi * 4 + 4, :], reads=[S], writes=[S_bf])
                for gi in range(4):
                    for i in range(4):
                        hv = gi * 4 + i
                        K.dve.tt(o_sb.ap[r0:r1, hv, :], abs_[gi].ap[r0:r1, i * 128:(i + 1) * 128], o_sb.ap[r0:r1, hv, :], ALU.add,
                                 reads=[abs_[gi], o_sb], writes=[o_sb])
            ov = o_sb.ap[:].rearrange("p a b -> p (a b)")
            if dr == 1:
                K.pool.tt(ov, ov, of_sb.ap[:], ALU.add, reads=[o_sb, of_sb], writes=[o_sb])
            K.pool.dma(o_sb, [(OF[t0:t0 + 128, :], ov)], reads=[o_sb], writes=[K.ofbuf[ti]])
        K.end_phase()


def gdn_finish_pass(K, cfg, Xin, Xout, OF, W, j, lng, lnb):
    L = cfg["L"]
    NT = L // 128
    with ExitStack() as ps_:
        K.begin_phase(ps_)
        K.set_rot(list(range(8)))
        stage = [K.sb("wst%d" % i, [128, 1024], F32) for i in range(2)]
        Wz, Wzb = load_weight_bf16(K, W["gdn_w_in"][j][:, 4096:6144], D, 2048, "Wz", stage, col_group=1024)
        Wout, Woutb = load_weight_bf16(K, W["gdn_w_out"][j], 2048, D, "Wout", stage, col_group=1024)
        g_b = load_bcast(K, lng, D, "lng")
        b_b = load_bcast(K, lnb, D, "lnb")
        nw_b = load_bcast(K, W["gdn_norm_w"][j], 128, "nw")
        ident, identb = K.ident, K.identb
        xts = [K.sb("xt%d" % i, [128, D], F32) for i in range(2)]
        xTs = [K.sb("xT%d" % i, [128, 8, 128], BF16) for i in range(2)]
        os_ = [K.sb("o%d" % i, [128, 16, 128], F32) for i in range(2)]
        sqo = K.sb("sqo", [128, 16, 128], F32)
        ss = K.sb("ss", [128, 16], F32)
        sz_bf = K.sb("sz_bf", [128, 2048], BF16)
        og_bf = K.sb("og_bf", [128, 2048], BF16)
        ogT = K.sb("ogT", [128, 16, 128], BF16)
        zs = [K.sb("z%d" % i, [128, D], F32) for i in range(2)]
        stats = K.sb("stats", [128, 2, 6], F32)
        mv = K.sb("mv", [128, 2], F32)
        rstd = K.sb("rstd", [128, 1], F32)
        eps_l2 = K.eps_tiles["l2"]
        for ti in range(NT):
            t0 = ti * 128
            xt, xT, o_sb = xts[ti % 2], xTs[ti % 2], os_[ti % 2]
            ov = o_sb.ap[:].rearrange("p a b -> p (a b)")
            K.sp.dma(xt, [(xt.ap[:], Xin[t0:t0 + 128, :])], writes=[xt])
            K.sp.dma(o_sb, [(ov, OF[t0:t0 + 128, :])], reads=[K.ofbuf[ti]], writes=[o_sb])
            transpose_x(K, xt.ap[:], xt, xT.ap, xT, ident.ap, ident, col0=0)
            for gi in range(4):
                bank = K.ps()
                mm(K, bank, bank.ap[:, 0:512], [(xT.ap[:, kc, :], Wz[:, kc, gi * 512:(gi + 1) * 512]) for kc in range(8)], reads=[xT] + Wzb)
                K.act.actv(sz_bf.ap[:, gi * 512:(gi + 1) * 512], bank.ap[:, 0:512], AF.Silu, reads=[bank], writes=[sz_bf])
            K.pool.tt(sqo.ap[:], o_sb.ap[:], o_sb.ap[:], ALU.mult, reads=[o_sb], writes=[sqo])
            K.dve.op(lambda e, a=ss.ap[:], b=sqo.ap[:]: e.tensor_reduce(a, b, mybir.AxisListType.X, ALU.add), reads=[sqo], writes=[ss])
            K.act.actv(ss.ap[:], ss.ap[:], AF.Sqrt, bias=eps_l2.ap[:], scale=1.0 / 128.0, reads=[ss, eps_l2], writes=[ss])
            K.dve.recip(ss.ap[:], ss.ap[:], reads=[ss], writes=[ss])
            for hv in range(16):
                K.dve.stt(o_sb.ap[:, hv, :], o_sb.ap[:, hv, :], ss.ap[:, hv:hv + 1], nw_b.ap[:], ALU.mult, ALU.mult, reads=[o_sb, ss, nw_b], writes=[o_sb])
            K.pool.tt(og_bf.ap[:], ov, sz_bf.ap[:], ALU.mult, reads=[o_sb, sz_bf], writes=[og_bf])
            for g in range(2):
                bank = K.ps()
                bb = bank_bf(bank)
                for i in range(8):
                    c = g * 8 + i
                    tr(K, bank, bb[:, i * 128:(i + 1) * 128], og_bf.ap[:, c * 128:(c + 1) * 128], identb.ap[:], reads=[og_bf, identb], signal=(i == 7))
                q = K.ev()
                q.cp(ogT.ap[:, g * 8:(g + 1) * 8, :], bb.rearrange("p (a b) -> p a b", a=8), reads=[bank], writes=[ogT])
            banks = []
            for fh in range(2):
                bank = K.ps()
                mm(K, bank, bank.ap[:, 0:512], [(ogT.ap[:, c, :], Wout[:, c, fh * 512:(fh + 1) * 512]) for c in range(16)], reads=[ogT] + Woutb)
                banks.append(bank)
            z = zs[ti % 2]
            residual_ln(K, banks, xt, xt.ap[:], 1.0 / ALPHA, g_b, b_b, z, "lna", (stats, mv, rstd))
            K.pool.dma(z, [(Xout[t0:t0 + 128, :], z.ap[:])], reads=[z])
        K.end_phase()


def gdn_phase(K, cfg, Xin, Xout, OF, W, j, lng, lnb):
    gdn_scan_pass(K, cfg, Xin, OF, W, j, 0)
    gdn_scan_pass(K, cfg, Xin, OF, W, j, 1)
    gdn_finish_pass(K, cfg, Xin, Xout, OF, W, j, lng, lnb)


_PROGRAM_CACHE = {}


def kernel(**inputs):
    xp = np.asarray(inputs["x_prompt"], dtype=np.float32)
    xsm = np.asarray(inputs["x_sample"], dtype=np.float32)
    mp = np.asarray(inputs["mem_prompt"], dtype=np.float32)
    ms = np.asarray(inputs["mem_sample"], dtype=np.float32)
    L, NSEG, SEG = 16384, 4, 4096
    consts = make_consts()
    cfg = {"L": L, "NSEG": NSEG, "SEG": SEG, "NCONST": consts.shape[1]}
    if "nc" not in _PROGRAM_CACHE:
        _PROGRAM_CACHE["nc"] = build_program(cfg)
    nc = _PROGRAM_CACHE["nc"]
    wmap = {name: np.ascontiguousarray(np.asarray(inputs[name], dtype=np.float32)) for name, _ in WEIGHT_SPECS}
    one = np.ones((128, 1), np.float32)
    zero = np.zeros((128, 1), np.float32)
    streams = [
        (np.ascontiguousarray(xp.reshape(L, D)), np.ascontiguousarray(np.broadcast_to(mp, (NSEG, NMEM, D))), one),
        (np.ascontiguousarray(xsm[0:4].reshape(L, D)), np.ascontiguousarray(ms[0:4]), zero),
        (np.ascontiguousarray(xsm[4:8].reshape(L, D)), np.ascontiguousarray(ms[4:8]), zero),
    ]
    in_maps = []
    for c in range(8):
        xs_, mem_, carry_ = streams[min(c, 2)]
        m = {"xs": xs_, "mem": mem_, "carry": carry_, "consts": consts}
        m.update(wmap)
        in_maps.append(m)
    res = run_bass_kernel_spmd(nc, in_maps, core_ids=list(range(8)))
    y_prompt = np.asarray(res.results[0]["y"], dtype=np.float32).reshape(1, L, D)
    y_sample = np.concatenate([np.asarray(res.results[1]["y"], dtype=np.float32).reshape(4, SEG, D),
                               np.asarray(res.results[2]["y"], dtype=np.float32).reshape(4, SEG, D)], axis=0)
    return (y_prompt, y_sample)
```

```python
import numpy as np
from contextlib import ExitStack
import concourse.bass as bass
import concourse.mybir as mybir
from concourse.bass_utils import run_bass_kernel_spmd

F32 = mybir.dt.float32
BF16 = mybir.dt.bfloat16
AF = mybir.ActivationFunctionType
ALU = mybir.AluOpType

D = 1024
DFF = 2816
NMEM = 256
DEPTH = 2
ALPHA = (2.0 * DEPTH) ** 0.25
LN_EPS = 1e-5
NORM_EPS = 1e-6
GLA_H, GLA_DK, GLA_DV = 4, 128, 256
GDN_HK, GDN_HV = 8, 16


class Sem:
    def __init__(self, h, name):
        self.h = h
        self.name = name
        self.count = 0


class Buf:
    def __init__(self, name, ap=None):
        self.name = name
        self.ap = ap
        self.wtoks = {}
        self.rtoks = {}
        self.dsem = None
        self.excl = False


class Q:
    def __init__(self, K, name):
        self.K = K
        self.name = name
        self.sem = K.new_sem("q_" + name)
        self.seen = {}
        self.ops = []

    def wait_tok(self, sem, val):
        if sem is self.sem and self.name == "pe":
            return
        if self.seen.get(sem, 0) >= val:
            return
        self.seen[sem] = val
        self.ops.append(("w", sem, val))

    def deps(self, reads, writes):
        for b in reads:
            for s, v in b.wtoks.items():
                self.wait_tok(s, v)
            if b.excl:
                for s, v in b.rtoks.items():
                    if s is not self.sem:
                        self.wait_tok(s, v)
        for b in writes:
            for s, v in b.wtoks.items():
                self.wait_tok(s, v)
            for s, v in b.rtoks.items():
                self.wait_tok(s, v)

    def mark(self, tok, reads, writes):
        s, v = tok
        for b in reads:
            if b.rtoks.get(s, 0) < v:
                b.rtoks[s] = v
        for b in writes:
            b.wtoks[s] = v
            b.rtoks = {}

    def op(self, fn, reads=(), writes=(), signal=True):
        self.deps(reads, writes)
        if signal:
            self.sem.count += 1
            tok = (self.sem, self.sem.count)
            self.ops.append(("i", fn, self.sem, 1))
        else:
            tok = (self.sem, self.sem.count + 1)
            self.ops.append(("i", fn, None, 0))
        self.mark(tok, reads, writes)
        return tok

    def tt(self, out, in0, in1, op, reads=(), writes=()):
        return self.op(lambda e: e.tensor_tensor(out, in0, in1, op), reads, writes)

    def ts(self, out, in0, s1, s2, op0, op1=None, reads=(), writes=(), accum=None):
        if op1 is None:
            return self.op(lambda e: e.tensor_scalar(out, in0, s1, None, op0), reads, writes)
        if accum is not None:
            return self.op(lambda e: e.tensor_scalar(out, in0, s1, s2, op0, op1, accum), reads, writes)
        return self.op(lambda e: e.tensor_scalar(out, in0, s1, s2, op0, op1), reads, writes)

    def stt(self, out, in0, scalar, in1, op0, op1, reads=(), writes=()):
        return self.op(lambda e: e.scalar_tensor_tensor(out, in0, scalar, in1, op0, op1), reads, writes)

    def actv(self, out, in_, func, bias=None, scale=None, accum=None, reads=(), writes=()):
        kw = {}
        if bias is not None:
            kw["bias"] = bias
        if scale is not None:
            kw["scale"] = scale
        if accum is not None:
            kw["accum_out"] = accum
        return self.op(lambda e: e.activation(out, in_, func, **kw), reads, writes)

    def cp(self, out, in_, scale=None, reads=(), writes=()):
        return self.op(copy_op(self, out, in_, scale), reads, writes)

    def memset(self, ap, val, writes=()):
        return self.op(lambda e: e.memset(ap, float(val)), (), writes)

    def recip(self, out, in_, reads=(), writes=()):
        return self.op(lambda e: e.reciprocal(out, in_), reads, writes)

    def rmax(self, out, in_, reads=(), writes=()):
        return self.op(lambda e: e.reduce_max(out, in_, mybir.AxisListType.X), reads, writes)

    def bnstats(self, out, in_, reads=(), writes=()):
        return self.op(lambda e: e.bn_stats(out, in_), reads, writes)

    def bnaggr(self, out, in_, reads=(), writes=()):
        return self.op(lambda e: e.bn_aggr(out, in_), reads, writes)

    def dma(self, sb, pairs, reads=(), writes=()):
        K = self.K
        if sb.dsem is None:
            sb.dsem = K.get_dsem()
        ds = sb.dsem
        if ds.count > 0:
            self.wait_tok(ds, ds.count)
        self.deps(reads, writes)
        for (o, i) in pairs:
            ds.count += 16
            self.ops.append(("d", o, i, ds))
        tok = (ds, ds.count)
        self.mark(tok, reads, writes)
        return tok

    def replay(self, eng):
        for o in self.ops:
            if o[0] == "w":
                eng.wait_ge(o[1].h, o[2])
            elif o[0] == "i":
                ins = o[1](eng)
                if o[2] is not None:
                    ins.then_inc(o[2].h, o[3])
            elif o[0] == "d":
                eng.dma_start(out=o[1], in_=o[2]).then_inc(o[3].h, 16)


class Kern:
    def __init__(self):
        self.nc = bass.Bass("TRN2", target_bir_lowering=False)
        self.sems = []
        self.free_dsems = []
        self.phase_bufs = []

    def new_sem(self, name):
        h = self.gstack.enter_context(self.nc.semaphore(name))
        s = Sem(h, name)
        self.sems.append(s)
        return s

    def get_dsem(self):
        if self.free_dsems:
            return self.free_dsems.pop()
        return self.new_sem("d%d" % len(self.sems))

    def setup(self, gstack):
        self.gstack = gstack
        self.pe = Q(self, "pe")
        self.act = Q(self, "act")
        self.dve = Q(self, "dve")
        self.pool = Q(self, "pool")
        self.sp = Q(self, "sp")
        self.queues = [self.pe, self.act, self.dve, self.pool, self.sp]
        self.banks = []
        for i in range(8):
            t = gstack.enter_context(self.nc.psum_tensor("psb%d" % i, [128, 512], F32))
            self.banks.append(Buf("psb%d" % i, t))
            self.banks[-1].excl = True
        self.bank_i = 0
        self.rot = list(range(8))
        self.rr = 0

    def set_rot(self, idx):
        self.rot = list(idx)
        self.bank_i = 0

    def ps(self):
        b = self.banks[self.rot[self.bank_i % len(self.rot)]]
        self.bank_i += 1
        return b

    def cst(self, name):
        o, w = CONST_LAYOUT[name]
        return self.consts.ap[:, o:o + w]

    def sb(self, name, shape, dt):
        nbytes = int(np.prod(shape[1:])) * (2 if dt == BF16 else 4)
        base = self.nc.sbuf_base
        t = self.pstack.enter_context(self.nc.sbuf_tensor("p%d_%s" % (self.phase_id, name), shape, dt))
        b = Buf(name, t)
        self.phase_bufs.append(b)
        return b

    def begin_phase(self, pstack):
        self.phase_id = getattr(self, "phase_id", 0) + 1
        self.pstack = pstack
        self.phase_bufs = []

    def barrier(self):
        for q in self.queues:
            for s in self.sems:
                if s.count > 0:
                    q.wait_tok(s, s.count)

    def end_phase(self):
        self.barrier()
        for b in self.phase_bufs:
            if b.dsem is not None:
                self.free_dsems.append(b.dsem)
                b.dsem = None
        self.phase_bufs = []

    def ev(self):
        self.rr += 1
        return self.act if (self.rr & 1) else self.dve

    def finish(self):
        self.barrier()
        nc = self.nc
        with nc.Block() as block:
            @block.tensor
            def _(e):
                self.pe.replay(e)

            @block.scalar
            def _(e):
                self.act.replay(e)

            @block.vector
            def _(e):
                self.dve.replay(e)

            @block.gpsimd
            def _(e):
                self.pool.replay(e)

            @block.sync
            def _(e):
                self.sp.replay(e)


def copy_op(q, out_ap, in_ap, scale=None):
    if q.name == "act":
        if scale is None:
            return lambda e: e.activation(out_ap, in_ap, AF.Copy)
        return lambda e: e.activation(out_ap, in_ap, AF.Copy, scale=float(scale))
    if scale is None:
        return lambda e: e.tensor_copy(out_ap, in_ap)
    return lambda e: e.tensor_scalar(out_ap, in_ap, float(scale), None, ALU.mult)


def mm(K, bank, out_ap, pairs, reads, start=True, stop=True):
    n = len(pairs)
    tok = None
    for i, (l, r) in enumerate(pairs):
        st = start and i == 0
        sp = stop and i == n - 1
        last = i == n - 1
        tok = K.pe.op(lambda e, l=l, r=r, st=st, sp=sp: e.matmul(out_ap, l, r, start=st, stop=sp),
                      reads=reads if (last or i == 0) else (), writes=[bank], signal=last)
    return tok


def tr(K, bank, out_ap, in_ap, ident_ap, reads, signal=True):
    return K.pe.op(lambda e: e.transpose(out_ap, in_ap, ident_ap), reads=reads, writes=[bank], signal=signal)


def load_weight_bf16(K, wdram, rows, cols, name, stage, col_group=2048):
    nch = rows // 128
    t = K.sb(name, [128, nch, cols], BF16)
    bufs = []
    si = 0
    for c in range(nch):
        b = Buf("%s_%d" % (name, c), t.ap)
        eng = K.act if (c & 1) else K.dve
        for c0 in range(0, cols, col_group):
            c1 = min(cols, c0 + col_group)
            st = stage[si % len(stage)]
            si += 1
            K.sp.dma(st, [(st.ap[:, 0:c1 - c0], wdram[c * 128:(c + 1) * 128, c0:c1])], writes=[st])
            o = t.ap[:, c, c0:c1]
            i = st.ap[:, 0:c1 - c0]
            eng.cp(o, i, reads=[st], writes=[b])
        bufs.append(b)
    return t.ap, bufs


def load_bcast(K, dram_row, n, name):
    b = K.sb(name, [128, n], F32)
    K.sp.dma(b, [(b.ap[:], dram_row.partition_broadcast(128))], writes=[b])
    return b


def transpose_x(K, xt_ap, x_buf, xT_ap, xT_buf, ident_ap, ident_buf, tcols=128, col0=0, nrows=128):
    for g in range(2):
        bank = K.ps()
        for j in range(4):
            kc = g * 4 + j
            tr(K, bank, bank.ap[:, j * 128:j * 128 + nrows], xt_ap[0:nrows, kc * 128:(kc + 1) * 128],
               ident_ap[0:nrows, 0:nrows], reads=[x_buf, ident_buf], signal=(j == 3))
        q = K.ev()
        src = bank.ap[:].rearrange("p (j t) -> p j t", j=4)[:, :, 0:nrows]
        dst = xT_ap[:, g * 4:(g + 1) * 4, col0:col0 + nrows]
        q.cp(dst, src, reads=[bank], writes=[xT_buf])


def residual_ln(K, bank_list, x_buf, x_ap, fscale, g_b, b_b, z_buf, eps_adj, st_bufs):
    z = z_buf.ap
    stats, mv, rstd = st_bufs
    for h, bank in enumerate(bank_list):
        sl = slice(h * 512, (h + 1) * 512)
        K.dve.stt(z[:, sl], bank.ap[:, 0:512], float(fscale), x_ap[:, sl], ALU.mult, ALU.add, reads=[bank, x_buf], writes=[z_buf])
        K.dve.bnstats(stats.ap[:, h, :], z[:, sl], reads=[z_buf], writes=[stats])
    K.dve.bnaggr(mv.ap[:], stats.ap[:].rearrange("p a b -> p (a b)"), reads=[stats], writes=[mv])
    eps = K.eps_tiles[eps_adj]
    K.act.actv(rstd.ap[:], mv.ap[:, 1:2], AF.Sqrt, bias=eps.ap[:], scale=1.0, reads=[mv, eps], writes=[rstd])
    K.dve.recip(rstd.ap[:], rstd.ap[:], reads=[rstd], writes=[rstd])
    K.dve.ts(z[:], z[:], mv.ap[:, 0:1], rstd.ap[:], ALU.subtract, ALU.mult, reads=[z_buf, mv, rstd], writes=[z_buf])
    K.pool.tt(z[:], z[:], g_b.ap[:], ALU.mult, reads=[z_buf, g_b], writes=[z_buf])
    K.pool.tt(z[:], z[:], b_b.ap[:], ALU.add, reads=[z_buf, b_b], writes=[z_buf])


def make_eps_tiles(K, gstack):
    K.eps_tiles = {}
    for key, val in (("ln1", LN_EPS), ("lna", LN_EPS / (ALPHA * ALPHA)), ("l2", NORM_EPS), ("one", 1.0)):
        t = gstack.enter_context(K.nc.sbuf_tensor("eps_" + key, [128, 1], F32))
        b = Buf("eps_" + key, t)
        K.dve.memset(t[:], val, writes=[b])
        K.eps_tiles[key] = b


def ffn_phase(K, cfg, Xin, Xout, w_in, w_out, lng, lnb, TW=256):
    L = cfg["L"]
    NS = TW // 128
    with ExitStack() as ps_:
        K.begin_phase(ps_)
        stage = [K.sb("wst%d" % i, [128, 1024], F32) for i in range(2)]
        W1, W1b = load_weight_bf16(K, w_in, D, 2 * DFF, "W1", stage, col_group=704)
        W2, W2b = load_weight_bf16(K, w_out, DFF, D, "W2", stage, col_group=1024)
        g_b = load_bcast(K, lng, D, "lng")
        b_b = load_bcast(K, lnb, D, "lnb")
        xts = [K.sb("xt%d" % i, [128, NS, D], F32) for i in range(2)]
        xTs = [K.sb("xT%d" % i, [128, 8, TW], BF16) for i in range(2)]
        hTs = [K.sb("hT0", [128, 22, TW], BF16)] * 2
        sgs = [K.sb("sg%d" % i, [128, TW], F32) for i in range(2)]
        zs = stage
        stats = K.sb("stats", [128, 2, 6], F32)
        mv = K.sb("mv", [128, 2], F32)
        rstd = K.sb("rstd", [128, 1], F32)
        ident = K.ident
        zi = 0
        import os
        dbg = os.environ.get("DBG", "")
        if "a" in dbg: xts = [xts[1], xts[1]]
        if "b" in dbg: xTs = [xTs[1], xTs[1]]
        if "c" in dbg: hTs = [hTs[1], hTs[1]]
        if "d" in dbg: xts = [xts[0], xts[0]]
        if "e" in dbg: xTs = [xTs[0], xTs[0]]
        if "f" in dbg: hTs = [hTs[0], hTs[0]]
        for ti in range(L // TW):
            t0 = ti * TW
            xt = xts[ti % 2]
            xT = xTs[ti % 2]
            hT = hTs[ti % 2]
            K.sp.dma(xt, [(xt.ap[:, s, :], Xin[t0 + s * 128:t0 + (s + 1) * 128, :]) for s in range(NS)], writes=[xt])
            for s in range(NS):
                transpose_x(K, xt.ap[:, s, :], xt, xT.ap, xT, ident.ap, ident, col0=s * 128)
            for hc in range(22):
                bank = K.ps()
                mm(K, bank, bank.ap[:, 0:TW],
                   [(W1[:, kc, hc * 128:(hc + 1) * 128], xT.ap[:, kc, :]) for kc in range(8)], reads=[xT] + W1b)
                mm(K, bank, bank.ap[:, TW:2 * TW],
                   [(W1[:, kc, DFF + hc * 128:DFF + (hc + 1) * 128], xT.ap[:, kc, :]) for kc in range(8)], reads=[xT] + W1b)
                sg = sgs[hc % 2]
                K.act.actv(sg.ap[:], bank.ap[:, 0:TW], AF.Silu, reads=[bank], writes=[sg])
                K.dve.tt(hT.ap[:, hc, :], sg.ap[:], bank.ap[:, TW:2 * TW], ALU.mult, reads=[bank, sg], writes=[hT])
            for s in range(NS):
                banks = []
                for fh in range(2):
                    bank = K.ps()
                    mm(K, bank, bank.ap[:, 0:512],
                       [(hT.ap[:, hc, s * 128:(s + 1) * 128], W2[:, hc, fh * 512:(fh + 1) * 512]) for hc in range(22)],
                       reads=[hT] + W2b)
                    banks.append(bank)
                z = zs[zi % 2]
                zi += 1
                residual_ln(K, banks, xt, xt.ap[:, s, :], 0.5 / ALPHA, g_b, b_b, z, "lna", (stats, mv, rstd))
                K.pool.dma(z, [(Xout[t0 + s * 128:t0 + (s + 1) * 128, :], z.ap[:])], reads=[z])
            if "B" in dbg:
                K.barrier()
        K.end_phase()


WEIGHT_SPECS = [
    ("ffn_w_in", [2, 2, D, 2 * DFF]), ("ffn_w_out", [2, 2, DFF, D]), ("ln_g", [2, 4, D]), ("ln_b", [2, 4, D]),
    ("gla_w_in", [1, D, 3072]), ("gla_w_gate_down", [1, 2, D, 16]), ("gla_w_gate_up", [1, 2, 16, 512]),
    ("gla_b_gate", [1, 2, 512]), ("gla_norm_w", [1, 256]), ("gla_w_out", [1, D, D]),
    ("gdn_w_in", [1, D, 6144]), ("gdn_conv_w", [1, 4, 4096]), ("gdn_w_ab", [1, 2, D, 32]),
    ("gdn_a_log", [1, 2, 16]), ("gdn_dt_bias", [1, 2, 16]), ("gdn_norm_w", [1, 128]), ("gdn_w_out", [1, 2048, D]),
    ("xa_w_q", [2, D, D]), ("xa_w_kv", [2, D, 2 * D]), ("xa_w_o", [2, D, D]),
]


def build_program(cfg):
    L = cfg["L"]
    NSEG = cfg["NSEG"]
    phases = cfg.get("phases", None)
    K = Kern()
    nc = K.nc
    xs = nc.dram_tensor("xs", [L, D], F32, kind="ExternalInput").ap()
    mem = nc.dram_tensor("mem", [NSEG, NMEM, D], F32, kind="ExternalInput").ap()
    carry = nc.dram_tensor("carry", [128, 1], F32, kind="ExternalInput").ap()
    consts = nc.dram_tensor("consts", [128, cfg["NCONST"]], F32, kind="ExternalInput").ap()
    W = {}
    for name, shape in WEIGHT_SPECS:
        W[name] = nc.dram_tensor(name, shape, F32, kind="ExternalInput").ap()
    y = nc.dram_tensor("y", [L, D], F32, kind="ExternalOutput").ap()
    XA = nc.dram_tensor("scrA", [L, D], F32).ap()
    XB = nc.dram_tensor("scrB", [L, D], F32).ap()
    OF = nc.dram_tensor("scrO", [L, 2048], F32).ap()
    QKVs = nc.dram_tensor("scrQKV", [L, 4096], BF16).ap()
    with ExitStack() as gs:
        K.setup(gs)
        make_eps_tiles(K, gs)
        ct = gs.enter_context(nc.sbuf_tensor("consts_sb", [128, cfg["NCONST"]], F32))
        K.consts = Buf("consts", ct)
        K.sp.dma(K.consts, [(ct[:], consts)], writes=[K.consts])
        K.ident = Buf("ident", ct[:, 0:128])
        K.ident.wtoks = K.consts.wtoks
        ibt = gs.enter_context(nc.sbuf_tensor("identb", [128, 128], BF16))
        K.identb = Buf("identb", ibt)
        K.dve.cp(ibt[:], ct[:, 0:128], reads=[K.consts], writes=[K.identb])
        cyt = gs.enter_context(nc.sbuf_tensor("carry_sb", [128, 1], F32))
        K.carry = Buf("carry", cyt)
        K.sp.dma(K.carry, [(cyt[:], carry)], writes=[K.carry])
        K.cfg = cfg
        K.ofbuf = [Buf("of%d" % i) for i in range(L // 128)]
        K.qkvbuf = [Buf("qkv%d" % i) for i in range(L // 128)]
        K.QKV = QKVs
        plan = []
        for i in range(DEPTH):
            plan.append(("ffn", i, 0))
            plan.append(("mix", i))
            plan.append(("xa", i))
            plan.append(("ffn", i, 1))
        if phases is not None:
            plan = plan[:phases]
        if cfg.get("only"):
            plan = {"ffn": [("ffn", 0, 0)], "xa": [("xa", 0)], "gla": [("mix", 0)], "gdn": [("mix", 1)]}[cfg["only"]]
        cur = xs
        for pi, ph in enumerate(plan):
            last = pi == len(plan) - 1
            dst = y if last else (XA if (pi % 2 == 0) else XB)
            if ph[0] == "ffn":
                i, j = ph[1], ph[2]
                li = 0 if j == 0 else 3
                ffn_phase(K, cfg, cur, dst, W["ffn_w_in"][i, j], W["ffn_w_out"][i, j], W["ln_g"][i, li], W["ln_b"][i, li])
            elif ph[0] == "mix":
                i = ph[1]
                if i % 2 == 0:
                    gla_phase(K, cfg, cur, dst, OF, W, i // 2, W["ln_g"][i, 1], W["ln_b"][i, 1])
                else:
                    gdn_phase(K, cfg, cur, dst, OF, W, i // 2, W["ln_g"][i, 1], W["ln_b"][i, 1])
            elif ph[0] == "xa":
                i = ph[1]
                xa_phase(K, cfg, cur, dst, mem, W["xa_w_q"][i], W["xa_w_kv"][i], W["xa_w_o"][i], W["ln_g"][i, 2], W["ln_b"][i, 2])
            cur = dst
        K.finish()
    return nc


def _build_consts():
    s_ = np.arange(128)[:, None]
    c_ = np.arange(128)[None, :]
    same = (s_ // 64) == (c_ // 64)
    pos_s = s_ % 64
    out = {}
    out["IDENT"] = np.eye(128, dtype=np.float32)
    for dn in ("f", "b"):
        if dn == "f":
            tri = same & (s_ <= c_)
            mid = same & (pos_s <= 31) & (c_ >= 0)
        else:
            tri = same & (s_ >= c_)
            mid = same & (pos_s >= 32) & (c_ >= 0)
        tri = tri.astype(np.float32)
        mid = mid.astype(np.float32)
        sm = same.astype(np.float32)
        out["M1" + dn] = (-1.0 / 16.0) * (tri - mid)
        out["M3" + dn] = (-1.0 / 16.0) * tri
        out["M4" + dn] = (-1.0 / 16.0) * (sm - tri)
        out["MASK4" + dn] = np.tile(tri, (1, 4))
        out["TRI" + dn] = tri
        out["AFTER" + dn] = sm - tri
        out["NEG" + dn] = (1.0 - tri) * (-30000.0)
    ind = np.zeros((128, 2), np.float32)
    ind[0:64, 0] = -1.0 / 16.0
    ind[64:128, 1] = -1.0 / 16.0
    out["IND"] = ind
    out["ONES"] = np.ones((128, 128), np.float32)
    out["NEGONES"] = -np.ones((128, 128), np.float32)
    out["OFFD4"] = np.tile(1.0 - np.eye(128, dtype=np.float32), (1, 4))
    out["IDENT4"] = np.tile(np.eye(128, dtype=np.float32), (1, 4))
    ch0 = np.zeros((128, 128), np.float32)
    ch0[0:64, :] = 1.0
    out["CH0"] = ch0
    out["CH1"] = 1.0 - ch0
    return out


_CONSTS = _build_consts()
CONST_LAYOUT = {}
_off = 0
for _k, _v in _CONSTS.items():
    CONST_LAYOUT[_k] = (_off, _v.shape[1])
    _off += _v.shape[1]
NCONST = _off


def make_consts():
    return np.ascontiguousarray(np.concatenate([_CONSTS[k] for k in CONST_LAYOUT], axis=1).astype(np.float32))


def bank_bf(bank):
    return bank.ap[:].bitcast(BF16)


def xa_phase(K, cfg, Xin, Xout, mem, w_q, w_kv, w_o, lng, lnb):
    L, NSEG, SEG = cfg["L"], cfg["NSEG"], cfg["SEG"]
    TW = 256
    NS = TW // 128
    with ExitStack() as ps_:
        K.begin_phase(ps_)
        stage = [K.sb("wst%d" % i, [128, 1024], F32) for i in range(2)]
        Wq, Wqb = load_weight_bf16(K, w_q, D, D, "Wq", stage, col_group=1024)
        Wkv, Wkvb = load_weight_bf16(K, w_kv, D, 2 * D, "Wkv", stage, col_group=1024)
        Wo, Wob = load_weight_bf16(K, w_o, D, D, "Wo", stage, col_group=1024)
        g_b = load_bcast(K, lng, D, "lng")
        b_b = load_bcast(K, lnb, D, "lnb")
        ident, identb = K.ident, K.identb
        KT = K.sb("KT", [128, NSEG, 8, NMEM], BF16)
        V = K.sb("V", [128, NSEG, 2, D], BF16)
        memT = K.sb("memT", [128, 8, NMEM], BF16)
        xts = [K.sb("xt%d" % i, [128, NS, D], F32) for i in range(2)]
        xTs = [K.sb("xT%d" % i, [128, 8, TW], BF16) for i in range(2)]
        qT = K.sb("qT", [128, 8, TW], BF16)
        oT = K.sb("oT", [128, 8, TW], BF16)
        pT = K.sb("pT", [128, 4, 2, TW], BF16)
        pf = K.sb("pf", [128, 4, NMEM], F32)
        pb = K.sb("pb", [128, 4, NMEM], BF16)
        mx = K.sb("mx", [128, 4], F32)
        sm = K.sb("sm", [128, 4], F32)
        zs = [K.sb("z%d" % i, [128, D], F32) for i in range(2)]
        stats = K.sb("stats", [128, 2, 6], F32)
        mv = K.sb("mv", [128, 2], F32)
        rstd = K.sb("rstd", [128, 1], F32)
        for sg in range(NSEG):
            for mt in range(2):
                xt = xts[mt]
                K.sp.dma(xt, [(xt.ap[:, 0, :], mem[sg, mt * 128:(mt + 1) * 128, :])], writes=[xt])
                transpose_x(K, xt.ap[:, 0, :], xt, memT.ap, memT, ident.ap, ident, col0=mt * 128)
            for dhc in range(8):
                bank = K.ps()
                mm(K, bank, bank.ap[:, 0:NMEM], [(Wkv[:, kc, dhc * 128:(dhc + 1) * 128], memT.ap[:, kc, :]) for kc in range(8)],
                   reads=[memT] + Wkvb)
                q = K.ev()
                q.cp(KT.ap[:, sg, dhc, :], bank.ap[:, 0:NMEM], reads=[bank], writes=[KT])
            for mc in range(2):
                for half in range(2):
                    bank = K.ps()
                    mm(K, bank, bank.ap[:, 0:512],
                       [(memT.ap[:, kc, mc * 128:(mc + 1) * 128], Wkv[:, kc, D + half * 512:D + (half + 1) * 512]) for kc in range(8)],
                       reads=[memT] + Wkvb)
                    q = K.ev()
                    q.cp(V.ap[:, sg, mc, half * 512:(half + 1) * 512], bank.ap[:, 0:512], reads=[bank], writes=[V])
        zi = 0
        for ti in range(L // TW):
            t0 = ti * TW
            sg = t0 // SEG
            xt = xts[ti % 2]
            xT = xTs[ti % 2]
            K.sp.dma(xt, [(xt.ap[:, s, :], Xin[t0 + s * 128:t0 + (s + 1) * 128, :]) for s in range(NS)], writes=[xt])
            for s in range(NS):
                transpose_x(K, xt.ap[:, s, :], xt, xT.ap, xT, ident.ap, ident, col0=s * 128)
            for dhc in range(8):
                bank = K.ps()
                mm(K, bank, bank.ap[:, 0:TW], [(Wq[:, kc, dhc * 128:(dhc + 1) * 128], xT.ap[:, kc, :]) for kc in range(8)],
                   reads=[xT] + Wqb)
                q = K.ev()
                q.cp(qT.ap[:, dhc, :], bank.ap[:, 0:TW], scale=1.0 / 16.0, reads=[bank], writes=[qT])
            for s in range(NS):
                sb_ = []
                for j in range(2):
                    bank = K.ps()
                    for hh in range(2):
                        h = 2 * j + hh
                        mm(K, bank, bank.ap[:, hh * 256:(hh + 1) * 256],
                           [(qT.ap[:, 2 * h + c, s * 128:(s + 1) * 128], KT.ap[:, sg, 2 * h + c, :]) for c in range(2)], reads=[qT, KT])
                    K.dve.rmax(mx.ap[:, 2 * j:2 * j + 2], bank.ap[:].rearrange("p (h m) -> p h m", h=2), reads=[bank], writes=[mx])
                    sb_.append(bank)
                K.dve.ts(mx.ap[:], mx.ap[:], -1.0, None, ALU.mult, reads=[mx], writes=[mx])
                for h in range(4):
                    bank = sb_[h // 2]
                    K.act.actv(pf.ap[:, h, :], bank.ap[:, (h % 2) * 256:(h % 2 + 1) * 256], AF.Exp, bias=mx.ap[:, h:h + 1], scale=1.0,
                               accum=sm.ap[:, h:h + 1], reads=[bank, mx], writes=[pf, sm])
                K.dve.recip(sm.ap[:], sm.ap[:], reads=[sm], writes=[sm])
                for h in range(4):
                    K.dve.ts(pb.ap[:, h, :], pf.ap[:, h, :], sm.ap[:, h:h + 1], None, ALU.mult, reads=[pf, sm], writes=[pb])
                bank = K.ps()
                bb = bank_bf(bank)
                for h in range(4):
                    for mc in range(2):
                        idx = h * 2 + mc
                        tr(K, bank, bb[:, idx * 128:(idx + 1) * 128], pb.ap[:, h, mc * 128:(mc + 1) * 128], identb.ap[:],
                           reads=[pb, identb], signal=(idx == 7))
                q = K.ev()
                q.cp(pT.ap[:, :, :, s * 128:(s + 1) * 128], bb.rearrange("p (h c t) -> p h c t", h=4, c=2), reads=[bank], writes=[pT])
            for dhc in range(8):
                h = dhc // 2
                bank = K.ps()
                mm(K, bank, bank.ap[:, 0:TW], [(V.ap[:, sg, mc, dhc * 128:(dhc + 1) * 128], pT.ap[:, h, mc, :]) for mc in range(2)],
                   reads=[V, pT])
                q = K.ev()
                q.cp(oT.ap[:, dhc, :], bank.ap[:, 0:TW], reads=[bank], writes=[oT])
            for s in range(NS):
                banks = []
                for fh in range(2):
                    bank = K.ps()
                    mm(K, bank, bank.ap[:, 0:512],
                       [(oT.ap[:, dhc, s * 128:(s + 1) * 128], Wo[:, dhc, fh * 512:(fh + 1) * 512]) for dhc in range(8)],
                       reads=[oT] + Wob)
                    banks.append(bank)
                z = zs[zi % 2]
                zi += 1
                residual_ln(K, banks, xt, xt.ap[:, s, :], 1.0 / ALPHA, g_b, b_b, z, "lna", (stats, mv, rstd))
                K.pool.dma(z, [(Xout[t0 + s * 128:t0 + (s + 1) * 128, :], z.ap[:])], reads=[z])
        K.end_phase()


import os
GLA_STOP = os.environ.get('GLA_STOP', '')


def gla_phase(K, cfg, Xin, Xout, OF, W, j, lng, lnb):
    L, NSEG, SEG = cfg["L"], cfg["NSEG"], cfg["SEG"]
    NT = L // 128
    TPS = SEG // 128
    cst = K.cst
    with ExitStack() as ps_:
        K.begin_phase(ps_)
        K.set_rot([4, 5, 6, 7])
        OB = [K.banks[i] for i in range(4)]
        stage = [K.sb("wst%d" % i, [128, 1024], F32) for i in range(2)]
        Win, Winb = load_weight_bf16(K, W["gla_w_in"][j], D, 3072, "Win", stage, col_group=1024)
        Wout, Woutb = load_weight_bf16(K, W["gla_w_out"][j], D, D, "Wout", stage, col_group=1024)
        Wd = []
        for d in range(2):
            Wd.append(load_weight_bf16(K, W["gla_w_gate_down"][j, d], D, 16, "Wd%d" % d, stage, col_group=16))
        Wu = K.sb("Wu", [32, 2, 512], F32)
        K.dve.memset(Wu.ap[:], 0.0, writes=[Wu])
        K.sp.dma(Wu, [(Wu.ap[0:16, d, :], W["gla_w_gate_up"][j, d]) for d in range(2)] +
                 [(Wu.ap[16:17, d, :], W["gla_b_gate"][j, d:d + 1, :]) for d in range(2)], writes=[Wu])
        uext = K.sb("uext", [32, 128], F32)
        K.dve.memset(uext.ap[:], 1.0, writes=[uext])
        g_b = load_bcast(K, lng, D, "lng")
        b_b = load_bcast(K, lnb, D, "lnb")
        nw_b = load_bcast(K, W["gla_norm_w"][j], 256, "nw")
        ident, identb = K.ident, K.identb
        xts = [K.sb("xt%d" % i, [128, D], F32) for i in range(2)]
        xTs = [K.sb("xT%d" % i, [128, 8, 128], BF16) for i in range(2)]
        q_sb = K.sb("q_sb", [128, 512], F32)
        k_sb = K.sb("k_sb", [128, 512], F32)
        v_bf = K.sb("v_bf", [128, 1024], BF16)
        sr_bf = K.sb("sr_bf", [128, 1024], BF16)
        sp = K.sb("sp", [128, 512], F32)
        ets = [K.sb("et%d" % i, [128, 512], F32) for i in range(2)]
        qt_bf = K.sb("qt_bf", [128, 512], BF16)
        kt_bf = K.sb("kt_bf", [128, 512], BF16)
        qh_bf = K.sb("qh_bf", [128, 512], BF16)
        kh_bf = K.sb("kh_bf", [128, 512], BF16)
        qtT = K.sb("qtT", [128, 4, 128], BF16)
        ktT = K.sb("ktT", [128, 4, 128], BF16)
        qhA = K.sb("qhA", [128, 4, 128], BF16)
        qhB = K.sb("qhB", [128, 4, 128], BF16)
        K.dve.memset(qhA.ap[:], 0.0, writes=[qhA])
        K.dve.memset(qhB.ap[:], 0.0, writes=[qhB])
        scm = K.sb("scm", [128, 4, 128], BF16)
        S = K.sb("S", [128, 4, 256], F32)
        S_bf = K.sb("S_bf", [128, 4, 256], BF16)
        dec = K.sb("dec", [128, 8], F32)
        o_sb = K.sb("o_sb", [128, 1024], F32)
        of_sb = K.sb("of_sb", [128, 1024], F32)
        junk = K.sb("junk", [128, 256], F32)
        ss = K.sb("ss", [128, 4], F32)
        og_bf = K.sb("og_bf", [128, 1024], BF16)
        ogT = K.sb("ogT", [128, 8, 128], BF16)
        zs = [K.sb("z%d" % i, [128, D], F32) for i in range(2)]
        stats = K.sb("stats", [128, 2, 6], F32)
        mv = K.sb("mv", [128, 2], F32)
        rstd = K.sb("rstd", [128, 1], F32)
        eps_l2 = K.eps_tiles["l2"]
        one_t = K.eps_tiles["one"]
        zi = 0
        it = 0
        for dr in range(2):
            K.dve.memset(S.ap[:], 0.0, writes=[S])
            K.dve.memset(S_bf.ap[:], 0.0, writes=[S_bf])
            order = list(range(NT)) if dr == 0 else list(range(NT - 1, -1, -1))
            dn = "f" if dr == 0 else "b"
            for oi, ti in enumerate(order):
                t0 = ti * 128
                xt = xts[it % 2]
                xT = xTs[it % 2]
                it += 1
                if oi > 0 and (oi % TPS) == 0:
                    K.dve.ts(S.ap[:], S.ap[:], K.carry.ap[:, 0:1], None, ALU.mult, reads=[S, K.carry], writes=[S])
                    K.act.cp(S_bf.ap[:], S.ap[:], reads=[S], writes=[S_bf])
                K.sp.dma(xt, [(xt.ap[:], Xin[t0:t0 + 128, :])], writes=[xt])
                if dr == 1:
                    K.sp.dma(of_sb, [(of_sb.ap[:], OF[t0:t0 + 128, 0:1024])], reads=[K.ofbuf[ti]], writes=[of_sb])
                transpose_x(K, xt.ap[:], xt, xT.ap, xT, ident.ap, ident, col0=0)
                ngrp = 6 if dr == 1 else 4
                for gi in range(ngrp):
                    bank = K.ps()
                    mm(K, bank, bank.ap[:, 0:512], [(xT.ap[:, kc, :], Win[:, kc, gi * 512:(gi + 1) * 512]) for kc in range(8)],
                       reads=[xT] + Winb)
                    if gi == 0:
                        K.act.cp(q_sb.ap[:], bank.ap[:, 0:512], scale=128.0 ** -0.5, reads=[bank], writes=[q_sb])
                    elif gi == 1:
                        K.dve.cp(k_sb.ap[:], bank.ap[:, 0:512], reads=[bank], writes=[k_sb])
                    elif gi < 4:
                        q = K.ev()
                        q.cp(v_bf.ap[:, (gi - 2) * 512:(gi - 1) * 512], bank.ap[:, 0:512], reads=[bank], writes=[v_bf])
                    else:
                        K.act.actv(sr_bf.ap[:, (gi - 4) * 512:(gi - 3) * 512], bank.ap[:, 0:512], AF.Silu, reads=[bank], writes=[sr_bf])
                if GLA_STOP == "A":
                    continue
                bank = K.ps()
                mm(K, bank, bank.ap[0:16, 0:128], [(Wd[dr][0][:, kc, :], xT.ap[:, kc, :]) for kc in range(8)], reads=[xT] + Wd[dr][1])
                K.dve.cp(uext.ap[0:16, :], bank.ap[0:16, 0:128], reads=[bank], writes=[uext])
                bank = K.ps()
                mm(K, bank, bank.ap[:, 0:512], [(uext.ap[:], Wu.ap[:, dr, :])], reads=[uext, Wu])
                et = ets[0]
                K.act.actv(et.ap[:], bank.ap[:, 0:512], AF.Exp, scale=-1.0, reads=[bank], writes=[et])
                K.act.actv(sp.ap[:], et.ap[:], AF.Ln, bias=one_t.ap[:], scale=1.0, reads=[et, one_t], writes=[sp])
                if GLA_STOP == "B":
                    continue
                specs = [("M1" + dn, q_sb, qt_bf, 1.0), ("M1" + dn, k_sb, kt_bf, -1.0), ("M3" + dn, q_sb, qh_bf, 1.0), ("M4" + dn, k_sb, kh_bf, 1.0)]
                bank1 = None
                for si, (mname, src, dst, sc) in enumerate(specs):
                    if si != 1:
                        bank = K.ps()
                        mm(K, bank, bank.ap[:, 0:512], [(cst(mname), sp.ap[:])], reads=[K.consts, sp])
                        if si == 0:
                            bank1 = bank
                    else:
                        bank = bank1
                    et = ets[si % 2]
                    K.act.actv(et.ap[:], bank.ap[:, 0:512], AF.Exp, scale=sc, reads=[bank], writes=[et])
                    eng = K.dve if si % 2 == 0 else K.pool
                    eng.tt(dst.ap[:], src.ap[:], et.ap[:], ALU.mult, reads=[src, et], writes=[dst])
                if GLA_STOP == "C":
                    continue
                bank = K.ps()
                for h in range(4):
                    mm(K, bank, bank.ap[:, 2 * h:2 * h + 2], [(sp.ap[:, h * 128:(h + 1) * 128], cst("IND"))], reads=[sp, K.consts])
                K.act.actv(dec.ap[:], bank.ap[:, 0:8], AF.Exp, reads=[bank], writes=[dec])
                if GLA_STOP == "D":
                    continue
                bank = K.ps()
                bb = bank_bf(bank)
                for h in range(4):
                    tr(K, bank, bb[:, h * 128:(h + 1) * 128], qt_bf.ap[:, h * 128:(h + 1) * 128], identb.ap[:], reads=[qt_bf, identb], signal=False)
                for h in range(4):
                    tr(K, bank, bb[:, (4 + h) * 128:(5 + h) * 128], kt_bf.ap[:, h * 128:(h + 1) * 128], identb.ap[:], reads=[kt_bf, identb],
                       signal=(h == 3))
                K.act.cp(qtT.ap[:], bb[:, 0:512].rearrange("p (h t) -> p h t", h=4), reads=[bank], writes=[qtT])
                K.act.cp(ktT.ap[:], bb[:, 512:1024].rearrange("p (h t) -> p h t", h=4), reads=[bank], writes=[ktT])
                if GLA_STOP == "D1":
                    continue
                bank = K.ps()
                bb = bank_bf(bank)
                for h in range(4):
                    tr(K, bank, bb[:, h * 128:(h + 1) * 128], qh_bf.ap[:, h * 128:(h + 1) * 128], identb.ap[:], reads=[qh_bf, identb],
                       signal=(h == 3))
                bv = bb[:, 0:512].rearrange("p (h t) -> p h t", h=4)
                K.dve.cp(qhA.ap[:, :, 0:64], bv[:, :, 0:64], reads=[bank], writes=[qhA])
                K.dve.cp(qhB.ap[:, :, 64:128], bv[:, :, 64:128], reads=[bank], writes=[qhB])
                if GLA_STOP == "E":
                    continue
                bank = K.ps()
                for h in range(4):
                    mm(K, bank, bank.ap[:, h * 128:(h + 1) * 128], [(ktT.ap[:, h, :], qtT.ap[:, h, :])], reads=[ktT, qtT])
                K.dve.tt(scm.ap[:].rearrange("p h t -> p (h t)"), bank.ap[:, 0:512], cst("MASK4" + dn), ALU.mult,
                         reads=[bank, K.consts], writes=[scm])
                if GLA_STOP == "F":
                    continue
                for h in range(4):
                    ob = OB[h]
                    mm(K, ob, ob.ap[:, 0:256], [(scm.ap[:, h, :], v_bf.ap[:, h * 256:(h + 1) * 256])],
                       reads=[scm, v_bf], start=True, stop=False)
                chunks = [0, 1] if dr == 0 else [1, 0]
                for ci, ck in enumerate(chunks):
                    qh = qhA if ck == 0 else qhB
                    for h in range(4):
                        ob = OB[h]
                        mm(K, ob, ob.ap[:, 0:256], [(qh.ap[:, h, :], S_bf.ap[:, h, :])],
                           reads=[qh, S_bf], start=False, stop=(ci == 1))
                    r0 = ck * 64
                    kvb = []
                    for hp in range(2):
                        bank = K.ps()
                        for hh in range(2):
                            h = hp * 2 + hh
                            mm(K, bank, bank.ap[:, hh * 256:(hh + 1) * 256],
                               [(kh_bf.ap[r0:r0 + 64, h * 128:(h + 1) * 128], v_bf.ap[r0:r0 + 64, h * 256:(h + 1) * 256])], reads=[kh_bf, v_bf])
                        kvb.append(bank)
                    for h in range(4):
                        bank = kvb[h // 2]
                        K.dve.stt(S.ap[:, h, :], S.ap[:, h, :], dec.ap[:, 2 * h + ck:2 * h + ck + 1], bank.ap[:, (h % 2) * 256:(h % 2 + 1) * 256],
                                  ALU.mult, ALU.add, reads=[S, dec, bank], writes=[S])
                    K.act.cp(S_bf.ap[:], S.ap[:], reads=[S], writes=[S_bf])
                if dr == 0:
                    for h in range(4):
                        q = K.ev()
                        q.cp(o_sb.ap[:, h * 256:(h + 1) * 256], OB[h].ap[:, 0:256], reads=[OB[h]], writes=[o_sb])
                    K.pool.dma(o_sb, [(OF[t0:t0 + 128, 0:1024], o_sb.ap[:])], reads=[o_sb], writes=[K.ofbuf[ti]])
                else:
                    for h in range(4):
                        K.dve.tt(o_sb.ap[:, h * 256:(h + 1) * 256], OB[h].ap[:, 0:256], of_sb.ap[:, h * 256:(h + 1) * 256], ALU.add,
                                 reads=[OB[h], of_sb], writes=[o_sb])
                    if GLA_STOP == "O":
                        K.pool.dma(o_sb, [(Xout[t0:t0 + 128, :], o_sb.ap[:])], reads=[o_sb])
                        continue
                    for h in range(4):
                        K.act.actv(junk.ap[:], o_sb.ap[:, h * 256:(h + 1) * 256], AF.Square, accum=ss.ap[:, h:h + 1], reads=[o_sb], writes=[junk, ss])
                    K.act.actv(ss.ap[:], ss.ap[:], AF.Sqrt, bias=eps_l2.ap[:], scale=1.0 / 256.0, reads=[ss, eps_l2], writes=[ss])
                    K.dve.recip(ss.ap[:], ss.ap[:], reads=[ss], writes=[ss])
                    for h in range(4):
                        K.dve.stt(o_sb.ap[:, h * 256:(h + 1) * 256], o_sb.ap[:, h * 256:(h + 1) * 256], ss.ap[:, h:h + 1], nw_b.ap[:],
                                  ALU.mult, ALU.mult, reads=[o_sb, ss, nw_b], writes=[o_sb])
                    K.pool.tt(og_bf.ap[:], o_sb.ap[:], sr_bf.ap[:], ALU.mult, reads=[o_sb, sr_bf], writes=[og_bf])
                    bank = K.ps()
                    bb = bank_bf(bank)
                    for c in range(8):
                        tr(K, bank, bb[:, c * 128:(c + 1) * 128], og_bf.ap[:, c * 128:(c + 1) * 128], identb.ap[:], reads=[og_bf, identb],
                           signal=(c == 7))
                    K.act.cp(ogT.ap[:], bb.rearrange("p (c t) -> p c t", c=8), reads=[bank], writes=[ogT])
                    banks = []
                    for fh in range(2):
                        bank = K.ps()
                        mm(K, bank, bank.ap[:, 0:512], [(ogT.ap[:, c, :], Wout[:, c, fh * 512:(fh + 1) * 512]) for c in range(8)],
                           reads=[ogT] + Woutb)
                        banks.append(bank)
                    z = zs[zi % 2]
                    zi += 1
                    residual_ln(K, banks, xt, xt.ap[:], 1.0 / ALPHA, g_b, b_b, z, "lna", (stats, mv, rstd))
                    K.pool.dma(z, [(Xout[t0:t0 + 128, :], z.ap[:])], reads=[z])
        K.set_rot(list(range(8)))
        K.end_phase()


def gdn_scan_pass(K, cfg, Xin, OF, QKV, W, j, dr):
    L, NSEG, SEG = cfg["L"], cfg["NSEG"], cfg["SEG"]
    NT = L // 128
    TPS = SEG // 128
    cst = K.cst
    dn = "f" if dr == 0 else "b"
    with ExitStack() as ps_:
        K.begin_phase(ps_)
        K.set_rot(list(range(8)))
        stage_big = K.sb("wst", [128, 2048], F32)
        stage = [Buf("wst0", stage_big.ap[:, 0:1024]), Buf("wst1", stage_big.ap[:, 1024:2048])]
        K.phase_bufs.extend(stage)
        if dr == 0:
            Win, Winb = load_weight_bf16(K, W["gdn_w_in"][j][:, 0:4096], D, 4096, "Win", stage, col_group=1024)
        Wab, Wabb = load_weight_bf16(K, W["gdn_w_ab"][j, dr], D, 32, "Wab", stage, col_group=32)
        ident, identb = K.ident, K.identb
        cwT = K.sb("cwT", [128, 128], F32)
        cw = K.sb("cw", [128, 128], F32)
        K.sp.dma(cwT, [(cwT.ap[:], W["gdn_conv_w"][j].rearrange("t (c p) -> (t c) p", p=128))], writes=[cwT])
        bank = K.ps()
        tr(K, bank, bank.ap[:, 0:128], cwT.ap[:], ident.ap[:], reads=[cwT, ident])
        K.dve.cp(cw.ap[:], bank.ap[:, 0:128], reads=[bank], writes=[cw])
        alog = load_bcast(K, W["gdn_a_log"][j].rearrange("a b -> (a b)"), 32, "alog")
        dtb = load_bcast(K, W["gdn_dt_bias"][j].rearrange("a b -> (a b)"), 32, "dtb")
        negA = K.sb("negA", [128, 32], F32)
        K.act.actv(negA.ap[:], alog.ap[:], AF.Exp, reads=[alog], writes=[negA])
        K.dve.ts(negA.ap[:], negA.ap[:], -1.0, None, ALU.mult, reads=[negA], writes=[negA])
        ones_bf = K.sb("ones_bf", [128, 128], BF16)
        K.dve.cp(ones_bf.ap[:], cst("ONES"), reads=[K.consts], writes=[ones_bf])
        xts = [K.sb("xt0", [128, D], F32)] * 2
        xTs = [K.sb("xTe0", [128, 8, 132], BF16)] * 2
        xhp = K.sb("xhp", [2, D], F32)
        xhn = K.sb("xhn", [1, D], F32)
        accs = [K.sb("acc%d" % i, [128, 128], F32) for i in range(2)] * 2
        s4q = K.sb("s4q", [128, 8, 128], F32) if dr == 0 else None
        s4k = K.sb("s4k", [128, 8, 128], F32) if dr == 0 else None
        sq4s = [K.sb("sq4_%d" % i, [128, 4, 128], BF16) for i in range(2)]
        rn4s = [K.sb("rn4_%d" % i, [128, 512], F32) for i in range(2)]
        nqb = 1 if dr == 0 else 2
        qTns = [K.sb("qTn%d" % i, [128, 8, 128], BF16) for i in range(nqb)]
        kTns = [K.sb("kTn%d" % i, [128, 8, 128], BF16) for i in range(nqb)]
        v_toks = [K.sb("v_tok%d" % i, [128, 16, 128], BF16) for i in range(nqb)]
        vT = K.sb("vT", [128, 16, 128], BF16) if dr == 0 else None
        kend = K.sb("kend", [128, 16, 128], BF16)
        t16 = K.sb("t16", [128, 16], F32)
        e16 = K.sb("e16", [128, 16], F32)
        g16 = K.sb("g16", [128, 16], F32)
        beta16 = K.sb("beta16", [128, 16], F32)
        egk = K.sb("egk", [128, 64], F32)
        negeg = K.sb("negeg", [128, 16], F32)
        Pm = [K.sb("Pm%d" % i, [128, 128], F32) for i in range(2)]
        decT4s = [K.sb("decT4_%d" % i, [128, 4, 128], F32) for i in range(2)]
        decTs4s = [K.sb("decTs4_%d" % i, [128, 4, 128], F32) for i in range(2)]
        Ms = [K.sb("M%d" % i, [128, 16, 128], BF16) for i in range(2)]
        MTs = [K.sb("MT%d" % i, [128, 16, 128], BF16) for i in range(2)]
        Qf = Buf("Qf", stage_big.ap[:].rearrange("p (a b) -> p a b", a=16))
        K.phase_bufs.append(Qf)
        Qb = K.sb("Qb", [128, 16, 128], BF16)
        T2T = K.sb("T2T", [128, 16, 128], BF16)
        attnT = K.sb("attnT", [128, 16, 128], BF16)
        resid1 = K.sb("resid", [128, 16, 128], BF16)
        vn1 = K.sb("vn", [128, 16, 128], BF16)
        for b in (resid1, vn1):
            K.dve.memset(b.ap[:], 0.0, writes=[b])
        S = K.sb("S", [128, 16, 128], F32)
        S_bf = K.sb("S_bf", [128, 16, 128], BF16)
        K.dve.memset(S.ap[:], 0.0, writes=[S])
        K.dve.memset(S_bf.ap[:], 0.0, writes=[S_bf])
        o_sb = K.sb("o_sb", [128, 16, 128], F32)
        of_sb = K.sb("of_sb", [128, 2048], F32) if dr == 1 else None
        eps_l2 = K.eps_tiles["l2"]
        one_t = K.eps_tiles["one"]
        order = list(range(NT)) if dr == 0 else list(range(NT - 1, -1, -1))
        for oi, ti in enumerate(order):
            t0 = ti * 128
            xt = xts[oi % 2]
            xT = xTs[oi % 2]
            qTn, kTn, v_tok = qTns[oi % nqb], kTns[oi % nqb], v_toks[oi % nqb]
            if oi > 0 and (oi % TPS) == 0:
                K.dve.ts(S.ap[:], S.ap[:], K.carry.ap[:, 0:1], None, ALU.mult, reads=[S, K.carry], writes=[S])
                K.act.cp(S_bf.ap[:], S.ap[:], reads=[S], writes=[S_bf])
            K.sp.dma(xt, [(xt.ap[:], Xin[t0:t0 + 128, :])], writes=[xt])
            if dr == 0:
                if t0 == 0:
                    K.dve.memset(xhp.ap[:], 0.0, writes=[xhp])
                else:
                    K.sp.dma(xhp, [(xhp.ap[:], Xin[t0 - 2:t0, :])], writes=[xhp])
                    if t0 % SEG == 0:
                        K.dve.ts(xhp.ap[:], xhp.ap[:], K.carry.ap[0:2, 0:1], None, ALU.mult, reads=[xhp, K.carry], writes=[xhp])
                if t0 + 128 == L:
                    K.dve.memset(xhn.ap[:], 0.0, writes=[xhn])
                else:
                    K.sp.dma(xhn, [(xhn.ap[:], Xin[t0 + 128:t0 + 129, :])], writes=[xhn])
                    if (t0 + 128) % SEG == 0:
                        K.dve.ts(xhn.ap[:], xhn.ap[:], K.carry.ap[0:1, 0:1], None, ALU.mult, reads=[xhn, K.carry], writes=[xhn])
            if dr == 1:
                K.sp.dma(of_sb, [(of_sb.ap[:], OF[t0:t0 + 128, :])], reads=[K.ofbuf[ti]], writes=[of_sb])
                K.sp.dma(qTn, [(qTn.ap[:].rearrange("p a b -> p (a b)"), QKV[t0:t0 + 128, 0:1024])], reads=[K.qkvbuf[ti]], writes=[qTn])
                K.sp.dma(kTn, [(kTn.ap[:].rearrange("p a b -> p (a b)"), QKV[t0:t0 + 128, 1024:2048])], reads=[K.qkvbuf[ti]], writes=[kTn])
                K.sp.dma(v_tok, [(v_tok.ap[:].rearrange("p a b -> p (a b)"), QKV[t0:t0 + 128, 2048:4096])], reads=[K.qkvbuf[ti]], writes=[v_tok])
            transpose_x(K, xt.ap[:], xt, xT.ap, xT, ident.ap, ident, col0=2)
            if dr == 0:
                bank = K.ps()
                for kc in range(8):
                    tr(K, bank, bank.ap[:, kc * 4:kc * 4 + 2], xhp.ap[0:2, kc * 128:(kc + 1) * 128], ident.ap[0:2, 0:2], reads=[xhp, ident], signal=False)
                    tr(K, bank, bank.ap[:, kc * 4 + 2:kc * 4 + 3], xhn.ap[0:1, kc * 128:(kc + 1) * 128], ident.ap[0:1, 0:1], reads=[xhn, ident],
                       signal=(kc == 7))
                hv_ = bank.ap[:, 0:32].rearrange("p (k c) -> p k c", c=4)
                K.act.cp(xT.ap[:, :, 0:2], hv_[:, :, 0:2], reads=[bank], writes=[xT])
                K.act.cp(xT.ap[:, :, 130:131], hv_[:, :, 2:3], reads=[bank], writes=[xT])
                for c3 in range(0, 32, 3):
                    bank = K.ps()
                    cs = list(range(c3, min(32, c3 + 3)))
                    for i, c in enumerate(cs):
                        mm(K, bank, bank.ap[:, i * 132:i * 132 + 131], [(Win[:, kc, c * 128:(c + 1) * 128], xT.ap[:, kc, 0:131]) for kc in range(8)],
                           reads=[xT] + Winb)
                    for i, c in enumerate(cs):
                        acc = accs[c % 4]
                        eng = K.dve
                        src = bank.ap[:, i * 132:i * 132 + 131]
                        sbuf_ = bank
                        eng.ts(acc.ap[:], src[:, 0:128], cw.ap[:, c:c + 1], None, ALU.mult, reads=[sbuf_, cw], writes=[acc])
                        for tp in range(1, 4):
                            eng.stt(acc.ap[:], src[:, tp:tp + 128], cw.ap[:, tp * 32 + c:tp * 32 + c + 1], acc.ap[:], ALU.mult, ALU.add,
                                    reads=[sbuf_, cw, acc], writes=[acc])
                        if c < 8:
                            K.act.actv(s4q.ap[:, c, :], acc.ap[:], AF.Silu, reads=[acc], writes=[s4q])
                        elif c < 16:
                            K.act.actv(s4k.ap[:, c - 8, :], acc.ap[:], AF.Silu, reads=[acc], writes=[s4k])
                        else:
                            K.act.actv(vT.ap[:, c - 16, :], acc.ap[:], AF.Silu, reads=[acc], writes=[vT])
                for (src4, dstn, scl) in ((s4q, qTn, 128.0 ** -0.5), (s4k, kTn, None)):
                    for g in range(2):
                        sv = src4.ap[:, g * 4:(g + 1) * 4, :]
                        sq4, rn4 = sq4s[g], rn4s[g]
                        K.pool.tt(sq4.ap[:], sv, sv, ALU.mult, reads=[src4], writes=[sq4])
                        bank = K.ps()
                        for i in range(4):
                            mm(K, bank, bank.ap[:, i * 128:(i + 1) * 128], [(ones_bf.ap[:], sq4.ap[:, i, :])], reads=[ones_bf, sq4])
                        K.act.actv(rn4.ap[:], bank.ap[:, 0:512], AF.Ln, bias=eps_l2.ap[:], scale=1.0, reads=[bank, eps_l2], writes=[rn4])
                        K.act.actv(rn4.ap[:], rn4.ap[:], AF.Exp, scale=-0.5, reads=[rn4], writes=[rn4])
                        dv_ = dstn.ap[:, g * 4:(g + 1) * 4, :].rearrange("p a b -> p (a b)")
                        sv2 = sv.rearrange("p a b -> p (a b)")
                        if scl is None:
                            K.dve.tt(dv_, sv2, rn4.ap[:], ALU.mult, reads=[src4, rn4], writes=[dstn])
                        else:
                            K.dve.stt(dv_, sv2, float(scl), rn4.ap[:], ALU.mult, ALU.mult, reads=[src4, rn4], writes=[dstn])
            bank = K.ps()
            mm(K, bank, bank.ap[:, 0:32], [(xT.ap[:, kc, 2:130], Wab[:, kc, :]) for kc in range(8)], reads=[xT] + Wabb)
            K.dve.tt(t16.ap[:], bank.ap[:, 0:16], dtb.ap[:, dr * 16:(dr + 1) * 16], ALU.add, reads=[bank, dtb], writes=[t16])
            K.act.actv(e16.ap[:], t16.ap[:], AF.Exp, reads=[t16], writes=[e16])
            K.act.actv(e16.ap[:], e16.ap[:], AF.Ln, bias=one_t.ap[:], scale=1.0, reads=[e16, one_t], writes=[e16])
            K.dve.tt(g16.ap[:], e16.ap[:], negA.ap[:, dr * 16:(dr + 1) * 16], ALU.mult, reads=[e16, negA], writes=[g16])
            K.act.actv(beta16.ap[:], bank.ap[:, 16:32], AF.Exp, scale=-1.0, reads=[bank], writes=[beta16])
            K.dve.ts(beta16.ap[:], beta16.ap[:], 1.0, None, ALU.add, reads=[beta16], writes=[beta16])
            K.dve.recip(beta16.ap[:], beta16.ap[:], reads=[beta16], writes=[beta16])
            bank = K.ps()
            for i, mname in enumerate(("TRI" + dn, "AFTER" + dn, "CH0", "CH1")):
                mm(K, bank, bank.ap[:, i * 16:(i + 1) * 16], [(cst(mname), g16.ap[:])], reads=[K.consts, g16])
            K.act.actv(egk.ap[:], bank.ap[:, 0:64], AF.Exp, reads=[bank], writes=[egk])
            K.dve.ts(negeg.ap[:], egk.ap[:, 0:16], -1.0, None, ALU.mult, reads=[egk], writes=[negeg])
            bank = K.ps()
            bb = bank_bf(bank)
            for hk in range(8):
                tr(K, bank, bb[:, hk * 128:(hk + 1) * 128], kTn.ap[:, hk, :], identb.ap[:], reads=[kTn, identb], signal=(hk == 7))
            for hv in range(16):
                q = K.act if (hv % 2) else K.dve
                if q is K.act:
                    q.actv(kend.ap[:, hv, :], bb[:, (hv // 2) * 128:(hv // 2 + 1) * 128], AF.Copy, scale=egk.ap[:, 16 + hv:17 + hv],
                           reads=[bank, egk], writes=[kend])
                else:
                    q.ts(kend.ap[:, hv, :], bb[:, (hv // 2) * 128:(hv // 2 + 1) * 128], egk.ap[:, 16 + hv:17 + hv], None, ALU.mult,
                         reads=[bank, egk], writes=[kend])
            if dr == 0:
                for g in range(2):
                    bank = K.ps()
                    bb = bank_bf(bank)
                    for i in range(8):
                        tr(K, bank, bb[:, i * 128:(i + 1) * 128], vT.ap[:, g * 8 + i, :], identb.ap[:], reads=[vT, identb], signal=(i == 7))
                    q = K.ev()
                    q.cp(v_tok.ap[:, g * 8:(g + 1) * 8, :], bb.rearrange("p (a b) -> p a b", a=8), reads=[bank], writes=[v_tok])
                K.pool.dma(qTn, [(QKV[t0:t0 + 128, 0:1024], qTn.ap[:].rearrange("p a b -> p (a b)"))], reads=[qTn], writes=[K.qkvbuf[ti]])
                K.pool.dma(kTn, [(QKV[t0:t0 + 128, 1024:2048], kTn.ap[:].rearrange("p a b -> p (a b)"))], reads=[kTn], writes=[K.qkvbuf[ti]])
                K.pool.dma(v_tok, [(QKV[t0:t0 + 128, 2048:4096], v_tok.ap[:].rearrange("p a b -> p (a b)"))], reads=[v_tok], writes=[K.qkvbuf[ti]])
            M, MT = Ms[0], MTs[0]
            for gi in range(4):
                hv0 = gi * 4
                decT4, decTs4 = decT4s[gi % 2], decTs4s[gi % 2]
                gbank = K.ps()
                for i in range(2):
                    hk = 2 * gi + i
                    mm(K, gbank, gbank.ap[:, i * 128:(i + 1) * 128], [(kTn.ap[:, hk, :], kTn.ap[:, hk, :])], reads=[kTn])
                    mm(K, gbank, gbank.ap[:, (2 + i) * 128:(3 + i) * 128], [(kTn.ap[:, hk, :], qTn.ap[:, hk, :])], reads=[kTn, qTn])
                dbank = K.ps()
                for i in range(4):
                    hv = hv0 + i
                    P = Pm[i % 2]
                    K.dve.ts(P.ap[:], cst("TRI" + dn), g16.ap[:, hv:hv + 1], None, ALU.mult, reads=[K.consts, g16], writes=[P])
                    mm(K, dbank, dbank.ap[:, i * 128:(i + 1) * 128],
                       [(cst("ONES"), P.ap[:]), (P.ap[:], cst("NEGONES")), (cst("IDENT"), cst("NEG" + dn))], reads=[P, K.consts])
                K.act.actv(decT4.ap[:].rearrange("p a b -> p (a b)"), dbank.ap[:, 0:512], AF.Exp, reads=[dbank], writes=[decT4])
                K.pool.tt(decTs4.ap[:].rearrange("p a b -> p (a b)"), decT4.ap[:].rearrange("p a b -> p (a b)"), cst("OFFD4"), ALU.mult,
                          reads=[decT4, K.consts], writes=[decTs4])
                for i in range(4):
                    hv = hv0 + i
                    K.dve.stt(M.ap[:, hv, :], gbank.ap[:, (i // 2) * 128:(i // 2 + 1) * 128], beta16.ap[:, hv:hv + 1], decTs4.ap[:, i, :],
                              ALU.mult, ALU.mult, reads=[gbank, beta16, decTs4], writes=[M])
                    K.dve.tt(attnT.ap[:, hv, :], gbank.ap[:, (2 + i // 2) * 128:(3 + i // 2) * 128], decT4.ap[:, i, :], ALU.mult,
                             reads=[gbank, decT4], writes=[attnT])
                bank = K.ps()
                bb = bank_bf(bank)
                for i in range(4):
                    tr(K, bank, bb[:, i * 128:(i + 1) * 128], M.ap[:, hv0 + i, :], identb.ap[:], reads=[M, identb], signal=(i == 3))
                K.act.cp(MT.ap[:, hv0:hv0 + 4, :].rearrange("p a b -> p (a b)"), bb[:, 0:512], reads=[bank], writes=[MT])
                K.pool.tt(Qf.ap[:, hv0:hv0 + 4, :].rearrange("p a b -> p (a b)"), cst("IDENT4"),
                          M.ap[:, hv0:hv0 + 4, :].rearrange("p a b -> p (a b)"), ALU.subtract, reads=[K.consts, M], writes=[Qf])
                K.act.cp(Qb.ap[:, hv0:hv0 + 4, :], Qf.ap[:, hv0:hv0 + 4, :], reads=[Qf], writes=[Qb])
            for lev in range(1, 6):
                Mn, MTn = Ms[lev % 2], MTs[lev % 2]
                if lev < 5:
                    for gi in range(4):
                        hv0 = gi * 4
                        bank = K.ps()
                        for i in range(4):
                            mm(K, bank, bank.ap[:, i * 128:(i + 1) * 128], [(MT.ap[:, hv0 + i, :], M.ap[:, hv0 + i, :])], reads=[MT, M])
                        K.act.cp(Mn.ap[:, hv0:hv0 + 4, :].rearrange("p a b -> p (a b)"), bank.ap[:, 0:512], reads=[bank], writes=[Mn])
                for gi in range(4):
                    hv0 = gi * 4
                    bank = K.ps()
                    for i in range(4):
                        mm(K, bank, bank.ap[:, i * 128:(i + 1) * 128], [(M.ap[:, hv0 + i, :], MT.ap[:, hv0 + i, :])], reads=[MT, M])
                    q = K.act if (gi % 2) else K.dve
                    q.cp(MTn.ap[:, hv0:hv0 + 4, :].rearrange("p a b -> p (a b)"), bank.ap[:, 0:512], reads=[bank], writes=[MTn])
                for gi in range(4):
                    hv0 = gi * 4
                    bank = K.ps()
                    for i in range(4):
                        mm(K, bank, bank.ap[:, i * 128:(i + 1) * 128], [(MTn.ap[:, hv0 + i, :], Qb.ap[:, hv0 + i, :])], reads=[MTn, Qb])
                    qv = Qf.ap[:, hv0:hv0 + 4, :].rearrange("p a b -> p (a b)")
                    K.dve.tt(qv, qv, bank.ap[:, 0:512], ALU.add, reads=[Qf, bank], writes=[Qf])
                    if lev < 5:
                        K.act.cp(Qb.ap[:, hv0:hv0 + 4, :], Qf.ap[:, hv0:hv0 + 4, :], reads=[Qf], writes=[Qb])
                    else:
                        K.act.cp(T2T.ap[:, hv0:hv0 + 4, :], Qf.ap[:, hv0:hv0 + 4, :], reads=[Qf], writes=[T2T])
                M, MT = Mn, MTn
            chunks = [0, 1] if dr == 0 else [1, 0]
            for ck in chunks:
                r0, r1 = ck * 64, ck * 64 + 64
                z0, z1 = (64, 128) if ck == 0 else (0, 64)
                rs, vnb = resid1, vn1
                K.pool.memset(rs.ap[z0:z1, :, :], 0.0, writes=[rs])
                K.pool.memset(vnb.ap[z0:z1, :, :], 0.0, writes=[vnb])
                kbs, qbs, nbs, abs_, vbs = [], [], [], [], []
                for gi in range(4):
                    kb = K.ps()
                    for i in range(2):
                        hk = 2 * gi + i
                        mm(K, kb, kb.ap[:, i * 256:(i + 1) * 256], [(kTn.ap[:, hk, :], S_bf.ap[:, 2 * hk:2 * hk + 2, :])], reads=[kTn, S_bf])
                    kbs.append(kb)
                for gi in range(4):
                    qb_ = K.ps()
                    for i in range(2):
                        hk = 2 * gi + i
                        mm(K, qb_, qb_.ap[:, i * 256:(i + 1) * 256], [(qTn.ap[:, hk, :], S_bf.ap[:, 2 * hk:2 * hk + 2, :])], reads=[qTn, S_bf])
                    qbs.append(qb_)
                for gi in range(4):
                    for i in range(4):
                        hv = gi * 4 + i
                        K.dve.stt(rs.ap[r0:r1, hv, :], kbs[gi].ap[r0:r1, i * 128:(i + 1) * 128], negeg.ap[r0:r1, hv:hv + 1], v_tok.ap[r0:r1, hv, :],
                                  ALU.mult, ALU.add, reads=[kbs[gi], negeg, v_tok], writes=[rs])
                for gi in range(4):
                    for i in range(4):
                        hv = gi * 4 + i
                        K.act.actv(o_sb.ap[r0:r1, hv, :], qbs[gi].ap[r0:r1, i * 128:(i + 1) * 128], AF.Copy, scale=egk.ap[r0:r1, hv:hv + 1],
                                   reads=[qbs[gi], egk], writes=[o_sb])
                for gi in range(4):
                    nb = K.ps()
                    for i in range(4):
                        hv = gi * 4 + i
                        mm(K, nb, nb.ap[:, i * 128:(i + 1) * 128], [(T2T.ap[:, hv, :], rs.ap[:, hv, :])], reads=[T2T, rs])
                    nbs.append(nb)
                for gi in range(4):
                    for i in range(4):
                        hv = gi * 4 + i
                        K.act.actv(vnb.ap[r0:r1, hv, :], nbs[gi].ap[r0:r1, i * 128:(i + 1) * 128], AF.Copy, scale=beta16.ap[r0:r1, hv:hv + 1],
                                   reads=[nbs[gi], beta16], writes=[vnb])
                for gi in range(4):
                    ab_ = K.ps()
                    for i in range(4):
                        hv = gi * 4 + i
                        mm(K, ab_, ab_.ap[:, i * 128:(i + 1) * 128], [(attnT.ap[:, hv, :], vnb.ap[:, hv, :])], reads=[attnT, vnb])
                    abs_.append(ab_)
                for gi in range(4):
                    vb_ = K.ps()
                    for i in range(4):
                        hv = gi * 4 + i
                        mm(K, vb_, vb_.ap[:, i * 128:(i + 1) * 128], [(kend.ap[:, hv, :], vnb.ap[:, hv, :])], reads=[kend, vnb])
                    vbs.append(vb_)
                for gi in range(4):
                    for i in range(4):
                        hv = gi * 4 + i
                        K.dve.stt(S.ap[:, hv, :], S.ap[:, hv, :], egk.ap[:, 32 + ck * 16 + hv:33 + ck * 16 + hv], vbs[gi].ap[:, i * 128:(i + 1) * 128],
                                  ALU.mult, ALU.add, reads=[S, egk, vbs[gi]], writes=[S])
                    K.pool.cp(S_bf.ap[:, gi * 4:gi * 4 + 4, :], S.ap[:, gi * 4:gi * 4 + 4, :], reads=[S], writes=[S_bf])
                for gi in range(4):
                    for i in range(4):
                        hv = gi * 4 + i
                        K.dve.tt(o_sb.ap[r0:r1, hv, :], abs_[gi].ap[r0:r1, i * 128:(i + 1) * 128], o_sb.ap[r0:r1, hv, :], ALU.add,
                                 reads=[abs_[gi], o_sb], writes=[o_sb])
            ov = o_sb.ap[:].rearrange("p a b -> p (a b)")
            if dr == 1:
                K.pool.tt(ov, ov, of_sb.ap[:], ALU.add, reads=[o_sb, of_sb], writes=[o_sb])
            K.pool.dma(o_sb, [(OF[t0:t0 + 128, :], ov)], reads=[o_sb], writes=[K.ofbuf[ti]])
        K.end_phase()


def gdn_finish_pass(K, cfg, Xin, Xout, OF, W, j, lng, lnb):
    L = cfg["L"]
    NT = L // 128
    with ExitStack() as ps_:
        K.begin_phase(ps_)
        K.set_rot(list(range(8)))
        stage = [K.sb("wst%d" % i, [128, 1024], F32) for i in range(2)]
        Wz, Wzb = load_weight_bf16(K, W["gdn_w_in"][j][:, 4096:6144], D, 2048, "Wz", stage, col_group=1024)
        Wout, Woutb = load_weight_bf16(K, W["gdn_w_out"][j], 2048, D, "Wout", stage, col_group=1024)
        g_b = load_bcast(K, lng, D, "lng")
        b_b = load_bcast(K, lnb, D, "lnb")
        nw_b = load_bcast(K, W["gdn_norm_w"][j], 128, "nw")
        ident, identb = K.ident, K.identb
        xts = [K.sb("xt%d" % i, [128, D], F32) for i in range(2)]
        xTs = [K.sb("xT%d" % i, [128, 8, 128], BF16) for i in range(2)]
        os_ = [K.sb("o%d" % i, [128, 16, 128], F32) for i in range(2)]
        sqo = K.sb("sqo", [128, 16, 128], F32)
        ss = K.sb("ss", [128, 16], F32)
        sz_bf = K.sb("sz_bf", [128, 2048], BF16)
        og_bf = K.sb("og_bf", [128, 2048], BF16)
        ogT = K.sb("ogT", [128, 16, 128], BF16)
        zs = [K.sb("z%d" % i, [128, D], F32) for i in range(2)]
        stats = K.sb("stats", [128, 2, 6], F32)
        mv = K.sb("mv", [128, 2], F32)
        rstd = K.sb("rstd", [128, 1], F32)
        eps_l2 = K.eps_tiles["l2"]
        for ti in range(NT):
            t0 = ti * 128
            xt, xT, o_sb = xts[ti % 2], xTs[ti % 2], os_[ti % 2]
            ov = o_sb.ap[:].rearrange("p a b -> p (a b)")
            K.sp.dma(xt, [(xt.ap[:], Xin[t0:t0 + 128, :])], writes=[xt])
            K.sp.dma(o_sb, [(ov, OF[t0:t0 + 128, :])], reads=[K.ofbuf[ti]], writes=[o_sb])
            transpose_x(K, xt.ap[:], xt, xT.ap, xT, ident.ap, ident, col0=0)
            for gi in range(4):
                bank = K.ps()
                mm(K, bank, bank.ap[:, 0:512], [(xT.ap[:, kc, :], Wz[:, kc, gi * 512:(gi + 1) * 512]) for kc in range(8)], reads=[xT] + Wzb)
                K.act.actv(sz_bf.ap[:, gi * 512:(gi + 1) * 512], bank.ap[:, 0:512], AF.Silu, reads=[bank], writes=[sz_bf])
            K.pool.tt(sqo.ap[:], o_sb.ap[:], o_sb.ap[:], ALU.mult, reads=[o_sb], writes=[sqo])
            K.dve.op(lambda e, a=ss.ap[:], b=sqo.ap[:]: e.tensor_reduce(a, b, mybir.AxisListType.X, ALU.add), reads=[sqo], writes=[ss])
            K.act.actv(ss.ap[:], ss.ap[:], AF.Sqrt, bias=eps_l2.ap[:], scale=1.0 / 128.0, reads=[ss, eps_l2], writes=[ss])
            K.dve.recip(ss.ap[:], ss.ap[:], reads=[ss], writes=[ss])
            for hv in range(16):
                K.dve.stt(o_sb.ap[:, hv, :], o_sb.ap[:, hv, :], ss.ap[:, hv:hv + 1], nw_b.ap[:], ALU.mult, ALU.mult, reads=[o_sb, ss, nw_b], writes=[o_sb])
            K.pool.tt(og_bf.ap[:], ov, sz_bf.ap[:], ALU.mult, reads=[o_sb, sz_bf], writes=[og_bf])
            for g in range(2):
                bank = K.ps()
                bb = bank_bf(bank)
                for i in range(8):
                    c = g * 8 + i
                    tr(K, bank, bb[:, i * 128:(i + 1) * 128], og_bf.ap[:, c * 128:(c + 1) * 128], identb.ap[:], reads=[og_bf, identb], signal=(i == 7))
                q = K.ev()
                q.cp(ogT.ap[:, g * 8:(g + 1) * 8, :], bb.rearrange("p (a b) -> p a b", a=8), reads=[bank], writes=[ogT])
            banks = []
            for fh in range(2):
                bank = K.ps()
                mm(K, bank, bank.ap[:, 0:512], [(ogT.ap[:, c, :], Wout[:, c, fh * 512:(fh + 1) * 512]) for c in range(16)], reads=[ogT] + Woutb)
                banks.append(bank)
            z = zs[ti % 2]
            residual_ln(K, banks, xt, xt.ap[:], 1.0 / ALPHA, g_b, b_b, z, "lna", (stats, mv, rstd))
            K.pool.dma(z, [(Xout[t0:t0 + 128, :], z.ap[:])], reads=[z])
        K.end_phase()


def gdn_phase(K, cfg, Xin, Xout, OF, W, j, lng, lnb):
    gdn_scan_pass(K, cfg, Xin, OF, K.QKV, W, j, 0)
    gdn_scan_pass(K, cfg, Xin, OF, K.QKV, W, j, 1)
    gdn_finish_pass(K, cfg, Xin, Xout, OF, W, j, lng, lnb)


_PROGRAM_CACHE = {}


def kernel(**inputs):
    xp = np.asarray(inputs["x_prompt"], dtype=np.float32)
    xsm = np.asarray(inputs["x_sample"], dtype=np.float32)
    mp = np.asarray(inputs["mem_prompt"], dtype=np.float32)
    ms = np.asarray(inputs["mem_sample"], dtype=np.float32)
    L, NSEG, SEG = 16384, 4, 4096
    consts = make_consts()
    cfg = {"L": L, "NSEG": NSEG, "SEG": SEG, "NCONST": consts.shape[1]}
    if "nc" not in _PROGRAM_CACHE:
        _PROGRAM_CACHE["nc"] = build_program(cfg)
    nc = _PROGRAM_CACHE["nc"]
    wmap = {name: np.ascontiguousarray(np.asarray(inputs[name], dtype=np.float32)) for name, _ in WEIGHT_SPECS}
    one = np.ones((128, 1), np.float32)
    zero = np.zeros((128, 1), np.float32)
    streams = [
        (np.ascontiguousarray(xp.reshape(L, D)), np.ascontiguousarray(np.broadcast_to(mp, (NSEG, NMEM, D))), one),
        (np.ascontiguousarray(xsm[0:4].reshape(L, D)), np.ascontiguousarray(ms[0:4]), zero),
        (np.ascontiguousarray(xsm[4:8].reshape(L, D)), np.ascontiguousarray(ms[4:8]), zero),
    ]
    in_maps = []
    for c in range(8):
        xs_, mem_, carry_ = streams[min(c, 2)]
        m = {"xs": xs_, "mem": mem_, "carry": carry_, "consts": consts}
        m.update(wmap)
        in_maps.append(m)
    res = run_bass_kernel_spmd(nc, in_maps, core_ids=list(range(8)))
    y_prompt = np.asarray(res.results[0]["y"], dtype=np.float32).reshape(1, L, D)
    y_sample = np.concatenate([np.asarray(res.results[1]["y"], dtype=np.float32).reshape(4, SEG, D),
                               np.asarray(res.results[2]["y"], dtype=np.float32).reshape(4, SEG, D)], axis=0)
    return (y_prompt, y_sample)
```

```python
import numpy as np
from contextlib import ExitStack
import concourse.bass as bass
import concourse.mybir as mybir
from concourse.bass_utils import run_bass_kernel_spmd

F32 = mybir.dt.float32
BF16 = mybir.dt.bfloat16
AF = mybir.ActivationFunctionType
ALU = mybir.AluOpType

D = 1024
DFF = 2816
NMEM = 256
DEPTH = 2
ALPHA = (2.0 * DEPTH) ** 0.25
LN_EPS = 1e-5
NORM_EPS = 1e-6
GLA_H, GLA_DK, GLA_DV = 4, 128, 256
GDN_HK, GDN_HV = 8, 16


class Sem:
    def __init__(self, h, name):
        self.h = h
        self.name = name
        self.count = 0


class Buf:
    def __init__(self, name, ap=None):
        self.name = name
        self.ap = ap
        self.wtoks = {}
        self.rtoks = {}
        self.dsem = None
        self.excl = False


class Q:
    def __init__(self, K, name):
        self.K = K
        self.name = name
        self.sem = K.new_sem("q_" + name)
        self.seen = {}
        self.ops = []

    def wait_tok(self, sem, val):
        if sem is self.sem and self.name == "pe":
            return
        if self.seen.get(sem, 0) >= val:
            return
        self.seen[sem] = val
        self.ops.append(("w", sem, val))

    def deps(self, reads, writes):
        for b in reads:
            for s, v in b.wtoks.items():
                self.wait_tok(s, v)
            if b.excl:
                for s, v in b.rtoks.items():
                    if s is not self.sem:
                        self.wait_tok(s, v)
        for b in writes:
            for s, v in b.wtoks.items():
                self.wait_tok(s, v)
            for s, v in b.rtoks.items():
                self.wait_tok(s, v)

    def mark(self, tok, reads, writes):
        s, v = tok
        for b in reads:
            if b.rtoks.get(s, 0) < v:
                b.rtoks[s] = v
        for b in writes:
            b.wtoks[s] = v
            b.rtoks = {}

    def op(self, fn, reads=(), writes=(), signal=True):
        self.deps(reads, writes)
        if signal:
            self.sem.count += 1
            tok = (self.sem, self.sem.count)
            self.ops.append(("i", fn, self.sem, 1))
        else:
            tok = (self.sem, self.sem.count + 1)
            self.ops.append(("i", fn, None, 0))
        self.mark(tok, reads, writes)
        return tok

    def tt(self, out, in0, in1, op, reads=(), writes=()):
        return self.op(lambda e: e.tensor_tensor(out, in0, in1, op), reads, writes)

    def ts(self, out, in0, s1, s2, op0, op1=None, reads=(), writes=(), accum=None):
        if op1 is None:
            return self.op(lambda e: e.tensor_scalar(out, in0, s1, None, op0), reads, writes)
        if accum is not None:
            return self.op(lambda e: e.tensor_scalar(out, in0, s1, s2, op0, op1, accum), reads, writes)
        return self.op(lambda e: e.tensor_scalar(out, in0, s1, s2, op0, op1), reads, writes)

    def stt(self, out, in0, scalar, in1, op0, op1, reads=(), writes=()):
        return self.op(lambda e: e.scalar_tensor_tensor(out, in0, scalar, in1, op0, op1), reads, writes)

    def actv(self, out, in_, func, bias=None, scale=None, accum=None, reads=(), writes=()):
        kw = {}
        if bias is not None:
            kw["bias"] = bias
        if scale is not None:
            kw["scale"] = scale
        if accum is not None:
            kw["accum_out"] = accum
        return self.op(lambda e: e.activation(out, in_, func, **kw), reads, writes)

    def cp(self, out, in_, scale=None, reads=(), writes=()):
        return self.op(copy_op(self, out, in_, scale), reads, writes)

    def memset(self, ap, val, writes=()):
        return self.op(lambda e: e.memset(ap, float(val)), (), writes)

    def recip(self, out, in_, reads=(), writes=()):
        return self.op(lambda e: e.reciprocal(out, in_), reads, writes)

    def rmax(self, out, in_, reads=(), writes=()):
        return self.op(lambda e: e.reduce_max(out, in_, mybir.AxisListType.X), reads, writes)

    def bnstats(self, out, in_, reads=(), writes=()):
        return self.op(lambda e: e.bn_stats(out, in_), reads, writes)

    def bnaggr(self, out, in_, reads=(), writes=()):
        return self.op(lambda e: e.bn_aggr(out, in_), reads, writes)

    def dma(self, sb, pairs, reads=(), writes=()):
        K = self.K
        if sb.dsem is None:
            sb.dsem = K.get_dsem()
        ds = sb.dsem
        if ds.count > 0:
            self.wait_tok(ds, ds.count)
        self.deps(reads, writes)
        for (o, i) in pairs:
            ds.count += 16
            self.ops.append(("d", o, i, ds))
        tok = (ds, ds.count)
        self.mark(tok, reads, writes)
        return tok

    def replay(self, eng):
        for o in self.ops:
            if o[0] == "w":
                eng.wait_ge(o[1].h, o[2])
            elif o[0] == "i":
                ins = o[1](eng)
                if o[2] is not None:
                    ins.then_inc(o[2].h, o[3])
            elif o[0] == "d":
                eng.dma_start(out=o[1], in_=o[2]).then_inc(o[3].h, 16)


class Kern:
    def __init__(self):
        self.nc = bass.Bass("TRN2", target_bir_lowering=False)
        self.sems = []
        self.free_dsems = []
        self.phase_bufs = []

    def new_sem(self, name):
        h = self.gstack.enter_context(self.nc.semaphore(name))
        s = Sem(h, name)
        self.sems.append(s)
        return s

    def get_dsem(self):
        if self.free_dsems:
            return self.free_dsems.pop()
        return self.new_sem("d%d" % len(self.sems))

    def setup(self, gstack):
        self.gstack = gstack
        self.pe = Q(self, "pe")
        self.act = Q(self, "act")
        self.dve = Q(self, "dve")
        self.pool = Q(self, "pool")
        self.sp = Q(self, "sp")
        self.queues = [self.pe, self.act, self.dve, self.pool, self.sp]
        self.banks = []
        for i in range(8):
            t = gstack.enter_context(self.nc.psum_tensor("psb%d" % i, [128, 512], F32))
            self.banks.append(Buf("psb%d" % i, t))
            self.banks[-1].excl = True
        self.bank_i = 0
        self.rot = list(range(8))
        self.rr = 0

    def set_rot(self, idx):
        self.rot = list(idx)
        self.bank_i = 0

    def ps(self):
        b = self.banks[self.rot[self.bank_i % len(self.rot)]]
        self.bank_i += 1
        return b

    def cst(self, name):
        o, w = CONST_LAYOUT[name]
        return self.consts.ap[:, o:o + w]

    def sb(self, name, shape, dt):
        nbytes = int(np.prod(shape[1:])) * (2 if dt == BF16 else 4)
        base = self.nc.sbuf_base
        t = self.pstack.enter_context(self.nc.sbuf_tensor("p%d_%s" % (self.phase_id, name), shape, dt))
        b = Buf(name, t)
        self.phase_bufs.append(b)
        return b

    def begin_phase(self, pstack):
        self.phase_id = getattr(self, "phase_id", 0) + 1
        self.pstack = pstack
        self.phase_bufs = []

    def barrier(self):
        for q in self.queues:
            for s in self.sems:
                if s.count > 0:
                    q.wait_tok(s, s.count)

    def end_phase(self):
        self.barrier()
        for b in self.phase_bufs:
            if b.dsem is not None:
                self.free_dsems.append(b.dsem)
                b.dsem = None
        self.phase_bufs = []

    def ev(self):
        self.rr += 1
        return self.act if (self.rr & 1) else self.dve

    def finish(self):
        self.barrier()
        nc = self.nc
        with nc.Block() as block:
            @block.tensor
            def _(e):
                self.pe.replay(e)

            @block.scalar
            def _(e):
                self.act.replay(e)

            @block.vector
            def _(e):
                self.dve.replay(e)

            @block.gpsimd
            def _(e):
                self.pool.replay(e)

            @block.sync
            def _(e):
                self.sp.replay(e)


def copy_op(q, out_ap, in_ap, scale=None):
    if q.name == "act":
        if scale is None:
            return lambda e: e.activation(out_ap, in_ap, AF.Copy)
        return lambda e: e.activation(out_ap, in_ap, AF.Copy, scale=float(scale))
    if scale is None:
        return lambda e: e.tensor_copy(out_ap, in_ap)
    return lambda e: e.tensor_scalar(out_ap, in_ap, float(scale), None, ALU.mult)


def mm(K, bank, out_ap, pairs, reads, start=True, stop=True):
    n = len(pairs)
    tok = None
    for i, (l, r) in enumerate(pairs):
        st = start and i == 0
        sp = stop and i == n - 1
        last = i == n - 1
        tok = K.pe.op(lambda e, l=l, r=r, st=st, sp=sp: e.matmul(out_ap, l, r, start=st, stop=sp),
                      reads=reads if (last or i == 0) else (), writes=[bank], signal=last)
    return tok


def tr(K, bank, out_ap, in_ap, ident_ap, reads, signal=True):
    return K.pe.op(lambda e: e.transpose(out_ap, in_ap, ident_ap), reads=reads, writes=[bank], signal=signal)


def load_weight_bf16(K, wdram, rows, cols, name, stage, col_group=2048):
    nch = rows // 128
    t = K.sb(name, [128, nch, cols], BF16)
    bufs = []
    si = 0
    for c in range(nch):
        b = Buf("%s_%d" % (name, c), t.ap)
        eng = K.act if (c & 1) else K.dve
        for c0 in range(0, cols, col_group):
            c1 = min(cols, c0 + col_group)
            st = stage[si % len(stage)]
            si += 1
            K.sp.dma(st, [(st.ap[:, 0:c1 - c0], wdram[c * 128:(c + 1) * 128, c0:c1])], writes=[st])
            o = t.ap[:, c, c0:c1]
            i = st.ap[:, 0:c1 - c0]
            eng.cp(o, i, reads=[st], writes=[b])
        bufs.append(b)
    return t.ap, bufs


def load_bcast(K, dram_row, n, name):
    b = K.sb(name, [128, n], F32)
    K.sp.dma(b, [(b.ap[:], dram_row.partition_broadcast(128))], writes=[b])
    return b


def transpose_x(K, xt_ap, x_buf, xT_ap, xT_buf, ident_ap, ident_buf, tcols=128, col0=0, nrows=128):
    for g in range(2):
        bank = K.ps()
        for j in range(4):
            kc = g * 4 + j
            tr(K, bank, bank.ap[:, j * 128:j * 128 + nrows], xt_ap[0:nrows, kc * 128:(kc + 1) * 128],
               ident_ap[0:nrows, 0:nrows], reads=[x_buf, ident_buf], signal=(j == 3))
        q = K.ev()
        src = bank.ap[:].rearrange("p (j t) -> p j t", j=4)[:, :, 0:nrows]
        dst = xT_ap[:, g * 4:(g + 1) * 4, col0:col0 + nrows]
        q.cp(dst, src, reads=[bank], writes=[xT_buf])


def residual_ln(K, bank_list, x_buf, x_ap, fscale, g_b, b_b, z_buf, eps_adj, st_bufs):
    z = z_buf.ap
    stats, mv, rstd = st_bufs
    for h, bank in enumerate(bank_list):
        sl = slice(h * 512, (h + 1) * 512)
        K.dve.stt(z[:, sl], bank.ap[:, 0:512], float(fscale), x_ap[:, sl], ALU.mult, ALU.add, reads=[bank, x_buf], writes=[z_buf])
        K.dve.bnstats(stats.ap[:, h, :], z[:, sl], reads=[z_buf], writes=[stats])
    K.dve.bnaggr(mv.ap[:], stats.ap[:].rearrange("p a b -> p (a b)"), reads=[stats], writes=[mv])
    eps = K.eps_tiles[eps_adj]
    K.act.actv(rstd.ap[:], mv.ap[:, 1:2], AF.Sqrt, bias=eps.ap[:], scale=1.0, reads=[mv, eps], writes=[rstd])
    K.dve.recip(rstd.ap[:], rstd.ap[:], reads=[rstd], writes=[rstd])
    K.dve.ts(z[:], z[:], mv.ap[:, 0:1], rstd.ap[:], ALU.subtract, ALU.mult, reads=[z_buf, mv, rstd], writes=[z_buf])
    K.pool.tt(z[:], z[:], g_b.ap[:], ALU.mult, reads=[z_buf, g_b], writes=[z_buf])
    K.pool.tt(z[:], z[:], b_b.ap[:], ALU.add, reads=[z_buf, b_b], writes=[z_buf])


def make_eps_tiles(K, gstack):
    K.eps_tiles = {}
    for key, val in (("ln1", LN_EPS), ("lna", LN_EPS / (ALPHA * ALPHA)), ("l2", NORM_EPS), ("one", 1.0)):
        t = gstack.enter_context(K.nc.sbuf_tensor("eps_" + key, [128, 1], F32))
        b = Buf("eps_" + key, t)
        K.dve.memset(t[:], val, writes=[b])
        K.eps_tiles[key] = b


def ffn_phase(K, cfg, Xin, Xout, w_in, w_out, lng, lnb, TW=256):
    L = cfg["L"]
    NS = TW // 128
    with ExitStack() as ps_:
        K.begin_phase(ps_)
        stage = [K.sb("wst%d" % i, [128, 1024], F32) for i in range(2)]
        W1, W1b = load_weight_bf16(K, w_in, D, 2 * DFF, "W1", stage, col_group=704)
        W2, W2b = load_weight_bf16(K, w_out, DFF, D, "W2", stage, col_group=1024)
        g_b = load_bcast(K, lng, D, "lng")
        b_b = load_bcast(K, lnb, D, "lnb")
        xts = [K.sb("xt%d" % i, [128, NS, D], F32) for i in range(2)]
        xTs = [K.sb("xT%d" % i, [128, 8, TW], BF16) for i in range(2)]
        hTs = [K.sb("hT0", [128, 22, TW], BF16)] * 2
        sgs = [K.sb("sg%d" % i, [128, TW], F32) for i in range(2)]
        zs = stage
        stats = K.sb("stats", [128, 2, 6], F32)
        mv = K.sb("mv", [128, 2], F32)
        rstd = K.sb("rstd", [128, 1], F32)
        ident = K.ident
        zi = 0
        import os
        dbg = os.environ.get("DBG", "")
        if "a" in dbg: xts = [xts[1], xts[1]]
        if "b" in dbg: xTs = [xTs[1], xTs[1]]
        if "c" in dbg: hTs = [hTs[1], hTs[1]]
        if "d" in dbg: xts = [xts[0], xts[0]]
        if "e" in dbg: xTs = [xTs[0], xTs[0]]
        if "f" in dbg: hTs = [hTs[0], hTs[0]]
        for ti in range(L // TW):
            t0 = ti * TW
            xt = xts[ti % 2]
            xT = xTs[ti % 2]
            hT = hTs[ti % 2]
            K.sp.dma(xt, [(xt.ap[:, s, :], Xin[t0 + s * 128:t0 + (s + 1) * 128, :]) for s in range(NS)], writes=[xt])
            for s in range(NS):
                transpose_x(K, xt.ap[:, s, :], xt, xT.ap, xT, ident.ap, ident, col0=s * 128)
            for hc in range(22):
                bank = K.ps()
                mm(K, bank, bank.ap[:, 0:TW],
                   [(W1[:, kc, hc * 128:(hc + 1) * 128], xT.ap[:, kc, :]) for kc in range(8)], reads=[xT] + W1b)
                mm(K, bank, bank.ap[:, TW:2 * TW],
                   [(W1[:, kc, DFF + hc * 128:DFF + (hc + 1) * 128], xT.ap[:, kc, :]) for kc in range(8)], reads=[xT] + W1b)
                sg = sgs[hc % 2]
                K.act.actv(sg.ap[:], bank.ap[:, 0:TW], AF.Silu, reads=[bank], writes=[sg])
                K.dve.tt(hT.ap[:, hc, :], sg.ap[:], bank.ap[:, TW:2 * TW], ALU.mult, reads=[bank, sg], writes=[hT])
            for s in range(NS):
                banks = []
                for fh in range(2):
                    bank = K.ps()
                    mm(K, bank, bank.ap[:, 0:512],
                       [(hT.ap[:, hc, s * 128:(s + 1) * 128], W2[:, hc, fh * 512:(fh + 1) * 512]) for hc in range(22)],
                       reads=[hT] + W2b)
                    banks.append(bank)
                z = zs[zi % 2]
                zi += 1
                residual_ln(K, banks, xt, xt.ap[:, s, :], 0.5 / ALPHA, g_b, b_b, z, "lna", (stats, mv, rstd))
                K.pool.dma(z, [(Xout[t0 + s * 128:t0 + (s + 1) * 128, :], z.ap[:])], reads=[z])
            if "B" in dbg:
                K.barrier()
        K.end_phase()


WEIGHT_SPECS = [
    ("ffn_w_in", [2, 2, D, 2 * DFF]), ("ffn_w_out", [2, 2, DFF, D]), ("ln_g", [2, 4, D]), ("ln_b", [2, 4, D]),
    ("gla_w_in", [1, D, 3072]), ("gla_w_gate_down", [1, 2, D, 16]), ("gla_w_gate_up", [1, 2, 16, 512]),
    ("gla_b_gate", [1, 2, 512]), ("gla_norm_w", [1, 256]), ("gla_w_out", [1, D, D]),
    ("gdn_w_in", [1, D, 6144]), ("gdn_conv_w", [1, 4, 4096]), ("gdn_w_ab", [1, 2, D, 32]),
    ("gdn_a_log", [1, 2, 16]), ("gdn_dt_bias", [1, 2, 16]), ("gdn_norm_w", [1, 128]), ("gdn_w_out", [1, 2048, D]),
    ("xa_w_q", [2, D, D]), ("xa_w_kv", [2, D, 2 * D]), ("xa_w_o", [2, D, D]),
]


def build_program(cfg):
    L = cfg["L"]
    NSEG = cfg["NSEG"]
    phases = cfg.get("phases", None)
    K = Kern()
    nc = K.nc
    xs = nc.dram_tensor("xs", [L, D], F32, kind="ExternalInput").ap()
    mem = nc.dram_tensor("mem", [NSEG, NMEM, D], F32, kind="ExternalInput").ap()
    carry = nc.dram_tensor("carry", [128, 1], F32, kind="ExternalInput").ap()
    consts = nc.dram_tensor("consts", [128, cfg["NCONST"]], F32, kind="ExternalInput").ap()
    W = {}
    for name, shape in WEIGHT_SPECS:
        W[name] = nc.dram_tensor(name, shape, F32, kind="ExternalInput").ap()
    y = nc.dram_tensor("y", [L, D], F32, kind="ExternalOutput").ap()
    XA = nc.dram_tensor("scrA", [L, D], F32).ap()
    XB = nc.dram_tensor("scrB", [L, D], F32).ap()
    OF = nc.dram_tensor("scrO", [L, 2048], F32).ap()
    QKVs = nc.dram_tensor("scrQKV", [L, 4096], BF16).ap()
    with ExitStack() as gs:
        K.setup(gs)
        make_eps_tiles(K, gs)
        ct = gs.enter_context(nc.sbuf_tensor("consts_sb", [128, cfg["NCONST"]], F32))
        K.consts = Buf("consts", ct)
        K.sp.dma(K.consts, [(ct[:], consts)], writes=[K.consts])
        K.ident = Buf("ident", ct[:, 0:128])
        K.ident.wtoks = K.consts.wtoks
        ibt = gs.enter_context(nc.sbuf_tensor("identb", [128, 128], BF16))
        K.identb = Buf("identb", ibt)
        K.dve.cp(ibt[:], ct[:, 0:128], reads=[K.consts], writes=[K.identb])
        cyt = gs.enter_context(nc.sbuf_tensor("carry_sb", [128, 1], F32))
        K.carry = Buf("carry", cyt)
        K.sp.dma(K.carry, [(cyt[:], carry)], writes=[K.carry])
        K.cfg = cfg
        K.ofbuf = [Buf("of%d" % i) for i in range(L // 128)]
        K.qkvbuf = [Buf("qkv%d" % i) for i in range(L // 128)]
        K.QKV = QKVs
        plan = []
        for i in range(DEPTH):
            plan.append(("ffn", i, 0))
            plan.append(("mix", i))
            plan.append(("xa", i))
            plan.append(("ffn", i, 1))
        if phases is not None:
            plan = plan[:phases]
        if cfg.get("only"):
            plan = {"ffn": [("ffn", 0, 0)], "xa": [("xa", 0)], "gla": [("mix", 0)], "gdn": [("mix", 1)]}[cfg["only"]]
        cur = xs
        for pi, ph in enumerate(plan):
            last = pi == len(plan) - 1
            dst = y if last else (XA if (pi % 2 == 0) else XB)
            if ph[0] == "ffn":
                i, j = ph[1], ph[2]
                li = 0 if j == 0 else 3
                ffn_phase(K, cfg, cur, dst, W["ffn_w_in"][i, j], W["ffn_w_out"][i, j], W["ln_g"][i, li], W["ln_b"][i, li])
            elif ph[0] == "mix":
                i = ph[1]
                if i % 2 == 0:
                    gla_phase(K, cfg, cur, dst, OF, W, i // 2, W["ln_g"][i, 1], W["ln_b"][i, 1])
                else:
                    gdn_phase(K, cfg, cur, dst, OF, W, i // 2, W["ln_g"][i, 1], W["ln_b"][i, 1])
            elif ph[0] == "xa":
                i = ph[1]
                xa_phase(K, cfg, cur, dst, mem, W["xa_w_q"][i], W["xa_w_kv"][i], W["xa_w_o"][i], W["ln_g"][i, 2], W["ln_b"][i, 2])
            cur = dst
        K.finish()
    return nc


def _build_consts():
    s_ = np.arange(128)[:, None]
    c_ = np.arange(128)[None, :]
    same = (s_ // 64) == (c_ // 64)
    pos_s = s_ % 64
    out = {}
    out["IDENT"] = np.eye(128, dtype=np.float32)
    for dn in ("f", "b"):
        if dn == "f":
            tri = same & (s_ <= c_)
            mid = same & (pos_s <= 31) & (c_ >= 0)
        else:
            tri = same & (s_ >= c_)
            mid = same & (pos_s >= 32) & (c_ >= 0)
        tri = tri.astype(np.float32)
        mid = mid.astype(np.float32)
        sm = same.astype(np.float32)
        out["M1" + dn] = (-1.0 / 16.0) * (tri - mid)
        out["M3" + dn] = (-1.0 / 16.0) * tri
        out["M4" + dn] = (-1.0 / 16.0) * (sm - tri)
        out["MASK4" + dn] = np.tile(tri, (1, 4))
        out["TRI" + dn] = tri
        out["AFTER" + dn] = sm - tri
        out["NEG" + dn] = (1.0 - tri) * (-30000.0)
    ind = np.zeros((128, 2), np.float32)
    ind[0:64, 0] = -1.0 / 16.0
    ind[64:128, 1] = -1.0 / 16.0
    out["IND"] = ind
    out["ONES"] = np.ones((128, 128), np.float32)
    out["NEGONES"] = -np.ones((128, 128), np.float32)
    out["OFFD4"] = np.tile(1.0 - np.eye(128, dtype=np.float32), (1, 4))
    out["IDENT4"] = np.tile(np.eye(128, dtype=np.float32), (1, 4))
    ch0 = np.zeros((128, 128), np.float32)
    ch0[0:64, :] = 1.0
    out["CH0"] = ch0
    out["CH1"] = 1.0 - ch0
    return out


_CONSTS = _build_consts()
CONST_LAYOUT = {}
_off = 0
for _k, _v in _CONSTS.items():
    CONST_LAYOUT[_k] = (_off, _v.shape[1])
    _off += _v.shape[1]
NCONST = _off


def make_consts():
    return np.ascontiguousarray(np.concatenate([_CONSTS[k] for k in CONST_LAYOUT], axis=1).astype(np.float32))


def bank_bf(bank):
    return bank.ap[:].bitcast(BF16)


def xa_phase(K, cfg, Xin, Xout, mem, w_q, w_kv, w_o, lng, lnb):
    L, NSEG, SEG = cfg["L"], cfg["NSEG"], cfg["SEG"]
    TW = 256
    NS = TW // 128
    with ExitStack() as ps_:
        K.begin_phase(ps_)
        stage = [K.sb("wst%d" % i, [128, 1024], F32) for i in range(2)]
        Wq, Wqb = load_weight_bf16(K, w_q, D, D, "Wq", stage, col_group=1024)
        Wkv, Wkvb = load_weight_bf16(K, w_kv, D, 2 * D, "Wkv", stage, col_group=1024)
        Wo, Wob = load_weight_bf16(K, w_o, D, D, "Wo", stage, col_group=1024)
        g_b = load_bcast(K, lng, D, "lng")
        b_b = load_bcast(K, lnb, D, "lnb")
        ident, identb = K.ident, K.identb
        KT = K.sb("KT", [128, NSEG, 8, NMEM], BF16)
        V = K.sb("V", [128, NSEG, 2, D], BF16)
        memT = K.sb("memT", [128, 8, NMEM], BF16)
        xts = [K.sb("xt%d" % i, [128, NS, D], F32) for i in range(2)]
        xTs = [K.sb("xT%d" % i, [128, 8, TW], BF16) for i in range(2)]
        qT = K.sb("qT", [128, 8, TW], BF16)
        oT = K.sb("oT", [128, 8, TW], BF16)
        pT = K.sb("pT", [128, 4, 2, TW], BF16)
        pf = K.sb("pf", [128, 4, NMEM], F32)
        pb = K.sb("pb", [128, 4, NMEM], BF16)
        mx = K.sb("mx", [128, 4], F32)
        sm = K.sb("sm", [128, 4], F32)
        zs = [K.sb("z%d" % i, [128, D], F32) for i in range(2)]
        stats = K.sb("stats", [128, 2, 6], F32)
        mv = K.sb("mv", [128, 2], F32)
        rstd = K.sb("rstd", [128, 1], F32)
        for sg in range(NSEG):
            for mt in range(2):
                xt = xts[mt]
                K.sp.dma(xt, [(xt.ap[:, 0, :], mem[sg, mt * 128:(mt + 1) * 128, :])], writes=[xt])
                transpose_x(K, xt.ap[:, 0, :], xt, memT.ap, memT, ident.ap, ident, col0=mt * 128)
            for dhc in range(8):
                bank = K.ps()
                mm(K, bank, bank.ap[:, 0:NMEM], [(Wkv[:, kc, dhc * 128:(dhc + 1) * 128], memT.ap[:, kc, :]) for kc in range(8)],
                   reads=[memT] + Wkvb)
                q = K.ev()
                q.cp(KT.ap[:, sg, dhc, :], bank.ap[:, 0:NMEM], reads=[bank], writes=[KT])
            for mc in range(2):
                for half in range(2):
                    bank = K.ps()
                    mm(K, bank, bank.ap[:, 0:512],
                       [(memT.ap[:, kc, mc * 128:(mc + 1) * 128], Wkv[:, kc, D + half * 512:D + (half + 1) * 512]) for kc in range(8)],
                       reads=[memT] + Wkvb)
                    q = K.ev()
                    q.cp(V.ap[:, sg, mc, half * 512:(half + 1) * 512], bank.ap[:, 0:512], reads=[bank], writes=[V])
        zi = 0
        for ti in range(L // TW):
            t0 = ti * TW
            sg = t0 // SEG
            xt = xts[ti % 2]
            xT = xTs[ti % 2]
            K.sp.dma(xt, [(xt.ap[:, s, :], Xin[t0 + s * 128:t0 + (s + 1) * 128, :]) for s in range(NS)], writes=[xt])
            for s in range(NS):
                transpose_x(K, xt.ap[:, s, :], xt, xT.ap, xT, ident.ap, ident, col0=s * 128)
            for dhc in range(8):
                bank = K.ps()
                mm(K, bank, bank.ap[:, 0:TW], [(Wq[:, kc, dhc * 128:(dhc + 1) * 128], xT.ap[:, kc, :]) for kc in range(8)],
                   reads=[xT] + Wqb)
                q = K.ev()
                q.cp(qT.ap[:, dhc, :], bank.ap[:, 0:TW], scale=1.0 / 16.0, reads=[bank], writes=[qT])
            for s in range(NS):
                sb_ = []
                for j in range(2):
                    bank = K.ps()
                    for hh in range(2):
                        h = 2 * j + hh
                        mm(K, bank, bank.ap[:, hh * 256:(hh + 1) * 256],
                           [(qT.ap[:, 2 * h + c, s * 128:(s + 1) * 128], KT.ap[:, sg, 2 * h + c, :]) for c in range(2)], reads=[qT, KT])
                    K.dve.rmax(mx.ap[:, 2 * j:2 * j + 2], bank.ap[:].rearrange("p (h m) -> p h m", h=2), reads=[bank], writes=[mx])
                    sb_.append(bank)
                K.dve.ts(mx.ap[:], mx.ap[:], -1.0, None, ALU.mult, reads=[mx], writes=[mx])
                for h in range(4):
                    bank = sb_[h // 2]
                    K.act.actv(pf.ap[:, h, :], bank.ap[:, (h % 2) * 256:(h % 2 + 1) * 256], AF.Exp, bias=mx.ap[:, h:h + 1], scale=1.0,
                               accum=sm.ap[:, h:h + 1], reads=[bank, mx], writes=[pf, sm])
                K.dve.recip(sm.ap[:], sm.ap[:], reads=[sm], writes=[sm])
                for h in range(4):
                    K.dve.ts(pb.ap[:, h, :], pf.ap[:, h, :], sm.ap[:, h:h + 1], None, ALU.mult, reads=[pf, sm], writes=[pb])
                bank = K.ps()
                bb = bank_bf(bank)
                for h in range(4):
                    for mc in range(2):
                        idx = h * 2 + mc
                        tr(K, bank, bb[:, idx * 128:(idx + 1) * 128], pb.ap[:, h, mc * 128:(mc + 1) * 128], identb.ap[:],
                           reads=[pb, identb], signal=(idx == 7))
                q = K.ev()
                q.cp(pT.ap[:, :, :, s * 128:(s + 1) * 128], bb.rearrange("p (h c t) -> p h c t", h=4, c=2), reads=[bank], writes=[pT])
            for dhc in range(8):
                h = dhc // 2
                bank = K.ps()
                mm(K, bank, bank.ap[:, 0:TW], [(V.ap[:, sg, mc, dhc * 128:(dhc + 1) * 128], pT.ap[:, h, mc, :]) for mc in range(2)],
                   reads=[V, pT])
                q = K.ev()
                q.cp(oT.ap[:, dhc, :], bank.ap[:, 0:TW], reads=[bank], writes=[oT])
            for s in range(NS):
                banks = []
                for fh in range(2):
                    bank = K.ps()
                    mm(K, bank, bank.ap[:, 0:512],
                       [(oT.ap[:, dhc, s * 128:(s + 1) * 128], Wo[:, dhc, fh * 512:(fh + 1) * 512]) for dhc in range(8)],
                       reads=[oT] + Wob)
                    banks.append(bank)
                z = zs[zi % 2]
                zi += 1
                residual_ln(K, banks, xt, xt.ap[:, s, :], 1.0 / ALPHA, g_b, b_b, z, "lna", (stats, mv, rstd))
                K.pool.dma(z, [(Xout[t0 + s * 128:t0 + (s + 1) * 128, :], z.ap[:])], reads=[z])
        K.end_phase()


import os
GLA_STOP = os.environ.get('GLA_STOP', '')


def gla_phase(K, cfg, Xin, Xout, OF, W, j, lng, lnb):
    L, NSEG, SEG = cfg["L"], cfg["NSEG"], cfg["SEG"]
    NT = L // 128
    TPS = SEG // 128
    cst = K.cst
    with ExitStack() as ps_:
        K.begin_phase(ps_)
        K.set_rot([4, 5, 6, 7])
        OB = [K.banks[i] for i in range(4)]
        stage = [K.sb("wst%d" % i, [128, 1024], F32) for i in range(2)]
        Win, Winb = load_weight_bf16(K, W["gla_w_in"][j], D, 3072, "Win", stage, col_group=1024)
        Wout, Woutb = load_weight_bf16(K, W["gla_w_out"][j], D, D, "Wout", stage, col_group=1024)
        Wd = []
        for d in range(2):
            Wd.append(load_weight_bf16(K, W["gla_w_gate_down"][j, d], D, 16, "Wd%d" % d, stage, col_group=16))
        Wu = K.sb("Wu", [32, 2, 512], F32)
        K.dve.memset(Wu.ap[:], 0.0, writes=[Wu])
        K.sp.dma(Wu, [(Wu.ap[0:16, d, :], W["gla_w_gate_up"][j, d]) for d in range(2)] +
                 [(Wu.ap[16:17, d, :], W["gla_b_gate"][j, d:d + 1, :]) for d in range(2)], writes=[Wu])
        uext = K.sb("uext", [32, 128], F32)
        K.dve.memset(uext.ap[:], 1.0, writes=[uext])
        g_b = load_bcast(K, lng, D, "lng")
        b_b = load_bcast(K, lnb, D, "lnb")
        nw_b = load_bcast(K, W["gla_norm_w"][j], 256, "nw")
        ident, identb = K.ident, K.identb
        xts = [K.sb("xt%d" % i, [128, D], F32) for i in range(2)]
        xTs = [K.sb("xT%d" % i, [128, 8, 128], BF16) for i in range(2)]
        q_sb = K.sb("q_sb", [128, 512], F32)
        k_sb = K.sb("k_sb", [128, 512], F32)
        v_bf = K.sb("v_bf", [128, 1024], BF16)
        sr_bf = K.sb("sr_bf", [128, 1024], BF16)
        sp = K.sb("sp", [128, 512], F32)
        ets = [K.sb("et%d" % i, [128, 512], F32) for i in range(2)]
        qt_bf = K.sb("qt_bf", [128, 512], BF16)
        kt_bf = K.sb("kt_bf", [128, 512], BF16)
        qh_bf = K.sb("qh_bf", [128, 512], BF16)
        kh_bf = K.sb("kh_bf", [128, 512], BF16)
        qtT = K.sb("qtT", [128, 4, 128], BF16)
        ktT = K.sb("ktT", [128, 4, 128], BF16)
        qhA = K.sb("qhA", [128, 4, 128], BF16)
        qhB = K.sb("qhB", [128, 4, 128], BF16)
        K.dve.memset(qhA.ap[:], 0.0, writes=[qhA])
        K.dve.memset(qhB.ap[:], 0.0, writes=[qhB])
        scm = K.sb("scm", [128, 4, 128], BF16)
        S = K.sb("S", [128, 4, 256], F32)
        S_bf = K.sb("S_bf", [128, 4, 256], BF16)
        dec = K.sb("dec", [128, 8], F32)
        o_sb = K.sb("o_sb", [128, 1024], F32)
        of_sb = K.sb("of_sb", [128, 1024], F32)
        junk = K.sb("junk", [128, 256], F32)
        ss = K.sb("ss", [128, 4], F32)
        og_bf = K.sb("og_bf", [128, 1024], BF16)
        ogT = K.sb("ogT", [128, 8, 128], BF16)
        zs = [K.sb("z%d" % i, [128, D], F32) for i in range(2)]
        stats = K.sb("stats", [128, 2, 6], F32)
        mv = K.sb("mv", [128, 2], F32)
        rstd = K.sb("rstd", [128, 1], F32)
        eps_l2 = K.eps_tiles["l2"]
        one_t = K.eps_tiles["one"]
        zi = 0
        it = 0
        for dr in range(2):
            K.dve.memset(S.ap[:], 0.0, writes=[S])
            K.dve.memset(S_bf.ap[:], 0.0, writes=[S_bf])
            order = list(range(NT)) if dr == 0 else list(range(NT - 1, -1, -1))
            dn = "f" if dr == 0 else "b"
            for oi, ti in enumerate(order):
                t0 = ti * 128
                xt = xts[it % 2]
                xT = xTs[it % 2]
                it += 1
                if oi > 0 and (oi % TPS) == 0:
                    K.dve.ts(S.ap[:], S.ap[:], K.carry.ap[:, 0:1], None, ALU.mult, reads=[S, K.carry], writes=[S])
                    K.act.cp(S_bf.ap[:], S.ap[:], reads=[S], writes=[S_bf])
                K.sp.dma(xt, [(xt.ap[:], Xin[t0:t0 + 128, :])], writes=[xt])
                if dr == 1:
                    K.sp.dma(of_sb, [(of_sb.ap[:], OF[t0:t0 + 128, 0:1024])], reads=[K.ofbuf[ti]], writes=[of_sb])
                transpose_x(K, xt.ap[:], xt, xT.ap, xT, ident.ap, ident, col0=0)
                ngrp = 6 if dr == 1 else 4
                for gi in range(ngrp):
                    bank = K.ps()
                    mm(K, bank, bank.ap[:, 0:512], [(xT.ap[:, kc, :], Win[:, kc, gi * 512:(gi + 1) * 512]) for kc in range(8)],
                       reads=[xT] + Winb)
                    if gi == 0:
                        K.act.cp(q_sb.ap[:], bank.ap[:, 0:512], scale=128.0 ** -0.5, reads=[bank], writes=[q_sb])
                    elif gi == 1:
                        K.dve.cp(k_sb.ap[:], bank.ap[:, 0:512], reads=[bank], writes=[k_sb])
                    elif gi < 4:
                        q = K.ev()
                        q.cp(v_bf.ap[:, (gi - 2) * 512:(gi - 1) * 512], bank.ap[:, 0:512], reads=[bank], writes=[v_bf])
                    else:
                        K.act.actv(sr_bf.ap[:, (gi - 4) * 512:(gi - 3) * 512], bank.ap[:, 0:512], AF.Silu, reads=[bank], writes=[sr_bf])
                if GLA_STOP == "A":
                    continue
                bank = K.ps()
                mm(K, bank, bank.ap[0:16, 0:128], [(Wd[dr][0][:, kc, :], xT.ap[:, kc, :]) for kc in range(8)], reads=[xT] + Wd[dr][1])
                K.dve.cp(uext.ap[0:16, :], bank.ap[0:16, 0:128], reads=[bank], writes=[uext])
                bank = K.ps()
                mm(K, bank, bank.ap[:, 0:512], [(uext.ap[:], Wu.ap[:, dr, :])], reads=[uext, Wu])
                et = ets[0]
                K.act.actv(et.ap[:], bank.ap[:, 0:512], AF.Exp, scale=-1.0, reads=[bank], writes=[et])
                K.act.actv(sp.ap[:], et.ap[:], AF.Ln, bias=one_t.ap[:], scale=1.0, reads=[et, one_t], writes=[sp])
                if GLA_STOP == "B":
                    continue
                specs = [("M1" + dn, q_sb, qt_bf, 1.0), ("M1" + dn, k_sb, kt_bf, -1.0), ("M3" + dn, q_sb, qh_bf, 1.0), ("M4" + dn, k_sb, kh_bf, 1.0)]
                bank1 = None
                for si, (mname, src, dst, sc) in enumerate(specs):
                    if si != 1:
                        bank = K.ps()
                        mm(K, bank, bank.ap[:, 0:512], [(cst(mname), sp.ap[:])], reads=[K.consts, sp])
                        if si == 0:
                            bank1 = bank
                    else:
                        bank = bank1
                    et = ets[si % 2]
                    K.act.actv(et.ap[:], bank.ap[:, 0:512], AF.Exp, scale=sc, reads=[bank], writes=[et])
                    eng = K.dve if si % 2 == 0 else K.pool
                    eng.tt(dst.ap[:], src.ap[:], et.ap[:], ALU.mult, reads=[src, et], writes=[dst])
                if GLA_STOP == "C":
                    continue
                bank = K.ps()
                for h in range(4):
                    mm(K, bank, bank.ap[:, 2 * h:2 * h + 2], [(sp.ap[:, h * 128:(h + 1) * 128], cst("IND"))], reads=[sp, K.consts])
                K.act.actv(dec.ap[:], bank.ap[:, 0:8], AF.Exp, reads=[bank], writes=[dec])
                if GLA_STOP == "D":
                    continue
                bank = K.ps()
                bb = bank_bf(bank)
                for h in range(4):
                    tr(K, bank, bb[:, h * 128:(h + 1) * 128], qt_bf.ap[:, h * 128:(h + 1) * 128], identb.ap[:], reads=[qt_bf, identb], signal=False)
                for h in range(4):
                    tr(K, bank, bb[:, (4 + h) * 128:(5 + h) * 128], kt_bf.ap[:, h * 128:(h + 1) * 128], identb.ap[:], reads=[kt_bf, identb],
                       signal=(h == 3))
                K.act.cp(qtT.ap[:], bb[:, 0:512].rearrange("p (h t) -> p h t", h=4), reads=[bank], writes=[qtT])
                K.act.cp(ktT.ap[:], bb[:, 512:1024].rearrange("p (h t) -> p h t", h=4), reads=[bank], writes=[ktT])
                if GLA_STOP == "D1":
                    continue
                bank = K.ps()
                bb = bank_bf(bank)
                for h in range(4):
                    tr(K, bank, bb[:, h * 128:(h + 1) * 128], qh_bf.ap[:, h * 128:(h + 1) * 128], identb.ap[:], reads=[qh_bf, identb],
                       signal=(h == 3))
                bv = bb[:, 0:512].rearrange("p (h t) -> p h t", h=4)
                K.dve.cp(qhA.ap[:, :, 0:64], bv[:, :, 0:64], reads=[bank], writes=[qhA])
                K.dve.cp(qhB.ap[:, :, 64:128], bv[:, :, 64:128], reads=[bank], writes=[qhB])
                if GLA_STOP == "E":
                    continue
                bank = K.ps()
                for h in range(4):
                    mm(K, bank, bank.ap[:, h * 128:(h + 1) * 128], [(ktT.ap[:, h, :], qtT.ap[:, h, :])], reads=[ktT, qtT])
                K.dve.tt(scm.ap[:].rearrange("p h t -> p (h t)"), bank.ap[:, 0:512], cst("MASK4" + dn), ALU.mult,
                         reads=[bank, K.consts], writes=[scm])
                if GLA_STOP == "F":
                    continue
                for h in range(4):
                    ob = OB[h]
                    mm(K, ob, ob.ap[:, 0:256], [(scm.ap[:, h, :], v_bf.ap[:, h * 256:(h + 1) * 256])],
                       reads=[scm, v_bf], start=True, stop=False)
                chunks = [0, 1] if dr == 0 else [1, 0]
                for ci, ck in enumerate(chunks):
                    qh = qhA if ck == 0 else qhB
                    for h in range(4):
                        ob = OB[h]
                        mm(K, ob, ob.ap[:, 0:256], [(qh.ap[:, h, :], S_bf.ap[:, h, :])],
                           reads=[qh, S_bf], start=False, stop=(ci == 1))
                    r0 = ck * 64
                    kvb = []
                    for hp in range(2):
                        bank = K.ps()
                        for hh in range(2):
                            h = hp * 2 + hh
                            mm(K, bank, bank.ap[:, hh * 256:(hh + 1) * 256],
                               [(kh_bf.ap[r0:r0 + 64, h * 128:(h + 1) * 128], v_bf.ap[r0:r0 + 64, h * 256:(h + 1) * 256])], reads=[kh_bf, v_bf])
                        kvb.append(bank)
                    for h in range(4):
                        bank = kvb[h // 2]
                        K.dve.stt(S.ap[:, h, :], S.ap[:, h, :], dec.ap[:, 2 * h + ck:2 * h + ck + 1], bank.ap[:, (h % 2) * 256:(h % 2 + 1) * 256],
                                  ALU.mult, ALU.add, reads=[S, dec, bank], writes=[S])
                    K.act.cp(S_bf.ap[:], S.ap[:], reads=[S], writes=[S_bf])
                if dr == 0:
                    for h in range(4):
                        q = K.ev()
                        q.cp(o_sb.ap[:, h * 256:(h + 1) * 256], OB[h].ap[:, 0:256], reads=[OB[h]], writes=[o_sb])
                    K.pool.dma(o_sb, [(OF[t0:t0 + 128, 0:1024], o_sb.ap[:])], reads=[o_sb], writes=[K.ofbuf[ti]])
                else:
                    for h in range(4):
                        K.dve.tt(o_sb.ap[:, h * 256:(h + 1) * 256], OB[h].ap[:, 0:256], of_sb.ap[:, h * 256:(h + 1) * 256], ALU.add,
                                 reads=[OB[h], of_sb], writes=[o_sb])
                    if GLA_STOP == "O":
                        K.pool.dma(o_sb, [(Xout[t0:t0 + 128, :], o_sb.ap[:])], reads=[o_sb])
                        continue
                    for h in range(4):
                        K.act.actv(junk.ap[:], o_sb.ap[:, h * 256:(h + 1) * 256], AF.Square, accum=ss.ap[:, h:h + 1], reads=[o_sb], writes=[junk, ss])
                    K.act.actv(ss.ap[:], ss.ap[:], AF.Sqrt, bias=eps_l2.ap[:], scale=1.0 / 256.0, reads=[ss, eps_l2], writes=[ss])
                    K.dve.recip(ss.ap[:], ss.ap[:], reads=[ss], writes=[ss])
                    for h in range(4):
                        K.dve.stt(o_sb.ap[:, h * 256:(h + 1) * 256], o_sb.ap[:, h * 256:(h + 1) * 256], ss.ap[:, h:h + 1], nw_b.ap[:],
                                  ALU.mult, ALU.mult, reads=[o_sb, ss, nw_b], writes=[o_sb])
                    K.pool.tt(og_bf.ap[:], o_sb.ap[:], sr_bf.ap[:], ALU.mult, reads=[o_sb, sr_bf], writes=[og_bf])
                    bank = K.ps()
                    bb = bank_bf(bank)
                    for c in range(8):
                        tr(K, bank, bb[:, c * 128:(c + 1) * 128], og_bf.ap[:, c * 128:(c + 1) * 128], identb.ap[:], reads=[og_bf, identb],
                           signal=(c == 7))
                    K.act.cp(ogT.ap[:], bb.rearrange("p (c t) -> p c t", c=8), reads=[bank], writes=[ogT])
                    banks = []
                    for fh in range(2):
                        bank = K.ps()
                        mm(K, bank, bank.ap[:, 0:512], [(ogT.ap[:, c, :], Wout[:, c, fh * 512:(fh + 1) * 512]) for c in range(8)],
                           reads=[ogT] + Woutb)
                        banks.append(bank)
                    z = zs[zi % 2]
                    zi += 1
                    residual_ln(K, banks, xt, xt.ap[:], 1.0 / ALPHA, g_b, b_b, z, "lna", (stats, mv, rstd))
                    K.pool.dma(z, [(Xout[t0:t0 + 128, :], z.ap[:])], reads=[z])
        K.set_rot(list(range(8)))
        K.end_phase()


def gdn_scan_pass(K, cfg, Xin, OF, QKV, W, j, dr):
    L, NSEG, SEG = cfg["L"], cfg["NSEG"], cfg["SEG"]
    NT = L // 128
    TPS = SEG // 128
    cst = K.cst
    dn = "f" if dr == 0 else "b"
    with ExitStack() as ps_:
        K.begin_phase(ps_)
        K.set_rot(list(range(8)))
        stage_big = K.sb("wst", [128, 2048], F32)
        stage = [Buf("wst0", stage_big.ap[:, 0:1024]), Buf("wst1", stage_big.ap[:, 1024:2048])]
        K.phase_bufs.extend(stage)
        if dr == 0:
            Win, Winb = load_weight_bf16(K, W["gdn_w_in"][j][:, 0:4096], D, 4096, "Win", stage, col_group=1024)
        Wab, Wabb = load_weight_bf16(K, W["gdn_w_ab"][j, dr], D, 32, "Wab", stage, col_group=32)
        ident, identb = K.ident, K.identb
        cwT = K.sb("cwT", [128, 128], F32)
        cw = K.sb("cw", [128, 128], F32)
        K.sp.dma(cwT, [(cwT.ap[:], W["gdn_conv_w"][j].rearrange("t (c p) -> (t c) p", p=128))], writes=[cwT])
        bank = K.ps()
        tr(K, bank, bank.ap[:, 0:128], cwT.ap[:], ident.ap[:], reads=[cwT, ident])
        K.dve.cp(cw.ap[:], bank.ap[:, 0:128], reads=[bank], writes=[cw])
        alog = load_bcast(K, W["gdn_a_log"][j].rearrange("a b -> (a b)"), 32, "alog")
        dtb = load_bcast(K, W["gdn_dt_bias"][j].rearrange("a b -> (a b)"), 32, "dtb")
        negA = K.sb("negA", [128, 32], F32)
        K.act.actv(negA.ap[:], alog.ap[:], AF.Exp, reads=[alog], writes=[negA])
        K.dve.ts(negA.ap[:], negA.ap[:], -1.0, None, ALU.mult, reads=[negA], writes=[negA])
        ones_bf = K.sb("ones_bf", [128, 128], BF16)
        K.dve.cp(ones_bf.ap[:], cst("ONES"), reads=[K.consts], writes=[ones_bf])
        xts = [K.sb("xt0", [128, D], F32)] * 2
        xTs = [K.sb("xTe0", [128, 8, 132], BF16)] * 2
        xhp = K.sb("xhp", [2, D], F32)
        xhn = K.sb("xhn", [1, D], F32)
        accs = [K.sb("acc%d" % i, [128, 128], F32) for i in range(2)] * 2
        s4q = K.sb("s4q", [128, 8, 128], F32) if dr == 0 else None
        s4k = K.sb("s4k", [128, 8, 128], F32) if dr == 0 else None
        sq4s = [K.sb("sq4_%d" % i, [128, 4, 128], BF16) for i in range(2)]
        rn4s = [K.sb("rn4_%d" % i, [128, 512], F32) for i in range(2)]
        nqb = 2 if dr == 1 else 1
        qTns = [K.sb("qTn%d" % i, [128, 8, 128], BF16) for i in range(nqb)]
        kTns = [K.sb("kTn%d" % i, [128, 8, 128], BF16) for i in range(nqb)]
        v_toks = [K.sb("v_tok%d" % i, [128, 16, 128], BF16) for i in range(nqb)]
        vT = K.sb("vT", [128, 16, 128], BF16) if dr == 0 else None
        kends = [K.sb("kend%d" % i, [128, 16, 128], BF16) for i in range(nqb)] * (3 - nqb)
        t16 = K.sb("t16", [128, 16], F32)
        e16 = K.sb("e16", [128, 16], F32)
        g16 = K.sb("g16", [128, 16], F32)
        beta16s = [K.sb("beta16_%d" % i, [128, 16], F32) for i in range(2)]
        egks = [K.sb("egk%d" % i, [128, 64], F32) for i in range(2)]
        negegs = [K.sb("negeg%d" % i, [128, 16], F32) for i in range(2)]
        Pm = [K.sb("Pm%d" % i, [128, 128], F32) for i in range(2)]
        decT4s = [K.sb("decT4_0", [128, 4, 128], F32)] * 2
        decTs4s = [K.sb("decTs4_0", [128, 4, 128], F32)] * 2
        Ms = [K.sb("M%d" % i, [128, 16, 128], BF16) for i in range(2)]
        MTs = [K.sb("MT%d" % i, [128, 16, 128], BF16) for i in range(2)]
        Qf = Buf("Qf", stage_big.ap[:].rearrange("p (a b) -> p a b", a=16))
        K.phase_bufs.append(Qf)
        Qb = K.sb("Qb", [128, 16, 128], BF16)
        T2Ts = [K.sb("T2T%d" % i, [128, 16, 128], BF16) for i in range(nqb)] * (3 - nqb)
        attnTs = [K.sb("attnT%d" % i, [128, 16, 128], BF16) for i in range(nqb)] * (3 - nqb)
        resid1 = K.sb("resid", [128, 16, 128], BF16)
        vn1 = K.sb("vn", [128, 16, 128], BF16)
        for b in (resid1, vn1):
            K.dve.memset(b.ap[:], 0.0, writes=[b])
        S = K.sb("S", [128, 16, 128], F32)
        S_bf = K.sb("S_bf", [128, 16, 128], BF16)
        K.dve.memset(S.ap[:], 0.0, writes=[S])
        K.dve.memset(S_bf.ap[:], 0.0, writes=[S_bf])
        o_sb = K.sb("o_sb", [128, 16, 128], F32)
        of_sbs = [K.sb("of_sb%d" % i, [128, 2048], F32) for i in range(2)] if dr == 1 else [None, None]
        eps_l2 = K.eps_tiles["l2"]
        one_t = K.eps_tiles["one"]
        order = list(range(NT)) if dr == 0 else list(range(NT - 1, -1, -1))
        def setup_tile(oi, ti):
            t0 = ti * 128
            xt = xts[oi % 2]
            xT = xTs[oi % 2]
            qTn, kTn, v_tok = qTns[oi % nqb], kTns[oi % nqb], v_toks[oi % nqb]
            kend, beta16, egk, negeg = kends[oi % 2], beta16s[oi % 2], egks[oi % 2], negegs[oi % 2]
            T2T, attnT, of_sb = T2Ts[oi % 2], attnTs[oi % 2], of_sbs[oi % 2]
            K.sp.dma(xt, [(xt.ap[:], Xin[t0:t0 + 128, :])], writes=[xt])
            if dr == 0:
                if t0 == 0:
                    K.dve.memset(xhp.ap[:], 0.0, writes=[xhp])
                else:
                    K.sp.dma(xhp, [(xhp.ap[:], Xin[t0 - 2:t0, :])], writes=[xhp])
                    if t0 % SEG == 0:
                        K.dve.ts(xhp.ap[:], xhp.ap[:], K.carry.ap[0:2, 0:1], None, ALU.mult, reads=[xhp, K.carry], writes=[xhp])
                if t0 + 128 == L:
                    K.dve.memset(xhn.ap[:], 0.0, writes=[xhn])
                else:
                    K.sp.dma(xhn, [(xhn.ap[:], Xin[t0 + 128:t0 + 129, :])], writes=[xhn])
                    if (t0 + 128) % SEG == 0:
                        K.dve.ts(xhn.ap[:], xhn.ap[:], K.carry.ap[0:1, 0:1], None, ALU.mult, reads=[xhn, K.carry], writes=[xhn])
            if dr == 1:
                K.sp.dma(of_sb, [(of_sb.ap[:], OF[t0:t0 + 128, :])], reads=[K.ofbuf[ti]], writes=[of_sb])
                K.sp.dma(qTn, [(qTn.ap[:].rearrange("p a b -> p (a b)"), QKV[t0:t0 + 128, 0:1024])], reads=[K.qkvbuf[ti]], writes=[qTn])
                K.sp.dma(kTn, [(kTn.ap[:].rearrange("p a b -> p (a b)"), QKV[t0:t0 + 128, 1024:2048])], reads=[K.qkvbuf[ti]], writes=[kTn])
                K.sp.dma(v_tok, [(v_tok.ap[:].rearrange("p a b -> p (a b)"), QKV[t0:t0 + 128, 2048:4096])], reads=[K.qkvbuf[ti]], writes=[v_tok])
            transpose_x(K, xt.ap[:], xt, xT.ap, xT, ident.ap, ident, col0=2)
            if dr == 0:
                bank = K.ps()
                for kc in range(8):
                    tr(K, bank, bank.ap[:, kc * 4:kc * 4 + 2], xhp.ap[0:2, kc * 128:(kc + 1) * 128], ident.ap[0:2, 0:2], reads=[xhp, ident], signal=False)
                    tr(K, bank, bank.ap[:, kc * 4 + 2:kc * 4 + 3], xhn.ap[0:1, kc * 128:(kc + 1) * 128], ident.ap[0:1, 0:1], reads=[xhn, ident],
                       signal=(kc == 7))
                hv_ = bank.ap[:, 0:32].rearrange("p (k c) -> p k c", c=4)
                K.act.cp(xT.ap[:, :, 0:2], hv_[:, :, 0:2], reads=[bank], writes=[xT])
                K.act.cp(xT.ap[:, :, 130:131], hv_[:, :, 2:3], reads=[bank], writes=[xT])
                yield
                for c3 in range(0, 32, 3):
                    bank = K.ps()
                    cs = list(range(c3, min(32, c3 + 3)))
                    for i, c in enumerate(cs):
                        mm(K, bank, bank.ap[:, i * 132:i * 132 + 131], [(Win[:, kc, c * 128:(c + 1) * 128], xT.ap[:, kc, 0:131]) for kc in range(8)],
                           reads=[xT] + Winb)
                    for i, c in enumerate(cs):
                        acc = accs[c % 4]
                        eng = K.dve
                        src = bank.ap[:, i * 132:i * 132 + 131]
                        sbuf_ = bank
                        eng.ts(acc.ap[:], src[:, 0:128], cw.ap[:, c:c + 1], None, ALU.mult, reads=[sbuf_, cw], writes=[acc])
                        for tp in range(1, 4):
                            eng.stt(acc.ap[:], src[:, tp:tp + 128], cw.ap[:, tp * 32 + c:tp * 32 + c + 1], acc.ap[:], ALU.mult, ALU.add,
                                    reads=[sbuf_, cw, acc], writes=[acc])
                        if c < 8:
                            K.act.actv(s4q.ap[:, c, :], acc.ap[:], AF.Silu, reads=[acc], writes=[s4q])
                        elif c < 16:
                            K.act.actv(s4k.ap[:, c - 8, :], acc.ap[:], AF.Silu, reads=[acc], writes=[s4k])
                        else:
                            K.act.actv(vT.ap[:, c - 16, :], acc.ap[:], AF.Silu, reads=[acc], writes=[vT])
                for (src4, dstn, scl) in ((s4q, qTn, 128.0 ** -0.5), (s4k, kTn, None)):
                    for g in range(2):
                        sv = src4.ap[:, g * 4:(g + 1) * 4, :]
                        sq4, rn4 = sq4s[g], rn4s[g]
                        K.pool.tt(sq4.ap[:], sv, sv, ALU.mult, reads=[src4], writes=[sq4])
                        bank = K.ps()
                        for i in range(4):
                            mm(K, bank, bank.ap[:, i * 128:(i + 1) * 128], [(ones_bf.ap[:], sq4.ap[:, i, :])], reads=[ones_bf, sq4])
                        K.act.actv(rn4.ap[:], bank.ap[:, 0:512], AF.Ln, bias=eps_l2.ap[:], scale=1.0, reads=[bank, eps_l2], writes=[rn4])
                        K.act.actv(rn4.ap[:], rn4.ap[:], AF.Exp, scale=-0.5, reads=[rn4], writes=[rn4])
                        dv_ = dstn.ap[:, g * 4:(g + 1) * 4, :].rearrange("p a b -> p (a b)")
                        sv2 = sv.rearrange("p a b -> p (a b)")
                        if scl is None:
                            K.dve.tt(dv_, sv2, rn4.ap[:], ALU.mult, reads=[src4, rn4], writes=[dstn])
                        else:
                            K.dve.stt(dv_, sv2, float(scl), rn4.ap[:], ALU.mult, ALU.mult, reads=[src4, rn4], writes=[dstn])
            yield
            bank = K.ps()
            mm(K, bank, bank.ap[:, 0:32], [(xT.ap[:, kc, 2:130], Wab[:, kc, :]) for kc in range(8)], reads=[xT] + Wabb)
            K.dve.tt(t16.ap[:], bank.ap[:, 0:16], dtb.ap[:, dr * 16:(dr + 1) * 16], ALU.add, reads=[bank, dtb], writes=[t16])
            K.act.actv(e16.ap[:], t16.ap[:], AF.Exp, reads=[t16], writes=[e16])
            K.act.actv(e16.ap[:], e16.ap[:], AF.Ln, bias=one_t.ap[:], scale=1.0, reads=[e16, one_t], writes=[e16])
            K.dve.tt(g16.ap[:], e16.ap[:], negA.ap[:, dr * 16:(dr + 1) * 16], ALU.mult, reads=[e16, negA], writes=[g16])
            K.act.actv(beta16.ap[:], bank.ap[:, 16:32], AF.Exp, scale=-1.0, reads=[bank], writes=[beta16])
            K.dve.ts(beta16.ap[:], beta16.ap[:], 1.0, None, ALU.add, reads=[beta16], writes=[beta16])
            K.dve.recip(beta16.ap[:], beta16.ap[:], reads=[beta16], writes=[beta16])
            bank = K.ps()
            for i, mname in enumerate(("TRI" + dn, "AFTER" + dn, "CH0", "CH1")):
                mm(K, bank, bank.ap[:, i * 16:(i + 1) * 16], [(cst(mname), g16.ap[:])], reads=[K.consts, g16])
            K.act.actv(egk.ap[:], bank.ap[:, 0:64], AF.Exp, reads=[bank], writes=[egk])
            K.dve.ts(negeg.ap[:], egk.ap[:, 0:16], -1.0, None, ALU.mult, reads=[egk], writes=[negeg])
            yield
            bank = K.ps()
            bb = bank_bf(bank)
            for hk in range(8):
                tr(K, bank, bb[:, hk * 128:(hk + 1) * 128], kTn.ap[:, hk, :], identb.ap[:], reads=[kTn, identb], signal=(hk == 7))
            for hv in range(16):
                q = K.act if (hv % 2) else K.dve
                if q is K.act:
                    q.actv(kend.ap[:, hv, :], bb[:, (hv // 2) * 128:(hv // 2 + 1) * 128], AF.Copy, scale=egk.ap[:, 16 + hv:17 + hv],
                           reads=[bank, egk], writes=[kend])
                else:
                    q.ts(kend.ap[:, hv, :], bb[:, (hv // 2) * 128:(hv // 2 + 1) * 128], egk.ap[:, 16 + hv:17 + hv], None, ALU.mult,
                         reads=[bank, egk], writes=[kend])
            if dr == 0:
                for g in range(2):
                    bank = K.ps()
                    bb = bank_bf(bank)
                    for i in range(8):
                        tr(K, bank, bb[:, i * 128:(i + 1) * 128], vT.ap[:, g * 8 + i, :], identb.ap[:], reads=[vT, identb], signal=(i == 7))
                    q = K.ev()
                    q.cp(v_tok.ap[:, g * 8:(g + 1) * 8, :], bb.rearrange("p (a b) -> p a b", a=8), reads=[bank], writes=[v_tok])
                K.pool.dma(qTn, [(QKV[t0:t0 + 128, 0:1024], qTn.ap[:].rearrange("p a b -> p (a b)"))], reads=[qTn], writes=[K.qkvbuf[ti]])
                K.pool.dma(kTn, [(QKV[t0:t0 + 128, 1024:2048], kTn.ap[:].rearrange("p a b -> p (a b)"))], reads=[kTn], writes=[K.qkvbuf[ti]])
                K.pool.dma(v_tok, [(QKV[t0:t0 + 128, 2048:4096], v_tok.ap[:].rearrange("p a b -> p (a b)"))], reads=[v_tok], writes=[K.qkvbuf[ti]])
            yield
            M, MT = Ms[0], MTs[0]
            for gi in range(4):
                hv0 = gi * 4
                decT4, decTs4 = decT4s[gi % 2], decTs4s[gi % 2]
                gbank = K.ps()
                for i in range(2):
                    hk = 2 * gi + i
                    mm(K, gbank, gbank.ap[:, i * 128:(i + 1) * 128], [(kTn.ap[:, hk, :], kTn.ap[:, hk, :])], reads=[kTn])
                    mm(K, gbank, gbank.ap[:, (2 + i) * 128:(3 + i) * 128], [(kTn.ap[:, hk, :], qTn.ap[:, hk, :])], reads=[kTn, qTn])
                dbank = K.ps()
                for i in range(4):
                    hv = hv0 + i
                    P = Pm[i % 2]
                    K.dve.ts(P.ap[:], cst("TRI" + dn), g16.ap[:, hv:hv + 1], None, ALU.mult, reads=[K.consts, g16], writes=[P])
                    mm(K, dbank, dbank.ap[:, i * 128:(i + 1) * 128],
                       [(cst("ONES"), P.ap[:]), (P.ap[:], cst("NEGONES")), (cst("IDENT"), cst("NEG" + dn))], reads=[P, K.consts])
                K.act.actv(decT4.ap[:].rearrange("p a b -> p (a b)"), dbank.ap[:, 0:512], AF.Exp, reads=[dbank], writes=[decT4])
                K.pool.tt(decTs4.ap[:].rearrange("p a b -> p (a b)"), decT4.ap[:].rearrange("p a b -> p (a b)"), cst("OFFD4"), ALU.mult,
                          reads=[decT4, K.consts], writes=[decTs4])
                for i in range(4):
                    hv = hv0 + i
                    K.dve.stt(M.ap[:, hv, :], gbank.ap[:, (i // 2) * 128:(i // 2 + 1) * 128], beta16.ap[:, hv:hv + 1], decTs4.ap[:, i, :],
                              ALU.mult, ALU.mult, reads=[gbank, beta16, decTs4], writes=[M])
                    K.dve.tt(attnT.ap[:, hv, :], gbank.ap[:, (2 + i // 2) * 128:(3 + i // 2) * 128], decT4.ap[:, i, :], ALU.mult,
                             reads=[gbank, decT4], writes=[attnT])
                bank = K.ps()
                bb = bank_bf(bank)
                for i in range(4):
                    tr(K, bank, bb[:, i * 128:(i + 1) * 128], M.ap[:, hv0 + i, :], identb.ap[:], reads=[M, identb], signal=(i == 3))
                K.act.cp(MT.ap[:, hv0:hv0 + 4, :].rearrange("p a b -> p (a b)"), bb[:, 0:512], reads=[bank], writes=[MT])
                K.pool.tt(Qf.ap[:, hv0:hv0 + 4, :].rearrange("p a b -> p (a b)"), cst("IDENT4"),
                          M.ap[:, hv0:hv0 + 4, :].rearrange("p a b -> p (a b)"), ALU.subtract, reads=[K.consts, M], writes=[Qf])
                K.act.cp(Qb.ap[:, hv0:hv0 + 4, :], Qf.ap[:, hv0:hv0 + 4, :], reads=[Qf], writes=[Qb])
                yield
            for lev in range(1, 6):
                Mn, MTn = Ms[lev % 2], MTs[lev % 2]
                if lev < 5:
                    for gi in range(4):
                        hv0 = gi * 4
                        bank = K.ps()
                        for i in range(4):
                            mm(K, bank, bank.ap[:, i * 128:(i + 1) * 128], [(MT.ap[:, hv0 + i, :], M.ap[:, hv0 + i, :])], reads=[MT, M])
                        K.act.cp(Mn.ap[:, hv0:hv0 + 4, :].rearrange("p a b -> p (a b)"), bank.ap[:, 0:512], reads=[bank], writes=[Mn])
                for gi in range(4):
                    hv0 = gi * 4
                    bank = K.ps()
                    for i in range(4):
                        mm(K, bank, bank.ap[:, i * 128:(i + 1) * 128], [(M.ap[:, hv0 + i, :], MT.ap[:, hv0 + i, :])], reads=[MT, M])
                    q = K.act if (gi % 2) else K.dve
                    q.cp(MTn.ap[:, hv0:hv0 + 4, :].rearrange("p a b -> p (a b)"), bank.ap[:, 0:512], reads=[bank], writes=[MTn])
                for gi in range(4):
                    hv0 = gi * 4
                    bank = K.ps()
                    for i in range(4):
                        mm(K, bank, bank.ap[:, i * 128:(i + 1) * 128], [(MTn.ap[:, hv0 + i, :], Qb.ap[:, hv0 + i, :])], reads=[MTn, Qb])
                    qv = Qf.ap[:, hv0:hv0 + 4, :].rearrange("p a b -> p (a b)")
                    K.dve.tt(qv, qv, bank.ap[:, 0:512], ALU.add, reads=[Qf, bank], writes=[Qf])
                    if lev < 5:
                        K.act.cp(Qb.ap[:, hv0:hv0 + 4, :], Qf.ap[:, hv0:hv0 + 4, :], reads=[Qf], writes=[Qb])
                    else:
                        K.act.cp(T2T.ap[:, hv0:hv0 + 4, :], Qf.ap[:, hv0:hv0 + 4, :], reads=[Qf], writes=[T2T])
                M, MT = Mn, MTn
                yield
            yield

        def scan_tile(oi, ti):
            t0 = ti * 128
            xt = xts[oi % 2]
            xT = xTs[oi % 2]
            qTn, kTn, v_tok = qTns[oi % nqb], kTns[oi % nqb], v_toks[oi % nqb]
            kend, beta16, egk, negeg = kends[oi % 2], beta16s[oi % 2], egks[oi % 2], negegs[oi % 2]
            T2T, attnT, of_sb = T2Ts[oi % 2], attnTs[oi % 2], of_sbs[oi % 2]
            if oi > 0 and (oi % TPS) == 0:
                K.dve.ts(S.ap[:], S.ap[:], K.carry.ap[:, 0:1], None, ALU.mult, reads=[S, K.carry], writes=[S])
                K.act.cp(S_bf.ap[:], S.ap[:], reads=[S], writes=[S_bf])
            chunks = [0, 1] if dr == 0 else [1, 0]
            for ck in chunks:
                r0, r1 = ck * 64, ck * 64 + 64
                z0, z1 = (64, 128) if ck == 0 else (0, 64)
                rs, vnb = resid1, vn1
                K.pool.memset(rs.ap[z0:z1, :, :], 0.0, writes=[rs])
                K.pool.memset(vnb.ap[z0:z1, :, :], 0.0, writes=[vnb])
                kbs, qbs, nbs, abs_, vbs = [], [], [], [], []
                for gi in range(4):
                    kb = K.ps()
                    for i in range(2):
                        hk = 2 * gi + i
                        mm(K, kb, kb.ap[:, i * 256:(i + 1) * 256], [(kTn.ap[:, hk, :], S_bf.ap[:, 2 * hk:2 * hk + 2, :])], reads=[kTn, S_bf])
                    kbs.append(kb)
                for gi in range(4):
                    qb_ = K.ps()
                    for i in range(2):
                        hk = 2 * gi + i
                        mm(K, qb_, qb_.ap[:, i * 256:(i + 1) * 256], [(qTn.ap[:, hk, :], S_bf.ap[:, 2 * hk:2 * hk + 2, :])], reads=[qTn, S_bf])
                    qbs.append(qb_)
                for gi in range(4):
                    for i in range(4):
                        hv = gi * 4 + i
                        K.dve.stt(rs.ap[r0:r1, hv, :], kbs[gi].ap[r0:r1, i * 128:(i + 1) * 128], negeg.ap[r0:r1, hv:hv + 1], v_tok.ap[r0:r1, hv, :],
                                  ALU.mult, ALU.add, reads=[kbs[gi], negeg, v_tok], writes=[rs])
                for gi in range(4):
                    for i in range(4):
                        hv = gi * 4 + i
                        K.act.actv(o_sb.ap[r0:r1, hv, :], qbs[gi].ap[r0:r1, i * 128:(i + 1) * 128], AF.Copy, scale=egk.ap[r0:r1, hv:hv + 1],
                                   reads=[qbs[gi], egk], writes=[o_sb])
                yield
                for gi in range(4):
                    nb = K.ps()
                    for i in range(4):
                        hv = gi * 4 + i
                        mm(K, nb, nb.ap[:, i * 128:(i + 1) * 128], [(T2T.ap[:, hv, :], rs.ap[:, hv, :])], reads=[T2T, rs])
                    nbs.append(nb)
                for gi in range(4):
                    for i in range(4):
                        hv = gi * 4 + i
                        K.act.actv(vnb.ap[r0:r1, hv, :], nbs[gi].ap[r0:r1, i * 128:(i + 1) * 128], AF.Copy, scale=beta16.ap[r0:r1, hv:hv + 1],
                                   reads=[nbs[gi], beta16], writes=[vnb])
                yield
                for gi in range(4):
                    ab_ = K.ps()
                    for i in range(4):
                        hv = gi * 4 + i
                        mm(K, ab_, ab_.ap[:, i * 128:(i + 1) * 128], [(attnT.ap[:, hv, :], vnb.ap[:, hv, :])], reads=[attnT, vnb])
                    abs_.append(ab_)
                for gi in range(4):
                    vb_ = K.ps()
                    for i in range(4):
                        hv = gi * 4 + i
                        mm(K, vb_, vb_.ap[:, i * 128:(i + 1) * 128], [(kend.ap[:, hv, :], vnb.ap[:, hv, :])], reads=[kend, vnb])
                    vbs.append(vb_)
                for gi in range(4):
                    for i in range(4):
                        hv = gi * 4 + i
                        K.dve.stt(S.ap[:, hv, :], S.ap[:, hv, :], egk.ap[:, 32 + ck * 16 + hv:33 + ck * 16 + hv], vbs[gi].ap[:, i * 128:(i + 1) * 128],
                                  ALU.mult, ALU.add, reads=[S, egk, vbs[gi]], writes=[S])
                    K.pool.cp(S_bf.ap[:, gi * 4:gi * 4 + 4, :], S.ap[:, gi * 4:gi * 4 + 4, :], reads=[S], writes=[S_bf])
                for gi in range(4):
                    for i in range(4):
                        hv = gi * 4 + i
                        K.dve.tt(o_sb.ap[r0:r1, hv, :], abs_[gi].ap[r0:r1, i * 128:(i + 1) * 128], o_sb.ap[r0:r1, hv, :], ALU.add,
                                 reads=[abs_[gi], o_sb], writes=[o_sb])
            yield
            ov = o_sb.ap[:].rearrange("p a b -> p (a b)")
            if dr == 1:
                K.pool.tt(ov, ov, of_sb.ap[:], ALU.add, reads=[o_sb, of_sb], writes=[o_sb])
            K.pool.dma(o_sb, [(OF[t0:t0 + 128, :], ov)], reads=[o_sb], writes=[K.ofbuf[ti]])
            yield

        if dr == 1:
            for _ in setup_tile(0, order[0]):
                pass
        for oi, ti in enumerate(order):
            if dr == 0:
                for _ in setup_tile(oi, ti):
                    pass
                for _ in scan_tile(oi, ti):
                    pass
                continue
            gs = scan_tile(oi, ti)
            gn = setup_tile(oi + 1, order[oi + 1]) if oi + 1 < len(order) else iter(())
            done_s = done_n = False
            while not (done_s and done_n):
                if not done_s:
                    try:
                        next(gs)
                    except StopIteration:
                        done_s = True
                if not done_n:
                    try:
                        next(gn)
                    except StopIteration:
                        done_n = True
        K.end_phase()


def gdn_finish_pass(K, cfg, Xin, Xout, OF, W, j, lng, lnb):
    L = cfg["L"]
    NT = L // 128
    with ExitStack() as ps_:
        K.begin_phase(ps_)
        K.set_rot(list(range(8)))
        stage = [K.sb("wst%d" % i, [128, 1024], F32) for i in range(2)]
        Wz, Wzb = load_weight_bf16(K, W["gdn_w_in"][j][:, 4096:6144], D, 2048, "Wz", stage, col_group=1024)
        Wout, Woutb = load_weight_bf16(K, W["gdn_w_out"][j], 2048, D, "Wout", stage, col_group=1024)
        g_b = load_bcast(K, lng, D, "lng")
        b_b = load_bcast(K, lnb, D, "lnb")
        nw_b = load_bcast(K, W["gdn_norm_w"][j], 128, "nw")
        ident, identb = K.ident, K.identb
        xts = [K.sb("xt%d" % i, [128, D], F32) for i in range(2)]
        xTs = [K.sb("xT%d" % i, [128, 8, 128], BF16) for i in range(2)]
        os_ = [K.sb("o%d" % i, [128, 16, 128], F32) for i in range(2)]
        sqo = K.sb("sqo", [128, 16, 128], F32)
        ss = K.sb("ss", [128, 16], F32)
        sz_bf = K.sb("sz_bf", [128, 2048], BF16)
        og_bf = K.sb("og_bf", [128, 2048], BF16)
        ogT = K.sb("ogT", [128, 16, 128], BF16)
        zs = [K.sb("z%d" % i, [128, D], F32) for i in range(2)]
        stats = K.sb("stats", [128, 2, 6], F32)
        mv = K.sb("mv", [128, 2], F32)
        rstd = K.sb("rstd", [128, 1], F32)
        eps_l2 = K.eps_tiles["l2"]
        for ti in range(NT):
            t0 = ti * 128
            xt, xT, o_sb = xts[ti % 2], xTs[ti % 2], os_[ti % 2]
            ov = o_sb.ap[:].rearrange("p a b -> p (a b)")
            K.sp.dma(xt, [(xt.ap[:], Xin[t0:t0 + 128, :])], writes=[xt])
            K.sp.dma(o_sb, [(ov, OF[t0:t0 + 128, :])], reads=[K.ofbuf[ti]], writes=[o_sb])
            transpose_x(K, xt.ap[:], xt, xT.ap, xT, ident.ap, ident, col0=0)
            for gi in range(4):
                bank = K.ps()
                mm(K, bank, bank.ap[:, 0:512], [(xT.ap[:, kc, :], Wz[:, kc, gi * 512:(gi + 1) * 512]) for kc in range(8)], reads=[xT] + Wzb)
                K.act.actv(sz_bf.ap[:, gi * 512:(gi + 1) * 512], bank.ap[:, 0:512], AF.Silu, reads=[bank], writes=[sz_bf])
            K.pool.tt(sqo.ap[:], o_sb.ap[:], o_sb.ap[:], ALU.mult, reads=[o_sb], writes=[sqo])
            K.dve.op(lambda e, a=ss.ap[:], b=sqo.ap[:]: e.tensor_reduce(a, b, mybir.AxisListType.X, ALU.add), reads=[sqo], writes=[ss])
            K.act.actv(ss.ap[:], ss.ap[:], AF.Sqrt, bias=eps_l2.ap[:], scale=1.0 / 128.0, reads=[ss, eps_l2], writes=[ss])
            K.dve.recip(ss.ap[:], ss.ap[:], reads=[ss], writes=[ss])
            for hv in range(16):
                K.dve.stt(o_sb.ap[:, hv, :], o_sb.ap[:, hv, :], ss.ap[:, hv:hv + 1], nw_b.ap[:], ALU.mult, ALU.mult, reads=[o_sb, ss, nw_b], writes=[o_sb])
            K.pool.tt(og_bf.ap[:], ov, sz_bf.ap[:], ALU.mult, reads=[o_sb, sz_bf], writes=[og_bf])
            for g in range(2):
                bank = K.ps()
                bb = bank_bf(bank)
                for i in range(8):
                    c = g * 8 + i
                    tr(K, bank, bb[:, i * 128:(i + 1) * 128], og_bf.ap[:, c * 128:(c + 1) * 128], identb.ap[:], reads=[og_bf, identb], signal=(i == 7))
                q = K.ev()
                q.cp(ogT.ap[:, g * 8:(g + 1) * 8, :], bb.rearrange("p (a b) -> p a b", a=8), reads=[bank], writes=[ogT])
            banks = []
            for fh in range(2):
                bank = K.ps()
                mm(K, bank, bank.ap[:, 0:512], [(ogT.ap[:, c, :], Wout[:, c, fh * 512:(fh + 1) * 512]) for c in range(16)], reads=[ogT] + Woutb)
                banks.append(bank)
            z = zs[ti % 2]
            residual_ln(K, banks, xt, xt.ap[:], 1.0 / ALPHA, g_b, b_b, z, "lna", (stats, mv, rstd))
            K.pool.dma(z, [(Xout[t0:t0 + 128, :], z.ap[:])], reads=[z])
        K.end_phase()


def gdn_phase(K, cfg, Xin, Xout, OF, W, j, lng, lnb):
    gdn_scan_pass(K, cfg, Xin, OF, K.QKV, W, j, 0)
    gdn_scan_pass(K, cfg, Xin, OF, K.QKV, W, j, 1)
    gdn_finish_pass(K, cfg, Xin, Xout, OF, W, j, lng, lnb)


_PROGRAM_CACHE = {}


def kernel(**inputs):
    xp = np.asarray(inputs["x_prompt"], dtype=np.float32)
    xsm = np.asarray(inputs["x_sample"], dtype=np.float32)
    mp = np.asarray(inputs["mem_prompt"], dtype=np.float32)
    ms = np.asarray(inputs["mem_sample"], dtype=np.float32)
    L, NSEG, SEG = 16384, 4, 4096
    consts = make_consts()
    cfg = {"L": L, "NSEG": NSEG, "SEG": SEG, "NCONST": consts.shape[1]}
    if "nc" not in _PROGRAM_CACHE:
        _PROGRAM_CACHE["nc"] = build_program(cfg)
    nc = _PROGRAM_CACHE["nc"]
    wmap = {name: np.ascontiguousarray(np.asarray(inputs[name], dtype=np.float32)) for name, _ in WEIGHT_SPECS}
    one = np.ones((128, 1), np.float32)
    zero = np.zeros((128, 1), np.float32)
    streams = [
        (np.ascontiguousarray(xp.reshape(L, D)), np.ascontiguousarray(np.broadcast_to(mp, (NSEG, NMEM, D))), one),
        (np.ascontiguousarray(xsm[0:4].reshape(L, D)), np.ascontiguousarray(ms[0:4]), zero),
        (np.ascontiguousarray(xsm[4:8].reshape(L, D)), np.ascontiguousarray(ms[4:8]), zero),
    ]
    in_maps = []
    for c in range(8):
        xs_, mem_, carry_ = streams[min(c, 2)]
        m = {"xs": xs_, "mem": mem_, "carry": carry_, "consts": consts}
        m.update(wmap)
        in_maps.append(m)
    res = run_bass_kernel_spmd(nc, in_maps, core_ids=list(range(8)))
    y_prompt = np.asarray(res.results[0]["y"], dtype=np.float32).reshape(1, L, D)
    y_sample = np.concatenate([np.asarray(res.results[1]["y"], dtype=np.float32).reshape(4, SEG, D),
                               np.asarray(res.results[2]["y"], dtype=np.float32).reshape(4, SEG, D)], axis=0)
    return (y_prompt, y_sample)
```

```python
import numpy as np
from contextlib import ExitStack
import concourse.bass as bass
import concourse.mybir as mybir
from concourse.bass_utils import run_bass_kernel_spmd

F32 = mybir.dt.float32
BF16 = mybir.dt.bfloat16
AF = mybir.ActivationFunctionType
ALU = mybir.AluOpType

D = 1024
DFF = 2816
NMEM = 256
DEPTH = 2
ALPHA = (2.0 * DEPTH) ** 0.25
LN_EPS = 1e-5
NORM_EPS = 1e-6
GLA_H, GLA_DK, GLA_DV = 4, 128, 256
GDN_HK, GDN_HV = 8, 16


class Sem:
    def __init__(self, h, name):
        self.h = h
        self.name = name
        self.count = 0


class Buf:
    def __init__(self, name, ap=None):
        self.name = name
        self.ap = ap
        self.wtoks = {}
        self.rtoks = {}
        self.dsem = None
        self.excl = False


class Q:
    def __init__(self, K, name):
        self.K = K
        self.name = name
        self.sem = K.new_sem("q_" + name)
        self.seen = {}
        self.ops = []

    def wait_tok(self, sem, val):
        if sem is self.sem and self.name == "pe":
            return
        if self.seen.get(sem, 0) >= val:
            return
        self.seen[sem] = val
        self.ops.append(("w", sem, val))

    def deps(self, reads, writes):
        for b in reads:
            for s, v in b.wtoks.items():
                self.wait_tok(s, v)
            if b.excl:
                for s, v in b.rtoks.items():
                    if s is not self.sem:
                        self.wait_tok(s, v)
        for b in writes:
            for s, v in b.wtoks.items():
                self.wait_tok(s, v)
            for s, v in b.rtoks.items():
                self.wait_tok(s, v)

    def mark(self, tok, reads, writes):
        s, v = tok
        for b in reads:
            if b.rtoks.get(s, 0) < v:
                b.rtoks[s] = v
        for b in writes:
            b.wtoks[s] = v
            b.rtoks = {}

    def op(self, fn, reads=(), writes=(), signal=True):
        self.deps(reads, writes)
        if signal:
            self.sem.count += 1
            tok = (self.sem, self.sem.count)
            self.ops.append(("i", fn, self.sem, 1))
        else:
            tok = (self.sem, self.sem.count + 1)
            self.ops.append(("i", fn, None, 0))
        self.mark(tok, reads, writes)
        return tok

    def tt(self, out, in0, in1, op, reads=(), writes=()):
        return self.op(lambda e: e.tensor_tensor(out, in0, in1, op), reads, writes)

    def ts(self, out, in0, s1, s2, op0, op1=None, reads=(), writes=(), accum=None):
        if op1 is None:
            return self.op(lambda e: e.tensor_scalar(out, in0, s1, None, op0), reads, writes)
        if accum is not None:
            return self.op(lambda e: e.tensor_scalar(out, in0, s1, s2, op0, op1, accum), reads, writes)
        return self.op(lambda e: e.tensor_scalar(out, in0, s1, s2, op0, op1), reads, writes)

    def stt(self, out, in0, scalar, in1, op0, op1, reads=(), writes=()):
        return self.op(lambda e: e.scalar_tensor_tensor(out, in0, scalar, in1, op0, op1), reads, writes)

    def actv(self, out, in_, func, bias=None, scale=None, accum=None, reads=(), writes=()):
        kw = {}
        if bias is not None:
            kw["bias"] = bias
        if scale is not None:
            kw["scale"] = scale
        if accum is not None:
            kw["accum_out"] = accum
        return self.op(lambda e: e.activation(out, in_, func, **kw), reads, writes)

    def cp(self, out, in_, scale=None, reads=(), writes=()):
        return self.op(copy_op(self, out, in_, scale), reads, writes)

    def memset(self, ap, val, writes=()):
        return self.op(lambda e: e.memset(ap, float(val)), (), writes)

    def recip(self, out, in_, reads=(), writes=()):
        return self.op(lambda e: e.reciprocal(out, in_), reads, writes)

    def rmax(self, out, in_, reads=(), writes=()):
        return self.op(lambda e: e.reduce_max(out, in_, mybir.AxisListType.X), reads, writes)

    def bnstats(self, out, in_, reads=(), writes=()):
        return self.op(lambda e: e.bn_stats(out, in_), reads, writes)

    def bnaggr(self, out, in_, reads=(), writes=()):
        return self.op(lambda e: e.bn_aggr(out, in_), reads, writes)

    def dma(self, sb, pairs, reads=(), writes=()):
        K = self.K
        if sb.dsem is None:
            sb.dsem = K.get_dsem()
        ds = sb.dsem
        if ds.count > 0:
            self.wait_tok(ds, ds.count)
        self.deps(reads, writes)
        for (o, i) in pairs:
            ds.count += 16
            self.ops.append(("d", o, i, ds))
        tok = (ds, ds.count)
        self.mark(tok, reads, writes)
        return tok

    def replay(self, eng):
        for o in self.ops:
            if o[0] == "w":
                eng.wait_ge(o[1].h, o[2])
            elif o[0] == "i":
                ins = o[1](eng)
                if o[2] is not None:
                    ins.then_inc(o[2].h, o[3])
            elif o[0] == "d":
                eng.dma_start(out=o[1], in_=o[2]).then_inc(o[3].h, 16)


class Kern:
    def __init__(self):
        self.nc = bass.Bass("TRN2", target_bir_lowering=False)
        self.sems = []
        self.free_dsems = []
        self.phase_bufs = []

    def new_sem(self, name):
        h = self.gstack.enter_context(self.nc.semaphore(name))
        s = Sem(h, name)
        self.sems.append(s)
        return s

    def get_dsem(self):
        if self.free_dsems:
            return self.free_dsems.pop()
        return self.new_sem("d%d" % len(self.sems))

    def setup(self, gstack):
        self.gstack = gstack
        self.pe = Q(self, "pe")
        self.act = Q(self, "act")
        self.dve = Q(self, "dve")
        self.pool = Q(self, "pool")
        self.sp = Q(self, "sp")
        self.queues = [self.pe, self.act, self.dve, self.pool, self.sp]
        self.banks = []
        for i in range(8):
            t = gstack.enter_context(self.nc.psum_tensor("psb%d" % i, [128, 512], F32))
            self.banks.append(Buf("psb%d" % i, t))
            self.banks[-1].excl = True
        self.bank_i = 0
        self.rot = list(range(8))
        self.rr = 0

    def set_rot(self, idx):
        self.rot = list(idx)
        self.bank_i = 0

    def ps(self):
        b = self.banks[self.rot[self.bank_i % len(self.rot)]]
        self.bank_i += 1
        return b

    def cst(self, name):
        o, w = CONST_LAYOUT[name]
        return self.consts.ap[:, o:o + w]

    def sb(self, name, shape, dt):
        nbytes = int(np.prod(shape[1:])) * (2 if dt == BF16 else 4)
        base = self.nc.sbuf_base
        t = self.pstack.enter_context(self.nc.sbuf_tensor("p%d_%s" % (self.phase_id, name), shape, dt))
        b = Buf(name, t)
        self.phase_bufs.append(b)
        return b

    def begin_phase(self, pstack):
        self.phase_id = getattr(self, "phase_id", 0) + 1
        self.pstack = pstack
        self.phase_bufs = []

    def barrier(self):
        for q in self.queues:
            for s in self.sems:
                if s.count > 0:
                    q.wait_tok(s, s.count)

    def end_phase(self):
        self.barrier()
        for b in self.phase_bufs:
            if b.dsem is not None:
                self.free_dsems.append(b.dsem)
                b.dsem = None
        self.phase_bufs = []

    def ev(self):
        self.rr += 1
        return self.act if (self.rr & 1) else self.dve

    def finish(self):
        self.barrier()
        nc = self.nc
        with nc.Block() as block:
            @block.tensor
            def _(e):
                self.pe.replay(e)

            @block.scalar
            def _(e):
                self.act.replay(e)

            @block.vector
            def _(e):
                self.dve.replay(e)

            @block.gpsimd
            def _(e):
                self.pool.replay(e)

            @block.sync
            def _(e):
                self.sp.replay(e)


def copy_op(q, out_ap, in_ap, scale=None):
    if q.name == "act":
        if scale is None:
            return lambda e: e.activation(out_ap, in_ap, AF.Copy)
        return lambda e: e.activation(out_ap, in_ap, AF.Copy, scale=float(scale))
    if scale is None:
        return lambda e: e.tensor_copy(out_ap, in_ap)
    return lambda e: e.tensor_scalar(out_ap, in_ap, float(scale), None, ALU.mult)


def mm(K, bank, out_ap, pairs, reads, start=True, stop=True):
    n = len(pairs)
    tok = None
    for i, (l, r) in enumerate(pairs):
        st = start and i == 0
        sp = stop and i == n - 1
        last = i == n - 1
        tok = K.pe.op(lambda e, l=l, r=r, st=st, sp=sp: e.matmul(out_ap, l, r, start=st, stop=sp),
                      reads=reads if (last or i == 0) else (), writes=[bank], signal=last)
    return tok


def tr(K, bank, out_ap, in_ap, ident_ap, reads, signal=True):
    return K.pe.op(lambda e: e.transpose(out_ap, in_ap, ident_ap), reads=reads, writes=[bank], signal=signal)


def load_weight_bf16(K, wdram, rows, cols, name, stage, col_group=2048):
    nch = rows // 128
    t = K.sb(name, [128, nch, cols], BF16)
    bufs = []
    si = 0
    for c in range(nch):
        b = Buf("%s_%d" % (name, c), t.ap)
        eng = K.act if (c & 1) else K.dve
        for c0 in range(0, cols, col_group):
            c1 = min(cols, c0 + col_group)
            st = stage[si % len(stage)]
            si += 1
            K.sp.dma(st, [(st.ap[:, 0:c1 - c0], wdram[c * 128:(c + 1) * 128, c0:c1])], writes=[st])
            o = t.ap[:, c, c0:c1]
            i = st.ap[:, 0:c1 - c0]
            eng.cp(o, i, reads=[st], writes=[b])
        bufs.append(b)
    return t.ap, bufs


def load_bcast(K, dram_row, n, name):
    b = K.sb(name, [128, n], F32)
    K.sp.dma(b, [(b.ap[:], dram_row.partition_broadcast(128))], writes=[b])
    return b


def transpose_x(K, xt_ap, x_buf, xT_ap, xT_buf, ident_ap, ident_buf, tcols=128, col0=0, nrows=128):
    for g in range(2):
        bank = K.ps()
        for j in range(4):
            kc = g * 4 + j
            tr(K, bank, bank.ap[:, j * 128:j * 128 + nrows], xt_ap[0:nrows, kc * 128:(kc + 1) * 128],
               ident_ap[0:nrows, 0:nrows], reads=[x_buf, ident_buf], signal=(j == 3))
        q = K.ev()
        src = bank.ap[:].rearrange("p (j t) -> p j t", j=4)[:, :, 0:nrows]
        dst = xT_ap[:, g * 4:(g + 1) * 4, col0:col0 + nrows]
        q.cp(dst, src, reads=[bank], writes=[xT_buf])


def residual_ln(K, bank_list, x_buf, x_ap, fscale, g_b, b_b, z_buf, eps_adj, st_bufs):
    z = z_buf.ap
    stats, mv, rstd = st_bufs
    for h, bank in enumerate(bank_list):
        sl = slice(h * 512, (h + 1) * 512)
        K.dve.stt(z[:, sl], bank.ap[:, 0:512], float(fscale), x_ap[:, sl], ALU.mult, ALU.add, reads=[bank, x_buf], writes=[z_buf])
        K.dve.bnstats(stats.ap[:, h, :], z[:, sl], reads=[z_buf], writes=[stats])
    K.dve.bnaggr(mv.ap[:], stats.ap[:].rearrange("p a b -> p (a b)"), reads=[stats], writes=[mv])
    eps = K.eps_tiles[eps_adj]
    K.act.actv(rstd.ap[:], mv.ap[:, 1:2], AF.Sqrt, bias=eps.ap[:], scale=1.0, reads=[mv, eps], writes=[rstd])
    K.dve.recip(rstd.ap[:], rstd.ap[:], reads=[rstd], writes=[rstd])
    K.dve.ts(z[:], z[:], mv.ap[:, 0:1], rstd.ap[:], ALU.subtract, ALU.mult, reads=[z_buf, mv, rstd], writes=[z_buf])
    K.pool.tt(z[:], z[:], g_b.ap[:], ALU.mult, reads=[z_buf, g_b], writes=[z_buf])
    K.pool.tt(z[:], z[:], b_b.ap[:], ALU.add, reads=[z_buf, b_b], writes=[z_buf])


def make_eps_tiles(K, gstack):
    K.eps_tiles = {}
    for key, val in (("ln1", LN_EPS), ("lna", LN_EPS / (ALPHA * ALPHA)), ("l2", NORM_EPS), ("one", 1.0)):
        t = gstack.enter_context(K.nc.sbuf_tensor("eps_" + key, [128, 1], F32))
        b = Buf("eps_" + key, t)
        K.dve.memset(t[:], val, writes=[b])
        K.eps_tiles[key] = b


def ffn_phase(K, cfg, Xin, Xout, w_in, w_out, lng, lnb, TW=256):
    L = cfg["L"]
    NS = TW // 128
    with ExitStack() as ps_:
        K.begin_phase(ps_)
        stage = [K.sb("wst%d" % i, [128, 1024], F32) for i in range(2)]
        W1, W1b = load_weight_bf16(K, w_in, D, 2 * DFF, "W1", stage, col_group=704)
        W2, W2b = load_weight_bf16(K, w_out, DFF, D, "W2", stage, col_group=1024)
        g_b = load_bcast(K, lng, D, "lng")
        b_b = load_bcast(K, lnb, D, "lnb")
        xts = [K.sb("xt%d" % i, [128, NS, D], F32) for i in range(2)]
        xTs = [K.sb("xT%d" % i, [128, 8, TW], BF16) for i in range(2)]
        hTs = [K.sb("hT0", [128, 22, TW], BF16)] * 2
        sgs = [K.sb("sg%d" % i, [128, TW], F32) for i in range(2)]
        zs = stage
        stats = K.sb("stats", [128, 2, 6], F32)
        mv = K.sb("mv", [128, 2], F32)
        rstd = K.sb("rstd", [128, 1], F32)
        ident = K.ident
        zi = 0
        for ti in range(L // TW):
            t0 = ti * TW
            xt = xts[ti % 2]
            xT = xTs[ti % 2]
            hT = hTs[ti % 2]
            K.sp.dma(xt, [(xt.ap[:, s, :], Xin[t0 + s * 128:t0 + (s + 1) * 128, :]) for s in range(NS)], writes=[xt])
            for s in range(NS):
                transpose_x(K, xt.ap[:, s, :], xt, xT.ap, xT, ident.ap, ident, col0=s * 128)
            for hc in range(22):
                bank = K.ps()
                mm(K, bank, bank.ap[:, 0:TW],
                   [(W1[:, kc, hc * 128:(hc + 1) * 128], xT.ap[:, kc, :]) for kc in range(8)], reads=[xT] + W1b)
                mm(K, bank, bank.ap[:, TW:2 * TW],
                   [(W1[:, kc, DFF + hc * 128:DFF + (hc + 1) * 128], xT.ap[:, kc, :]) for kc in range(8)], reads=[xT] + W1b)
                sg = sgs[hc % 2]
                K.act.actv(sg.ap[:], bank.ap[:, 0:TW], AF.Silu, reads=[bank], writes=[sg])
                K.dve.tt(hT.ap[:, hc, :], sg.ap[:], bank.ap[:, TW:2 * TW], ALU.mult, reads=[bank, sg], writes=[hT])
            for s in range(NS):
                banks = []
                for fh in range(2):
                    bank = K.ps()
                    mm(K, bank, bank.ap[:, 0:512],
                       [(hT.ap[:, hc, s * 128:(s + 1) * 128], W2[:, hc, fh * 512:(fh + 1) * 512]) for hc in range(22)],
                       reads=[hT] + W2b)
                    banks.append(bank)
                z = zs[zi % 2]
                zi += 1
                residual_ln(K, banks, xt, xt.ap[:, s, :], 0.5 / ALPHA, g_b, b_b, z, "lna", (stats, mv, rstd))
                K.pool.dma(z, [(Xout[t0 + s * 128:t0 + (s + 1) * 128, :], z.ap[:])], reads=[z])
        K.end_phase()


WEIGHT_SPECS = [
    ("ffn_w_in", [2, 2, D, 2 * DFF]), ("ffn_w_out", [2, 2, DFF, D]), ("ln_g", [2, 4, D]), ("ln_b", [2, 4, D]),
    ("gla_w_in", [1, D, 3072]), ("gla_w_gate_down", [1, 2, D, 16]), ("gla_w_gate_up", [1, 2, 16, 512]),
    ("gla_b_gate", [1, 2, 512]), ("gla_norm_w", [1, 256]), ("gla_w_out", [1, D, D]),
    ("gdn_w_in", [1, D, 6144]), ("gdn_conv_w", [1, 4, 4096]), ("gdn_w_ab", [1, 2, D, 32]),
    ("gdn_a_log", [1, 2, 16]), ("gdn_dt_bias", [1, 2, 16]), ("gdn_norm_w", [1, 128]), ("gdn_w_out", [1, 2048, D]),
    ("xa_w_q", [2, D, D]), ("xa_w_kv", [2, D, 2 * D]), ("xa_w_o", [2, D, D]),
]


def build_program(cfg):
    L = cfg["L"]
    NSEG = cfg["NSEG"]
    phases = cfg.get("phases", None)
    K = Kern()
    nc = K.nc
    xs = nc.dram_tensor("xs", [L, D], F32, kind="ExternalInput").ap()
    mem = nc.dram_tensor("mem", [NSEG, NMEM, D], F32, kind="ExternalInput").ap()
    carry = nc.dram_tensor("carry", [128, 1], F32, kind="ExternalInput").ap()
    consts = nc.dram_tensor("consts", [128, cfg["NCONST"]], F32, kind="ExternalInput").ap()
    W = {}
    for name, shape in WEIGHT_SPECS:
        W[name] = nc.dram_tensor(name, shape, F32, kind="ExternalInput").ap()
    y = nc.dram_tensor("y", [L, D], F32, kind="ExternalOutput").ap()
    XA = nc.dram_tensor("scrA", [L, D], F32).ap()
    XB = nc.dram_tensor("scrB", [L, D], F32).ap()
    OF = nc.dram_tensor("scrO", [L, 2048], F32).ap()
    QKVs = nc.dram_tensor("scrQKV", [L, 4096], BF16).ap()
    with ExitStack() as gs:
        K.setup(gs)
        make_eps_tiles(K, gs)
        ct = gs.enter_context(nc.sbuf_tensor("consts_sb", [128, cfg["NCONST"]], F32))
        K.consts = Buf("consts", ct)
        K.sp.dma(K.consts, [(ct[:], consts)], writes=[K.consts])
        K.ident = Buf("ident", ct[:, 0:128])
        K.ident.wtoks = K.consts.wtoks
        ibt = gs.enter_context(nc.sbuf_tensor("identb", [128, 128], BF16))
        K.identb = Buf("identb", ibt)
        K.dve.cp(ibt[:], ct[:, 0:128], reads=[K.consts], writes=[K.identb])
        cyt = gs.enter_context(nc.sbuf_tensor("carry_sb", [128, 1], F32))
        K.carry = Buf("carry", cyt)
        K.sp.dma(K.carry, [(cyt[:], carry)], writes=[K.carry])
        K.cfg = cfg
        K.ofbuf = [Buf("of%d" % i) for i in range(L // 128)]
        K.qkvbuf = [Buf("qkv%d" % i) for i in range(L // 128)]
        K.QKV = QKVs
        plan = []
        for i in range(DEPTH):
            plan.append(("ffn", i, 0))
            plan.append(("mix", i))
            plan.append(("xa", i))
            plan.append(("ffn", i, 1))
        if phases is not None:
            plan = plan[:phases]
        if cfg.get("only"):
            plan = {"ffn": [("ffn", 0, 0)], "xa": [("xa", 0)], "gla": [("mix", 0)], "gdn": [("mix", 1)]}[cfg["only"]]
        cur = xs
        for pi, ph in enumerate(plan):
            last = pi == len(plan) - 1
            dst = y if last else (XA if (pi % 2 == 0) else XB)
            if ph[0] == "ffn":
                i, j = ph[1], ph[2]
                li = 0 if j == 0 else 3
                ffn_phase(K, cfg, cur, dst, W["ffn_w_in"][i, j], W["ffn_w_out"][i, j], W["ln_g"][i, li], W["ln_b"][i, li])
            elif ph[0] == "mix":
                i = ph[1]
                if i % 2 == 0:
                    gla_phase(K, cfg, cur, dst, OF, W, i // 2, W["ln_g"][i, 1], W["ln_b"][i, 1])
                else:
                    gdn_phase(K, cfg, cur, dst, OF, W, i // 2, W["ln_g"][i, 1], W["ln_b"][i, 1])
            elif ph[0] == "xa":
                i = ph[1]
                xa_phase(K, cfg, cur, dst, mem, W["xa_w_q"][i], W["xa_w_kv"][i], W["xa_w_o"][i], W["ln_g"][i, 2], W["ln_b"][i, 2])
            cur = dst
        K.finish()
    return nc


def _build_consts():
    s_ = np.arange(128)[:, None]
    c_ = np.arange(128)[None, :]
    same = (s_ // 64) == (c_ // 64)
    pos_s = s_ % 64
    out = {}
    out["IDENT"] = np.eye(128, dtype=np.float32)
    for dn in ("f", "b"):
        if dn == "f":
            tri = same & (s_ <= c_)
            mid = same & (pos_s <= 31) & (c_ >= 0)
        else:
            tri = same & (s_ >= c_)
            mid = same & (pos_s >= 32) & (c_ >= 0)
        tri = tri.astype(np.float32)
        mid = mid.astype(np.float32)
        sm = same.astype(np.float32)
        out["M1" + dn] = (-1.0 / 16.0) * (tri - mid)
        out["M3" + dn] = (-1.0 / 16.0) * tri
        out["M4" + dn] = (-1.0 / 16.0) * (sm - tri)
        out["MASK4" + dn] = np.tile(tri, (1, 4))
        out["TRI" + dn] = tri
        out["AFTER" + dn] = sm - tri
        out["NEG" + dn] = (1.0 - tri) * (-30000.0)
    ind = np.zeros((128, 2), np.float32)
    ind[0:64, 0] = -1.0 / 16.0
    ind[64:128, 1] = -1.0 / 16.0
    out["IND"] = ind
    out["ONES"] = np.ones((128, 128), np.float32)
    out["NEGONES"] = -np.ones((128, 128), np.float32)
    out["OFFD4"] = np.tile(1.0 - np.eye(128, dtype=np.float32), (1, 4))
    out["IDENT4"] = np.tile(np.eye(128, dtype=np.float32), (1, 4))
    ch0 = np.zeros((128, 128), np.float32)
    ch0[0:64, :] = 1.0
    out["CH0"] = ch0
    out["CH1"] = 1.0 - ch0
    return out


_CONSTS = _build_consts()
CONST_LAYOUT = {}
_off = 0
for _k, _v in _CONSTS.items():
    CONST_LAYOUT[_k] = (_off, _v.shape[1])
    _off += _v.shape[1]
NCONST = _off


def make_consts():
    return np.ascontiguousarray(np.concatenate([_CONSTS[k] for k in CONST_LAYOUT], axis=1).astype(np.float32))


def bank_bf(bank):
    return bank.ap[:].bitcast(BF16)


def xa_phase(K, cfg, Xin, Xout, mem, w_q, w_kv, w_o, lng, lnb):
    L, NSEG, SEG = cfg["L"], cfg["NSEG"], cfg["SEG"]
    TW = 256
    NS = TW // 128
    with ExitStack() as ps_:
        K.begin_phase(ps_)
        stage = [K.sb("wst%d" % i, [128, 1024], F32) for i in range(2)]
        Wq, Wqb = load_weight_bf16(K, w_q, D, D, "Wq", stage, col_group=1024)
        Wkv, Wkvb = load_weight_bf16(K, w_kv, D, 2 * D, "Wkv", stage, col_group=1024)
        Wo, Wob = load_weight_bf16(K, w_o, D, D, "Wo", stage, col_group=1024)
        g_b = load_bcast(K, lng, D, "lng")
        b_b = load_bcast(K, lnb, D, "lnb")
        ident, identb = K.ident, K.identb
        KT = K.sb("KT", [128, NSEG, 8, NMEM], BF16)
        V = K.sb("V", [128, NSEG, 2, D], BF16)
        memT = K.sb("memT", [128, 8, NMEM], BF16)
        xts = [K.sb("xt%d" % i, [128, NS, D], F32) for i in range(2)]
        xTs = [K.sb("xT%d" % i, [128, 8, TW], BF16) for i in range(2)]
        qT = K.sb("qT", [128, 8, TW], BF16)
        oT = K.sb("oT", [128, 8, TW], BF16)
        pT = K.sb("pT", [128, 4, 2, TW], BF16)
        pf = K.sb("pf", [128, 4, NMEM], F32)
        pb = K.sb("pb", [128, 4, NMEM], BF16)
        mx = K.sb("mx", [128, 4], F32)
        sm = K.sb("sm", [128, 4], F32)
        zs = [K.sb("z%d" % i, [128, D], F32) for i in range(2)]
        stats = K.sb("stats", [128, 2, 6], F32)
        mv = K.sb("mv", [128, 2], F32)
        rstd = K.sb("rstd", [128, 1], F32)
        for sg in range(NSEG):
            for mt in range(2):
                xt = xts[mt]
                K.sp.dma(xt, [(xt.ap[:, 0, :], mem[sg, mt * 128:(mt + 1) * 128, :])], writes=[xt])
                transpose_x(K, xt.ap[:, 0, :], xt, memT.ap, memT, ident.ap, ident, col0=mt * 128)
            for dhc in range(8):
                bank = K.ps()
                mm(K, bank, bank.ap[:, 0:NMEM], [(Wkv[:, kc, dhc * 128:(dhc + 1) * 128], memT.ap[:, kc, :]) for kc in range(8)],
                   reads=[memT] + Wkvb)
                q = K.ev()
                q.cp(KT.ap[:, sg, dhc, :], bank.ap[:, 0:NMEM], reads=[bank], writes=[KT])
            for mc in range(2):
                for half in range(2):
                    bank = K.ps()
                    mm(K, bank, bank.ap[:, 0:512],
                       [(memT.ap[:, kc, mc * 128:(mc + 1) * 128], Wkv[:, kc, D + half * 512:D + (half + 1) * 512]) for kc in range(8)],
                       reads=[memT] + Wkvb)
                    q = K.ev()
                    q.cp(V.ap[:, sg, mc, half * 512:(half + 1) * 512], bank.ap[:, 0:512], reads=[bank], writes=[V])
        zi = 0
        for ti in range(L // TW):
            t0 = ti * TW
            sg = t0 // SEG
            xt = xts[ti % 2]
            xT = xTs[ti % 2]
            K.sp.dma(xt, [(xt.ap[:, s, :], Xin[t0 + s * 128:t0 + (s + 1) * 128, :]) for s in range(NS)], writes=[xt])
            for s in range(NS):
                transpose_x(K, xt.ap[:, s, :], xt, xT.ap, xT, ident.ap, ident, col0=s * 128)
            for dhc in range(8):
                bank = K.ps()
                mm(K, bank, bank.ap[:, 0:TW], [(Wq[:, kc, dhc * 128:(dhc + 1) * 128], xT.ap[:, kc, :]) for kc in range(8)],
                   reads=[xT] + Wqb)
                q = K.ev()
                q.cp(qT.ap[:, dhc, :], bank.ap[:, 0:TW], scale=1.0 / 16.0, reads=[bank], writes=[qT])
            for s in range(NS):
                sb_ = []
                for j in range(2):
                    bank = K.ps()
                    for hh in range(2):
                        h = 2 * j + hh
                        mm(K, bank, bank.ap[:, hh * 256:(hh + 1) * 256],
                           [(qT.ap[:, 2 * h + c, s * 128:(s + 1) * 128], KT.ap[:, sg, 2 * h + c, :]) for c in range(2)], reads=[qT, KT])
                    K.dve.rmax(mx.ap[:, 2 * j:2 * j + 2], bank.ap[:].rearrange("p (h m) -> p h m", h=2), reads=[bank], writes=[mx])
                    sb_.append(bank)
                K.dve.ts(mx.ap[:], mx.ap[:], -1.0, None, ALU.mult, reads=[mx], writes=[mx])
                for h in range(4):
                    bank = sb_[h // 2]
                    K.act.actv(pf.ap[:, h, :], bank.ap[:, (h % 2) * 256:(h % 2 + 1) * 256], AF.Exp, bias=mx.ap[:, h:h + 1], scale=1.0,
                               accum=sm.ap[:, h:h + 1], reads=[bank, mx], writes=[pf, sm])
                K.dve.recip(sm.ap[:], sm.ap[:], reads=[sm], writes=[sm])
                for h in range(4):
                    K.dve.ts(pb.ap[:, h, :], pf.ap[:, h, :], sm.ap[:, h:h + 1], None, ALU.mult, reads=[pf, sm], writes=[pb])
                bank = K.ps()
                bb = bank_bf(bank)
                for h in range(4):
                    for mc in range(2):
                        idx = h * 2 + mc
                        tr(K, bank, bb[:, idx * 128:(idx + 1) * 128], pb.ap[:, h, mc * 128:(mc + 1) * 128], identb.ap[:],
                           reads=[pb, identb], signal=(idx == 7))
                q = K.ev()
                q.cp(pT.ap[:, :, :, s * 128:(s + 1) * 128], bb.rearrange("p (h c t) -> p h c t", h=4, c=2), reads=[bank], writes=[pT])
            for dhc in range(8):
                h = dhc // 2
                bank = K.ps()
                mm(K, bank, bank.ap[:, 0:TW], [(V.ap[:, sg, mc, dhc * 128:(dhc + 1) * 128], pT.ap[:, h, mc, :]) for mc in range(2)],
                   reads=[V, pT])
                q = K.ev()
                q.cp(oT.ap[:, dhc, :], bank.ap[:, 0:TW], reads=[bank], writes=[oT])
            for s in range(NS):
                banks = []
                for fh in range(2):
                    bank = K.ps()
                    mm(K, bank, bank.ap[:, 0:512],
                       [(oT.ap[:, dhc, s * 128:(s + 1) * 128], Wo[:, dhc, fh * 512:(fh + 1) * 512]) for dhc in range(8)],
                       reads=[oT] + Wob)
                    banks.append(bank)
                z = zs[zi % 2]
                zi += 1
                residual_ln(K, banks, xt, xt.ap[:, s, :], 1.0 / ALPHA, g_b, b_b, z, "lna", (stats, mv, rstd))
                K.pool.dma(z, [(Xout[t0 + s * 128:t0 + (s + 1) * 128, :], z.ap[:])], reads=[z])
        K.end_phase()


def gla_phase(K, cfg, Xin, Xout, OF, W, j, lng, lnb):
    L, NSEG, SEG = cfg["L"], cfg["NSEG"], cfg["SEG"]
    NT = L // 128
    TPS = SEG // 128
    cst = K.cst
    with ExitStack() as ps_:
        K.begin_phase(ps_)
        K.set_rot([4, 5, 6, 7])
        OB = [K.banks[i] for i in range(4)]
        stage = [K.sb("wst%d" % i, [128, 1024], F32) for i in range(2)]
        Win, Winb = load_weight_bf16(K, W["gla_w_in"][j], D, 3072, "Win", stage, col_group=1024)
        Wout, Woutb = load_weight_bf16(K, W["gla_w_out"][j], D, D, "Wout", stage, col_group=1024)
        Wd = []
        for d in range(2):
            Wd.append(load_weight_bf16(K, W["gla_w_gate_down"][j, d], D, 16, "Wd%d" % d, stage, col_group=16))
        Wu = K.sb("Wu", [32, 2, 512], F32)
        K.dve.memset(Wu.ap[:], 0.0, writes=[Wu])
        K.sp.dma(Wu, [(Wu.ap[0:16, d, :], W["gla_w_gate_up"][j, d]) for d in range(2)] +
                 [(Wu.ap[16:17, d, :], W["gla_b_gate"][j, d:d + 1, :]) for d in range(2)], writes=[Wu])
        uext = K.sb("uext", [32, 128], F32)
        K.dve.memset(uext.ap[:], 1.0, writes=[uext])
        g_b = load_bcast(K, lng, D, "lng")
        b_b = load_bcast(K, lnb, D, "lnb")
        nw_b = load_bcast(K, W["gla_norm_w"][j], 256, "nw")
        ident, identb = K.ident, K.identb
        xts = [K.sb("xt%d" % i, [128, D], F32) for i in range(2)]
        xTs = [K.sb("xT%d" % i, [128, 8, 128], BF16) for i in range(2)]
        q_sb = K.sb("q_sb", [128, 512], F32)
        k_sb = K.sb("k_sb", [128, 512], F32)
        v_bf = K.sb("v_bf", [128, 1024], BF16)
        sr_bf = K.sb("sr_bf", [128, 1024], BF16)
        sp = K.sb("sp", [128, 512], F32)
        ets = [K.sb("et%d" % i, [128, 512], F32) for i in range(2)]
        qt_bf = K.sb("qt_bf", [128, 512], BF16)
        kt_bf = K.sb("kt_bf", [128, 512], BF16)
        qh_bf = K.sb("qh_bf", [128, 512], BF16)
        kh_bf = K.sb("kh_bf", [128, 512], BF16)
        qtT = K.sb("qtT", [128, 4, 128], BF16)
        ktT = K.sb("ktT", [128, 4, 128], BF16)
        qhA = K.sb("qhA", [128, 4, 128], BF16)
        qhB = K.sb("qhB", [128, 4, 128], BF16)
        K.dve.memset(qhA.ap[:], 0.0, writes=[qhA])
        K.dve.memset(qhB.ap[:], 0.0, writes=[qhB])
        scm = K.sb("scm", [128, 4, 128], BF16)
        S = K.sb("S", [128, 4, 256], F32)
        S_bf = K.sb("S_bf", [128, 4, 256], BF16)
        dec = K.sb("dec", [128, 8], F32)
        o_sb = K.sb("o_sb", [128, 1024], F32)
        of_sb = K.sb("of_sb", [128, 1024], F32)
        junk = K.sb("junk", [128, 256], F32)
        ss = K.sb("ss", [128, 4], F32)
        og_bf = K.sb("og_bf", [128, 1024], BF16)
        ogT = K.sb("ogT", [128, 8, 128], BF16)
        zs = [K.sb("z%d" % i, [128, D], F32) for i in range(2)]
        stats = K.sb("stats", [128, 2, 6], F32)
        mv = K.sb("mv", [128, 2], F32)
        rstd = K.sb("rstd", [128, 1], F32)
        eps_l2 = K.eps_tiles["l2"]
        one_t = K.eps_tiles["one"]
        zi = 0
        it = 0
        for dr in range(2):
            K.dve.memset(S.ap[:], 0.0, writes=[S])
            K.dve.memset(S_bf.ap[:], 0.0, writes=[S_bf])
            order = list(range(NT)) if dr == 0 else list(range(NT - 1, -1, -1))
            dn = "f" if dr == 0 else "b"
            for oi, ti in enumerate(order):
                t0 = ti * 128
                xt = xts[it % 2]
                xT = xTs[it % 2]
                it += 1
                if oi > 0 and (oi % TPS) == 0:
                    K.dve.ts(S.ap[:], S.ap[:], K.carry.ap[:, 0:1], None, ALU.mult, reads=[S, K.carry], writes=[S])
                    K.act.cp(S_bf.ap[:], S.ap[:], reads=[S], writes=[S_bf])
                K.sp.dma(xt, [(xt.ap[:], Xin[t0:t0 + 128, :])], writes=[xt])
                if dr == 1:
                    K.sp.dma(of_sb, [(of_sb.ap[:], OF[t0:t0 + 128, 0:1024])], reads=[K.ofbuf[ti]], writes=[of_sb])
                transpose_x(K, xt.ap[:], xt, xT.ap, xT, ident.ap, ident, col0=0)
                ngrp = 6 if dr == 1 else 4
                for gi in range(ngrp):
                    bank = K.ps()
                    mm(K, bank, bank.ap[:, 0:512], [(xT.ap[:, kc, :], Win[:, kc, gi * 512:(gi + 1) * 512]) for kc in range(8)],
                       reads=[xT] + Winb)
                    if gi == 0:
                        K.act.cp(q_sb.ap[:], bank.ap[:, 0:512], scale=128.0 ** -0.5, reads=[bank], writes=[q_sb])
                    elif gi == 1:
                        K.dve.cp(k_sb.ap[:], bank.ap[:, 0:512], reads=[bank], writes=[k_sb])
                    elif gi < 4:
                        q = K.ev()
                        q.cp(v_bf.ap[:, (gi - 2) * 512:(gi - 1) * 512], bank.ap[:, 0:512], reads=[bank], writes=[v_bf])
                    else:
                        K.act.actv(sr_bf.ap[:, (gi - 4) * 512:(gi - 3) * 512], bank.ap[:, 0:512], AF.Silu, reads=[bank], writes=[sr_bf])
                bank = K.ps()
                mm(K, bank, bank.ap[0:16, 0:128], [(Wd[dr][0][:, kc, :], xT.ap[:, kc, :]) for kc in range(8)], reads=[xT] + Wd[dr][1])
                K.dve.cp(uext.ap[0:16, :], bank.ap[0:16, 0:128], reads=[bank], writes=[uext])
                bank = K.ps()
                mm(K, bank, bank.ap[:, 0:512], [(uext.ap[:], Wu.ap[:, dr, :])], reads=[uext, Wu])
                et = ets[0]
                K.act.actv(et.ap[:], bank.ap[:, 0:512], AF.Exp, scale=-1.0, reads=[bank], writes=[et])
                K.act.actv(sp.ap[:], et.ap[:], AF.Ln, bias=one_t.ap[:], scale=1.0, reads=[et, one_t], writes=[sp])
                specs = [("M1" + dn, q_sb, qt_bf, 1.0), ("M1" + dn, k_sb, kt_bf, -1.0), ("M3" + dn, q_sb, qh_bf, 1.0), ("M4" + dn, k_sb, kh_bf, 1.0)]
                bank1 = None
                for si, (mname, src, dst, sc) in enumerate(specs):
                    if si != 1:
                        bank = K.ps()
                        mm(K, bank, bank.ap[:, 0:512], [(cst(mname), sp.ap[:])], reads=[K.consts, sp])
                        if si == 0:
                            bank1 = bank
                    else:
                        bank = bank1
                    et = ets[si % 2]
                    K.act.actv(et.ap[:], bank.ap[:, 0:512], AF.Exp, scale=sc, reads=[bank], writes=[et])
                    eng = K.dve if si % 2 == 0 else K.pool
                    eng.tt(dst.ap[:], src.ap[:], et.ap[:], ALU.mult, reads=[src, et], writes=[dst])
                bank = K.ps()
                for h in range(4):
                    mm(K, bank, bank.ap[:, 2 * h:2 * h + 2], [(sp.ap[:, h * 128:(h + 1) * 128], cst("IND"))], reads=[sp, K.consts])
                K.act.actv(dec.ap[:], bank.ap[:, 0:8], AF.Exp, reads=[bank], writes=[dec])
                bank = K.ps()
                bb = bank_bf(bank)
                for h in range(4):
                    tr(K, bank, bb[:, h * 128:(h + 1) * 128], qt_bf.ap[:, h * 128:(h + 1) * 128], identb.ap[:], reads=[qt_bf, identb], signal=False)
                for h in range(4):
                    tr(K, bank, bb[:, (4 + h) * 128:(5 + h) * 128], kt_bf.ap[:, h * 128:(h + 1) * 128], identb.ap[:], reads=[kt_bf, identb],
                       signal=(h == 3))
                K.act.cp(qtT.ap[:], bb[:, 0:512].rearrange("p (h t) -> p h t", h=4), reads=[bank], writes=[qtT])
                K.act.cp(ktT.ap[:], bb[:, 512:1024].rearrange("p (h t) -> p h t", h=4), reads=[bank], writes=[ktT])
                bank = K.ps()
                bb = bank_bf(bank)
                for h in range(4):
                    tr(K, bank, bb[:, h * 128:(h + 1) * 128], qh_bf.ap[:, h * 128:(h + 1) * 128], identb.ap[:], reads=[qh_bf, identb],
                       signal=(h == 3))
                bv = bb[:, 0:512].rearrange("p (h t) -> p h t", h=4)
                K.dve.cp(qhA.ap[:, :, 0:64], bv[:, :, 0:64], reads=[bank], writes=[qhA])
                K.dve.cp(qhB.ap[:, :, 64:128], bv[:, :, 64:128], reads=[bank], writes=[qhB])
                bank = K.ps()
                for h in range(4):
                    mm(K, bank, bank.ap[:, h * 128:(h + 1) * 128], [(ktT.ap[:, h, :], qtT.ap[:, h, :])], reads=[ktT, qtT])
                K.dve.tt(scm.ap[:].rearrange("p h t -> p (h t)"), bank.ap[:, 0:512], cst("MASK4" + dn), ALU.mult,
                         reads=[bank, K.consts], writes=[scm])
                for h in range(4):
                    ob = OB[h]
                    mm(K, ob, ob.ap[:, 0:256], [(scm.ap[:, h, :], v_bf.ap[:, h * 256:(h + 1) * 256])],
                       reads=[scm, v_bf], start=True, stop=False)
                chunks = [0, 1] if dr == 0 else [1, 0]
                for ci, ck in enumerate(chunks):
                    qh = qhA if ck == 0 else qhB
                    for h in range(4):
                        ob = OB[h]
                        mm(K, ob, ob.ap[:, 0:256], [(qh.ap[:, h, :], S_bf.ap[:, h, :])],
                           reads=[qh, S_bf], start=False, stop=(ci == 1))
                    r0 = ck * 64
                    kvb = []
                    for hp in range(2):
                        bank = K.ps()
                        for hh in range(2):
                            h = hp * 2 + hh
                            mm(K, bank, bank.ap[:, hh * 256:(hh + 1) * 256],
                               [(kh_bf.ap[r0:r0 + 64, h * 128:(h + 1) * 128], v_bf.ap[r0:r0 + 64, h * 256:(h + 1) * 256])], reads=[kh_bf, v_bf])
                        kvb.append(bank)
                    for h in range(4):
                        bank = kvb[h // 2]
                        K.dve.stt(S.ap[:, h, :], S.ap[:, h, :], dec.ap[:, 2 * h + ck:2 * h + ck + 1], bank.ap[:, (h % 2) * 256:(h % 2 + 1) * 256],
                                  ALU.mult, ALU.add, reads=[S, dec, bank], writes=[S])
                    K.act.cp(S_bf.ap[:], S.ap[:], reads=[S], writes=[S_bf])
                if dr == 0:
                    for h in range(4):
                        q = K.ev()
                        q.cp(o_sb.ap[:, h * 256:(h + 1) * 256], OB[h].ap[:, 0:256], reads=[OB[h]], writes=[o_sb])
                    K.pool.dma(o_sb, [(OF[t0:t0 + 128, 0:1024], o_sb.ap[:])], reads=[o_sb], writes=[K.ofbuf[ti]])
                else:
                    for h in range(4):
                        K.dve.tt(o_sb.ap[:, h * 256:(h + 1) * 256], OB[h].ap[:, 0:256], of_sb.ap[:, h * 256:(h + 1) * 256], ALU.add,
                                 reads=[OB[h], of_sb], writes=[o_sb])
                    for h in range(4):
                        K.act.actv(junk.ap[:], o_sb.ap[:, h * 256:(h + 1) * 256], AF.Square, accum=ss.ap[:, h:h + 1], reads=[o_sb], writes=[junk, ss])
                    K.act.actv(ss.ap[:], ss.ap[:], AF.Sqrt, bias=eps_l2.ap[:], scale=1.0 / 256.0, reads=[ss, eps_l2], writes=[ss])
                    K.dve.recip(ss.ap[:], ss.ap[:], reads=[ss], writes=[ss])
                    for h in range(4):
                        K.dve.stt(o_sb.ap[:, h * 256:(h + 1) * 256], o_sb.ap[:, h * 256:(h + 1) * 256], ss.ap[:, h:h + 1], nw_b.ap[:],
                                  ALU.mult, ALU.mult, reads=[o_sb, ss, nw_b], writes=[o_sb])
                    K.pool.tt(og_bf.ap[:], o_sb.ap[:], sr_bf.ap[:], ALU.mult, reads=[o_sb, sr_bf], writes=[og_bf])
                    bank = K.ps()
                    bb = bank_bf(bank)
                    for c in range(8):
                        tr(K, bank, bb[:, c * 128:(c + 1) * 128], og_bf.ap[:, c * 128:(c + 1) * 128], identb.ap[:], reads=[og_bf, identb],
                           signal=(c == 7))
                    K.act.cp(ogT.ap[:], bb.rearrange("p (c t) -> p c t", c=8), reads=[bank], writes=[ogT])
                    banks = []
                    for fh in range(2):
                        bank = K.ps()
                        mm(K, bank, bank.ap[:, 0:512], [(ogT.ap[:, c, :], Wout[:, c, fh * 512:(fh + 1) * 512]) for c in range(8)],
                           reads=[ogT] + Woutb)
                        banks.append(bank)
                    z = zs[zi % 2]
                    zi += 1
                    residual_ln(K, banks, xt, xt.ap[:], 1.0 / ALPHA, g_b, b_b, z, "lna", (stats, mv, rstd))
                    K.pool.dma(z, [(Xout[t0:t0 + 128, :], z.ap[:])], reads=[z])
        K.set_rot(list(range(8)))
        K.end_phase()


def gdn_scan_pass(K, cfg, Xin, OF, QKV, W, j, dr):
    L, NSEG, SEG = cfg["L"], cfg["NSEG"], cfg["SEG"]
    NT = L // 128
    TPS = SEG // 128
    cst = K.cst
    dn = "f" if dr == 0 else "b"
    with ExitStack() as ps_:
        K.begin_phase(ps_)
        K.set_rot(list(range(8)))
        stage_big = K.sb("wst", [128, 2048], F32)
        stage = [Buf("wst0", stage_big.ap[:, 0:1024]), Buf("wst1", stage_big.ap[:, 1024:2048])]
        K.phase_bufs.extend(stage)
        if dr == 0:
            Win, Winb = load_weight_bf16(K, W["gdn_w_in"][j][:, 0:4096], D, 4096, "Win", stage, col_group=1024)
        Wab, Wabb = load_weight_bf16(K, W["gdn_w_ab"][j, dr], D, 32, "Wab", stage, col_group=32)
        ident, identb = K.ident, K.identb
        cwT = K.sb("cwT", [128, 128], F32)
        cw = K.sb("cw", [128, 128], F32)
        K.sp.dma(cwT, [(cwT.ap[:], W["gdn_conv_w"][j].rearrange("t (c p) -> (t c) p", p=128))], writes=[cwT])
        bank = K.ps()
        tr(K, bank, bank.ap[:, 0:128], cwT.ap[:], ident.ap[:], reads=[cwT, ident])
        K.dve.cp(cw.ap[:], bank.ap[:, 0:128], reads=[bank], writes=[cw])
        alog = load_bcast(K, W["gdn_a_log"][j].rearrange("a b -> (a b)"), 32, "alog")
        dtb = load_bcast(K, W["gdn_dt_bias"][j].rearrange("a b -> (a b)"), 32, "dtb")
        negA = K.sb("negA", [128, 32], F32)
        K.act.actv(negA.ap[:], alog.ap[:], AF.Exp, reads=[alog], writes=[negA])
        K.dve.ts(negA.ap[:], negA.ap[:], -1.0, None, ALU.mult, reads=[negA], writes=[negA])
        ones_bf = K.sb("ones_bf", [128, 128], BF16)
        K.dve.cp(ones_bf.ap[:], cst("ONES"), reads=[K.consts], writes=[ones_bf])
        xts = [K.sb("xt0", [128, D], F32)] * 2
        xTs = [K.sb("xTe0", [128, 8, 132], BF16)] * 2
        xhp = K.sb("xhp", [2, D], F32)
        xhn = K.sb("xhn", [1, D], F32)
        accs = [K.sb("acc%d" % i, [128, 128], F32) for i in range(2)] * 2
        s4q = K.sb("s4q", [128, 8, 128], F32) if dr == 0 else None
        s4k = K.sb("s4k", [128, 8, 128], F32) if dr == 0 else None
        sq4s = [K.sb("sq4_%d" % i, [128, 4, 128], BF16) for i in range(2)]
        rn4s = [K.sb("rn4_%d" % i, [128, 512], F32) for i in range(2)]
        nqb = 2 if dr == 1 else 1
        qTns = [K.sb("qTn%d" % i, [128, 8, 128], BF16) for i in range(nqb)]
        kTns = [K.sb("kTn%d" % i, [128, 8, 128], BF16) for i in range(nqb)]
        v_toks = [K.sb("v_tok%d" % i, [128, 16, 128], BF16) for i in range(nqb)]
        vT = K.sb("vT", [128, 16, 128], BF16) if dr == 0 else None
        kends = [K.sb("kend%d" % i, [128, 16, 128], BF16) for i in range(nqb)] * (3 - nqb)
        t16 = K.sb("t16", [128, 16], F32)
        e16 = K.sb("e16", [128, 16], F32)
        g16 = K.sb("g16", [128, 16], F32)
        beta16s = [K.sb("beta16_%d" % i, [128, 16], F32) for i in range(2)]
        egks = [K.sb("egk%d" % i, [128, 64], F32) for i in range(2)]
        negegs = [K.sb("negeg%d" % i, [128, 16], F32) for i in range(2)]
        Pm = [K.sb("Pm%d" % i, [128, 128], F32) for i in range(2)]
        decT4s = [K.sb("decT4_0", [128, 4, 128], F32)] * 2
        decTs4s = [K.sb("decTs4_0", [128, 4, 128], F32)] * 2
        Ms = [K.sb("M%d" % i, [128, 16, 128], BF16) for i in range(2)]
        MTs = [K.sb("MT%d" % i, [128, 16, 128], BF16) for i in range(2)]
        Qf = Buf("Qf", stage_big.ap[:].rearrange("p (a b) -> p a b", a=16))
        K.phase_bufs.append(Qf)
        Qb = K.sb("Qb", [128, 16, 128], BF16)
        T2Ts = [K.sb("T2T%d" % i, [128, 16, 128], BF16) for i in range(nqb)] * (3 - nqb)
        attnTs = [K.sb("attnT%d" % i, [128, 16, 128], BF16) for i in range(nqb)] * (3 - nqb)
        resid1 = K.sb("resid", [128, 16, 128], BF16)
        vn1 = K.sb("vn", [128, 16, 128], BF16)
        for b in (resid1, vn1):
            K.dve.memset(b.ap[:], 0.0, writes=[b])
        S = K.sb("S", [128, 16, 128], F32)
        S_bf = K.sb("S_bf", [128, 16, 128], BF16)
        K.dve.memset(S.ap[:], 0.0, writes=[S])
        K.dve.memset(S_bf.ap[:], 0.0, writes=[S_bf])
        o_sb = K.sb("o_sb", [128, 16, 128], F32)
        of_sbs = [K.sb("of_sb%d" % i, [128, 2048], F32) for i in range(2)] if dr == 1 else [None, None]
        eps_l2 = K.eps_tiles["l2"]
        one_t = K.eps_tiles["one"]
        order = list(range(NT)) if dr == 0 else list(range(NT - 1, -1, -1))
        def setup_tile(oi, ti):
            t0 = ti * 128
            xt = xts[oi % 2]
            xT = xTs[oi % 2]
            qTn, kTn, v_tok = qTns[oi % nqb], kTns[oi % nqb], v_toks[oi % nqb]
            kend, beta16, egk, negeg = kends[oi % 2], beta16s[oi % 2], egks[oi % 2], negegs[oi % 2]
            T2T, attnT, of_sb = T2Ts[oi % 2], attnTs[oi % 2], of_sbs[oi % 2]
            K.sp.dma(xt, [(xt.ap[:], Xin[t0:t0 + 128, :])], writes=[xt])
            if dr == 0:
                if t0 == 0:
                    K.dve.memset(xhp.ap[:], 0.0, writes=[xhp])
                else:
                    K.sp.dma(xhp, [(xhp.ap[:], Xin[t0 - 2:t0, :])], writes=[xhp])
                    if t0 % SEG == 0:
                        K.dve.ts(xhp.ap[:], xhp.ap[:], K.carry.ap[0:2, 0:1], None, ALU.mult, reads=[xhp, K.carry], writes=[xhp])
                if t0 + 128 == L:
                    K.dve.memset(xhn.ap[:], 0.0, writes=[xhn])
                else:
                    K.sp.dma(xhn, [(xhn.ap[:], Xin[t0 + 128:t0 + 129, :])], writes=[xhn])
                    if (t0 + 128) % SEG == 0:
                        K.dve.ts(xhn.ap[:], xhn.ap[:], K.carry.ap[0:1, 0:1], None, ALU.mult, reads=[xhn, K.carry], writes=[xhn])
            if dr == 1:
                K.sp.dma(of_sb, [(of_sb.ap[:], OF[t0:t0 + 128, :])], reads=[K.ofbuf[ti]], writes=[of_sb])
                K.sp.dma(qTn, [(qTn.ap[:].rearrange("p a b -> p (a b)"), QKV[t0:t0 + 128, 0:1024])], reads=[K.qkvbuf[ti]], writes=[qTn])
                K.sp.dma(kTn, [(kTn.ap[:].rearrange("p a b -> p (a b)"), QKV[t0:t0 + 128, 1024:2048])], reads=[K.qkvbuf[ti]], writes=[kTn])
                K.sp.dma(v_tok, [(v_tok.ap[:].rearrange("p a b -> p (a b)"), QKV[t0:t0 + 128, 2048:4096])], reads=[K.qkvbuf[ti]], writes=[v_tok])
            transpose_x(K, xt.ap[:], xt, xT.ap, xT, ident.ap, ident, col0=2)
            if dr == 0:
                bank = K.ps()
                for kc in range(8):
                    tr(K, bank, bank.ap[:, kc * 4:kc * 4 + 2], xhp.ap[0:2, kc * 128:(kc + 1) * 128], ident.ap[0:2, 0:2], reads=[xhp, ident], signal=False)
                    tr(K, bank, bank.ap[:, kc * 4 + 2:kc * 4 + 3], xhn.ap[0:1, kc * 128:(kc + 1) * 128], ident.ap[0:1, 0:1], reads=[xhn, ident],
                       signal=(kc == 7))
                hv_ = bank.ap[:, 0:32].rearrange("p (k c) -> p k c", c=4)
                K.act.cp(xT.ap[:, :, 0:2], hv_[:, :, 0:2], reads=[bank], writes=[xT])
                K.act.cp(xT.ap[:, :, 130:131], hv_[:, :, 2:3], reads=[bank], writes=[xT])
                yield
                for c3 in range(0, 32, 3):
                    bank = K.ps()
                    cs = list(range(c3, min(32, c3 + 3)))
                    for i, c in enumerate(cs):
                        mm(K, bank, bank.ap[:, i * 132:i * 132 + 131], [(Win[:, kc, c * 128:(c + 1) * 128], xT.ap[:, kc, 0:131]) for kc in range(8)],
                           reads=[xT] + Winb)
                    for i, c in enumerate(cs):
                        acc = accs[c % 4]
                        eng = K.dve
                        src = bank.ap[:, i * 132:i * 132 + 131]
                        sbuf_ = bank
                        eng.ts(acc.ap[:], src[:, 0:128], cw.ap[:, c:c + 1], None, ALU.mult, reads=[sbuf_, cw], writes=[acc])
                        for tp in range(1, 4):
                            eng.stt(acc.ap[:], src[:, tp:tp + 128], cw.ap[:, tp * 32 + c:tp * 32 + c + 1], acc.ap[:], ALU.mult, ALU.add,
                                    reads=[sbuf_, cw, acc], writes=[acc])
                        if c < 8:
                            K.act.actv(s4q.ap[:, c, :], acc.ap[:], AF.Silu, reads=[acc], writes=[s4q])
                        elif c < 16:
                            K.act.actv(s4k.ap[:, c - 8, :], acc.ap[:], AF.Silu, reads=[acc], writes=[s4k])
                        else:
                            K.act.actv(vT.ap[:, c - 16, :], acc.ap[:], AF.Silu, reads=[acc], writes=[vT])
                for (src4, dstn, scl) in ((s4q, qTn, 128.0 ** -0.5), (s4k, kTn, None)):
                    for g in range(2):
                        sv = src4.ap[:, g * 4:(g + 1) * 4, :]
                        sq4, rn4 = sq4s[g], rn4s[g]
                        K.pool.tt(sq4.ap[:], sv, sv, ALU.mult, reads=[src4], writes=[sq4])
                        bank = K.ps()
                        for i in range(4):
                            mm(K, bank, bank.ap[:, i * 128:(i + 1) * 128], [(ones_bf.ap[:], sq4.ap[:, i, :])], reads=[ones_bf, sq4])
                        K.act.actv(rn4.ap[:], bank.ap[:, 0:512], AF.Ln, bias=eps_l2.ap[:], scale=1.0, reads=[bank, eps_l2], writes=[rn4])
                        K.act.actv(rn4.ap[:], rn4.ap[:], AF.Exp, scale=-0.5, reads=[rn4], writes=[rn4])
                        dv_ = dstn.ap[:, g * 4:(g + 1) * 4, :].rearrange("p a b -> p (a b)")
                        sv2 = sv.rearrange("p a b -> p (a b)")
                        if scl is None:
                            K.dve.tt(dv_, sv2, rn4.ap[:], ALU.mult, reads=[src4, rn4], writes=[dstn])
                        else:
                            K.dve.stt(dv_, sv2, float(scl), rn4.ap[:], ALU.mult, ALU.mult, reads=[src4, rn4], writes=[dstn])
            yield
            bank = K.ps()
            mm(K, bank, bank.ap[:, 0:32], [(xT.ap[:, kc, 2:130], Wab[:, kc, :]) for kc in range(8)], reads=[xT] + Wabb)
            K.dve.tt(t16.ap[:], bank.ap[:, 0:16], dtb.ap[:, dr * 16:(dr + 1) * 16], ALU.add, reads=[bank, dtb], writes=[t16])
            K.act.actv(e16.ap[:], t16.ap[:], AF.Exp, reads=[t16], writes=[e16])
            K.act.actv(e16.ap[:], e16.ap[:], AF.Ln, bias=one_t.ap[:], scale=1.0, reads=[e16, one_t], writes=[e16])
            K.dve.tt(g16.ap[:], e16.ap[:], negA.ap[:, dr * 16:(dr + 1) * 16], ALU.mult, reads=[e16, negA], writes=[g16])
            K.act.actv(beta16.ap[:], bank.ap[:, 16:32], AF.Exp, scale=-1.0, reads=[bank], writes=[beta16])
            K.dve.ts(beta16.ap[:], beta16.ap[:], 1.0, None, ALU.add, reads=[beta16], writes=[beta16])
            K.dve.recip(beta16.ap[:], beta16.ap[:], reads=[beta16], writes=[beta16])
            bank = K.ps()
            for i, mname in enumerate(("TRI" + dn, "AFTER" + dn, "CH0", "CH1")):
                mm(K, bank, bank.ap[:, i * 16:(i + 1) * 16], [(cst(mname), g16.ap[:])], reads=[K.consts, g16])
            K.act.actv(egk.ap[:], bank.ap[:, 0:64], AF.Exp, reads=[bank], writes=[egk])
            K.dve.ts(negeg.ap[:], egk.ap[:, 0:16], -1.0, None, ALU.mult, reads=[egk], writes=[negeg])
            yield
            bank = K.ps()
            bb = bank_bf(bank)
            for hk in range(8):
                tr(K, bank, bb[:, hk * 128:(hk + 1) * 128], kTn.ap[:, hk, :], identb.ap[:], reads=[kTn, identb], signal=(hk == 7))
            for hv in range(16):
                q = K.act if (hv % 2) else K.dve
                if q is K.act:
                    q.actv(kend.ap[:, hv, :], bb[:, (hv // 2) * 128:(hv // 2 + 1) * 128], AF.Copy, scale=egk.ap[:, 16 + hv:17 + hv],
                           reads=[bank, egk], writes=[kend])
                else:
                    q.ts(kend.ap[:, hv, :], bb[:, (hv // 2) * 128:(hv // 2 + 1) * 128], egk.ap[:, 16 + hv:17 + hv], None, ALU.mult,
                         reads=[bank, egk], writes=[kend])
            if dr == 0:
                for g in range(2):
                    bank = K.ps()
                    bb = bank_bf(bank)
                    for i in range(8):
                        tr(K, bank, bb[:, i * 128:(i + 1) * 128], vT.ap[:, g * 8 + i, :], identb.ap[:], reads=[vT, identb], signal=(i == 7))
                    q = K.ev()
                    q.cp(v_tok.ap[:, g * 8:(g + 1) * 8, :], bb.rearrange("p (a b) -> p a b", a=8), reads=[bank], writes=[v_tok])
                K.pool.dma(qTn, [(QKV[t0:t0 + 128, 0:1024], qTn.ap[:].rearrange("p a b -> p (a b)"))], reads=[qTn], writes=[K.qkvbuf[ti]])
                K.pool.dma(kTn, [(QKV[t0:t0 + 128, 1024:2048], kTn.ap[:].rearrange("p a b -> p (a b)"))], reads=[kTn], writes=[K.qkvbuf[ti]])
                K.pool.dma(v_tok, [(QKV[t0:t0 + 128, 2048:4096], v_tok.ap[:].rearrange("p a b -> p (a b)"))], reads=[v_tok], writes=[K.qkvbuf[ti]])
            yield
            M, MT = Ms[0], MTs[0]
            for gi in range(4):
                hv0 = gi * 4
                decT4, decTs4 = decT4s[gi % 2], decTs4s[gi % 2]
                gbank = K.ps()
                for i in range(2):
                    hk = 2 * gi + i
                    mm(K, gbank, gbank.ap[:, i * 128:(i + 1) * 128], [(kTn.ap[:, hk, :], kTn.ap[:, hk, :])], reads=[kTn])
                    mm(K, gbank, gbank.ap[:, (2 + i) * 128:(3 + i) * 128], [(kTn.ap[:, hk, :], qTn.ap[:, hk, :])], reads=[kTn, qTn])
                dbank = K.ps()
                for i in range(4):
                    hv = hv0 + i
                    P = Pm[i % 2]
                    K.dve.ts(P.ap[:], cst("TRI" + dn), g16.ap[:, hv:hv + 1], None, ALU.mult, reads=[K.consts, g16], writes=[P])
                    mm(K, dbank, dbank.ap[:, i * 128:(i + 1) * 128],
                       [(cst("ONES"), P.ap[:]), (P.ap[:], cst("NEGONES")), (cst("IDENT"), cst("NEG" + dn))], reads=[P, K.consts])
                K.act.actv(decT4.ap[:].rearrange("p a b -> p (a b)"), dbank.ap[:, 0:512], AF.Exp, reads=[dbank], writes=[decT4])
                K.pool.tt(decTs4.ap[:].rearrange("p a b -> p (a b)"), decT4.ap[:].rearrange("p a b -> p (a b)"), cst("OFFD4"), ALU.mult,
                          reads=[decT4, K.consts], writes=[decTs4])
                for i in range(4):
                    hv = hv0 + i
                    K.dve.stt(M.ap[:, hv, :], gbank.ap[:, (i // 2) * 128:(i // 2 + 1) * 128], beta16.ap[:, hv:hv + 1], decTs4.ap[:, i, :],
                              ALU.mult, ALU.mult, reads=[gbank, beta16, decTs4], writes=[M])
                    K.dve.tt(attnT.ap[:, hv, :], gbank.ap[:, (2 + i // 2) * 128:(3 + i // 2) * 128], decT4.ap[:, i, :], ALU.mult,
                             reads=[gbank, decT4], writes=[attnT])
                bank = K.ps()
                bb = bank_bf(bank)
                for i in range(4):
                    tr(K, bank, bb[:, i * 128:(i + 1) * 128], M.ap[:, hv0 + i, :], identb.ap[:], reads=[M, identb], signal=(i == 3))
                K.act.cp(MT.ap[:, hv0:hv0 + 4, :].rearrange("p a b -> p (a b)"), bb[:, 0:512], reads=[bank], writes=[MT])
                K.pool.tt(Qf.ap[:, hv0:hv0 + 4, :].rearrange("p a b -> p (a b)"), cst("IDENT4"),
                          M.ap[:, hv0:hv0 + 4, :].rearrange("p a b -> p (a b)"), ALU.subtract, reads=[K.consts, M], writes=[Qf])
                K.act.cp(Qb.ap[:, hv0:hv0 + 4, :], Qf.ap[:, hv0:hv0 + 4, :], reads=[Qf], writes=[Qb])
                yield
            for lev in range(1, 6):
                Mn, MTn = Ms[lev % 2], MTs[lev % 2]
                if lev < 5:
                    for gi in range(4):
                        hv0 = gi * 4
                        bank = K.ps()
                        for i in range(4):
                            mm(K, bank, bank.ap[:, i * 128:(i + 1) * 128], [(MT.ap[:, hv0 + i, :], M.ap[:, hv0 + i, :])], reads=[MT, M])
                        K.act.cp(Mn.ap[:, hv0:hv0 + 4, :].rearrange("p a b -> p (a b)"), bank.ap[:, 0:512], reads=[bank], writes=[Mn])
                for gi in range(4):
                    hv0 = gi * 4
                    bank = K.ps()
                    for i in range(4):
                        mm(K, bank, bank.ap[:, i * 128:(i + 1) * 128], [(M.ap[:, hv0 + i, :], MT.ap[:, hv0 + i, :])], reads=[MT, M])
                    q = K.act if (gi % 2) else K.dve
                    q.cp(MTn.ap[:, hv0:hv0 + 4, :].rearrange("p a b -> p (a b)"), bank.ap[:, 0:512], reads=[bank], writes=[MTn])
                for gi in range(4):
                    hv0 = gi * 4
                    bank = K.ps()
                    for i in range(4):
                        mm(K, bank, bank.ap[:, i * 128:(i + 1) * 128], [(MTn.ap[:, hv0 + i, :], Qb.ap[:, hv0 + i, :])], reads=[MTn, Qb])
                    qv = Qf.ap[:, hv0:hv0 + 4, :].rearrange("p a b -> p (a b)")
                    K.dve.tt(qv, qv, bank.ap[:, 0:512], ALU.add, reads=[Qf, bank], writes=[Qf])
                    if lev < 5:
                        K.act.cp(Qb.ap[:, hv0:hv0 + 4, :], Qf.ap[:, hv0:hv0 + 4, :], reads=[Qf], writes=[Qb])
                    else:
                        K.act.cp(T2T.ap[:, hv0:hv0 + 4, :], Qf.ap[:, hv0:hv0 + 4, :], reads=[Qf], writes=[T2T])
                M, MT = Mn, MTn
                yield
            yield

        def scan_tile(oi, ti):
            t0 = ti * 128
            xt = xts[oi % 2]
            xT = xTs[oi % 2]
            qTn, kTn, v_tok = qTns[oi % nqb], kTns[oi % nqb], v_toks[oi % nqb]
            kend, beta16, egk, negeg = kends[oi % 2], beta16s[oi % 2], egks[oi % 2], negegs[oi % 2]
            T2T, attnT, of_sb = T2Ts[oi % 2], attnTs[oi % 2], of_sbs[oi % 2]
            if oi > 0 and (oi % TPS) == 0:
                K.dve.ts(S.ap[:], S.ap[:], K.carry.ap[:, 0:1], None, ALU.mult, reads=[S, K.carry], writes=[S])
                K.act.cp(S_bf.ap[:], S.ap[:], reads=[S], writes=[S_bf])
            chunks = [0, 1] if dr == 0 else [1, 0]
            for ck in chunks:
                r0, r1 = ck * 64, ck * 64 + 64
                z0, z1 = (64, 128) if ck == 0 else (0, 64)
                rs, vnb = resid1, vn1
                K.pool.memset(rs.ap[z0:z1, :, :], 0.0, writes=[rs])
                K.pool.memset(vnb.ap[z0:z1, :, :], 0.0, writes=[vnb])
                kbs, qbs, nbs, abs_, vbs = [], [], [], [], []
                for gi in range(4):
                    kb = K.ps()
                    for i in range(2):
                        hk = 2 * gi + i
                        mm(K, kb, kb.ap[:, i * 256:(i + 1) * 256], [(kTn.ap[:, hk, :], S_bf.ap[:, 2 * hk:2 * hk + 2, :])], reads=[kTn, S_bf])
                    kbs.append(kb)
                for gi in range(4):
                    qb_ = K.ps()
                    for i in range(2):
                        hk = 2 * gi + i
                        mm(K, qb_, qb_.ap[:, i * 256:(i + 1) * 256], [(qTn.ap[:, hk, :], S_bf.ap[:, 2 * hk:2 * hk + 2, :])], reads=[qTn, S_bf])
                    qbs.append(qb_)
                for gi in range(4):
                    for i in range(4):
                        hv = gi * 4 + i
                        K.dve.stt(rs.ap[r0:r1, hv, :], kbs[gi].ap[r0:r1, i * 128:(i + 1) * 128], negeg.ap[r0:r1, hv:hv + 1], v_tok.ap[r0:r1, hv, :],
                                  ALU.mult, ALU.add, reads=[kbs[gi], negeg, v_tok], writes=[rs])
                for gi in range(4):
                    for i in range(4):
                        hv = gi * 4 + i
                        K.act.actv(o_sb.ap[r0:r1, hv, :], qbs[gi].ap[r0:r1, i * 128:(i + 1) * 128], AF.Copy, scale=egk.ap[r0:r1, hv:hv + 1],
                                   reads=[qbs[gi], egk], writes=[o_sb])
                yield
                for gi in range(4):
                    nb = K.ps()
                    for i in range(4):
                        hv = gi * 4 + i
                        mm(K, nb, nb.ap[:, i * 128:(i + 1) * 128], [(T2T.ap[:, hv, :], rs.ap[:, hv, :])], reads=[T2T, rs])
                    nbs.append(nb)
                for gi in range(4):
                    for i in range(4):
                        hv = gi * 4 + i
                        K.act.actv(vnb.ap[r0:r1, hv, :], nbs[gi].ap[r0:r1, i * 128:(i + 1) * 128], AF.Copy, scale=beta16.ap[r0:r1, hv:hv + 1],
                                   reads=[nbs[gi], beta16], writes=[vnb])
                yield
                for gi in range(4):
                    ab_ = K.ps()
                    for i in range(4):
                        hv = gi * 4 + i
                        mm(K, ab_, ab_.ap[:, i * 128:(i + 1) * 128], [(attnT.ap[:, hv, :], vnb.ap[:, hv, :])], reads=[attnT, vnb])
                    abs_.append(ab_)
                for gi in range(4):
                    vb_ = K.ps()
                    for i in range(4):
                        hv = gi * 4 + i
                        mm(K, vb_, vb_.ap[:, i * 128:(i + 1) * 128], [(kend.ap[:, hv, :], vnb.ap[:, hv, :])], reads=[kend, vnb])
                    vbs.append(vb_)
                for gi in range(4):
                    for i in range(4):
                        hv = gi * 4 + i
                        K.dve.stt(S.ap[:, hv, :], S.ap[:, hv, :], egk.ap[:, 32 + ck * 16 + hv:33 + ck * 16 + hv], vbs[gi].ap[:, i * 128:(i + 1) * 128],
                                  ALU.mult, ALU.add, reads=[S, egk, vbs[gi]], writes=[S])
                    K.pool.cp(S_bf.ap[:, gi * 4:gi * 4 + 4, :], S.ap[:, gi * 4:gi * 4 + 4, :], reads=[S], writes=[S_bf])
                for gi in range(4):
                    for i in range(4):
                        hv = gi * 4 + i
                        K.dve.tt(o_sb.ap[r0:r1, hv, :], abs_[gi].ap[r0:r1, i * 128:(i + 1) * 128], o_sb.ap[r0:r1, hv, :], ALU.add,
                                 reads=[abs_[gi], o_sb], writes=[o_sb])
            yield
            ov = o_sb.ap[:].rearrange("p a b -> p (a b)")
            if dr == 1:
                K.pool.tt(ov, ov, of_sb.ap[:], ALU.add, reads=[o_sb, of_sb], writes=[o_sb])
            K.pool.dma(o_sb, [(OF[t0:t0 + 128, :], ov)], reads=[o_sb], writes=[K.ofbuf[ti]])
            yield

        if dr == 1:
            for _ in setup_tile(0, order[0]):
                pass
        for oi, ti in enumerate(order):
            if dr == 0:
                for _ in setup_tile(oi, ti):
                    pass
                for _ in scan_tile(oi, ti):
                    pass
                continue
            gs = scan_tile(oi, ti)
            gn = setup_tile(oi + 1, order[oi + 1]) if oi + 1 < len(order) else iter(())
            done_s = done_n = False
            while not (done_s and done_n):
                if not done_s:
                    try:
                        next(gs)
                    except StopIteration:
                        done_s = True
                if not done_n:
                    try:
                        next(gn)
                    except StopIteration:
                        done_n = True
        K.end_phase()


def gdn_finish_pass(K, cfg, Xin, Xout, OF, W, j, lng, lnb):
    L = cfg["L"]
    NT = L // 128
    with ExitStack() as ps_:
        K.begin_phase(ps_)
        K.set_rot(list(range(8)))
        stage = [K.sb("wst%d" % i, [128, 1024], F32) for i in range(2)]
        Wz, Wzb = load_weight_bf16(K, W["gdn_w_in"][j][:, 4096:6144], D, 2048, "Wz", stage, col_group=1024)
        Wout, Woutb = load_weight_bf16(K, W["gdn_w_out"][j], 2048, D, "Wout", stage, col_group=1024)
        g_b = load_bcast(K, lng, D, "lng")
        b_b = load_bcast(K, lnb, D, "lnb")
        nw_b = load_bcast(K, W["gdn_norm_w"][j], 128, "nw")
        ident, identb = K.ident, K.identb
        xts = [K.sb("xt%d" % i, [128, D], F32) for i in range(2)]
        xTs = [K.sb("xT%d" % i, [128, 8, 128], BF16) for i in range(2)]
        os_ = [K.sb("o%d" % i, [128, 16, 128], F32) for i in range(2)]
        sqo = K.sb("sqo", [128, 16, 128], F32)
        ss = K.sb("ss", [128, 16], F32)
        sz_bf = K.sb("sz_bf", [128, 2048], BF16)
        og_bf = K.sb("og_bf", [128, 2048], BF16)
        ogT = K.sb("ogT", [128, 16, 128], BF16)
        zs = [K.sb("z%d" % i, [128, D], F32) for i in range(2)]
        stats = K.sb("stats", [128, 2, 6], F32)
        mv = K.sb("mv", [128, 2], F32)
        rstd = K.sb("rstd", [128, 1], F32)
        eps_l2 = K.eps_tiles["l2"]
        for ti in range(NT):
            t0 = ti * 128
            xt, xT, o_sb = xts[ti % 2], xTs[ti % 2], os_[ti % 2]
            ov = o_sb.ap[:].rearrange("p a b -> p (a b)")
            K.sp.dma(xt, [(xt.ap[:], Xin[t0:t0 + 128, :])], writes=[xt])
            K.sp.dma(o_sb, [(ov, OF[t0:t0 + 128, :])], reads=[K.ofbuf[ti]], writes=[o_sb])
            transpose_x(K, xt.ap[:], xt, xT.ap, xT, ident.ap, ident, col0=0)
            for gi in range(4):
                bank = K.ps()
                mm(K, bank, bank.ap[:, 0:512], [(xT.ap[:, kc, :], Wz[:, kc, gi * 512:(gi + 1) * 512]) for kc in range(8)], reads=[xT] + Wzb)
                K.act.actv(sz_bf.ap[:, gi * 512:(gi + 1) * 512], bank.ap[:, 0:512], AF.Silu, reads=[bank], writes=[sz_bf])
            K.pool.tt(sqo.ap[:], o_sb.ap[:], o_sb.ap[:], ALU.mult, reads=[o_sb], writes=[sqo])
            K.dve.op(lambda e, a=ss.ap[:], b=sqo.ap[:]: e.tensor_reduce(a, b, mybir.AxisListType.X, ALU.add), reads=[sqo], writes=[ss])
            K.act.actv(ss.ap[:], ss.ap[:], AF.Sqrt, bias=eps_l2.ap[:], scale=1.0 / 128.0, reads=[ss, eps_l2], writes=[ss])
            K.dve.recip(ss.ap[:], ss.ap[:], reads=[ss], writes=[ss])
            for hv in range(16):
                K.dve.stt(o_sb.ap[:, hv, :], o_sb.ap[:, hv, :], ss.ap[:, hv:hv + 1], nw_b.ap[:], ALU.mult, ALU.mult, reads=[o_sb, ss, nw_b], writes=[o_sb])
            K.pool.tt(og_bf.ap[:], ov, sz_bf.ap[:], ALU.mult, reads=[o_sb, sz_bf], writes=[og_bf])
            for g in range(2):
                bank = K.ps()
                bb = bank_bf(bank)
                for i in range(8):
                    c = g * 8 + i
                    tr(K, bank, bb[:, i * 128:(i + 1) * 128], og_bf.ap[:, c * 128:(c + 1) * 128], identb.ap[:], reads=[og_bf, identb], signal=(i == 7))
                q = K.ev()
                q.cp(ogT.ap[:, g * 8:(g + 1) * 8, :], bb.rearrange("p (a b) -> p a b", a=8), reads=[bank], writes=[ogT])
            banks = []
            for fh in range(2):
                bank = K.ps()
                mm(K, bank, bank.ap[:, 0:512], [(ogT.ap[:, c, :], Wout[:, c, fh * 512:(fh + 1) * 512]) for c in range(16)], reads=[ogT] + Woutb)
                banks.append(bank)
            z = zs[ti % 2]
            residual_ln(K, banks, xt, xt.ap[:], 1.0 / ALPHA, g_b, b_b, z, "lna", (stats, mv, rstd))
            K.pool.dma(z, [(Xout[t0:t0 + 128, :], z.ap[:])], reads=[z])
        K.end_phase()


def gdn_phase(K, cfg, Xin, Xout, OF, W, j, lng, lnb):
    gdn_scan_pass(K, cfg, Xin, OF, K.QKV, W, j, 0)
    gdn_scan_pass(K, cfg, Xin, OF, K.QKV, W, j, 1)
    gdn_finish_pass(K, cfg, Xin, Xout, OF, W, j, lng, lnb)


_PROGRAM_CACHE = {}


def kernel(**inputs):
    xp = np.asarray(inputs["x_prompt"], dtype=np.float32)
    xsm = np.asarray(inputs["x_sample"], dtype=np.float32)
    mp = np.asarray(inputs["mem_prompt"], dtype=np.float32)
    ms = np.asarray(inputs["mem_sample"], dtype=np.float32)
    L, NSEG, SEG = 16384, 4, 4096
    consts = make_consts()
    cfg = {"L": L, "NSEG": NSEG, "SEG": SEG, "NCONST": consts.shape[1]}
    if "nc" not in _PROGRAM_CACHE:
        _PROGRAM_CACHE["nc"] = build_program(cfg)
    nc = _PROGRAM_CACHE["nc"]
    wmap = {name: np.ascontiguousarray(np.asarray(inputs[name], dtype=np.float32)) for name, _ in WEIGHT_SPECS}
    one = np.ones((128, 1), np.float32)
    zero = np.zeros((128, 1), np.float32)
    streams = [
        (np.ascontiguousarray(xp.reshape(L, D)), np.ascontiguousarray(np.broadcast_to(mp, (NSEG, NMEM, D))), one),
        (np.ascontiguousarray(xsm[0:4].reshape(L, D)), np.ascontiguousarray(ms[0:4]), zero),
        (np.ascontiguousarray(xsm[4:8].reshape(L, D)), np.ascontiguousarray(ms[4:8]), zero),
    ]
    in_maps = []
    for c in range(8):
        xs_, mem_, carry_ = streams[min(c, 2)]
        m = {"xs": xs_, "mem": mem_, "carry": carry_, "consts": consts}
        m.update(wmap)
        in_maps.append(m)
    res = run_bass_kernel_spmd(nc, in_maps, core_ids=list(range(8)))
    y_prompt = np.asarray(res.results[0]["y"], dtype=np.float32).reshape(1, L, D)
    y_sample = np.concatenate([np.asarray(res.results[1]["y"], dtype=np.float32).reshape(4, SEG, D),
                               np.asarray(res.results[2]["y"], dtype=np.float32).reshape(4, SEG, D)], axis=0)
    return (y_prompt, y_sample)
```
